# Optimizing a Trainium2 kernel written in Bass

```python
import math
import jax, jax.numpy as jnp
from jax import lax
import numpy as np

D_MODEL = 2048
BATCH = 8
SEQ = 2048
DEPTH = 4

GRID_W = 64
CTX_LEN = 256
N_MIXERS = 3
D_INNER = 2 * D_MODEL
EPS = 1e-6
HY_ORDER = 2
HY_SHORT = 3
HY_EMB = 33
HY_BANDS = (HY_EMB - 1) // 2
HY_FILTER_W = 64
HY_FAST_PCT = 0.3
HY_SLOW_PCT = 1.5
HY_TARGET = 1e-2
CF_KERNEL = 31
RT_HEADS = 8
RT_QK = D_MODEL
RT_V = D_INNER
RT_DK = RT_QK // RT_HEADS
RT_DV = RT_V // RT_HEADS
RT_CHUNK = 128
ROPE_BASE = 10000.0
N_HY = (DEPTH + 2) // 3
N_CF = (DEPTH + 1) // 3
N_RT = DEPTH // 3

kernel_name = "hybrid_hyena_conformer_retention_dit"


def rmsnorm(x, g):
    xf = x.astype(jnp.float32)
    y = xf * lax.rsqrt(jnp.mean(xf * xf, axis=-1, keepdims=True) + EPS)
    return (y * g.astype(jnp.float32)).astype(x.dtype)


def layernorm(x, g, b):
    xf = x.astype(jnp.float32)
    mu = jnp.mean(xf, axis=-1, keepdims=True)
    var = jnp.mean(jnp.square(xf - mu), axis=-1, keepdims=True)
    y = (xf - mu) * lax.rsqrt(var + EPS)
    return (y * g.astype(jnp.float32) + b.astype(jnp.float32)).astype(x.dtype)


def dwconv(x, w, b):
    k = w.shape[0]
    pad = (k - 1) // 2
    y = lax.conv_general_dilated(x, w[:, None, :].astype(x.dtype), window_strides=(1,),
                                 padding=[(pad, pad)], dimension_numbers=('NWC', 'WIO', 'NWC'),
                                 feature_group_count=x.shape[-1])
    return y + b.astype(x.dtype)


def hyena_filters(L, f_w1, f_b1, f_fr1, f_w2, f_b2, f_fr2, f_w3):
    f32 = lambda a: a.astype(jnp.float32)
    t = jnp.linspace(0.0, 1.0, L, dtype=jnp.float32)[:, None]
    w = (2.0 * math.pi / L) * jnp.arange(L, dtype=jnp.float32)[:, None]
    f = jnp.linspace(1e-4, HY_BANDS - 1, HY_BANDS, dtype=jnp.float32)[None, :]
    feats = jnp.concatenate([t, jnp.cos(f * w), -jnp.sin(f * w)], axis=-1)
    h = jnp.sin(f32(f_fr1) * (feats @ f32(f_w1) + f32(f_b1)))
    h = jnp.sin(f32(f_fr2) * (h @ f32(f_w2) + f32(f_b2)))
    h = h @ f32(f_w3)
    deltas = jnp.abs(jnp.linspace(math.log(HY_TARGET) / HY_SLOW_PCT,
                                  math.log(HY_TARGET) / HY_FAST_PCT, D_INNER, dtype=jnp.float32))
    window = jnp.exp(-t * deltas)
    return h.reshape(L, HY_ORDER, 2, D_INNER) * window[:, None, None, :]


def long_conv(z, h_fwd, h_bwd, skip):
    L = z.shape[1]
    g = jnp.concatenate([h_fwd, jnp.zeros_like(h_fwd[:1]), h_bwd[:0:-1]], axis=0)
    spec = jnp.fft.rfft(g, axis=0)
    zf = z.astype(jnp.float32)
    y = jnp.fft.irfft(jnp.fft.rfft(zf, n=2 * L, axis=1) * spec[None], n=2 * L, axis=1)[:, :L]
    return (y + skip.astype(jnp.float32) * zf).astype(z.dtype)


def hyena_seq(h, w_in, conv_w, conv_b, filt, skip, w_out):
    L = h.shape[1]
    proj = h @ w_in
    u, gate = proj[..., :3 * D_INNER], proj[..., 3 * D_INNER:]
    u = dwconv(u, conv_w, conv_b)
    v, x1, x2 = jnp.split(u, 3, axis=-1)
    hf = hyena_filters(L, *filt)
    z = v
    for n, x_n in enumerate((x1, x2)):
        z = x_n * long_conv(z, hf[:, n, 0], hf[:, n, 1], skip[n])
    return (z * jax.nn.silu(gate)) @ w_out


def conformer_seq(h, w_in, dw_w, dw_b, ln_g, ln_b, w_out):
    proj = h @ w_in
    a, b, gate = jnp.split(proj, 3, axis=-1)
    u = a * jax.nn.sigmoid(b)
    u = dwconv(u, dw_w, dw_b)
    u = jax.nn.silu(layernorm(u, ln_g, ln_b))
    return (u * jax.nn.silu(gate)) @ w_out


def rope(x, pos):
    half = x.shape[-1] // 2
    inv = ROPE_BASE ** (-jnp.arange(half, dtype=jnp.float32) / half)
    ang = pos.astype(jnp.float32)[:, None] * inv
    cos, sin = jnp.cos(ang)[:, None, :], jnp.sin(ang)[:, None, :]
    x1, x2 = x[..., :half], x[..., half:]
    return jnp.concatenate([x1 * cos - x2 * sin, x1 * sin + x2 * cos], axis=-1).astype(x.dtype)


def axial_rope(x, rows, cols):
    d2 = x.shape[-1] // 2
    return jnp.concatenate([rope(x[..., :d2], rows), rope(x[..., d2:], cols)], axis=-1)


def to_chunks(t):
    b, L, h, d = t.shape
    return t.reshape(b, L // RT_CHUNK, RT_CHUNK, h, d).transpose(1, 0, 3, 2, 4)


def from_chunks(t):
    n, b, h, c, d = t.shape
    return t.transpose(1, 0, 3, 2, 4).reshape(b, n * c, h, d)


def retention_scan(q, k, v, log_g, s0):
    q, k, v = (a.astype(jnp.float32) for a in (q, k, v))
    idx = jnp.arange(RT_CHUNK, dtype=jnp.float32)
    rel = idx[:, None] - idx[None, :]
    lg = log_g.astype(jnp.float32)
    dmask = jnp.where(rel >= 0, jnp.exp(lg[:, None, None] * jnp.maximum(rel, 0.0)), 0.0)
    q_dec = jnp.exp(lg[:, None] * (idx + 1.0))[:, :, None]
    k_dec = jnp.exp(lg[:, None] * (RT_CHUNK - 1.0 - idx))[:, :, None]
    c_dec = jnp.exp(lg * RT_CHUNK)[:, None, None]

    def step(state, blk):
        qc, kc, vc = blk
        scores = jnp.einsum('bhid,bhjd->bhij', qc, kc) * dmask
        out = (jnp.einsum('bhij,bhjv->bhiv', scores, vc)
               + jnp.einsum('bhid,bhdv->bhiv', qc * q_dec, state))
        state = state * c_dec + jnp.einsum('bhjd,bhjv->bhdv', kc * k_dec, vc)
        return state, out

    s_final, out = lax.scan(step, s0, (to_chunks(q), to_chunks(k), to_chunks(v)))
    return from_chunks(out), s_final


def retention_bidir(q, k, v, log_g_f, log_g_b, s0_f, s0_b):
    o_f, s_f = retention_scan(q, k, v, log_g_f, s0_f)
    o_b, s_b = retention_scan(q[:, ::-1], k[:, ::-1], v[:, ::-1], log_g_b, s0_b)
    return o_f + o_b[:, ::-1], s_f, s_b


def retention_mixer(h_ctx, h_lat, w_in, decay_logit, gn_g, gn_b, w_out):
    log_g = jax.nn.log_sigmoid(decay_logit.astype(jnp.float32))

    def project(h):
        b, L, _ = h.shape
        q, k, v, gate = jnp.split(h @ w_in, [RT_QK, 2 * RT_QK, 2 * RT_QK + RT_V], axis=-1)
        q = q.reshape(b, L, RT_HEADS, RT_DK)
        k = k.reshape(b, L, RT_HEADS, RT_DK) * (RT_DK ** -0.5)
        v = v.reshape(b, L, RT_HEADS, RT_DV)
        return q, k, v, gate

    def finish(o, gate):
        b, L = o.shape[:2]
        mu = jnp.mean(o, axis=-1, keepdims=True)
        var = jnp.mean(jnp.square(o - mu), axis=-1, keepdims=True)
        o = ((o - mu) * lax.rsqrt(var + EPS)).reshape(b, L, RT_V)
        o = o * gn_g.astype(jnp.float32) + gn_b.astype(jnp.float32)
        return (o.astype(gate.dtype) * jax.nn.silu(gate)) @ w_out

    qc, kc, vc, gc = project(h_ctx)
    zero = jnp.zeros((h_ctx.shape[0], RT_HEADS, RT_DK, RT_DV), jnp.float32)
    o_ctx, s_ctx_f, s_ctx_b = retention_bidir(qc, kc, vc, log_g[0], log_g[1], zero, zero)
    ql, kl, vl, gl = project(h_lat)
    L = h_lat.shape[1]
    ROWS = L // GRID_W
    rows = jnp.repeat(jnp.arange(ROWS), GRID_W)
    cols = jnp.tile(jnp.arange(GRID_W), ROWS)
    ql, kl = axial_rope(ql, rows, cols), axial_rope(kl, rows, cols)
    o_lat, _, _ = retention_bidir(ql, kl, vl, log_g[0], log_g[1], s_ctx_f, s_ctx_b)
    return finish(o_ctx, gc), finish(o_lat, gl)


def setup_inputs(seed: int = 0) -> dict:
    key = jax.random.key(seed)
    ks = iter(jax.random.split(key, 40))
    nrm = lambda shape, s: s * jax.random.normal(next(ks), shape, jnp.float32)
    D, E = D_MODEL, D_INNER
    decay_logit0 = np.log(2.0 ** (5.0 + np.arange(RT_HEADS)) - 1.0).astype(np.float32)
    return {
        'x': nrm((BATCH, SEQ, D), 1.0),
        'c': nrm((BATCH, D), 1.0),
        'ctx': nrm((BATCH, CTX_LEN, D), 1.0),
        'c_ctx': nrm((D,), 1.0),
        'ada_w': nrm((DEPTH, D, 3 * D), 0.5 * D ** -0.5),
        'ada_b': nrm((DEPTH, 3 * D), 0.01),
        'norm_g': 1.0 + nrm((DEPTH, D), 0.02),
        'final_norm_g': 1.0 + nrm((D,), 0.02),
        'hy_w_in': nrm((N_HY, D, 4 * E), D ** -0.5),
        'hy_conv_w': nrm((N_HY, HY_SHORT, 3 * E), HY_SHORT ** -0.5),
        'hy_conv_b': nrm((N_HY, 3 * E), 0.01),
        'hy_f_w1': nrm((N_HY, HY_EMB, HY_FILTER_W), HY_EMB ** -0.5),
        'hy_f_b1': nrm((N_HY, HY_FILTER_W), 0.1),
        'hy_f_fr1': 1.0 + nrm((N_HY, HY_FILTER_W), 0.1),
        'hy_f_w2': nrm((N_HY, HY_FILTER_W, HY_FILTER_W), HY_FILTER_W ** -0.5),
        'hy_f_b2': nrm((N_HY, HY_FILTER_W), 0.1),
        'hy_f_fr2': 1.0 + nrm((N_HY, HY_FILTER_W), 0.1),
        'hy_f_w3': nrm((N_HY, HY_FILTER_W, HY_ORDER * 2 * E), 0.03 * HY_FILTER_W ** -0.5),
        'hy_skip': nrm((N_HY, HY_ORDER, E), 1.0),
        'hy_w_out': nrm((N_HY, E, D), E ** -0.5),
        'cf_w_in': nrm((N_CF, D, 3 * E), D ** -0.5),
        'cf_dw_w': nrm((N_CF, CF_KERNEL, E), CF_KERNEL ** -0.5),
        'cf_dw_b': nrm((N_CF, E), 0.01),
        'cf_ln_g': 1.0 + nrm((N_CF, E), 0.02),
        'cf_ln_b': nrm((N_CF, E), 0.01),
        'cf_w_out': nrm((N_CF, E, D), E ** -0.5),
        'rt_w_in': nrm((N_RT, D, 2 * RT_QK + 2 * RT_V), D ** -0.5),
        'rt_decay_logit': jnp.asarray(decay_logit0) + nrm((N_RT, 2, RT_HEADS), 0.1),
        'rt_gn_g': 1.0 + nrm((N_RT, RT_V), 0.02),
        'rt_gn_b': nrm((N_RT, RT_V), 0.01),
        'rt_w_out': nrm((N_RT, RT_V, D), RT_V ** -0.5),
    }


def reference(x, c, ctx, c_ctx, ada_w, ada_b, norm_g, final_norm_g,
              hy_w_in, hy_conv_w, hy_conv_b, hy_f_w1, hy_f_b1, hy_f_fr1, hy_f_w2, hy_f_b2,
              hy_f_fr2, hy_f_w3, hy_skip, hy_w_out,
              cf_w_in, cf_dw_w, cf_dw_b, cf_ln_g, cf_ln_b, cf_w_out,
              rt_w_in, rt_decay_logit, rt_gn_g, rt_gn_b, rt_w_out):
    mod_lat = jax.nn.silu(c)[:, None, :]
    mod_ctx = jax.nn.silu(c_ctx)[None, None, :]
    for i in range(DEPTH):
        kind, j = i % N_MIXERS, i // N_MIXERS
        last = i == DEPTH - 1
        need_ctx = (not last) or kind == 2
        shift_l, scale_l, gate_l = jnp.split(mod_lat @ ada_w[i] + ada_b[i], 3, axis=-1)
        h_lat = rmsnorm(x, norm_g[i]) * (1.0 + scale_l) + shift_l
        y_ctx = None
        if need_ctx:
            shift_c, scale_c, gate_c = jnp.split(mod_ctx @ ada_w[i] + ada_b[i], 3, axis=-1)
            h_ctx = rmsnorm(ctx, norm_g[i]) * (1.0 + scale_c) + shift_c
        if kind == 0:
            filt = (hy_f_w1[j], hy_f_b1[j], hy_f_fr1[j], hy_f_w2[j], hy_f_b2[j], hy_f_fr2[j], hy_f_w3[j])
            p = (hy_w_in[j], hy_conv_w[j], hy_conv_b[j], filt, hy_skip[j], hy_w_out[j])
            y_lat = hyena_seq(h_lat, *p)
            if need_ctx:
                y_ctx = hyena_seq(h_ctx, *p)
        elif kind == 1:
            p = (cf_w_in[j], cf_dw_w[j], cf_dw_b[j], cf_ln_g[j], cf_ln_b[j], cf_w_out[j])
            y_lat = conformer_seq(h_lat, *p)
            if need_ctx:
                y_ctx = conformer_seq(h_ctx, *p)
        else:
            y_ctx, y_lat = retention_mixer(h_ctx, h_lat, rt_w_in[j], rt_decay_logit[j],
                                           rt_gn_g[j], rt_gn_b[j], rt_w_out[j])
        x = x + gate_l * y_lat
        if not last:
            ctx = ctx + gate_c * y_ctx
    return rmsnorm(x, final_norm_g)
```

```python
import contextlib, math
import numpy as np
import concourse.bass as bass
import concourse.mybir as mybir
from concourse.bass_utils import run_bass_kernel_spmd

COMPUTE = ("tensor", "vector", "scalar", "gpsimd")
DMAQ = ("sync", "gpsimd")
NDS = 6


class Op:
    __slots__ = ("eng", "fn", "waits", "idx", "is_dma", "tick", "dsem", "dval", "name")


class Rec:
    def __init__(self):
        self.ops = {e: [] for e in ("tensor", "vector", "scalar", "gpsimd", "sync")}
        self.last_w = {}
        self.readers = {}
        self.ndma = {q: 0 for q in DMAQ}
        self.dma_tok = {q: [] for q in DMAQ}

    def op(self, eng, fn, reads=(), writes=(), dma=False, pe_accum=False, name=None):
        o = Op(); o.eng = eng; o.fn = fn; o.waits = []; o.is_dma = dma; o.tick = False
        o.name = name; o.dsem = None; o.dval = None
        deps = []
        for r in reads:
            t = self.last_w.get(r)
            if t is not None: deps.append(t)
        for w in writes:
            t = self.last_w.get(w)
            if t is not None and not (pe_accum and t[0] == "c" and t[1] == "tensor"):
                deps.append(t)
            deps.extend(self.readers.get(w, ()))
        if dma:
            q = eng; i = self.ndma[q]; self.ndma[q] += 1
            if i >= NDS: deps.append(self.dma_tok[q][i - NDS])
            tok = ("d", q, i % NDS, 16 * (i // NDS + 1))
            self.dma_tok[q].append(tok); o.dsem = (q, i % NDS)
        else:
            tok = ("c", eng, o)
        seen = set()
        for d in deps:
            if d is tok or id(d) in seen: continue
            seen.add(id(d))
            if d[0] == "c": d[2].tick = True
            o.waits.append(d)
        self.ops[eng].append(o)
        for w in writes:
            self.last_w[w] = tok; self.readers[w] = []
        for r in reads:
            if r not in writes:
                lst = self.readers.setdefault(r, [])
                if tok[0] == "c":
                    lst[:] = [t for t in lst if not (t[0] == "c" and t[1] == eng)]
                lst.append(tok)
        return tok

    def final_wait(self, eng, toks):
        o = Op(); o.eng = eng; o.fn = None; o.waits = list(toks); o.is_dma = False
        o.tick = False; o.name = "final"; o.dsem = None; o.dval = None
        for d in toks:
            if d[0] == "c": d[2].tick = True
        self.ops[eng].append(o)

    def emit(self, nc, block_engines):
        import contextlib
        for e in COMPUTE:
            n = 0
            for o in self.ops[e]:
                if o.tick and not o.is_dma:
                    n += 1; o.idx = n
        with contextlib.ExitStack() as st:
            csem = {e: st.enter_context(nc.semaphore("c_" + e)) for e in COMPUTE}
            dsem = {(q, k): st.enter_context(nc.semaphore(f"d_{q}{k}")) for q in DMAQ for k in range(NDS)}
            blk = st.enter_context(nc.Block())
            for e, lst in self.ops.items():
                if not lst: continue
                def body(eng, lst=lst, e=e):
                    known = {}
                    for o in lst:
                        need = {}
                        for d in o.waits:
                            if d[0] == "c": s, v = csem[d[1]], d[2].idx
                            else: s, v = dsem[(d[1], d[2])], d[3]
                            k = id(s)
                            if known.get(k, 0) >= v: continue
                            if k not in need or need[k][1] < v: need[k] = (s, v)
                        for k, (s, v) in need.items():
                            eng.wait_ge(s, v); known[k] = v
                        if o.fn is None: continue
                        ins = o.fn(eng)
                        if o.is_dma: ins.then_inc(dsem[o.dsem], 16)
                        elif o.tick: ins.then_inc(csem[e], 1)
                getattr(blk, e)(body)


F32 = mybir.dt.float32
AF = mybir.ActivationFunctionType
ALU = mybir.AluOpType
AX = mybir.AxisListType


class DT:
    def __init__(self, nc, name, shape, kind="Internal", rb=128, cb=512):
        self.name = name; self.shape = tuple(shape)
        self.ap = nc.dram_tensor(name, list(shape), F32, kind=kind).ap()
        self.rb = rb; self.cb = cb

    def keys(self, r0, r1, c0, c1):
        return [(self.name, i, j) for i in range(r0 // self.rb, (r1 - 1) // self.rb + 1)
                for j in range(c0 // self.cb, (c1 - 1) // self.cb + 1)]


class Pool:
    def __init__(self, nc, st, name, shape, n, psum=False):
        mk = nc.psum_tensor if psum else nc.sbuf_tensor
        self.bufs = [st.enter_context(mk(f"{name}{i}", list(shape), F32)) for i in range(n)]
        self.keys = [(name, i) for i in range(n)]
        self.i = 0; self.pinned = set()

    def next(self):
        while True:
            k = self.i % len(self.bufs); self.i += 1
            if k not in self.pinned:
                return self.bufs[k], self.keys[k]

    def pin(self):
        b, key = self.next()
        self.pinned.add(self.keys.index(key))
        return b, key

    def unpin(self, key):
        self.pinned.discard(self.keys.index(key))


class K:
    def __init__(self, nc, st):
        self.nc = nc; self.st = st; self.R = Rec(); self.pools = {}
        self.qi = 0

    def pool(self, name, shape, n, psum=False):
        if name not in self.pools:
            self.pools[name] = Pool(self.nc, self.st, name, shape, n, psum)
        return self.pools[name]

    def dma(self, out, in_, reads, writes, q=None):
        if q is None:
            q = "sync"
        self.R.op(q, lambda e: e.dma_start(out=out, in_=in_), reads=reads, writes=writes, dma=True)

    def load(self, sb, sbkey, dt, r0, r1, c0, c1, pat=None, q="sync", **kw):
        src = dt.ap[r0:r1, c0:c1]
        if pat: src = src.rearrange(pat, **kw)
        self.dma(sb, src, dt.keys(r0, r1, c0, c1), [sbkey], q=q)

    def store(self, dt, r0, r1, c0, c1, sb, sbkey, pat=None, q="gpsimd", **kw):
        dst = dt.ap[r0:r1, c0:c1]
        if pat: dst = dst.rearrange(pat, **kw)
        self.dma(dst, sb, [sbkey], dt.keys(r0, r1, c0, c1), q=q)

    def mm(self, ps, pskey, lhsT, rhs, start, stop, reads):
        self.R.op("tensor", lambda e: e.matmul(ps, lhsT=lhsT, rhs=rhs, start=start, stop=stop),
                  reads=reads, writes=[pskey], pe_accum=not start)

    def act(self, out, in_, func, reads, writes, eng="scalar", **kw):
        self.R.op(eng, lambda e: e.activation(out=out, in_=in_, func=func, **kw), reads=reads, writes=writes)

    def tt(self, out, a, b, op, reads, writes, eng="vector"):
        self.R.op(eng, lambda e: e.tensor_tensor(out=out, in0=a, in1=b, op=op), reads=reads, writes=writes)

    def ts(self, out, a, s1, s2, op0, op1, reads, writes, eng="vector", **kw):
        if s2 is None:
            self.R.op(eng, lambda e: e.tensor_scalar(out=out, in0=a, scalar1=s1, scalar2=None, op0=op0, **kw), reads=reads, writes=writes)
        else:
            self.R.op(eng, lambda e: e.tensor_scalar(out=out, in0=a, scalar1=s1, scalar2=s2, op0=op0, op1=op1, **kw), reads=reads, writes=writes)

    def stt(self, out, a, s, b, op0, op1, reads, writes, eng="vector"):
        self.R.op(eng, lambda e: e.scalar_tensor_tensor(out=out, in0=a, scalar=s, in1=b, op0=op0, op1=op1), reads=reads, writes=writes)

    def copy(self, out, in_, reads, writes, eng="vector"):
        self.R.op(eng, lambda e: e.tensor_copy(out=out, in_=in_), reads=reads, writes=writes)

    def memset(self, ap, val, writes, eng="vector"):
        self.R.op(eng, lambda e: e.memset(ap, val), reads=[], writes=writes)


D = 2048; E = 4096; LAT = 2048; CTX = 256; T = LAT + CTX; EPS = 1e-6
NE = E // 128


class Net:
    def __init__(self, k):
        self.k = k; nc = k.nc
        self.EP = k.pool("E", [128, 4096], 11)
        self.PS = k.pool("ps", [128, 512], 8, psum=True)
        cp = k.pool("const", [128, 256], 1); cb, ck = cp.next()
        self.cb = cb; self.ck = ck
        self.ident = cb[:, 0:128]; self.ones = cb[:, 128:256]
        self.MREP = DT(nc, "MREP", [D, 256]); self.ADAB = DT(nc, "ADAB", [256, 3 * D])
        self.H = DT(nc, "H", [T, D]); self.HT = DT(nc, "HT", [D, T])
        self.PR = DT(nc, "PR", [4 * E, T]); self.U = DT(nc, "U", [E, T]); self.S = DT(nc, "S", [E, T])

    def init_consts(self, CONSTS):
        k = self.k
        k.load(self.cb[:, 0:128], self.ck, CONSTS, 0, 128, 0, 128)
        k.memset(self.cb[:, 128:256], 1.0, [self.ck])

    def rows_to_pp(self, SRC, r0, n, c0, C, dst, dkey):
        k = self.k
        for cb in range(0, C, 2048):
            cw = min(2048, C - cb)
            sb, sk = self.EP.next()
            k.load(sb[0:n, 0:cw], sk, SRC, r0, r0 + n, c0 + cb, c0 + cb + cw)
            per = 512 // n
            for j0 in range(0, cw // 128, per):
                jn = min(per, cw // 128 - j0)
                ps, pk = self.PS.next()
                for j in range(jn):
                    k.mm(ps[:, j * n:(j + 1) * n], pk, sb[0:n, (j0 + j) * 128:(j0 + j + 1) * 128], self.ident[0:n, 0:n],
                         True, True, [sk, self.ck])
                t0 = cb // 128 + j0
                k.copy(dst[:, t0:t0 + jn, :], ps[:, 0:jn * n].rearrange("p (j n) -> p j n", n=n), [pk], [dkey])

    def build_mrep(self, CC):
        k = self.k
        tp = k.pool("tmpv", [128, 16, 2], 1); tb, tk = tp.next()
        self.rows_to_pp(CC, 0, 2, 0, D, tb, tk)
        sp = k.pool("tmps", [128, 16, 2], 1); sb, sk = sp.next()
        k.act(sb[:], tb[:], AF.Silu, [tk], [sk])
        for kc in range(16):
            rb, rk = self.EP.next()
            for s in range(2):
                k.ts(rb[:, s * 128:(s + 1) * 128], self.ones, sb[:, kc, s:s + 1], None, ALU.mult, None, [sk, self.ck], [rk])
            k.store(self.MREP, kc * 128, kc * 128 + 128, 0, 256, rb[:, 0:256], rk)

    def adaln(self, ADA_W, ADA_B, i):
        k = self.k
        def epi(ps, pk, m0, mt, n0, nw):
            bb, bk = self.EP.next()
            k.dma(bb[:, 0:nw], ADA_B.ap[i:i + 1, n0:n0 + nw].to_broadcast([128, nw]), ADA_B.keys(i, i + 1, n0, n0 + nw), [bk])
            ob, ok = self.EP.next()
            k.tt(ob[0:mt, 0:nw], ps, bb[0:mt, 0:nw], ALU.add, [pk, bk], [ok])
            k.store(self.ADAB, m0, m0 + mt, n0, n0 + nw, ob[0:mt, 0:nw], ok)
        self.gemm(self.MREP, ADA_W, None, D, 256, 3 * D, r0=(i * D, 0), epi=epi)

    def prenorm(self, XS, NORM_G, i, ntile):
        k = self.k
        gs, gsk = self.EP.pin(); sh, shk = self.EP.pin()
        gs = gs[:, :].rearrange("p (s d) -> p s d", s=2); sh = sh[:, :].rearrange("p (s d) -> p s d", s=2)
        gb, gk = self.EP.next()
        k.dma(gb[:, 0:D], NORM_G.ap[i:i + 1, :].to_broadcast([128, D]), NORM_G.keys(i, i + 1, 0, D), [gk])
        for s in range(2):
            tb, tk = self.EP.next()
            k.load(tb[:, 0:D], tk, self.ADAB, s * 128, s * 128 + 128, D, 2 * D)
            k.stt(gs[:, s, :], tb[:, 0:D], 1.0, gb[:, 0:D], ALU.add, ALU.mult, [tk, gk], [gsk])
            k.load(sh[:, s, :], shk, self.ADAB, s * 128, s * 128 + 128, 0, D)
        stp = k.pool("st", [128, 8], 2)
        for t in range(ntile):
            s = 0 if t < LAT // 128 else 1
            xb, xk = self.EP.next(); sq, sqk = self.EP.next(); st, stk = stp.next()
            k.load(xb[:, 0:D], xk, XS, t * 128, t * 128 + 128, 0, D)
            k.act(sq[:, 0:D], xb[:, 0:D], AF.Square, [xk], [sqk])
            k.R.op("vector", lambda e, st=st, sq=sq: e.reduce_sum(out=st[:, 0:1], in_=sq[:, 0:D], axis=AX.X), reads=[sqk], writes=[stk])
            k.ts(st[:, 1:2], st[:, 0:1], 1.0 / D, EPS, ALU.mult, ALU.add, [stk], [stk])
            k.act(st[:, 2:3], st[:, 1:2], AF.Sqrt, [stk], [stk])
            k.R.op("vector", lambda e, st=st: e.reciprocal(out=st[:, 3:4], in_=st[:, 2:3]), reads=[stk], writes=[stk])
            k.stt(sq[:, 0:D], xb[:, 0:D], st[:, 3:4], gs[:, s, :], ALU.mult, ALU.mult, [xk, stk, gsk], [sqk])
            k.tt(sq[:, 0:D], sq[:, 0:D], sh[:, s, :], ALU.add, [sqk, shk], [sqk])
            k.store(self.H, t * 128, t * 128 + 128, 0, D, sq[:, 0:D], sqk)
        self.EP.unpin(gsk); self.EP.unpin(shk)
        self.transpose(self.H, self.HT, ntile * 128, D)

    def residual_epi(self, XS):
        k = self.k
        def epi(ps, pk, m0, mt, n0, nw):
            s = 0 if m0 < LAT else 1
            xb, xk = self.EP.next(); gb, gk = self.EP.next()
            k.load(xb[0:mt, 0:nw], xk, XS, m0, m0 + mt, n0, n0 + nw)
            k.load(gb[0:mt, 0:nw], gk, self.ADAB, s * 128, s * 128 + mt, 2 * D + n0, 2 * D + n0 + nw)
            k.tt(gb[0:mt, 0:nw], ps, gb[0:mt, 0:nw], ALU.mult, [pk, gk], [gk])
            k.tt(xb[0:mt, 0:nw], xb[0:mt, 0:nw], gb[0:mt, 0:nw], ALU.add, [xk, gk], [xk])
            k.store(XS, m0, m0 + mt, n0, n0 + nw, xb[0:mt, 0:nw], xk)
        return epi

    def copy_dram(self, SRC, DST, rows, cols, s0=(0, 0), d0=(0, 0)):
        k = self.k
        for r in range(0, rows, 128):
            rw = min(128, rows - r)
            k.dma(DST.ap[d0[0] + r:d0[0] + r + rw, d0[1]:d0[1] + cols], SRC.ap[s0[0] + r:s0[0] + r + rw, s0[1]:s0[1] + cols],
                  SRC.keys(s0[0] + r, s0[0] + r + rw, s0[1], s0[1] + cols), DST.keys(d0[0] + r, d0[0] + r + rw, d0[1], d0[1] + cols))

    def final_norm(self, XS, FG, OUT):
        k = self.k
        gb, gk = self.EP.pin()
        k.dma(gb[:, 0:D], FG.ap[0:1, :].to_broadcast([128, D]), FG.keys(0, 1, 0, D), [gk])
        stp = k.pool("st", [128, 8], 2)
        for t in range(LAT // 128):
            xb, xk = self.EP.next(); sq, sqk = self.EP.next(); st, stk = stp.next()
            k.load(xb[:, 0:D], xk, XS, t * 128, t * 128 + 128, 0, D)
            k.act(sq[:, 0:D], xb[:, 0:D], AF.Square, [xk], [sqk])
            k.R.op("vector", lambda e, st=st, sq=sq: e.reduce_sum(out=st[:, 0:1], in_=sq[:, 0:D], axis=AX.X), reads=[sqk], writes=[stk])
            k.ts(st[:, 1:2], st[:, 0:1], 1.0 / D, EPS, ALU.mult, ALU.add, [stk], [stk])
            k.act(st[:, 2:3], st[:, 1:2], AF.Sqrt, [stk], [stk])
            k.R.op("vector", lambda e, st=st: e.reciprocal(out=st[:, 3:4], in_=st[:, 2:3]), reads=[stk], writes=[stk])
            k.stt(sq[:, 0:D], xb[:, 0:D], st[:, 3:4], gb[:, 0:D], ALU.mult, ALU.mult, [xk, stk, gk], [sqk])
            k.store(OUT, t * 128, t * 128 + 128, 0, D, sq[:, 0:D], sqk)
        self.EP.unpin(gk)

    def gemm(self, L, R, OUT, K, M, N, l0=(0, 0), r0=(0, 0), o0=(0, 0), epi=None, act=None):
        k = self.k
        MG, NB, KP = 512, 512, 8
        kch = (K + 127) // 128; kparts = min(K, 128); npan = (kch + KP - 1) // KP
        for mg in range(0, M, MG):
            mw = min(MG, M - mg)
            lcache = None
            for nb in range(0, N, NB):
                nw = min(NB, N - nb)
                pss = [self.PS.next() for _ in range((mw + 127) // 128)]
                for kp in range(npan):
                    kc0 = kp * KP; kcn = min(KP, kch - kc0)
                    ka = kc0 * 128; kb = min(K, (kc0 + kcn) * 128)
                    if True:
                        lb, lk = self.EP.next()
                        k.load(lb[0:kparts, 0:kcn * mw].rearrange("p (c m) -> p c m", m=mw), lk, L, l0[0] + ka, l0[0] + kb,
                               l0[1] + mg, l0[1] + mg + mw, "(c p) m -> p c m", p=kparts)
                        lcache = (lb, lk)
                    rb, rk = self.EP.next()
                    k.load(rb[0:kparts, 0:kcn * nw].rearrange("p (c m) -> p c m", m=nw), rk, R, r0[0] + ka, r0[0] + kb,
                           r0[1] + nb, r0[1] + nb + nw, "(c p) m -> p c m", p=kparts)
                    for g, (ps, pk) in enumerate(pss):
                        mt = min(128, mw - g * 128)
                        for kc in range(kcn):
                            k.mm(ps[0:mt, 0:nw], pk, lb[0:kparts, kc * mw + g * 128:kc * mw + g * 128 + mt],
                                 rb[0:kparts, kc * nw:(kc + 1) * nw],
                                 start=(kp == 0 and kc == 0), stop=(kp == npan - 1 and kc == kcn - 1), reads=[lk, rk])
                for g, (ps, pk) in enumerate(pss):
                    mt = min(128, mw - g * 128); m0 = mg + g * 128
                    if epi is not None:
                        epi(ps[0:mt, 0:nw], pk, m0, mt, nb, nw)
                    else:
                        ob, ok = self.EP.next()
                        k.act(ob[0:mt, 0:nw], ps[0:mt, 0:nw], act(m0) if callable(act) else (act or AF.Copy), [pk], [ok])
                        k.store(OUT, o0[0] + m0, o0[0] + m0 + mt, o0[1] + nb, o0[1] + nb + nw, ob[0:mt, 0:nw], ok)

    def transpose(self, SRC, DST, Rn, Cn, s0=(0, 0), d0=(0, 0)):
        k = self.k
        for rb in range(0, Rn, 512):
            rw = min(512, Rn - rb); nrt = rw // 128
            for cb in range(0, Cn, 512):
                cw = min(512, Cn - cb)
                ib, ik = self.EP.next()
                k.load(ib[:, 0:nrt * cw].rearrange("p (t c) -> p t c", c=cw), ik, SRC, s0[0] + rb, s0[0] + rb + rw,
                       s0[1] + cb, s0[1] + cb + cw, "(t p) c -> p t c", p=128)
                for ct in range(cw // 128):
                    ps, pk = self.PS.next()
                    for t in range(nrt):
                        o = ps[:, t * 128:(t + 1) * 128]; i = ib[:, t * cw + ct * 128:t * cw + ct * 128 + 128]
                        k.R.op("tensor", lambda e, o=o, i=i: e.transpose(o, i, self.ident), reads=[ik, self.ck], writes=[pk], pe_accum=(t > 0))
                    ob, ok = self.EP.next()
                    k.copy(ob[:, 0:rw], ps[:, 0:rw], [pk], [ok])
                    k.store(DST, d0[0] + cb + ct * 128, d0[0] + cb + ct * 128 + 128, d0[1] + rb, d0[1] + rb + rw, ob[:, 0:rw], ok)

    def conformer(self, XS, W_IN, DW_W, DW_B, LN_G, LN_B, W_OUT, need_ctx=True):
        k = self.k
        Tn = T if need_ctx else LAT
        segs = [(0, LAT)] + ([(LAT, CTX)] if need_ctx else [])
        cwp = k.pool("cfw", [128, NE, 34], 1); cw, cwk = cwp.next()
        self.rows_to_pp(DW_W, 0, 31, 0, E, cw[:, :, 0:31], cwk)
        for n, V in enumerate((DW_B, LN_G, LN_B)):
            self.rows_to_pp(V, 0, 1, 0, E, cw[:, :, 31 + n:32 + n], cwk)
        fn = lambda m0: AF.Copy if m0 < E else (AF.Sigmoid if m0 < 2 * E else AF.Silu)
        self.gemm(W_IN, self.HT, self.PR, D, 3 * E, Tn, act=fn)
        st0, st0k = self.EP.pin(); st1, st1k = self.EP.pin()
        k.memset(st0[:, 0:T], 0.0, [st0k]); k.memset(st1[:, 0:T], 0.0, [st1k])
        PAD = 15
        for j in range(NE):
            a, ak = self.EP.next(); sb, sk = self.EP.next(); u0, uk = self.EP.next()
            acc, acck = self.EP.next(); acc2, acc2k = self.EP.next()
            k.load(a[:, 0:Tn], ak, self.PR, j * 128, j * 128 + 128, 0, Tn)
            k.load(sb[:, 0:Tn], sk, self.PR, E + j * 128, E + j * 128 + 128, 0, Tn)
            k.memset(u0[:, 0:Tn + 4 * PAD], 0.0, [uk], eng="gpsimd")
            for si, (s0, sl) in enumerate(segs):
                o = s0 + (2 * si + 1) * PAD
                k.tt(u0[:, o:o + sl], a[:, s0:s0 + sl], sb[:, s0:s0 + sl], ALU.mult, [ak, sk], [uk])
            NV = 31
            for si, (s0, sl) in enumerate(segs):
                o = s0 + 2 * si * PAD
                for tap in range(31):
                    eng, ac, ack = ("vector", acc, acck) if tap < NV else ("gpsimd", acc2, acc2k)
                    src = u0[:, o + tap:o + tap + sl]
                    if tap == 0:
                        k.ts(ac[:, s0:s0 + sl], src, cw[:, j, 0:1], cw[:, j, 31:32], ALU.mult, ALU.add, [uk, cwk], [ack], eng=eng)
                    elif tap == NV:
                        k.ts(ac[:, s0:s0 + sl], src, cw[:, j, tap:tap + 1], None, ALU.mult, None, [uk, cwk], [ack], eng=eng)
                    else:
                        k.stt(ac[:, s0:s0 + sl], src, cw[:, j, tap:tap + 1], ac[:, s0:s0 + sl], ALU.mult, ALU.add, [uk, cwk, ack], [ack], eng=eng)
            k.act(acc2[:, 0:Tn], acc[:, 0:Tn], AF.Square, [acck], [acc2k])
            k.tt(st0[:, 0:Tn], st0[:, 0:Tn], acc[:, 0:Tn], ALU.add, [st0k, acck], [st0k], eng="gpsimd")
            k.tt(st1[:, 0:Tn], st1[:, 0:Tn], acc2[:, 0:Tn], ALU.add, [st1k, acc2k], [st1k], eng="gpsimd")
            k.store(self.U, j * 128, j * 128 + 128, 0, Tn, acc[:, 0:Tn], acck)
        for nb in range(0, Tn, 512):
            nw = min(512, Tn - nb)
            p1, p1k = self.PS.next(); p2, p2k = self.PS.next()
            k.mm(p1[:, 0:nw], p1k, self.ones, st0[:, nb:nb + nw], True, True, [self.ck, st0k])
            k.mm(p2[:, 0:nw], p2k, self.ones, st1[:, nb:nb + nw], True, True, [self.ck, st1k])
            m, mk = self.EP.next()
            k.ts(st0[:, nb:nb + nw], p1[:, 0:nw], 1.0 / E, None, ALU.mult, None, [p1k], [st0k])
            k.tt(m[:, 0:nw], st0[:, nb:nb + nw], st0[:, nb:nb + nw], ALU.mult, [st0k], [mk])
            k.stt(m[:, 0:nw], p2[:, 0:nw], 1.0 / E, m[:, 0:nw], ALU.mult, ALU.subtract, [p2k, mk], [mk])
            k.ts(m[:, 0:nw], m[:, 0:nw], EPS, None, ALU.add, None, [mk], [mk])
            k.act(m[:, 0:nw], m[:, 0:nw], AF.Sqrt, [mk], [mk])
            k.R.op("vector", lambda e, o=st1[:, nb:nb + nw], i=m[:, 0:nw]: e.reciprocal(out=o, in_=i), reads=[mk], writes=[st1k])
        for j in range(NE):
            u, uk = self.EP.next(); g, gk = self.EP.next()
            k.load(u[:, 0:Tn], uk, self.U, j * 128, j * 128 + 128, 0, Tn)
            k.load(g[:, 0:Tn], gk, self.PR, 2 * E + j * 128, 2 * E + j * 128 + 128, 0, Tn)
            k.tt(u[:, 0:Tn], u[:, 0:Tn], st0[:, 0:Tn], ALU.subtract, [uk, st0k], [uk])
            k.tt(u[:, 0:Tn], u[:, 0:Tn], st1[:, 0:Tn], ALU.mult, [uk, st1k], [uk])
            k.act(u[:, 0:Tn], u[:, 0:Tn], AF.Silu, [uk, cwk], [uk], bias=cw[:, j, 33:34], scale=cw[:, j, 32:33])
            k.tt(u[:, 0:Tn], u[:, 0:Tn], g[:, 0:Tn], ALU.mult, [uk, gk], [uk], eng="gpsimd")
            k.store(self.S, j * 128, j * 128 + 128, 0, Tn, u[:, 0:Tn], uk)
        self.EP.unpin(st0k); self.EP.unpin(st1k)
        self.gemm(self.S, W_OUT, None, E, Tn, D, epi=self.residual_epi(XS))


TWO_PI = 2.0 * math.pi


def hyena_consts():
    c = {}
    def fmat(L):
        N = 2 * L
        t = np.arange(L, dtype=np.float64)[:, None]; kk = np.arange(L, dtype=np.float64)[None, :]
        ang = 2.0 * np.pi * ((t * kk) % N) / N
        A = np.cos(ang); B = -np.sin(ang); B[:, 0] = np.cos(np.pi * t[:, 0])
        return np.concatenate([A, B], 1).astype(np.float32)
    c["flat"] = fmat(LAT); c["fctx"] = fmat(CTX)
    negt = np.zeros((128, 18), np.float32)
    negt[:, 0:16] = -(np.linspace(0.0, 1.0, LAT, dtype=np.float32).reshape(16, 128).T)
    negt[:, 16:18] = -(np.linspace(0.0, 1.0, CTX, dtype=np.float32).reshape(2, 128).T)
    c["negt"] = negt
    c["delta"] = np.abs(np.linspace(math.log(1e-2) / 1.5, math.log(1e-2) / 0.3, E, dtype=np.float32))[None, :]
    def feats(L):
        t = np.linspace(0.0, 1.0, L, dtype=np.float32)[:, None]
        w = (2.0 * math.pi / L) * np.arange(L, dtype=np.float32)[:, None]
        f = np.linspace(1e-4, 15, 16, dtype=np.float32)[None, :]
        return np.concatenate([t, np.cos(f * w), -np.sin(f * w)], -1).astype(np.float32).T
    c["feats"] = np.concatenate([feats(LAT), feats(CTX)], 1)
    return {k_: np.ascontiguousarray(v, dtype=np.float32) for k_, v in c.items()}


class Hyena:
    def __init__(self, net, FLAT, FCTX, NEGT, DELTA, FEATS):
        self.net = net; k = net.k; nc = k.nc
        self.F = {LAT: FLAT, CTX: FCTX}; self.NEGT = NEGT; self.DELTA = DELTA; self.FEATS = FEATS
        self.FT = {LAT: DT(nc, "FTLAT", [2 * LAT, LAT]), CTX: DT(nc, "FTCTX", [2 * CTX, CTX])}
        self.H1T = DT(nc, "H1T", [64, LAT]); self.H2T = DT(nc, "H2T", [64, LAT])
        self.HF = DT(nc, "HF", [LAT, 4 * E]); self.HS = [DT(nc, f"HS{n}", [LAT, E]) for n in range(2)]
        self.HD = [DT(nc, f"HD{n}", [LAT, E]) for n in range(2)]
        self.SPEC = [DT(nc, f"SPEC{n}", [2 * LAT, E]) for n in range(2)]
        self.ZT = DT(nc, "ZT", [LAT, E]); self.ZF = DT(nc, "ZF", [2 * LAT, E]); self.Y = DT(nc, "Y", [E, LAT])
        self.Z1 = DT(nc, "Z1", [E, T])
        self.ft_done = False

    def setup(self):
        net = self.net; k = net.k
        if not self.ft_done:
            for L in (LAT, CTX):
                net.transpose(self.F[L], self.FT[L], L, 2 * L)
            sp = k.pool("hyneg", [128, 18], 1); self.negt, self.negtk = sp.next()
            k.load(self.negt[:], self.negtk, self.NEGT, 0, 128, 0, 18)
            self.ft_done = True

    def layer(self, XS, W_IN, CONV_WB, HYF, F_W1, F_W2, F_W3, SKIP, W_OUT, need_ctx):
        net = self.net; k = net.k; EP = net.EP; PS = net.PS
        self.setup()
        Tn = T if need_ctx else LAT
        segs = [(0, LAT, 0)] + ([(LAT, CTX, 16)] if need_ctx else [])
        cvp = k.pool("hycv", [128, 96, 4], 1); cv, cvk = cvp.next()
        net.rows_to_pp(CONV_WB, 0, 4, 0, 3 * E, cv, cvk)
        skp = k.pool("hysk", [128, NE, 2], 1); sk, skk = skp.next()
        net.rows_to_pp(SKIP, 0, 2, 0, E, sk, skk)
        fpp_p = k.pool("hyf", [128, 4], 1); fpp, fppk = fpp_p.next()
        tb, tk = EP.next()
        k.load(tb[0:4, 0:64], tk, HYF, 0, 4, 0, 64)
        ps, pk = PS.next()
        k.mm(ps[0:64, 0:4], pk, tb[0:4, 0:64], net.ident[0:4, 0:4], True, True, [tk, net.ck])
        k.copy(fpp[0:64, :], ps[0:64, 0:4], [pk], [fppk])
        dl, dlk = EP.pin()
        k.dma(dl[:, 0:E], self.DELTA.ap[0:1, :].to_broadcast([128, E]), self.DELTA.keys(0, 1, 0, E), [dlk])
        net.gemm(W_IN, net.HT, net.PR, D, 4 * E, Tn, act=lambda m0: AF.Silu if m0 >= 3 * E else AF.Copy)
        for j in range(96):
            u, uk = EP.next(); o, ok = EP.next()
            for si, (s0, sl, _) in enumerate(segs):
                b = s0 + 3 * si
                k.memset(u[:, b:b + 1], 0.0, [uk], eng="gpsimd"); k.memset(u[:, b + sl + 1:b + sl + 2], 0.0, [uk], eng="gpsimd")
                k.load(u[:, b + 1:b + 1 + sl], uk, net.PR, j * 128, j * 128 + 128, s0, s0 + sl)
            for si, (s0, sl, _) in enumerate(segs):
                b = s0 + 3 * si
                k.ts(o[:, s0:s0 + sl], u[:, b:b + sl], cv[:, j, 0:1], cv[:, j, 3:4], ALU.mult, ALU.add, [uk, cvk], [ok])
                for tap in (1, 2):
                    k.stt(o[:, s0:s0 + sl], u[:, b + tap:b + tap + sl], cv[:, j, tap:tap + 1], o[:, s0:s0 + sl], ALU.mult, ALU.add, [uk, cvk, ok], [ok])
            k.store(net.PR, j * 128, j * 128 + 128, 0, Tn, o[:, 0:Tn], ok)
        for (s0, L, nt0) in segs:
            self.filters(L, s0, nt0, HYF, F_W1, F_W2, F_W3, fpp, fppk, dl, dlk)
            self.longconv(L, 0, net.PR, 0, s0)
            self.combine(L, s0, 0, sk, skk, zsrc=(net.PR, 0), xrow=E, dst=self.Z1)
            self.longconv(L, 1, self.Z1, 0, s0)
            self.combine(L, s0, 1, sk, skk, zsrc=(self.Z1, 0), xrow=2 * E, dst=net.S, gate_row=3 * E)
        EP.unpin(dlk)
        net.gemm(net.S, W_OUT, None, E, Tn, D, epi=net.residual_epi(XS))

    def sin_epi(self, OUT, fpp, fppk, cb, cfr):
        net = self.net; k = net.k; EP = net.EP
        def epi(ps, pk, m0, mt, n0, nw):
            a, ak = EP.next()
            k.ts(a[0:mt, 0:nw], ps, fpp[0:mt, cb:cb + 1], fpp[0:mt, cfr:cfr + 1], ALU.add, ALU.mult, [pk, fppk], [ak])
            m, mk = EP.next()
            for lvl in range(2):
                k.ts(m[0:mt, 0:nw], a[0:mt, 0:nw], math.pi, -TWO_PI, ALU.is_gt, ALU.mult, [ak], [mk])
                k.tt(a[0:mt, 0:nw], a[0:mt, 0:nw], m[0:mt, 0:nw], ALU.add, [ak, mk], [ak])
                k.ts(m[0:mt, 0:nw], a[0:mt, 0:nw], -math.pi, TWO_PI, ALU.is_lt, ALU.mult, [ak], [mk])
                k.tt(a[0:mt, 0:nw], a[0:mt, 0:nw], m[0:mt, 0:nw], ALU.add, [ak, mk], [ak])
            k.act(a[0:mt, 0:nw], a[0:mt, 0:nw], AF.Sin, [ak], [ak])
            k.store(OUT, m0, m0 + mt, n0, n0 + nw, a[0:mt, 0:nw], ak)
        return epi

    def filters(self, L, s0, nt0, HYF, F_W1, F_W2, F_W3, fpp, fppk, dl, dlk):
        net = self.net; k = net.k; EP = net.EP
        N = 2 * L; F = self.F[L]
        net.gemm(F_W1, self.FEATS, None, 33, 64, L, r0=(0, s0), epi=self.sin_epi(self.H1T, fpp, fppk, 0, 1))
        net.gemm(F_W2, self.H1T, None, 64, 64, L, epi=self.sin_epi(self.H2T, fpp, fppk, 2, 3))
        def wepi(ps, pk, m0, mt, n0, nw):
            e0 = n0 % E
            w, wk = EP.next(); ob, ok = EP.next()
            k.act(w[0:mt, 0:nw], dl[0:mt, e0:e0 + nw], AF.Exp, [dlk, self.negtk], [wk], scale=self.negt[0:mt, nt0 + m0 // 128:nt0 + m0 // 128 + 1])
            k.tt(ob[0:mt, 0:nw], ps, w[0:mt, 0:nw], ALU.mult, [pk, wk], [ok])
            k.store(self.HF, m0, m0 + mt, n0, n0 + nw, ob[0:mt, 0:nw], ok)
        net.gemm(self.H2T, F_W3, None, 64, L, 4 * E, epi=wepi)
        for n in range(2):
            for t in range(L // 128):
                for eb in range(0, E, 2048):
                    f, fk = EP.next(); b, bk = EP.next(); hs, hsk = EP.next(); hd, hdk = EP.next()
                    k.load(f[:, 0:2048], fk, self.HF, t * 128, t * 128 + 128, (2 * n) * E + eb, (2 * n) * E + eb + 2048)
                    k.load(b[:, 0:2048], bk, self.HF, t * 128, t * 128 + 128, (2 * n + 1) * E + eb, (2 * n + 1) * E + eb + 2048)
                    if t == 0:
                        k.memset(b[0:1, 0:2048], 0.0, [bk])
                    k.tt(hs[:, 0:2048], f[:, 0:2048], b[:, 0:2048], ALU.add, [fk, bk], [hsk])
                    k.tt(hd[:, 0:2048], f[:, 0:2048], b[:, 0:2048], ALU.subtract, [fk, bk], [hdk], eng="gpsimd")
                    k.store(self.HS[n], t * 128, t * 128 + 128, eb, eb + 2048, hs[:, 0:2048], hsk)
                    k.store(self.HD[n], t * 128, t * 128 + 128, eb, eb + 2048, hd[:, 0:2048], hdk)
            def sepi(row_off, fix0, scale, row0_only=False):
                def epi(ps, pk, m0, mt, n0, nw):
                    if row0_only:
                        mt = 1
                        ps = ps[0:1, :]
                    ob, ok = EP.next()
                    k.act(ob[0:mt, 0:nw], ps, AF.Copy, [pk], [ok], scale=scale)
                    if fix0 and m0 == 0:
                        k.ts(ob[0:1, 0:nw], ob[0:1, 0:nw], 0.5, None, ALU.mult, None, [ok], [ok])
                    k.store(self.SPEC[n], row_off + m0, row_off + m0 + mt, n0, n0 + nw, ob[0:mt, 0:nw], ok)
                return epi
            net.gemm(F, self.HS[n], None, L, L, E, l0=(0, 0), epi=sepi(0, True, 2.0 / N))
            net.gemm(F, self.HD[n], None, L, L, E, l0=(0, L), epi=sepi(L, False, 2.0 / N))
            net.gemm(F, self.HS[n], None, L, 128, E, l0=(0, L), epi=sepi(L, False, 1.0 / N, True))

    def longconv(self, L, n, SRC, row0, s0):
        net = self.net; k = net.k; EP = net.EP
        F = self.F[L]; FT = self.FT[L]; SP = self.SPEC[n]
        net.transpose(SRC, self.ZT, E, L, s0=(row0, s0))
        net.gemm(F, self.ZT, self.ZF, L, 2 * L, E)
        for kt in range(L // 128):
            for eb in range(0, E, 2048):
                a, ak = EP.next(); b, bk = EP.next(); sa, sak = EP.next(); sb, sbk = EP.next()
                ya, yak = EP.next(); yb, ybk = EP.next(); t1, t1k = EP.next()
                W = 2048
                r = kt * 128
                k.load(a[:, 0:W], ak, self.ZF, r, r + 128, eb, eb + W); k.load(b[:, 0:W], bk, self.ZF, L + r, L + r + 128, eb, eb + W)
                k.load(sa[:, 0:W], sak, SP, r, r + 128, eb, eb + W); k.load(sb[:, 0:W], sbk, SP, L + r, L + r + 128, eb, eb + W)
                k.tt(ya[:, 0:W], a[:, 0:W], sa[:, 0:W], ALU.mult, [ak, sak], [yak])
                k.tt(t1[:, 0:W], b[:, 0:W], sb[:, 0:W], ALU.mult, [bk, sbk], [t1k], eng="gpsimd")
                k.tt(yb[:, 0:W], a[:, 0:W], sb[:, 0:W], ALU.mult, [ak, sbk], [ybk])
                k.tt(t1[:, 0:W], ya[:, 0:W], t1[:, 0:W], ALU.subtract, [yak, t1k], [t1k])
                k.tt(b[:, 0:W], b[:, 0:W], sa[:, 0:W], ALU.mult, [bk, sak], [bk], eng="gpsimd")
                k.tt(yb[:, 0:W], yb[:, 0:W], b[:, 0:W], ALU.add, [ybk, bk], [ybk])
                if kt == 0:
                    k.copy(t1[0:1, 0:W], ya[0:1, 0:W], [yak], [t1k])
                    k.load(b[0:1, 0:W], bk, self.ZF, L, L + 1, eb, eb + W)
                    k.tt(yb[0:1, 0:W], b[0:1, 0:W], sb[0:1, 0:W], ALU.mult, [bk, sbk], [ybk])
                k.store(self.ZF, r, r + 128, eb, eb + W, t1[:, 0:W], t1k)
                k.store(self.ZF, L + r, L + r + 128, eb, eb + W, yb[:, 0:W], ybk)
        net.gemm(self.ZF, FT, self.Y, 2 * L, E, L)

    def combine(self, L, s0, n, sk, skk, zsrc, xrow, dst, gate_row=None):
        net = self.net; k = net.k; EP = net.EP
        ZS, zrow = zsrc
        for j in range(NE):
            y, yk = EP.next(); z, zk = EP.next(); x, xk = EP.next()
            k.load(y[:, 0:L], yk, self.Y, j * 128, j * 128 + 128, 0, L)
            k.load(z[:, 0:L], zk, ZS, zrow + j * 128, zrow + j * 128 + 128, s0, s0 + L)
            k.load(x[:, 0:L], xk, net.PR, xrow + j * 128, xrow + j * 128 + 128, s0, s0 + L)
            k.stt(y[:, 0:L], z[:, 0:L], sk[:, j, n:n + 1], y[:, 0:L], ALU.mult, ALU.add, [zk, skk, yk], [yk])
            k.tt(y[:, 0:L], y[:, 0:L], x[:, 0:L], ALU.mult, [yk, xk], [yk])
            if gate_row is not None:
                k.load(z[:, 0:L], zk, net.PR, gate_row + j * 128, gate_row + j * 128 + 128, s0, s0 + L)
                k.tt(y[:, 0:L], y[:, 0:L], z[:, 0:L], ALU.mult, [yk, zk], [yk], eng="gpsimd")
            k.store(dst, j * 128, j * 128 + 128, s0, s0 + L, y[:, 0:L], yk)


RT_H = 8; RT_DK = 256; RT_DV = 512; CH = 128


def ret_consts():
    c = {}
    half = 64
    inv = (10000.0 ** (-np.arange(half, dtype=np.float32) / half)).astype(np.float32)
    rows = np.repeat(np.arange(LAT // 64), 64).astype(np.float32); cols = np.tile(np.arange(64), LAT // 64).astype(np.float32)
    ar = rows[:, None] * inv; ac = cols[:, None] * inv
    c["rcos"] = np.concatenate([np.cos(ar), np.cos(ac)], 1)
    c["rsin"] = np.concatenate([np.sin(ar), np.sin(ac)], 1)
    p = np.arange(128, dtype=np.float32)
    j = p[:, None]; i = p[None, :]
    c["rmask"] = np.concatenate([np.maximum(i - j, 0), (i >= j).astype(np.float32), np.maximum(j - i, 0), (j >= i).astype(np.float32)], 1)
    c["rcols"] = np.stack([p + 1, CH - 1 - p, CH - p, p], 1)
    return {k_: np.ascontiguousarray(v, dtype=np.float32) for k_, v in c.items()}


class Retention:
    def __init__(self, net, RCOS, RSIN, RMASK, RCOLS):
        self.net = net; nc = net.k.nc
        self.RCOS = RCOS; self.RSIN = RSIN; self.RMASK = RMASK; self.RCOLS = RCOLS
        self.QKV = DT(nc, "QKV", [T, 2 * D + E]); self.QKT = DT(nc, "QKT", [2 * D, T])
        self.OACC = DT(nc, "OACC", [T, E]); self.OT = DT(nc, "OTR", [E, T])

    def layer(self, XS, W_IN, DECAY, GN_GB, W_OUT):
        net = self.net; k = net.k; EP = net.EP; PS = net.PS
        def qkv_epi(ps, pk, m0, mt, n0, nw):
            ob, ok = EP.next()
            k.act(ob[0:mt, 0:nw], ps, AF.Copy, [pk], [ok], scale=(RT_DK ** -0.5 if D <= n0 < 2 * D else 1.0))
            k.store(self.QKV, m0, m0 + mt, n0, n0 + nw, ob[0:mt, 0:nw], ok)
        net.gemm(net.HT, W_IN, None, D, T, 2 * D + E, epi=qkv_epi)
        net.gemm(W_IN, net.HT, net.PR, D, E, T, l0=(0, 2 * D + E), act=AF.Silu)
        for t in range(LAT // 128):
            x, xk = EP.next(); o, ok = EP.next(); cs, csk = EP.next(); t1, t1k = EP.next(); t2, t2k = EP.next()
            k.load(x[:, 0:2 * D], xk, self.QKV, t * 128, t * 128 + 128, 0, 2 * D)
            k.load(cs[:, 0:128], csk, self.RCOS, t * 128, t * 128 + 128, 0, 128)
            k.load(cs[:, 128:256], csk, self.RSIN, t * 128, t * 128 + 128, 0, 128)
            xv = x[:, 0:2 * D].rearrange("p (h a s f) -> p h a s f", h=16, a=2, s=2)
            ov = o[:, 0:2 * D].rearrange("p (h a s f) -> p h a s f", h=16, a=2, s=2)
            cv = cs[:, 0:128].rearrange("p (a f) -> p a f", a=2); sv = cs[:, 128:256].rearrange("p (a f) -> p a f", a=2)
            t1v = t1[:, 0:128].rearrange("p (a f) -> p a f", a=2); t2v = t2[:, 0:128].rearrange("p (a f) -> p a f", a=2)
            for h in range(16):
                e1, e2 = ("vector", "gpsimd")
                k.tt(t1v, xv[:, h, :, 0, :], cv, ALU.mult, [xk, csk], [t1k], eng=e1)
                k.tt(t2v, xv[:, h, :, 1, :], sv, ALU.mult, [xk, csk], [t2k], eng=e2)
                k.tt(ov[:, h, :, 0, :], t1v, t2v, ALU.subtract, [t1k, t2k], [ok], eng=e1)
                k.tt(t1v, xv[:, h, :, 0, :], sv, ALU.mult, [xk, csk], [t1k], eng=e1)
                k.tt(t2v, xv[:, h, :, 1, :], cv, ALU.mult, [xk, csk], [t2k], eng=e2)
                k.tt(ov[:, h, :, 1, :], t1v, t2v, ALU.add, [t1k, t2k], [ok], eng=e1)
            k.store(self.QKV, t * 128, t * 128 + 128, 0, 2 * D, o[:, 0:2 * D], ok)
        net.transpose(self.QKV, self.QKT, T, 2 * D)
        cp = k.pool("rtc", [128, 1024], 1); cb, cbk = cp.next()
        lg = cb[:, 0:16]; cdec = cb[:, 16:32]; cols = cb[:, 32:36]; dec = cb[:, 64:128]
        rmask = cb[:, 128:640]; mk = cb[:, 640:896]
        k.dma(lg, DECAY.ap[0:1, 0:16].to_broadcast([128, 16]), DECAY.keys(0, 1, 0, 16), [cbk])
        k.load(cols, cbk, self.RCOLS, 0, 128, 0, 4)
        k.load(rmask, cbk, self.RMASK, 0, 128, 0, 512)
        k.act(lg, lg, AF.Exp, [cbk], [cbk], scale=-1.0)
        k.act(lg, lg, AF.Ln, [cbk], [cbk], bias=1.0)
        k.ts(lg, lg, -1.0, None, ALU.mult, None, [cbk], [cbk])
        k.act(cdec, lg, AF.Exp, [cbk], [cbk], scale=float(CH))
        decv = dec.rearrange("p (c s) -> p c s", c=4)
        for c4 in range(4):
            k.act(decv[:, c4, :], lg, AF.Exp, [cbk], [cbk], scale=cols[:, c4:c4 + 1])
        stp = k.pool("rtst", [128, 2, 512], 2)
        chunks_f = [(LAT + c * CH) for c in range(CTX // CH)] + [c * CH for c in range(LAT // CH)]
        chunks_b = [(LAT + c * CH) for c in reversed(range(CTX // CH))] + [c * CH for c in reversed(range(LAT // CH))]
        for h in range(RT_H):
            qT, qTk = EP.pin(); kT, kTk = EP.pin(); cT, cTk = EP.pin()
            for dc in range(2):
                r = h * RT_DK + dc * 128
                k.load(qT[:, dc * LAT:(dc + 1) * LAT], qTk, self.QKT, r, r + 128, 0, LAT)
                k.load(kT[:, dc * LAT:(dc + 1) * LAT], kTk, self.QKT, D + r, D + r + 128, 0, LAT)
                k.load(cT[:, dc * CTX:(dc + 1) * CTX], cTk, self.QKT, r, r + 128, LAT, T)
                k.load(cT[:, (2 + dc) * CTX:(3 + dc) * CTX], cTk, self.QKT, D + r, D + r + 128, LAT, T)
            def qk_slices(t0):
                if t0 < LAT:
                    return ([qT[:, dc * LAT + t0:dc * LAT + t0 + CH] for dc in range(2)],
                            [kT[:, dc * LAT + t0:dc * LAT + t0 + CH] for dc in range(2)], [qTk, kTk])
                c0 = t0 - LAT
                return ([cT[:, dc * CTX + c0:dc * CTX + c0 + CH] for dc in range(2)],
                        [cT[:, (2 + dc) * CTX + c0:(2 + dc) * CTX + c0 + CH] for dc in range(2)], [cTk])
            for di, chunks in enumerate((chunks_f, chunks_b)):
                col = di * 8 + h
                mt_ = mk[:, di * 128:(di + 1) * 128]
                k.act(mt_, rmask[:, di * 256:di * 256 + 128], AF.Exp, [cbk], [cbk], scale=lg[:, col:col + 1])
                k.tt(mt_, mt_, rmask[:, di * 256 + 128:di * 256 + 256], ALU.mult, [cbk], [cbk])
                st, stk = stp.next()
                k.memset(st[:], 0.0, [stk])
                for t0 in chunks:
                    qs, ks, qkk = qk_slices(t0)
                    kv, kvk = EP.next()
                    k.load(kv[:, 0:256], kvk, self.QKV, t0, t0 + CH, D + h * RT_DK, D + (h + 1) * RT_DK)
                    k.load(kv[:, 256:768], kvk, self.QKV, t0, t0 + CH, 2 * D + h * RT_DV, 2 * D + (h + 1) * RT_DV)
                    ps_s, pssk = PS.next()
                    for dc in range(2):
                        k.mm(ps_s[:, 0:CH], pssk, ks[dc], qs[dc], dc == 0, dc == 1, qkk)
                    sc, sck = EP.next()
                    k.tt(sc[:, 0:CH], ps_s[:, 0:CH], mt_, ALU.mult, [pssk, cbk], [sck])
                    p1, p1k = PS.next(); p2, p2k = PS.next()
                    k.mm(p1[:, 0:RT_DV], p1k, sc[:, 0:CH], kv[:, 256:768], True, True, [sck, kvk])
                    for dc in range(2):
                        k.mm(p2[:, 0:RT_DV], p2k, qs[dc], st[:, dc, :], dc == 0, dc == 1, qkk + [stk])
                    o, ok = EP.next()
                    k.act(o[:, 0:RT_DV], p1[:, 0:RT_DV], AF.Copy, [p1k], [ok])
                    k.stt(o[:, 0:RT_DV], p2[:, 0:RT_DV], decv[:, 2 * di, col:col + 1], o[:, 0:RT_DV], ALU.mult, ALU.add, [p2k, cbk, ok], [ok])
                    if di == 1:
                        pv, pvk = EP.next()
                        k.load(pv[:, 0:RT_DV], pvk, self.OACC, t0, t0 + CH, h * RT_DV, (h + 1) * RT_DV)
                        k.tt(o[:, 0:RT_DV], o[:, 0:RT_DV], pv[:, 0:RT_DV], ALU.add, [ok, pvk], [ok], eng="gpsimd")
                    k.store(self.OACC, t0, t0 + CH, h * RT_DV, (h + 1) * RT_DV, o[:, 0:RT_DV], ok)
                    k.ts(kv[:, 768:1024], kv[:, 0:256], decv[:, 2 * di + 1, col:col + 1], None, ALU.mult, None, [kvk, cbk], [kvk])
                    for dc in range(2):
                        p3, p3k = PS.next()
                        k.mm(p3[:, 0:RT_DV], p3k, kv[:, 768 + dc * 128:768 + (dc + 1) * 128], kv[:, 256:768], True, True, [kvk])
                        k.stt(st[:, dc, :], st[:, dc, :], cdec[:, col:col + 1], p3[:, 0:RT_DV], ALU.mult, ALU.add, [stk, cbk, p3k], [stk])
            EP.unpin(qTk); EP.unpin(kTk); EP.unpin(cTk)
        gg, ggk = EP.pin(); gb, gbk = EP.pin()
        k.dma(gg[:, 0:E], GN_GB.ap[0:1, :].to_broadcast([128, E]), GN_GB.keys(0, 1, 0, E), [ggk])
        k.dma(gb[:, 0:E], GN_GB.ap[1:2, :].to_broadcast([128, E]), GN_GB.keys(1, 2, 0, E), [gbk])
        smp = k.pool("rtsm", [128, 64], 2)
        for t in range(T // 128):
            o, ok = EP.next(); sq, sqk = EP.next(); sm, smk = smp.next()
            k.load(o[:, 0:E], ok, self.OACC, t * 128, t * 128 + 128, 0, E)
            k.act(sq[:, 0:E], o[:, 0:E], AF.Square, [ok], [sqk])
            ov = o[:, 0:E].rearrange("p (h v) -> p h v", h=RT_H); sqv = sq[:, 0:E].rearrange("p (h v) -> p h v", h=RT_H)
            k.R.op("vector", lambda e, sm=sm, ov=ov: e.reduce_sum(out=sm[:, 0:8], in_=ov, axis=AX.X), reads=[ok], writes=[smk])
            k.R.op("vector", lambda e, sm=sm, sqv=sqv: e.reduce_sum(out=sm[:, 8:16], in_=sqv, axis=AX.X), reads=[sqk], writes=[smk])
            k.ts(sm[:, 0:16], sm[:, 0:16], 1.0 / RT_DV, None, ALU.mult, None, [smk], [smk])
            k.tt(sm[:, 16:24], sm[:, 0:8], sm[:, 0:8], ALU.mult, [smk], [smk])
            k.tt(sm[:, 24:32], sm[:, 8:16], sm[:, 16:24], ALU.subtract, [smk], [smk])
            k.ts(sm[:, 24:32], sm[:, 24:32], EPS, None, ALU.add, None, [smk], [smk])
            k.act(sm[:, 24:32], sm[:, 24:32], AF.Sqrt, [smk], [smk])
            k.R.op("vector", lambda e, sm=sm: e.reciprocal(out=sm[:, 32:40], in_=sm[:, 24:32]), reads=[smk], writes=[smk])
            for h in range(RT_H):
                k.ts(ov[:, h, :], ov[:, h, :], sm[:, h:h + 1], sm[:, 32 + h:33 + h], ALU.subtract, ALU.mult, [ok, smk], [ok])
            k.tt(o[:, 0:E], o[:, 0:E], gg[:, 0:E], ALU.mult, [ok, ggk], [ok])
            k.tt(o[:, 0:E], o[:, 0:E], gb[:, 0:E], ALU.add, [ok, gbk], [ok], eng="gpsimd")
            k.store(self.OACC, t * 128, t * 128 + 128, 0, E, o[:, 0:E], ok)
        EP.unpin(ggk); EP.unpin(gbk)
        net.transpose(self.OACC, self.OT, T, E)
        for j in range(NE):
            a, ak = EP.next(); g, gk = EP.next()
            k.load(a[:, 0:T], ak, self.OT, j * 128, j * 128 + 128, 0, T)
            k.load(g[:, 0:T], gk, net.PR, j * 128, j * 128 + 128, 0, T)
            k.tt(a[:, 0:T], a[:, 0:T], g[:, 0:T], ALU.mult, [ak, gk], [ak])
            k.store(net.S, j * 128, j * 128 + 128, 0, T, a[:, 0:T], ak)
        net.gemm(net.S, W_OUT, None, E, T, D, epi=net.residual_epi(XS))


DEPTH = 4; NCORES = 8


def build_program():
    nc = bass.Bass("TRN2", target_bir_lowering=False)
    st = contextlib.ExitStack()
    k = K(nc, st); net = Net(k)
    ext = lambda n, shp: DT(nc, n, shp, kind="ExternalInput")
    XIN = ext("xin", [T, D]); CC = ext("cc", [2, D]); CONSTS = ext("consts", [128, 128])
    ADA_W = ext("ada_w", [4 * D, 3 * D]); ADA_B = ext("ada_b", [4, 3 * D]); NORM_G = ext("norm_g", [4, D])
    FNG = ext("final_norm_g", [1, D])
    FLAT = ext("flat", [LAT, 2 * LAT]); FCTX = ext("fctx", [CTX, 2 * CTX]); NEGT = ext("negt", [128, 18])
    DELTA = ext("delta", [1, E]); FEATS = ext("feats", [33, T])
    HY = []
    for j in range(2):
        HY.append(dict(W_IN=ext(f"hy_w_in{j}", [D, 4 * E]), CONV_WB=ext(f"hy_conv_wb{j}", [4, 3 * E]), HYF=ext(f"hy_f{j}", [4, 64]),
                       F_W1=ext(f"hy_f_w1{j}", [33, 64]), F_W2=ext(f"hy_f_w2{j}", [64, 64]), F_W3=ext(f"hy_f_w3{j}", [64, 4 * E]),
                       SKIP=ext(f"hy_skip{j}", [2, E]), W_OUT=ext(f"hy_w_out{j}", [E, D])))
    CF = dict(W_IN=ext("cf_w_in", [D, 3 * E]), DW_W=ext("cf_dw_w", [31, E]), DW_B=ext("cf_dw_b", [1, E]),
              LN_G=ext("cf_ln_g", [1, E]), LN_B=ext("cf_ln_b", [1, E]), W_OUT=ext("cf_w_out", [E, D]))
    RCOS = ext("rcos", [LAT, 128]); RSIN = ext("rsin", [LAT, 128]); RMASK = ext("rmask", [128, 512]); RCOLS = ext("rcols", [128, 4])
    RT = dict(W_IN=ext("rt_w_in", [D, 2 * D + 2 * E]), DECAY=ext("rt_decay", [1, 16]), GN_GB=ext("rt_gn_gb", [2, E]),
              W_OUT=ext("rt_w_out", [E, D]))
    XS = DT(nc, "XS", [T, D]); OUT = DT(nc, "out", [LAT, D], kind="ExternalOutput")
    hy = Hyena(net, FLAT, FCTX, NEGT, DELTA, FEATS); rt = Retention(net, RCOS, RSIN, RMASK, RCOLS)
    net.init_consts(CONSTS)
    net.copy_dram(XIN, XS, T, D)
    net.build_mrep(CC)
    for i in range(DEPTH):
        kind, j = i % 3, i // 3
        last = i == DEPTH - 1
        need_ctx = (not last) or kind == 2
        net.adaln(ADA_W, ADA_B, i)
        net.prenorm(XS, NORM_G, i, (T if need_ctx else LAT) // 128)
        if kind == 0:
            hy.layer(XS, need_ctx=need_ctx, **HY[j])
        elif kind == 1:
            net.conformer(XS, CF["W_IN"], CF["DW_W"], CF["DW_B"], CF["LN_G"], CF["LN_B"], CF["W_OUT"], need_ctx=need_ctx)
        else:
            rt.layer(XS, RT["W_IN"], RT["DECAY"], RT["GN_GB"], RT["W_OUT"])
    net.final_norm(XS, FNG, OUT)
    k.R.final_wait("sync", [k.R.last_w[key] for key in OUT.keys(0, LAT, 0, D)])
    k.R.emit(nc, None)
    st.close()
    return nc


def kernel(x, c, ctx, c_ctx, ada_w, ada_b, norm_g, final_norm_g,
           hy_w_in, hy_conv_w, hy_conv_b, hy_f_w1, hy_f_b1, hy_f_fr1, hy_f_w2, hy_f_b2,
           hy_f_fr2, hy_f_w3, hy_skip, hy_w_out,
           cf_w_in, cf_dw_w, cf_dw_b, cf_ln_g, cf_ln_b, cf_w_out,
           rt_w_in, rt_decay_logit, rt_gn_g, rt_gn_b, rt_w_out):
    f = lambda a: np.ascontiguousarray(np.asarray(a), dtype=np.float32)
    x, c, ctx, c_ctx = f(x), f(c), f(ctx), f(c_ctx)
    shared = {"consts": np.eye(128, dtype=np.float32), "ada_w": f(ada_w).reshape(4 * D, 3 * D), "ada_b": f(ada_b),
              "norm_g": f(norm_g), "final_norm_g": f(final_norm_g).reshape(1, D),
              "cf_w_in": f(cf_w_in)[0], "cf_dw_w": f(cf_dw_w)[0], "cf_dw_b": f(cf_dw_b).reshape(1, E),
              "cf_ln_g": f(cf_ln_g).reshape(1, E), "cf_ln_b": f(cf_ln_b).reshape(1, E), "cf_w_out": f(cf_w_out)[0],
              "rt_w_in": f(rt_w_in)[0], "rt_decay": f(rt_decay_logit)[0].reshape(1, 16),
              "rt_gn_gb": np.stack([f(rt_gn_g)[0], f(rt_gn_b)[0]]), "rt_w_out": f(rt_w_out)[0]}
    for j in range(2):
        shared.update({f"hy_w_in{j}": f(hy_w_in)[j], f"hy_conv_wb{j}": np.concatenate([f(hy_conv_w)[j], f(hy_conv_b)[j][None]], 0),
                       f"hy_f{j}": np.stack([f(hy_f_b1)[j], f(hy_f_fr1)[j], f(hy_f_b2)[j], f(hy_f_fr2)[j]]),
                       f"hy_f_w1{j}": f(hy_f_w1)[j], f"hy_f_w2{j}": f(hy_f_w2)[j], f"hy_f_w3{j}": f(hy_f_w3)[j],
                       f"hy_skip{j}": f(hy_skip)[j], f"hy_w_out{j}": f(hy_w_out)[j]})
    shared.update(hyena_consts()); shared.update(ret_consts())
    shared = {k_: np.ascontiguousarray(v, dtype=np.float32) for k_, v in shared.items()}
    in_maps = []
    for b in range(NCORES):
        m = dict(shared)
        m["xin"] = np.ascontiguousarray(np.concatenate([x[b], ctx[b]], 0))
        m["cc"] = np.ascontiguousarray(np.stack([c[b], c_ctx]))
        in_maps.append(m)
    nc = build_program()
    res = run_bass_kernel_spmd(nc, in_maps, core_ids=list(range(NCORES)))
    return np.stack([np.asarray(res.results[b]["out"], dtype=np.float32) for b in range(NCORES)], 0)
```

```python
import contextlib, math
import numpy as np
import concourse.bass as bass
import concourse.mybir as mybir
from concourse.bass_utils import run_bass_kernel_spmd

COMPUTE = ("tensor", "vector", "scalar", "gpsimd")
DMAQ = ("sync", "gpsimd")
NDS = 6


class Op:
    __slots__ = ("eng", "fn", "waits", "idx", "is_dma", "tick", "dsem", "dval", "name")


class Rec:
    def __init__(self):
        self.ops = {e: [] for e in ("tensor", "vector", "scalar", "gpsimd", "sync")}
        self.last_w = {}
        self.readers = {}
        self.ndma = {q: 0 for q in DMAQ}
        self.dma_tok = {q: [] for q in DMAQ}

    def op(self, eng, fn, reads=(), writes=(), dma=False, pe_accum=False, name=None):
        o = Op(); o.eng = eng; o.fn = fn; o.waits = []; o.is_dma = dma; o.tick = False
        o.name = name; o.dsem = None; o.dval = None
        deps = []
        for r in reads:
            t = self.last_w.get(r)
            if t is not None: deps.append(t)
        for w in writes:
            t = self.last_w.get(w)
            if t is not None and not (pe_accum and t[0] == "c" and t[1] == "tensor"):
                deps.append(t)
            deps.extend(self.readers.get(w, ()))
        if dma:
            q = eng; i = self.ndma[q]; self.ndma[q] += 1
            if i >= NDS: deps.append(self.dma_tok[q][i - NDS])
            tok = ("d", q, i % NDS, 16 * (i // NDS + 1))
            self.dma_tok[q].append(tok); o.dsem = (q, i % NDS)
        else:
            tok = ("c", eng, o)
        seen = set()
        for d in deps:
            if d is tok or id(d) in seen: continue
            seen.add(id(d))
            if d[0] == "c": d[2].tick = True
            o.waits.append(d)
        self.ops[eng].append(o)
        for w in writes:
            self.last_w[w] = tok; self.readers[w] = []
        for r in reads:
            if r not in writes:
                lst = self.readers.setdefault(r, [])
                if tok[0] == "c":
                    lst[:] = [t for t in lst if not (t[0] == "c" and t[1] == eng)]
                lst.append(tok)
        return tok

    def final_wait(self, eng, toks):
        o = Op(); o.eng = eng; o.fn = None; o.waits = list(toks); o.is_dma = False
        o.tick = False; o.name = "final"; o.dsem = None; o.dval = None
        for d in toks:
            if d[0] == "c": d[2].tick = True
        self.ops[eng].append(o)

    def emit(self, nc, block_engines):
        import contextlib
        for e in COMPUTE:
            n = 0
            for o in self.ops[e]:
                if o.tick and not o.is_dma:
                    n += 1; o.idx = n
        with contextlib.ExitStack() as st:
            csem = {e: st.enter_context(nc.semaphore("c_" + e)) for e in COMPUTE}
            dsem = {(q, k): st.enter_context(nc.semaphore(f"d_{q}{k}")) for q in DMAQ for k in range(NDS)}
            blk = st.enter_context(nc.Block())
            for e, lst in self.ops.items():
                if not lst: continue
                def body(eng, lst=lst, e=e):
                    known = {}
                    for o in lst:
                        need = {}
                        for d in o.waits:
                            if d[0] == "c": s, v = csem[d[1]], d[2].idx
                            else: s, v = dsem[(d[1], d[2])], d[3]
                            k = id(s)
                            if known.get(k, 0) >= v: continue
                            if k not in need or need[k][1] < v: need[k] = (s, v)
                        for k, (s, v) in need.items():
                            eng.wait_ge(s, v); known[k] = v
                        if o.fn is None: continue
                        ins = o.fn(eng)
                        if o.is_dma: ins.then_inc(dsem[o.dsem], 16)
                        elif o.tick: ins.then_inc(csem[e], 1)
                getattr(blk, e)(body)


F32 = mybir.dt.float32
BF16 = mybir.dt.bfloat16
AF = mybir.ActivationFunctionType
ALU = mybir.AluOpType
AX = mybir.AxisListType


class DT:
    def __init__(self, nc, name, shape, kind="Internal", rb=128, cb=512, dtype=F32):
        self.name = name; self.shape = tuple(shape); self.dtype = dtype
        self.ap = nc.dram_tensor(name, list(shape), dtype, kind=kind).ap()
        self.rb = rb; self.cb = cb

    def keys(self, r0, r1, c0, c1):
        return [(self.name, i, j) for i in range(r0 // self.rb, (r1 - 1) // self.rb + 1)
                for j in range(c0 // self.cb, (c1 - 1) // self.cb + 1)]


class Pool:
    def __init__(self, nc, st, name, shape, n, psum=False, dtype=F32):
        mk = nc.psum_tensor if psum else nc.sbuf_tensor
        self.bufs = [st.enter_context(mk(f"{name}{i}", list(shape), dtype)) for i in range(n)]
        self.keys = [(name, i) for i in range(n)]
        self.i = 0; self.pinned = set()

    def next(self):
        while True:
            k = self.i % len(self.bufs); self.i += 1
            if k not in self.pinned:
                return self.bufs[k], self.keys[k]

    def pin(self):
        b, key = self.next()
        self.pinned.add(self.keys.index(key))
        return b, key

    def unpin(self, key):
        self.pinned.discard(self.keys.index(key))


class K:
    def __init__(self, nc, st):
        self.nc = nc; self.st = st; self.R = Rec(); self.pools = {}
        self.qi = 0

    def pool(self, name, shape, n, psum=False, dtype=F32):
        if name not in self.pools:
            self.pools[name] = Pool(self.nc, self.st, name, shape, n, psum, dtype)
        return self.pools[name]

    def dma(self, out, in_, reads, writes, q=None):
        if q is None:
            q = "sync"
        self.R.op(q, lambda e: e.dma_start(out=out, in_=in_), reads=reads, writes=writes, dma=True)

    def load(self, sb, sbkey, dt, r0, r1, c0, c1, pat=None, q="sync", **kw):
        src = dt.ap[r0:r1, c0:c1]
        if pat: src = src.rearrange(pat, **kw)
        self.dma(sb, src, dt.keys(r0, r1, c0, c1), [sbkey], q=q)

    def store(self, dt, r0, r1, c0, c1, sb, sbkey, pat=None, q="gpsimd", **kw):
        dst = dt.ap[r0:r1, c0:c1]
        if pat: dst = dst.rearrange(pat, **kw)
        self.dma(dst, sb, [sbkey], dt.keys(r0, r1, c0, c1), q=q)

    def mm(self, ps, pskey, lhsT, rhs, start, stop, reads):
        self.R.op("tensor", lambda e: e.matmul(ps, lhsT=lhsT, rhs=rhs, start=start, stop=stop),
                  reads=reads, writes=[pskey], pe_accum=not start)

    def act(self, out, in_, func, reads, writes, eng="scalar", **kw):
        self.R.op(eng, lambda e: e.activation(out=out, in_=in_, func=func, **kw), reads=reads, writes=writes)

    def tt(self, out, a, b, op, reads, writes, eng="vector"):
        self.R.op(eng, lambda e: e.tensor_tensor(out=out, in0=a, in1=b, op=op), reads=reads, writes=writes)

    def ts(self, out, a, s1, s2, op0, op1, reads, writes, eng="vector", **kw):
        if s2 is None:
            self.R.op(eng, lambda e: e.tensor_scalar(out=out, in0=a, scalar1=s1, scalar2=None, op0=op0, **kw), reads=reads, writes=writes)
        else:
            self.R.op(eng, lambda e: e.tensor_scalar(out=out, in0=a, scalar1=s1, scalar2=s2, op0=op0, op1=op1, **kw), reads=reads, writes=writes)

    def stt(self, out, a, s, b, op0, op1, reads, writes, eng="vector"):
        self.R.op(eng, lambda e: e.scalar_tensor_tensor(out=out, in0=a, scalar=s, in1=b, op0=op0, op1=op1), reads=reads, writes=writes)

    def copy(self, out, in_, reads, writes, eng="vector"):
        self.R.op(eng, lambda e: e.tensor_copy(out=out, in_=in_), reads=reads, writes=writes)

    def memset(self, ap, val, writes, eng="vector"):
        self.R.op(eng, lambda e: e.memset(ap, val), reads=[], writes=writes)


D = 2048; E = 4096; LAT = 2048; CTX = 256; T = LAT + CTX; EPS = 1e-6
NE = E // 128


class Net:
    def __init__(self, k):
        self.k = k; nc = k.nc
        self.EP = k.pool("E", [128, 4096], 9)
        self.EB = k.pool("EB", [128, 4096], 5, dtype=BF16)
        self.PS = k.pool("ps", [128, 512], 8, psum=True)
        cp = k.pool("const", [128, 256], 1); cb, ck = cp.next()
        self.cb = cb; self.ck = ck
        self.ident = cb[:, 0:128]; self.ones = cb[:, 128:256]
        self.MREP = DT(nc, "MREP", [D, 256], dtype=BF16); self.ADAB = DT(nc, "ADAB", [256, 3 * D])
        self.H = DT(nc, "H", [T, D]); self.HT = DT(nc, "HT", [D, T], dtype=BF16)
        self.PR = DT(nc, "PR", [4 * E, T]); self.U = DT(nc, "U", [E, T]); self.S = DT(nc, "S", [E, T], dtype=BF16)
        self.WB = DT(nc, "WB", [D, 4 * E], dtype=BF16); self.WOB = DT(nc, "WOB", [E, D], dtype=BF16); self.ADAWB = DT(nc, "ADAWB", [D, 3 * D], dtype=BF16)

    def init_consts(self, CONSTS):
        k = self.k
        k.load(self.cb[:, 0:128], self.ck, CONSTS, 0, 128, 0, 128)
        k.memset(self.cb[:, 128:256], 1.0, [self.ck])

    def rows_to_pp(self, SRC, r0, n, c0, C, dst, dkey):
        k = self.k
        for cb in range(0, C, 2048):
            cw = min(2048, C - cb)
            sb, sk = self.EP.next()
            k.load(sb[0:n, 0:cw], sk, SRC, r0, r0 + n, c0 + cb, c0 + cb + cw)
            per = 512 // n
            for j0 in range(0, cw // 128, per):
                jn = min(per, cw // 128 - j0)
                ps, pk = self.PS.next()
                for j in range(jn):
                    k.mm(ps[:, j * n:(j + 1) * n], pk, sb[0:n, (j0 + j) * 128:(j0 + j + 1) * 128], self.ident[0:n, 0:n],
                         True, True, [sk, self.ck])
                t0 = cb // 128 + j0
                k.copy(dst[:, t0:t0 + jn, :], ps[:, 0:jn * n].rearrange("p (j n) -> p j n", n=n), [pk], [dkey])

    def build_mrep(self, CC):
        k = self.k
        tp = k.pool("tmpv", [128, 16, 2], 1); tb, tk = tp.next()
        self.rows_to_pp(CC, 0, 2, 0, D, tb, tk)
        sp = k.pool("tmps", [128, 16, 2], 1); sb, sk = sp.next()
        k.act(sb[:], tb[:], AF.Silu, [tk], [sk])
        for kc in range(16):
            rb, rk = self.EB.next()
            for s in range(2):
                k.ts(rb[:, s * 128:(s + 1) * 128], self.ones, sb[:, kc, s:s + 1], None, ALU.mult, None, [sk, self.ck], [rk])
            k.store(self.MREP, kc * 128, kc * 128 + 128, 0, 256, rb[:, 0:256], rk)

    def adaln(self, ADA_W, ADA_B, i):
        k = self.k
        def epi(ps, pk, m0, mt, n0, nw):
            bb, bk = self.EP.next()
            k.dma(bb[:, 0:nw], ADA_B.ap[i:i + 1, n0:n0 + nw].to_broadcast([128, nw]), ADA_B.keys(i, i + 1, n0, n0 + nw), [bk])
            ob, ok = self.EP.next()
            k.tt(ob[0:mt, 0:nw], ps, bb[0:mt, 0:nw], ALU.add, [pk, bk], [ok])
            k.store(self.ADAB, m0, m0 + mt, n0, n0 + nw, ob[0:mt, 0:nw], ok)
        self.cast_dram(ADA_W, self.ADAWB, D, 3 * D, s0=(i * D, 0))
        self.gemm(self.MREP, self.ADAWB, None, D, 256, 3 * D, epi=epi)

    def prenorm(self, XS, NORM_G, i, ntile):
        k = self.k
        gs, gsk = self.EP.pin(); sh, shk = self.EP.pin()
        gs = gs[:, :].rearrange("p (s d) -> p s d", s=2); sh = sh[:, :].rearrange("p (s d) -> p s d", s=2)
        gb, gk = self.EP.next()
        k.dma(gb[:, 0:D], NORM_G.ap[i:i + 1, :].to_broadcast([128, D]), NORM_G.keys(i, i + 1, 0, D), [gk])
        for s in range(2):
            tb, tk = self.EP.next()
            k.load(tb[:, 0:D], tk, self.ADAB, s * 128, s * 128 + 128, D, 2 * D)
            k.stt(gs[:, s, :], tb[:, 0:D], 1.0, gb[:, 0:D], ALU.add, ALU.mult, [tk, gk], [gsk])
            k.load(sh[:, s, :], shk, self.ADAB, s * 128, s * 128 + 128, 0, D)
        stp = k.pool("st", [128, 8], 2)
        for t in range(ntile):
            s = 0 if t < LAT // 128 else 1
            xb, xk = self.EP.next(); sq, sqk = self.EP.next(); st, stk = stp.next()
            k.load(xb[:, 0:D], xk, XS, t * 128, t * 128 + 128, 0, D)
            k.act(sq[:, 0:D], xb[:, 0:D], AF.Square, [xk], [sqk])
            k.R.op("vector", lambda e, st=st, sq=sq: e.reduce_sum(out=st[:, 0:1], in_=sq[:, 0:D], axis=AX.X), reads=[sqk], writes=[stk])
            k.ts(st[:, 1:2], st[:, 0:1], 1.0 / D, EPS, ALU.mult, ALU.add, [stk], [stk])
            k.act(st[:, 2:3], st[:, 1:2], AF.Sqrt, [stk], [stk])
            k.R.op("vector", lambda e, st=st: e.reciprocal(out=st[:, 3:4], in_=st[:, 2:3]), reads=[stk], writes=[stk])
            k.stt(sq[:, 0:D], xb[:, 0:D], st[:, 3:4], gs[:, s, :], ALU.mult, ALU.mult, [xk, stk, gsk], [sqk])
            k.tt(sq[:, 0:D], sq[:, 0:D], sh[:, s, :], ALU.add, [sqk, shk], [sqk])
            k.store(self.H, t * 128, t * 128 + 128, 0, D, sq[:, 0:D], sqk)
        self.EP.unpin(gsk); self.EP.unpin(shk)
        self.transpose(self.H, self.HT, ntile * 128, D)

    def residual_epi(self, XS):
        k = self.k
        def epi(ps, pk, m0, mt, n0, nw):
            s = 0 if m0 < LAT else 1
            xb, xk = self.EP.next(); gb, gk = self.EP.next()
            k.load(xb[0:mt, 0:nw], xk, XS, m0, m0 + mt, n0, n0 + nw)
            k.load(gb[0:mt, 0:nw], gk, self.ADAB, s * 128, s * 128 + mt, 2 * D + n0, 2 * D + n0 + nw)
            k.tt(gb[0:mt, 0:nw], ps, gb[0:mt, 0:nw], ALU.mult, [pk, gk], [gk])
            k.tt(xb[0:mt, 0:nw], xb[0:mt, 0:nw], gb[0:mt, 0:nw], ALU.add, [xk, gk], [xk])
            k.store(XS, m0, m0 + mt, n0, n0 + nw, xb[0:mt, 0:nw], xk)
        return epi

    def copy_dram(self, SRC, DST, rows, cols, s0=(0, 0), d0=(0, 0)):
        k = self.k
        for r in range(0, rows, 128):
            rw = min(128, rows - r)
            k.dma(DST.ap[d0[0] + r:d0[0] + r + rw, d0[1]:d0[1] + cols], SRC.ap[s0[0] + r:s0[0] + r + rw, s0[1]:s0[1] + cols],
                  SRC.keys(s0[0] + r, s0[0] + r + rw, s0[1], s0[1] + cols), DST.keys(d0[0] + r, d0[0] + r + rw, d0[1], d0[1] + cols))

    def cast_dram(self, SRC, DST, rows, cols, s0=(0, 0), d0=(0, 0)):
        k = self.k; engs = ("vector", "gpsimd", "scalar"); n = 0
        for r in range(0, rows, 128):
            rw = min(128, rows - r)
            for c in range(0, cols, 4096):
                cw = min(4096, cols - c)
                a, ak = self.EP.next(); b, bk = self.EB.next()
                k.load(a[0:rw, 0:cw], ak, SRC, s0[0] + r, s0[0] + r + rw, s0[1] + c, s0[1] + c + cw)
                e = engs[n % 3]; n += 1
                if e == "scalar": k.act(b[0:rw, 0:cw], a[0:rw, 0:cw], AF.Copy, [ak], [bk])
                else: k.copy(b[0:rw, 0:cw], a[0:rw, 0:cw], [ak], [bk], eng=e)
                k.store(DST, d0[0] + r, d0[0] + r + rw, d0[1] + c, d0[1] + c + cw, b[0:rw, 0:cw], bk)

    def final_norm(self, XS, FG, OUT):
        k = self.k
        gb, gk = self.EP.pin()
        k.dma(gb[:, 0:D], FG.ap[0:1, :].to_broadcast([128, D]), FG.keys(0, 1, 0, D), [gk])
        stp = k.pool("st", [128, 8], 2)
        for t in range(LAT // 128):
            xb, xk = self.EP.next(); sq, sqk = self.EP.next(); st, stk = stp.next()
            k.load(xb[:, 0:D], xk, XS, t * 128, t * 128 + 128, 0, D)
            k.act(sq[:, 0:D], xb[:, 0:D], AF.Square, [xk], [sqk])
            k.R.op("vector", lambda e, st=st, sq=sq: e.reduce_sum(out=st[:, 0:1], in_=sq[:, 0:D], axis=AX.X), reads=[sqk], writes=[stk])
            k.ts(st[:, 1:2], st[:, 0:1], 1.0 / D, EPS, ALU.mult, ALU.add, [stk], [stk])
            k.act(st[:, 2:3], st[:, 1:2], AF.Sqrt, [stk], [stk])
            k.R.op("vector", lambda e, st=st: e.reciprocal(out=st[:, 3:4], in_=st[:, 2:3]), reads=[stk], writes=[stk])
            k.stt(sq[:, 0:D], xb[:, 0:D], st[:, 3:4], gb[:, 0:D], ALU.mult, ALU.mult, [xk, stk, gk], [sqk])
            k.store(OUT, t * 128, t * 128 + 128, 0, D, sq[:, 0:D], sqk)
        self.EP.unpin(gk)

    def gemm(self, L, R, OUT, K, M, N, l0=(0, 0), r0=(0, 0), o0=(0, 0), epi=None, act=None):
        k = self.k
        MG, NB, KP = 512, 512, 8
        assert L.dtype == R.dtype, (L.name, R.name)
        PP = self.EB if L.dtype == BF16 else self.EP
        kch = (K + 127) // 128; kparts = min(K, 128); npan = (kch + KP - 1) // KP
        for mg in range(0, M, MG):
            mw = min(MG, M - mg)
            lcache = None
            for nb in range(0, N, NB):
                nw = min(NB, N - nb)
                pss = [self.PS.next() for _ in range((mw + 127) // 128)]
                for kp in range(npan):
                    kc0 = kp * KP; kcn = min(KP, kch - kc0)
                    ka = kc0 * 128; kb = min(K, (kc0 + kcn) * 128)
                    if True:
                        lb, lk = PP.next()
                        k.load(lb[0:kparts, 0:kcn * mw].rearrange("p (c m) -> p c m", m=mw), lk, L, l0[0] + ka, l0[0] + kb,
                               l0[1] + mg, l0[1] + mg + mw, "(c p) m -> p c m", p=kparts)
                        lcache = (lb, lk)
                    rb, rk = PP.next()
                    k.load(rb[0:kparts, 0:kcn * nw].rearrange("p (c m) -> p c m", m=nw), rk, R, r0[0] + ka, r0[0] + kb,
                           r0[1] + nb, r0[1] + nb + nw, "(c p) m -> p c m", p=kparts)
                    for g, (ps, pk) in enumerate(pss):
                        mt = min(128, mw - g * 128)
                        for kc in range(kcn):
                            k.mm(ps[0:mt, 0:nw], pk, lb[0:kparts, kc * mw + g * 128:kc * mw + g * 128 + mt],
                                 rb[0:kparts, kc * nw:(kc + 1) * nw],
                                 start=(kp == 0 and kc == 0), stop=(kp == npan - 1 and kc == kcn - 1), reads=[lk, rk])
                for g, (ps, pk) in enumerate(pss):
                    mt = min(128, mw - g * 128); m0 = mg + g * 128
                    if epi is not None:
                        epi(ps[0:mt, 0:nw], pk, m0, mt, nb, nw)
                    else:
                        ob, ok = self.EP.next()
                        k.act(ob[0:mt, 0:nw], ps[0:mt, 0:nw], act(m0) if callable(act) else (act or AF.Copy), [pk], [ok])
                        k.store(OUT, o0[0] + m0, o0[0] + m0 + mt, o0[1] + nb, o0[1] + nb + nw, ob[0:mt, 0:nw], ok)

    def transpose(self, SRC, DST, Rn, Cn, s0=(0, 0), d0=(0, 0)):
        k = self.k
        for rb in range(0, Rn, 512):
            rw = min(512, Rn - rb); nrt = rw // 128
            for cb in range(0, Cn, 512):
                cw = min(512, Cn - cb)
                ib, ik = self.EP.next()
                k.load(ib[:, 0:nrt * cw].rearrange("p (t c) -> p t c", c=cw), ik, SRC, s0[0] + rb, s0[0] + rb + rw,
                       s0[1] + cb, s0[1] + cb + cw, "(t p) c -> p t c", p=128)
                for ct in range(cw // 128):
                    ps, pk = self.PS.next()
                    for t in range(nrt):
                        o = ps[:, t * 128:(t + 1) * 128]; i = ib[:, t * cw + ct * 128:t * cw + ct * 128 + 128]
                        k.R.op("tensor", lambda e, o=o, i=i: e.transpose(o, i, self.ident), reads=[ik, self.ck], writes=[pk], pe_accum=(t > 0))
                    ob, ok = (self.EB if DST.dtype == BF16 else self.EP).next()
                    k.copy(ob[:, 0:rw], ps[:, 0:rw], [pk], [ok])
                    k.store(DST, d0[0] + cb + ct * 128, d0[0] + cb + ct * 128 + 128, d0[1] + rb, d0[1] + rb + rw, ob[:, 0:rw], ok)

    def conformer(self, XS, W_IN, DW_W, DW_B, LN_G, LN_B, W_OUT, need_ctx=True):
        k = self.k
        Tn = T if need_ctx else LAT
        segs = [(0, LAT)] + ([(LAT, CTX)] if need_ctx else [])
        cwp = k.pool("cfw", [128, NE, 34], 1); cw, cwk = cwp.next()
        self.rows_to_pp(DW_W, 0, 31, 0, E, cw[:, :, 0:31], cwk)
        for n, V in enumerate((DW_B, LN_G, LN_B)):
            self.rows_to_pp(V, 0, 1, 0, E, cw[:, :, 31 + n:32 + n], cwk)
        fn = lambda m0: AF.Copy if m0 < E else (AF.Sigmoid if m0 < 2 * E else AF.Silu)
        self.cast_dram(W_IN, self.WB, D, 3 * E); self.cast_dram(W_OUT, self.WOB, E, D)
        self.gemm(self.WB, self.HT, self.PR, D, 3 * E, Tn, act=fn)
        st0, st0k = self.EP.pin(); st1, st1k = self.EP.pin()
        k.memset(st0[:, 0:T], 0.0, [st0k]); k.memset(st1[:, 0:T], 0.0, [st1k])
        PAD = 15
        for j in range(NE):
            a, ak = self.EP.next(); sb, sk = self.EP.next(); u0, uk = self.EP.next()
            acc, acck = self.EP.next(); acc2, acc2k = self.EP.next()
            k.load(a[:, 0:Tn], ak, self.PR, j * 128, j * 128 + 128, 0, Tn)
            k.load(sb[:, 0:Tn], sk, self.PR, E + j * 128, E + j * 128 + 128, 0, Tn)
            k.memset(u0[:, 0:Tn + 4 * PAD], 0.0, [uk], eng="gpsimd")
            for si, (s0, sl) in enumerate(segs):
                o = s0 + (2 * si + 1) * PAD
                k.tt(u0[:, o:o + sl], a[:, s0:s0 + sl], sb[:, s0:s0 + sl], ALU.mult, [ak, sk], [uk])
            NV = 31
            for si, (s0, sl) in enumerate(segs):
                o = s0 + 2 * si * PAD
                for tap in range(31):
                    eng, ac, ack = ("vector", acc, acck) if tap < NV else ("gpsimd", acc2, acc2k)
                    src = u0[:, o + tap:o + tap + sl]
                    if tap == 0:
                        k.ts(ac[:, s0:s0 + sl], src, cw[:, j, 0:1], cw[:, j, 31:32], ALU.mult, ALU.add, [uk, cwk], [ack], eng=eng)
                    elif tap == NV:
                        k.ts(ac[:, s0:s0 + sl], src, cw[:, j, tap:tap + 1], None, ALU.mult, None, [uk, cwk], [ack], eng=eng)
                    else:
                        k.stt(ac[:, s0:s0 + sl], src, cw[:, j, tap:tap + 1], ac[:, s0:s0 + sl], ALU.mult, ALU.add, [uk, cwk, ack], [ack], eng=eng)
            k.act(acc2[:, 0:Tn], acc[:, 0:Tn], AF.Square, [acck], [acc2k])
            k.tt(st0[:, 0:Tn], st0[:, 0:Tn], acc[:, 0:Tn], ALU.add, [st0k, acck], [st0k], eng="gpsimd")
            k.tt(st1[:, 0:Tn], st1[:, 0:Tn], acc2[:, 0:Tn], ALU.add, [st1k, acc2k], [st1k], eng="gpsimd")
            k.store(self.U, j * 128, j * 128 + 128, 0, Tn, acc[:, 0:Tn], acck)
        for nb in range(0, Tn, 512):
            nw = min(512, Tn - nb)
            p1, p1k = self.PS.next(); p2, p2k = self.PS.next()
            k.mm(p1[:, 0:nw], p1k, self.ones, st0[:, nb:nb + nw], True, True, [self.ck, st0k])
            k.mm(p2[:, 0:nw], p2k, self.ones, st1[:, nb:nb + nw], True, True, [self.ck, st1k])
            m, mk = self.EP.next()
            k.ts(st0[:, nb:nb + nw], p1[:, 0:nw], 1.0 / E, None, ALU.mult, None, [p1k], [st0k])
            k.tt(m[:, 0:nw], st0[:, nb:nb + nw], st0[:, nb:nb + nw], ALU.mult, [st0k], [mk])
            k.stt(m[:, 0:nw], p2[:, 0:nw], 1.0 / E, m[:, 0:nw], ALU.mult, ALU.subtract, [p2k, mk], [mk])
            k.ts(m[:, 0:nw], m[:, 0:nw], EPS, None, ALU.add, None, [mk], [mk])
            k.act(m[:, 0:nw], m[:, 0:nw], AF.Sqrt, [mk], [mk])
            k.R.op("vector", lambda e, o=st1[:, nb:nb + nw], i=m[:, 0:nw]: e.reciprocal(out=o, in_=i), reads=[mk], writes=[st1k])
        for j in range(NE):
            u, uk = self.EP.next(); g, gk = self.EP.next()
            k.load(u[:, 0:Tn], uk, self.U, j * 128, j * 128 + 128, 0, Tn)
            k.load(g[:, 0:Tn], gk, self.PR, 2 * E + j * 128, 2 * E + j * 128 + 128, 0, Tn)
            k.tt(u[:, 0:Tn], u[:, 0:Tn], st0[:, 0:Tn], ALU.subtract, [uk, st0k], [uk])
            k.tt(u[:, 0:Tn], u[:, 0:Tn], st1[:, 0:Tn], ALU.mult, [uk, st1k], [uk])
            k.act(u[:, 0:Tn], u[:, 0:Tn], AF.Silu, [uk, cwk], [uk], bias=cw[:, j, 33:34], scale=cw[:, j, 32:33])
            sb_, sbk_ = self.EB.next()
            k.tt(sb_[:, 0:Tn], u[:, 0:Tn], g[:, 0:Tn], ALU.mult, [uk, gk], [sbk_], eng="gpsimd")
            k.store(self.S, j * 128, j * 128 + 128, 0, Tn, sb_[:, 0:Tn], sbk_)
        self.EP.unpin(st0k); self.EP.unpin(st1k)
        self.gemm(self.S, self.WOB, None, E, Tn, D, epi=self.residual_epi(XS))


TWO_PI = 2.0 * math.pi


def hyena_consts():
    c = {}
    def fmat(L):
        N = 2 * L
        t = np.arange(L, dtype=np.float64)[:, None]; kk = np.arange(L, dtype=np.float64)[None, :]
        ang = 2.0 * np.pi * ((t * kk) % N) / N
        A = np.cos(ang); B = -np.sin(ang); B[:, 0] = np.cos(np.pi * t[:, 0])
        return np.concatenate([A, B], 1).astype(np.float32)
    c["flat"] = fmat(LAT); c["fctx"] = fmat(CTX)
    negt = np.zeros((128, 18), np.float32)
    negt[:, 0:16] = -(np.linspace(0.0, 1.0, LAT, dtype=np.float32).reshape(16, 128).T)
    negt[:, 16:18] = -(np.linspace(0.0, 1.0, CTX, dtype=np.float32).reshape(2, 128).T)
    c["negt"] = negt
    c["delta"] = np.abs(np.linspace(math.log(1e-2) / 1.5, math.log(1e-2) / 0.3, E, dtype=np.float32))[None, :]
    def feats(L):
        t = np.linspace(0.0, 1.0, L, dtype=np.float32)[:, None]
        w = (2.0 * math.pi / L) * np.arange(L, dtype=np.float32)[:, None]
        f = np.linspace(1e-4, 15, 16, dtype=np.float32)[None, :]
        return np.concatenate([t, np.cos(f * w), -np.sin(f * w)], -1).astype(np.float32).T
    c["feats"] = np.concatenate([feats(LAT), feats(CTX)], 1)
    return {k_: np.ascontiguousarray(v, dtype=np.float32) for k_, v in c.items()}


class Hyena:
    def __init__(self, net, FLAT, FCTX, NEGT, DELTA, FEATS):
        self.net = net; k = net.k; nc = k.nc
        self.F = {LAT: FLAT, CTX: FCTX}; self.NEGT = NEGT; self.DELTA = DELTA; self.FEATS = FEATS
        self.FT = {LAT: DT(nc, "FTLAT", [2 * LAT, LAT]), CTX: DT(nc, "FTCTX", [2 * CTX, CTX])}
        self.H1T = DT(nc, "H1T", [64, LAT]); self.H2T = DT(nc, "H2T", [64, LAT])
        self.HF = DT(nc, "HF", [LAT, 4 * E]); self.HS = [DT(nc, f"HS{n}", [LAT, E]) for n in range(2)]
        self.HD = [DT(nc, f"HD{n}", [LAT, E]) for n in range(2)]
        self.SPEC = [DT(nc, f"SPEC{n}", [2 * LAT, E]) for n in range(2)]
        self.ZT = DT(nc, "ZT", [LAT, E]); self.ZF = DT(nc, "ZF", [2 * LAT, E]); self.Y = DT(nc, "Y", [E, LAT])
        self.Z1 = DT(nc, "Z1", [E, T])
        self.ft_done = False

    def setup(self):
        net = self.net; k = net.k
        if not self.ft_done:
            for L in (LAT, CTX):
                net.transpose(self.F[L], self.FT[L], L, 2 * L)
            sp = k.pool("hyneg", [128, 18], 1); self.negt, self.negtk = sp.next()
            k.load(self.negt[:], self.negtk, self.NEGT, 0, 128, 0, 18)
            self.ft_done = True

    def layer(self, XS, W_IN, CONV_WB, HYF, F_W1, F_W2, F_W3, SKIP, W_OUT, need_ctx):
        net = self.net; k = net.k; EP = net.EP; PS = net.PS
        self.setup()
        Tn = T if need_ctx else LAT
        segs = [(0, LAT, 0)] + ([(LAT, CTX, 16)] if need_ctx else [])
        cvp = k.pool("hycv", [128, 96, 4], 1); cv, cvk = cvp.next()
        net.rows_to_pp(CONV_WB, 0, 4, 0, 3 * E, cv, cvk)
        skp = k.pool("hysk", [128, NE, 2], 1); sk, skk = skp.next()
        net.rows_to_pp(SKIP, 0, 2, 0, E, sk, skk)
        fpp_p = k.pool("hyf", [128, 4], 1); fpp, fppk = fpp_p.next()
        tb, tk = EP.next()
        k.load(tb[0:4, 0:64], tk, HYF, 0, 4, 0, 64)
        ps, pk = PS.next()
        k.mm(ps[0:64, 0:4], pk, tb[0:4, 0:64], net.ident[0:4, 0:4], True, True, [tk, net.ck])
        k.copy(fpp[0:64, :], ps[0:64, 0:4], [pk], [fppk])
        dl, dlk = EP.pin()
        k.dma(dl[:, 0:E], self.DELTA.ap[0:1, :].to_broadcast([128, E]), self.DELTA.keys(0, 1, 0, E), [dlk])
        net.cast_dram(W_IN, net.WB, D, 4 * E); net.cast_dram(W_OUT, net.WOB, E, D)
        net.gemm(net.WB, net.HT, net.PR, D, 4 * E, Tn, act=lambda m0: AF.Silu if m0 >= 3 * E else AF.Copy)
        for j in range(96):
            u, uk = EP.next(); o, ok = EP.next()
            for si, (s0, sl, _) in enumerate(segs):
                b = s0 + 3 * si
                k.memset(u[:, b:b + 1], 0.0, [uk], eng="gpsimd"); k.memset(u[:, b + sl + 1:b + sl + 2], 0.0, [uk], eng="gpsimd")
                k.load(u[:, b + 1:b + 1 + sl], uk, net.PR, j * 128, j * 128 + 128, s0, s0 + sl)
            for si, (s0, sl, _) in enumerate(segs):
                b = s0 + 3 * si
                k.ts(o[:, s0:s0 + sl], u[:, b:b + sl], cv[:, j, 0:1], cv[:, j, 3:4], ALU.mult, ALU.add, [uk, cvk], [ok])
                for tap in (1, 2):
                    k.stt(o[:, s0:s0 + sl], u[:, b + tap:b + tap + sl], cv[:, j, tap:tap + 1], o[:, s0:s0 + sl], ALU.mult, ALU.add, [uk, cvk, ok], [ok])
            k.store(net.PR, j * 128, j * 128 + 128, 0, Tn, o[:, 0:Tn], ok)
        for (s0, L, nt0) in segs:
            self.filters(L, s0, nt0, HYF, F_W1, F_W2, F_W3, fpp, fppk, dl, dlk)
            self.longconv(L, 0, net.PR, 0, s0)
            self.combine(L, s0, 0, sk, skk, zsrc=(net.PR, 0), xrow=E, dst=self.Z1)
            self.longconv(L, 1, self.Z1, 0, s0)
            self.combine(L, s0, 1, sk, skk, zsrc=(self.Z1, 0), xrow=2 * E, dst=net.S, gate_row=3 * E)
        EP.unpin(dlk)
        net.gemm(net.S, net.WOB, None, E, Tn, D, epi=net.residual_epi(XS))

    def sin_epi(self, OUT, fpp, fppk, cb, cfr):
        net = self.net; k = net.k; EP = net.EP
        def epi(ps, pk, m0, mt, n0, nw):
            a, ak = EP.next()
            k.ts(a[0:mt, 0:nw], ps, fpp[0:mt, cb:cb + 1], fpp[0:mt, cfr:cfr + 1], ALU.add, ALU.mult, [pk, fppk], [ak])
            m, mk = EP.next()
            for lvl in range(2):
                k.ts(m[0:mt, 0:nw], a[0:mt, 0:nw], math.pi, -TWO_PI, ALU.is_gt, ALU.mult, [ak], [mk])
                k.tt(a[0:mt, 0:nw], a[0:mt, 0:nw], m[0:mt, 0:nw], ALU.add, [ak, mk], [ak])
                k.ts(m[0:mt, 0:nw], a[0:mt, 0:nw], -math.pi, TWO_PI, ALU.is_lt, ALU.mult, [ak], [mk])
                k.tt(a[0:mt, 0:nw], a[0:mt, 0:nw], m[0:mt, 0:nw], ALU.add, [ak, mk], [ak])
            k.act(a[0:mt, 0:nw], a[0:mt, 0:nw], AF.Sin, [ak], [ak])
            k.store(OUT, m0, m0 + mt, n0, n0 + nw, a[0:mt, 0:nw], ak)
        return epi

    def filters(self, L, s0, nt0, HYF, F_W1, F_W2, F_W3, fpp, fppk, dl, dlk):
        net = self.net; k = net.k; EP = net.EP
        N = 2 * L; F = self.F[L]
        net.gemm(F_W1, self.FEATS, None, 33, 64, L, r0=(0, s0), epi=self.sin_epi(self.H1T, fpp, fppk, 0, 1))
        net.gemm(F_W2, self.H1T, None, 64, 64, L, epi=self.sin_epi(self.H2T, fpp, fppk, 2, 3))
        def wepi(ps, pk, m0, mt, n0, nw):
            e0 = n0 % E
            w, wk = EP.next(); ob, ok = EP.next()
            k.act(w[0:mt, 0:nw], dl[0:mt, e0:e0 + nw], AF.Exp, [dlk, self.negtk], [wk], scale=self.negt[0:mt, nt0 + m0 // 128:nt0 + m0 // 128 + 1])
            k.tt(ob[0:mt, 0:nw], ps, w[0:mt, 0:nw], ALU.mult, [pk, wk], [ok])
            k.store(self.HF, m0, m0 + mt, n0, n0 + nw, ob[0:mt, 0:nw], ok)
        net.gemm(self.H2T, F_W3, None, 64, L, 4 * E, epi=wepi)
        for n in range(2):
            for t in range(L // 128):
                for eb in range(0, E, 2048):
                    f, fk = EP.next(); b, bk = EP.next(); hs, hsk = EP.next(); hd, hdk = EP.next()
                    k.load(f[:, 0:2048], fk, self.HF, t * 128, t * 128 + 128, (2 * n) * E + eb, (2 * n) * E + eb + 2048)
                    k.load(b[:, 0:2048], bk, self.HF, t * 128, t * 128 + 128, (2 * n + 1) * E + eb, (2 * n + 1) * E + eb + 2048)
                    if t == 0:
                        k.memset(b[0:1, 0:2048], 0.0, [bk])
                    k.tt(hs[:, 0:2048], f[:, 0:2048], b[:, 0:2048], ALU.add, [fk, bk], [hsk])
                    k.tt(hd[:, 0:2048], f[:, 0:2048], b[:, 0:2048], ALU.subtract, [fk, bk], [hdk], eng="gpsimd")
                    k.store(self.HS[n], t * 128, t * 128 + 128, eb, eb + 2048, hs[:, 0:2048], hsk)
                    k.store(self.HD[n], t * 128, t * 128 + 128, eb, eb + 2048, hd[:, 0:2048], hdk)
            def sepi(row_off, fix0, scale, row0_only=False):
                def epi(ps, pk, m0, mt, n0, nw):
                    if row0_only:
                        mt = 1
                        ps = ps[0:1, :]
                    ob, ok = EP.next()
                    k.act(ob[0:mt, 0:nw], ps, AF.Copy, [pk], [ok], scale=scale)
                    if fix0 and m0 == 0:
                        k.ts(ob[0:1, 0:nw], ob[0:1, 0:nw], 0.5, None, ALU.mult, None, [ok], [ok])
                    k.store(self.SPEC[n], row_off + m0, row_off + m0 + mt, n0, n0 + nw, ob[0:mt, 0:nw], ok)
                return epi
            net.gemm(F, self.HS[n], None, L, L, E, l0=(0, 0), epi=sepi(0, True, 2.0 / N))
            net.gemm(F, self.HD[n], None, L, L, E, l0=(0, L), epi=sepi(L, False, 2.0 / N))
            net.gemm(F, self.HS[n], None, L, 128, E, l0=(0, L), epi=sepi(L, False, 1.0 / N, True))

    def longconv(self, L, n, SRC, row0, s0):
        net = self.net; k = net.k; EP = net.EP
        F = self.F[L]; FT = self.FT[L]; SP = self.SPEC[n]
        net.transpose(SRC, self.ZT, E, L, s0=(row0, s0))
        net.gemm(F, self.ZT, self.ZF, L, 2 * L, E)
        for kt in range(L // 128):
            for eb in range(0, E, 2048):
                a, ak = EP.next(); b, bk = EP.next(); sa, sak = EP.next(); sb, sbk = EP.next()
                ya, yak = EP.next(); yb, ybk = EP.next(); t1, t1k = EP.next()
                W = 2048
                r = kt * 128
                k.load(a[:, 0:W], ak, self.ZF, r, r + 128, eb, eb + W); k.load(b[:, 0:W], bk, self.ZF, L + r, L + r + 128, eb, eb + W)
                k.load(sa[:, 0:W], sak, SP, r, r + 128, eb, eb + W); k.load(sb[:, 0:W], sbk, SP, L + r, L + r + 128, eb, eb + W)
                k.tt(ya[:, 0:W], a[:, 0:W], sa[:, 0:W], ALU.mult, [ak, sak], [yak])
                k.tt(t1[:, 0:W], b[:, 0:W], sb[:, 0:W], ALU.mult, [bk, sbk], [t1k], eng="gpsimd")
                k.tt(yb[:, 0:W], a[:, 0:W], sb[:, 0:W], ALU.mult, [ak, sbk], [ybk])
                k.tt(t1[:, 0:W], ya[:, 0:W], t1[:, 0:W], ALU.subtract, [yak, t1k], [t1k])
                k.tt(b[:, 0:W], b[:, 0:W], sa[:, 0:W], ALU.mult, [bk, sak], [bk], eng="gpsimd")
                k.tt(yb[:, 0:W], yb[:, 0:W], b[:, 0:W], ALU.add, [ybk, bk], [ybk])
                if kt == 0:
                    k.copy(t1[0:1, 0:W], ya[0:1, 0:W], [yak], [t1k])
                    k.load(b[0:1, 0:W], bk, self.ZF, L, L + 1, eb, eb + W)
                    k.tt(yb[0:1, 0:W], b[0:1, 0:W], sb[0:1, 0:W], ALU.mult, [bk, sbk], [ybk])
                k.store(self.ZF, r, r + 128, eb, eb + W, t1[:, 0:W], t1k)
                k.store(self.ZF, L + r, L + r + 128, eb, eb + W, yb[:, 0:W], ybk)
        net.gemm(self.ZF, FT, self.Y, 2 * L, E, L)

    def combine(self, L, s0, n, sk, skk, zsrc, xrow, dst, gate_row=None):
        net = self.net; k = net.k; EP = net.EP
        ZS, zrow = zsrc
        for j in range(NE):
            y, yk = EP.next(); z, zk = EP.next(); x, xk = EP.next()
            k.load(y[:, 0:L], yk, self.Y, j * 128, j * 128 + 128, 0, L)
            k.load(z[:, 0:L], zk, ZS, zrow + j * 128, zrow + j * 128 + 128, s0, s0 + L)
            k.load(x[:, 0:L], xk, net.PR, xrow + j * 128, xrow + j * 128 + 128, s0, s0 + L)
            k.stt(y[:, 0:L], z[:, 0:L], sk[:, j, n:n + 1], y[:, 0:L], ALU.mult, ALU.add, [zk, skk, yk], [yk])
            k.tt(y[:, 0:L], y[:, 0:L], x[:, 0:L], ALU.mult, [yk, xk], [yk])
            if gate_row is not None:
                k.load(z[:, 0:L], zk, net.PR, gate_row + j * 128, gate_row + j * 128 + 128, s0, s0 + L)
                ob, obk = net.EB.next()
                k.tt(ob[:, 0:L], y[:, 0:L], z[:, 0:L], ALU.mult, [yk, zk], [obk], eng="gpsimd")
                k.store(dst, j * 128, j * 128 + 128, s0, s0 + L, ob[:, 0:L], obk)
                continue
            k.store(dst, j * 128, j * 128 + 128, s0, s0 + L, y[:, 0:L], yk)


RT_H = 8; RT_DK = 256; RT_DV = 512; CH = 128


def ret_consts():
    c = {}
    half = 64
    inv = (10000.0 ** (-np.arange(half, dtype=np.float32) / half)).astype(np.float32)
    rows = np.repeat(np.arange(LAT // 64), 64).astype(np.float32); cols = np.tile(np.arange(64), LAT // 64).astype(np.float32)
    ar = rows[:, None] * inv; ac = cols[:, None] * inv
    c["rcos"] = np.concatenate([np.cos(ar), np.cos(ac)], 1)
    c["rsin"] = np.concatenate([np.sin(ar), np.sin(ac)], 1)
    p = np.arange(128, dtype=np.float32)
    j = p[:, None]; i = p[None, :]
    c["rmask"] = np.concatenate([np.maximum(i - j, 0), (i >= j).astype(np.float32), np.maximum(j - i, 0), (j >= i).astype(np.float32)], 1)
    c["rcols"] = np.stack([p + 1, CH - 1 - p, CH - p, p], 1)
    return {k_: np.ascontiguousarray(v, dtype=np.float32) for k_, v in c.items()}


class Retention:
    def __init__(self, net, RCOS, RSIN, RMASK, RCOLS):
        self.net = net; nc = net.k.nc
        self.RCOS = RCOS; self.RSIN = RSIN; self.RMASK = RMASK; self.RCOLS = RCOLS
        self.QKV = DT(nc, "QKV", [T, 2 * D + E]); self.QKT = DT(nc, "QKT", [2 * D, T])
        self.OACC = DT(nc, "OACC", [T, E]); self.OT = DT(nc, "OTR", [E, T])

    def layer(self, XS, W_IN, DECAY, GN_GB, W_OUT):
        net = self.net; k = net.k; EP = net.EP; PS = net.PS
        def qkv_epi(ps, pk, m0, mt, n0, nw):
            ob, ok = EP.next()
            k.act(ob[0:mt, 0:nw], ps, AF.Copy, [pk], [ok], scale=(RT_DK ** -0.5 if D <= n0 < 2 * D else 1.0))
            k.store(self.QKV, m0, m0 + mt, n0, n0 + nw, ob[0:mt, 0:nw], ok)
        net.cast_dram(W_IN, net.WB, D, 2 * D + 2 * E); net.cast_dram(W_OUT, net.WOB, E, D)
        net.gemm(net.HT, net.WB, None, D, T, 2 * D + E, epi=qkv_epi)
        net.gemm(net.WB, net.HT, net.PR, D, E, T, l0=(0, 2 * D + E), act=AF.Silu)
        for t in range(LAT // 128):
            x, xk = EP.next(); o, ok = EP.next(); cs, csk = EP.next(); t1, t1k = EP.next(); t2, t2k = EP.next()
            k.load(x[:, 0:2 * D], xk, self.QKV, t * 128, t * 128 + 128, 0, 2 * D)
            k.load(cs[:, 0:128], csk, self.RCOS, t * 128, t * 128 + 128, 0, 128)
            k.load(cs[:, 128:256], csk, self.RSIN, t * 128, t * 128 + 128, 0, 128)
            xv = x[:, 0:2 * D].rearrange("p (h a s f) -> p h a s f", h=16, a=2, s=2)
            ov = o[:, 0:2 * D].rearrange("p (h a s f) -> p h a s f", h=16, a=2, s=2)
            cv = cs[:, 0:128].rearrange("p (a f) -> p a f", a=2); sv = cs[:, 128:256].rearrange("p (a f) -> p a f", a=2)
            t1v = t1[:, 0:128].rearrange("p (a f) -> p a f", a=2); t2v = t2[:, 0:128].rearrange("p (a f) -> p a f", a=2)
            for h in range(16):
                e1, e2 = ("vector", "gpsimd")
                k.tt(t1v, xv[:, h, :, 0, :], cv, ALU.mult, [xk, csk], [t1k], eng=e1)
                k.tt(t2v, xv[:, h, :, 1, :], sv, ALU.mult, [xk, csk], [t2k], eng=e2)
                k.tt(ov[:, h, :, 0, :], t1v, t2v, ALU.subtract, [t1k, t2k], [ok], eng=e1)
                k.tt(t1v, xv[:, h, :, 0, :], sv, ALU.mult, [xk, csk], [t1k], eng=e1)
                k.tt(t2v, xv[:, h, :, 1, :], cv, ALU.mult, [xk, csk], [t2k], eng=e2)
                k.tt(ov[:, h, :, 1, :], t1v, t2v, ALU.add, [t1k, t2k], [ok], eng=e1)
            k.store(self.QKV, t * 128, t * 128 + 128, 0, 2 * D, o[:, 0:2 * D], ok)
        net.transpose(self.QKV, self.QKT, T, 2 * D)
        cp = k.pool("rtc", [128, 1024], 1); cb, cbk = cp.next()
        lg = cb[:, 0:16]; cdec = cb[:, 16:32]; cols = cb[:, 32:36]; dec = cb[:, 64:128]
        rmask = cb[:, 128:640]; mk = cb[:, 640:896]
        k.dma(lg, DECAY.ap[0:1, 0:16].to_broadcast([128, 16]), DECAY.keys(0, 1, 0, 16), [cbk])
        k.load(cols, cbk, self.RCOLS, 0, 128, 0, 4)
        k.load(rmask, cbk, self.RMASK, 0, 128, 0, 512)
        k.act(lg, lg, AF.Exp, [cbk], [cbk], scale=-1.0)
        k.act(lg, lg, AF.Ln, [cbk], [cbk], bias=1.0)
        k.ts(lg, lg, -1.0, None, ALU.mult, None, [cbk], [cbk])
        k.act(cdec, lg, AF.Exp, [cbk], [cbk], scale=float(CH))
        decv = dec.rearrange("p (c s) -> p c s", c=4)
        for c4 in range(4):
            k.act(decv[:, c4, :], lg, AF.Exp, [cbk], [cbk], scale=cols[:, c4:c4 + 1])
        stp = k.pool("rtst", [128, 2, 512], 2)
        chunks_f = [(LAT + c * CH) for c in range(CTX // CH)] + [c * CH for c in range(LAT // CH)]
        chunks_b = [(LAT + c * CH) for c in reversed(range(CTX // CH))] + [c * CH for c in reversed(range(LAT // CH))]
        for h in range(RT_H):
            qT, qTk = EP.pin(); kT, kTk = EP.pin(); cT, cTk = EP.pin()
            for dc in range(2):
                r = h * RT_DK + dc * 128
                k.load(qT[:, dc * LAT:(dc + 1) * LAT], qTk, self.QKT, r, r + 128, 0, LAT)
                k.load(kT[:, dc * LAT:(dc + 1) * LAT], kTk, self.QKT, D + r, D + r + 128, 0, LAT)
                k.load(cT[:, dc * CTX:(dc + 1) * CTX], cTk, self.QKT, r, r + 128, LAT, T)
                k.load(cT[:, (2 + dc) * CTX:(3 + dc) * CTX], cTk, self.QKT, D + r, D + r + 128, LAT, T)
            def qk_slices(t0):
                if t0 < LAT:
                    return ([qT[:, dc * LAT + t0:dc * LAT + t0 + CH] for dc in range(2)],
                            [kT[:, dc * LAT + t0:dc * LAT + t0 + CH] for dc in range(2)], [qTk, kTk])
                c0 = t0 - LAT
                return ([cT[:, dc * CTX + c0:dc * CTX + c0 + CH] for dc in range(2)],
                        [cT[:, (2 + dc) * CTX + c0:(2 + dc) * CTX + c0 + CH] for dc in range(2)], [cTk])
            for di, chunks in enumerate((chunks_f, chunks_b)):
                col = di * 8 + h
                mt_ = mk[:, di * 128:(di + 1) * 128]
                k.act(mt_, rmask[:, di * 256:di * 256 + 128], AF.Exp, [cbk], [cbk], scale=lg[:, col:col + 1])
                k.tt(mt_, mt_, rmask[:, di * 256 + 128:di * 256 + 256], ALU.mult, [cbk], [cbk])
                st, stk = stp.next()
                k.memset(st[:], 0.0, [stk])
                for t0 in chunks:
                    qs, ks, qkk = qk_slices(t0)
                    kv, kvk = EP.next()
                    k.load(kv[:, 0:256], kvk, self.QKV, t0, t0 + CH, D + h * RT_DK, D + (h + 1) * RT_DK)
                    k.load(kv[:, 256:768], kvk, self.QKV, t0, t0 + CH, 2 * D + h * RT_DV, 2 * D + (h + 1) * RT_DV)
                    ps_s, pssk = PS.next()
                    for dc in range(2):
                        k.mm(ps_s[:, 0:CH], pssk, ks[dc], qs[dc], dc == 0, dc == 1, qkk)
                    sc, sck = EP.next()
                    k.tt(sc[:, 0:CH], ps_s[:, 0:CH], mt_, ALU.mult, [pssk, cbk], [sck])
                    p1, p1k = PS.next(); p2, p2k = PS.next()
                    k.mm(p1[:, 0:RT_DV], p1k, sc[:, 0:CH], kv[:, 256:768], True, True, [sck, kvk])
                    for dc in range(2):
                        k.mm(p2[:, 0:RT_DV], p2k, qs[dc], st[:, dc, :], dc == 0, dc == 1, qkk + [stk])
                    o, ok = EP.next()
                    k.act(o[:, 0:RT_DV], p1[:, 0:RT_DV], AF.Copy, [p1k], [ok])
                    k.stt(o[:, 0:RT_DV], p2[:, 0:RT_DV], decv[:, 2 * di, col:col + 1], o[:, 0:RT_DV], ALU.mult, ALU.add, [p2k, cbk, ok], [ok])
                    if di == 1:
                        pv, pvk = EP.next()
                        k.load(pv[:, 0:RT_DV], pvk, self.OACC, t0, t0 + CH, h * RT_DV, (h + 1) * RT_DV)
                        k.tt(o[:, 0:RT_DV], o[:, 0:RT_DV], pv[:, 0:RT_DV], ALU.add, [ok, pvk], [ok], eng="gpsimd")
                    k.store(self.OACC, t0, t0 + CH, h * RT_DV, (h + 1) * RT_DV, o[:, 0:RT_DV], ok)
                    k.ts(kv[:, 768:1024], kv[:, 0:256], decv[:, 2 * di + 1, col:col + 1], None, ALU.mult, None, [kvk, cbk], [kvk])
                    for dc in range(2):
                        p3, p3k = PS.next()
                        k.mm(p3[:, 0:RT_DV], p3k, kv[:, 768 + dc * 128:768 + (dc + 1) * 128], kv[:, 256:768], True, True, [kvk])
                        k.stt(st[:, dc, :], st[:, dc, :], cdec[:, col:col + 1], p3[:, 0:RT_DV], ALU.mult, ALU.add, [stk, cbk, p3k], [stk])
            EP.unpin(qTk); EP.unpin(kTk); EP.unpin(cTk)
        gg, ggk = EP.pin(); gb, gbk = EP.pin()
        k.dma(gg[:, 0:E], GN_GB.ap[0:1, :].to_broadcast([128, E]), GN_GB.keys(0, 1, 0, E), [ggk])
        k.dma(gb[:, 0:E], GN_GB.ap[1:2, :].to_broadcast([128, E]), GN_GB.keys(1, 2, 0, E), [gbk])
        smp = k.pool("rtsm", [128, 64], 2)
        for t in range(T // 128):
            o, ok = EP.next(); sq, sqk = EP.next(); sm, smk = smp.next()
            k.load(o[:, 0:E], ok, self.OACC, t * 128, t * 128 + 128, 0, E)
            k.act(sq[:, 0:E], o[:, 0:E], AF.Square, [ok], [sqk])
            ov = o[:, 0:E].rearrange("p (h v) -> p h v", h=RT_H); sqv = sq[:, 0:E].rearrange("p (h v) -> p h v", h=RT_H)
            k.R.op("vector", lambda e, sm=sm, ov=ov: e.reduce_sum(out=sm[:, 0:8], in_=ov, axis=AX.X), reads=[ok], writes=[smk])
            k.R.op("vector", lambda e, sm=sm, sqv=sqv: e.reduce_sum(out=sm[:, 8:16], in_=sqv, axis=AX.X), reads=[sqk], writes=[smk])
            k.ts(sm[:, 0:16], sm[:, 0:16], 1.0 / RT_DV, None, ALU.mult, None, [smk], [smk])
            k.tt(sm[:, 16:24], sm[:, 0:8], sm[:, 0:8], ALU.mult, [smk], [smk])
            k.tt(sm[:, 24:32], sm[:, 8:16], sm[:, 16:24], ALU.subtract, [smk], [smk])
            k.ts(sm[:, 24:32], sm[:, 24:32], EPS, None, ALU.add, None, [smk], [smk])
            k.act(sm[:, 24:32], sm[:, 24:32], AF.Sqrt, [smk], [smk])
            k.R.op("vector", lambda e, sm=sm: e.reciprocal(out=sm[:, 32:40], in_=sm[:, 24:32]), reads=[smk], writes=[smk])
            for h in range(RT_H):
                k.ts(ov[:, h, :], ov[:, h, :], sm[:, h:h + 1], sm[:, 32 + h:33 + h], ALU.subtract, ALU.mult, [ok, smk], [ok])
            k.tt(o[:, 0:E], o[:, 0:E], gg[:, 0:E], ALU.mult, [ok, ggk], [ok])
            k.tt(o[:, 0:E], o[:, 0:E], gb[:, 0:E], ALU.add, [ok, gbk], [ok], eng="gpsimd")
            k.store(self.OACC, t * 128, t * 128 + 128, 0, E, o[:, 0:E], ok)
        EP.unpin(ggk); EP.unpin(gbk)
        net.transpose(self.OACC, self.OT, T, E)
        for j in range(NE):
            a, ak = EP.next(); g, gk = EP.next()
            k.load(a[:, 0:T], ak, self.OT, j * 128, j * 128 + 128, 0, T)
            k.load(g[:, 0:T], gk, net.PR, j * 128, j * 128 + 128, 0, T)
            ob, obk = net.EB.next()
            k.tt(ob[:, 0:T], a[:, 0:T], g[:, 0:T], ALU.mult, [ak, gk], [obk])
            k.store(net.S, j * 128, j * 128 + 128, 0, T, ob[:, 0:T], obk)
        net.gemm(net.S, net.WOB, None, E, T, D, epi=net.residual_epi(XS))


DEPTH = 4; NCORES = 8


def build_program():
    nc = bass.Bass("TRN2", target_bir_lowering=False)
    st = contextlib.ExitStack()
    k = K(nc, st); net = Net(k)
    ext = lambda n, shp: DT(nc, n, shp, kind="ExternalInput")
    XIN = ext("xin", [T, D]); CC = ext("cc", [2, D]); CONSTS = ext("consts", [128, 128])
    ADA_W = ext("ada_w", [4 * D, 3 * D]); ADA_B = ext("ada_b", [4, 3 * D]); NORM_G = ext("norm_g", [4, D])
    FNG = ext("final_norm_g", [1, D])
    FLAT = ext("flat", [LAT, 2 * LAT]); FCTX = ext("fctx", [CTX, 2 * CTX]); NEGT = ext("negt", [128, 18])
    DELTA = ext("delta", [1, E]); FEATS = ext("feats", [33, T])
    HY = []
    for j in range(2):
        HY.append(dict(W_IN=ext(f"hy_w_in{j}", [D, 4 * E]), CONV_WB=ext(f"hy_conv_wb{j}", [4, 3 * E]), HYF=ext(f"hy_f{j}", [4, 64]),
                       F_W1=ext(f"hy_f_w1{j}", [33, 64]), F_W2=ext(f"hy_f_w2{j}", [64, 64]), F_W3=ext(f"hy_f_w3{j}", [64, 4 * E]),
                       SKIP=ext(f"hy_skip{j}", [2, E]), W_OUT=ext(f"hy_w_out{j}", [E, D])))
    CF = dict(W_IN=ext("cf_w_in", [D, 3 * E]), DW_W=ext("cf_dw_w", [31, E]), DW_B=ext("cf_dw_b", [1, E]),
              LN_G=ext("cf_ln_g", [1, E]), LN_B=ext("cf_ln_b", [1, E]), W_OUT=ext("cf_w_out", [E, D]))
    RCOS = ext("rcos", [LAT, 128]); RSIN = ext("rsin", [LAT, 128]); RMASK = ext("rmask", [128, 512]); RCOLS = ext("rcols", [128, 4])
    RT = dict(W_IN=ext("rt_w_in", [D, 2 * D + 2 * E]), DECAY=ext("rt_decay", [1, 16]), GN_GB=ext("rt_gn_gb", [2, E]),
              W_OUT=ext("rt_w_out", [E, D]))
    XS = DT(nc, "XS", [T, D]); OUT = DT(nc, "out", [LAT, D], kind="ExternalOutput")
    hy = Hyena(net, FLAT, FCTX, NEGT, DELTA, FEATS); rt = Retention(net, RCOS, RSIN, RMASK, RCOLS)
    net.init_consts(CONSTS)
    net.copy_dram(XIN, XS, T, D)
    net.build_mrep(CC)
    for i in range(DEPTH):
        kind, j = i % 3, i // 3
        last = i == DEPTH - 1
        need_ctx = (not last) or kind == 2
        net.adaln(ADA_W, ADA_B, i)
        net.prenorm(XS, NORM_G, i, (T if need_ctx else LAT) // 128)
        if kind == 0:
            hy.layer(XS, need_ctx=need_ctx, **HY[j])
        elif kind == 1:
            net.conformer(XS, CF["W_IN"], CF["DW_W"], CF["DW_B"], CF["LN_G"], CF["LN_B"], CF["W_OUT"], need_ctx=need_ctx)
        else:
            rt.layer(XS, RT["W_IN"], RT["DECAY"], RT["GN_GB"], RT["W_OUT"])
    net.final_norm(XS, FNG, OUT)
    k.R.final_wait("sync", [k.R.last_w[key] for key in OUT.keys(0, LAT, 0, D)])
    k.R.emit(nc, None)
    st.close()
    return nc


def kernel(x, c, ctx, c_ctx, ada_w, ada_b, norm_g, final_norm_g,
           hy_w_in, hy_conv_w, hy_conv_b, hy_f_w1, hy_f_b1, hy_f_fr1, hy_f_w2, hy_f_b2,
           hy_f_fr2, hy_f_w3, hy_skip, hy_w_out,
           cf_w_in, cf_dw_w, cf_dw_b, cf_ln_g, cf_ln_b, cf_w_out,
           rt_w_in, rt_decay_logit, rt_gn_g, rt_gn_b, rt_w_out):
    f = lambda a: np.ascontiguousarray(np.asarray(a), dtype=np.float32)
    x, c, ctx, c_ctx = f(x), f(c), f(ctx), f(c_ctx)
    shared = {"consts": np.eye(128, dtype=np.float32), "ada_w": f(ada_w).reshape(4 * D, 3 * D), "ada_b": f(ada_b),
              "norm_g": f(norm_g), "final_norm_g": f(final_norm_g).reshape(1, D),
              "cf_w_in": f(cf_w_in)[0], "cf_dw_w": f(cf_dw_w)[0], "cf_dw_b": f(cf_dw_b).reshape(1, E),
              "cf_ln_g": f(cf_ln_g).reshape(1, E), "cf_ln_b": f(cf_ln_b).reshape(1, E), "cf_w_out": f(cf_w_out)[0],
              "rt_w_in": f(rt_w_in)[0], "rt_decay": f(rt_decay_logit)[0].reshape(1, 16),
              "rt_gn_gb": np.stack([f(rt_gn_g)[0], f(rt_gn_b)[0]]), "rt_w_out": f(rt_w_out)[0]}
    for j in range(2):
        shared.update({f"hy_w_in{j}": f(hy_w_in)[j], f"hy_conv_wb{j}": np.concatenate([f(hy_conv_w)[j], f(hy_conv_b)[j][None]], 0),
                       f"hy_f{j}": np.stack([f(hy_f_b1)[j], f(hy_f_fr1)[j], f(hy_f_b2)[j], f(hy_f_fr2)[j]]),
                       f"hy_f_w1{j}": f(hy_f_w1)[j], f"hy_f_w2{j}": f(hy_f_w2)[j], f"hy_f_w3{j}": f(hy_f_w3)[j],
                       f"hy_skip{j}": f(hy_skip)[j], f"hy_w_out{j}": f(hy_w_out)[j]})
    shared.update(hyena_consts()); shared.update(ret_consts())
    shared = {k_: np.ascontiguousarray(v, dtype=np.float32) for k_, v in shared.items()}
    in_maps = []
    for b in range(NCORES):
        m = dict(shared)
        m["xin"] = np.ascontiguousarray(np.concatenate([x[b], ctx[b]], 0))
        m["cc"] = np.ascontiguousarray(np.stack([c[b], c_ctx]))
        in_maps.append(m)
    nc = build_program()
    res = run_bass_kernel_spmd(nc, in_maps, core_ids=list(range(NCORES)))
    return np.stack([np.asarray(res.results[b]["out"], dtype=np.float32) for b in range(NCORES)], 0)
```

```python
import contextlib, math
import numpy as np
import concourse.bass as bass
import concourse.mybir as mybir
from concourse.bass_utils import run_bass_kernel_spmd

COMPUTE = ("tensor", "vector", "scalar", "gpsimd")
DMAQ = ("sync", "gpsimd")
NDS = 6


class Op:
    __slots__ = ("eng", "fn", "waits", "idx", "is_dma", "tick", "dsem", "dval", "name")


class Rec:
    def __init__(self):
        self.ops = {e: [] for e in ("tensor", "vector", "scalar", "gpsimd", "sync")}
        self.last_w = {}
        self.readers = {}
        self.ndma = {q: 0 for q in DMAQ}
        self.dma_tok = {q: [] for q in DMAQ}

    def op(self, eng, fn, reads=(), writes=(), dma=False, pe_accum=False, name=None):
        o = Op(); o.eng = eng; o.fn = fn; o.waits = []; o.is_dma = dma; o.tick = False
        o.name = name; o.dsem = None; o.dval = None
        deps = []
        for r in reads:
            t = self.last_w.get(r)
            if t is not None: deps.append(t)
        for w in writes:
            t = self.last_w.get(w)
            if t is not None and not (pe_accum and t[0] == "c" and t[1] == "tensor"):
                deps.append(t)
            deps.extend(self.readers.get(w, ()))
        if dma:
            q = eng; i = self.ndma[q]; self.ndma[q] += 1
            if i >= NDS: deps.append(self.dma_tok[q][i - NDS])
            tok = ("d", q, i % NDS, 16 * (i // NDS + 1))
            self.dma_tok[q].append(tok); o.dsem = (q, i % NDS)
        else:
            tok = ("c", eng, o)
        seen = set()
        for d in deps:
            if d is tok or id(d) in seen: continue
            seen.add(id(d))
            if d[0] == "c": d[2].tick = True
            o.waits.append(d)
        self.ops[eng].append(o)
        for w in writes:
            self.last_w[w] = tok; self.readers[w] = []
        for r in reads:
            if r not in writes:
                lst = self.readers.setdefault(r, [])
                if tok[0] == "c":
                    lst[:] = [t for t in lst if not (t[0] == "c" and t[1] == eng)]
                lst.append(tok)
        return tok

    def final_wait(self, eng, toks):
        o = Op(); o.eng = eng; o.fn = None; o.waits = list(toks); o.is_dma = False
        o.tick = False; o.name = "final"; o.dsem = None; o.dval = None
        for d in toks:
            if d[0] == "c": d[2].tick = True
        self.ops[eng].append(o)

    def emit(self, nc, block_engines):
        import contextlib
        for e in COMPUTE:
            n = 0
            for o in self.ops[e]:
                if o.tick and not o.is_dma:
                    n += 1; o.idx = n
        with contextlib.ExitStack() as st:
            csem = {e: st.enter_context(nc.semaphore("c_" + e)) for e in COMPUTE}
            dsem = {(q, k): st.enter_context(nc.semaphore(f"d_{q}{k}")) for q in DMAQ for k in range(NDS)}
            blk = st.enter_context(nc.Block())
            for e, lst in self.ops.items():
                if not lst: continue
                def body(eng, lst=lst, e=e):
                    known = {}
                    for o in lst:
                        need = {}
                        for d in o.waits:
                            if d[0] == "c": s, v = csem[d[1]], d[2].idx
                            else: s, v = dsem[(d[1], d[2])], d[3]
                            k = id(s)
                            if known.get(k, 0) >= v: continue
                            if k not in need or need[k][1] < v: need[k] = (s, v)
                        for k, (s, v) in need.items():
                            eng.wait_ge(s, v); known[k] = v
                        if o.fn is None: continue
                        ins = o.fn(eng)
                        if o.is_dma: ins.then_inc(dsem[o.dsem], 16)
                        elif o.tick: ins.then_inc(csem[e], 1)
                getattr(blk, e)(body)


F32 = mybir.dt.float32
BF16 = mybir.dt.bfloat16
AF = mybir.ActivationFunctionType
ALU = mybir.AluOpType
AX = mybir.AxisListType


class DT:
    def __init__(self, nc, name, shape, kind="Internal", rb=128, cb=512, dtype=F32):
        self.name = name; self.shape = tuple(shape); self.dtype = dtype
        self.ap = nc.dram_tensor(name, list(shape), dtype, kind=kind).ap()
        self.rb = rb; self.cb = cb

    def keys(self, r0, r1, c0, c1):
        return [(self.name, i, j) for i in range(r0 // self.rb, (r1 - 1) // self.rb + 1)
                for j in range(c0 // self.cb, (c1 - 1) // self.cb + 1)]


class Pool:
    def __init__(self, nc, st, name, shape, n, psum=False, dtype=F32):
        mk = nc.psum_tensor if psum else nc.sbuf_tensor
        self.bufs = [st.enter_context(mk(f"{name}{i}", list(shape), dtype)) for i in range(n)]
        self.keys = [(name, i) for i in range(n)]
        self.i = 0; self.pinned = set()

    def next(self):
        while True:
            k = self.i % len(self.bufs); self.i += 1
            if k not in self.pinned:
                return self.bufs[k], self.keys[k]

    def pin(self):
        b, key = self.next()
        self.pinned.add(self.keys.index(key))
        return b, key

    def unpin(self, key):
        self.pinned.discard(self.keys.index(key))


class K:
    def __init__(self, nc, st):
        self.nc = nc; self.st = st; self.R = Rec(); self.pools = {}
        self.qi = 0

    def pool(self, name, shape, n, psum=False, dtype=F32):
        if name not in self.pools:
            self.pools[name] = Pool(self.nc, self.st, name, shape, n, psum, dtype)
        return self.pools[name]

    def dma(self, out, in_, reads, writes, q=None):
        if q is None:
            q = "sync"
        self.R.op(q, lambda e: e.dma_start(out=out, in_=in_), reads=reads, writes=writes, dma=True)

    def load(self, sb, sbkey, dt, r0, r1, c0, c1, pat=None, q="sync", **kw):
        src = dt.ap[r0:r1, c0:c1]
        if pat: src = src.rearrange(pat, **kw)
        self.dma(sb, src, dt.keys(r0, r1, c0, c1), [sbkey], q=q)

    def store(self, dt, r0, r1, c0, c1, sb, sbkey, pat=None, q="gpsimd", **kw):
        dst = dt.ap[r0:r1, c0:c1]
        if pat: dst = dst.rearrange(pat, **kw)
        self.dma(dst, sb, [sbkey], dt.keys(r0, r1, c0, c1), q=q)

    def mm(self, ps, pskey, lhsT, rhs, start, stop, reads):
        self.R.op("tensor", lambda e: e.matmul(ps, lhsT=lhsT, rhs=rhs, start=start, stop=stop),
                  reads=reads, writes=[pskey], pe_accum=not start)

    def act(self, out, in_, func, reads, writes, eng="scalar", **kw):
        self.R.op(eng, lambda e: e.activation(out=out, in_=in_, func=func, **kw), reads=reads, writes=writes)

    def tt(self, out, a, b, op, reads, writes, eng="vector"):
        self.R.op(eng, lambda e: e.tensor_tensor(out=out, in0=a, in1=b, op=op), reads=reads, writes=writes)

    def ts(self, out, a, s1, s2, op0, op1, reads, writes, eng="vector", **kw):
        if s2 is None:
            self.R.op(eng, lambda e: e.tensor_scalar(out=out, in0=a, scalar1=s1, scalar2=None, op0=op0, **kw), reads=reads, writes=writes)
        else:
            self.R.op(eng, lambda e: e.tensor_scalar(out=out, in0=a, scalar1=s1, scalar2=s2, op0=op0, op1=op1, **kw), reads=reads, writes=writes)

    def stt(self, out, a, s, b, op0, op1, reads, writes, eng="vector"):
        self.R.op(eng, lambda e: e.scalar_tensor_tensor(out=out, in0=a, scalar=s, in1=b, op0=op0, op1=op1), reads=reads, writes=writes)

    def copy(self, out, in_, reads, writes, eng="vector"):
        self.R.op(eng, lambda e: e.tensor_copy(out=out, in_=in_), reads=reads, writes=writes)

    def memset(self, ap, val, writes, eng="vector"):
        self.R.op(eng, lambda e: e.memset(ap, val), reads=[], writes=writes)


D = 2048; E = 4096; LAT = 2048; CTX = 256; T = LAT + CTX; EPS = 1e-6
NE = E // 128


class Net:
    def __init__(self, k):
        self.k = k; nc = k.nc
        self.EP = k.pool("E", [128, 4096], 9)
        self.EB = k.pool("EB", [128, 4096], 5, dtype=BF16)
        self.PS = k.pool("ps", [128, 512], 8, psum=True)
        cp = k.pool("const", [128, 256], 1); cb, ck = cp.next()
        self.cb = cb; self.ck = ck
        self.ident = cb[:, 0:128]; self.ones = cb[:, 128:256]
        self.MREP = DT(nc, "MREP", [D, 256], dtype=BF16); self.ADAB = DT(nc, "ADAB", [256, 3 * D])
        self.H = DT(nc, "H", [T, D]); self.HT = DT(nc, "HT", [D, T], dtype=BF16)
        self.PR = DT(nc, "PR", [4 * E, T]); self.U = DT(nc, "U", [E, T]); self.S = DT(nc, "S", [E, T], dtype=BF16)
        self.WB = DT(nc, "WB", [D, 4 * E], dtype=BF16); self.WOB = DT(nc, "WOB", [E, D], dtype=BF16); self.ADAWB = DT(nc, "ADAWB", [D, 3 * D], dtype=BF16)

    def init_consts(self, CONSTS):
        k = self.k
        k.load(self.cb[:, 0:128], self.ck, CONSTS, 0, 128, 0, 128)
        k.memset(self.cb[:, 128:256], 1.0, [self.ck])

    def rows_to_pp(self, SRC, r0, n, c0, C, dst, dkey):
        k = self.k
        for cb in range(0, C, 2048):
            cw = min(2048, C - cb)
            sb, sk = self.EP.next()
            k.load(sb[0:n, 0:cw], sk, SRC, r0, r0 + n, c0 + cb, c0 + cb + cw)
            per = 512 // n
            for j0 in range(0, cw // 128, per):
                jn = min(per, cw // 128 - j0)
                ps, pk = self.PS.next()
                for j in range(jn):
                    k.mm(ps[:, j * n:(j + 1) * n], pk, sb[0:n, (j0 + j) * 128:(j0 + j + 1) * 128], self.ident[0:n, 0:n],
                         True, True, [sk, self.ck])
                t0 = cb // 128 + j0
                k.copy(dst[:, t0:t0 + jn, :], ps[:, 0:jn * n].rearrange("p (j n) -> p j n", n=n), [pk], [dkey])

    def build_mrep(self, CC):
        k = self.k
        tp = k.pool("tmpv", [128, 16, 2], 1); tb, tk = tp.next()
        self.rows_to_pp(CC, 0, 2, 0, D, tb, tk)
        sp = k.pool("tmps", [128, 16, 2], 1); sb, sk = sp.next()
        k.act(sb[:], tb[:], AF.Silu, [tk], [sk])
        for kc in range(16):
            rb, rk = self.EB.next()
            for s in range(2):
                k.ts(rb[:, s * 128:(s + 1) * 128], self.ones, sb[:, kc, s:s + 1], None, ALU.mult, None, [sk, self.ck], [rk])
            k.store(self.MREP, kc * 128, kc * 128 + 128, 0, 256, rb[:, 0:256], rk)

    def adaln(self, ADA_W, ADA_B, i):
        k = self.k
        def epi(ps, pk, m0, mt, n0, nw):
            bb, bk = self.EP.next()
            k.dma(bb[:, 0:nw], ADA_B.ap[i:i + 1, n0:n0 + nw].to_broadcast([128, nw]), ADA_B.keys(i, i + 1, n0, n0 + nw), [bk])
            ob, ok = self.EP.next()
            k.tt(ob[0:mt, 0:nw], ps, bb[0:mt, 0:nw], ALU.add, [pk, bk], [ok])
            k.store(self.ADAB, m0, m0 + mt, n0, n0 + nw, ob[0:mt, 0:nw], ok)
        self.cast_dram(ADA_W, self.ADAWB, D, 3 * D, s0=(i * D, 0))
        self.gemm(self.MREP, self.ADAWB, None, D, 256, 3 * D, epi=epi)

    def prenorm(self, XS, NORM_G, i, ntile):
        k = self.k
        gs, gsk = self.EP.pin(); sh, shk = self.EP.pin()
        gs = gs[:, :].rearrange("p (s d) -> p s d", s=2); sh = sh[:, :].rearrange("p (s d) -> p s d", s=2)
        gb, gk = self.EP.next()
        k.dma(gb[:, 0:D], NORM_G.ap[i:i + 1, :].to_broadcast([128, D]), NORM_G.keys(i, i + 1, 0, D), [gk])
        for s in range(2):
            tb, tk = self.EP.next()
            k.load(tb[:, 0:D], tk, self.ADAB, s * 128, s * 128 + 128, D, 2 * D)
            k.stt(gs[:, s, :], tb[:, 0:D], 1.0, gb[:, 0:D], ALU.add, ALU.mult, [tk, gk], [gsk])
            k.load(sh[:, s, :], shk, self.ADAB, s * 128, s * 128 + 128, 0, D)
        stp = k.pool("st", [128, 8], 2)
        for t in range(ntile):
            s = 0 if t < LAT // 128 else 1
            xb, xk = self.EP.next(); sq, sqk = self.EP.next(); st, stk = stp.next()
            k.load(xb[:, 0:D], xk, XS, t * 128, t * 128 + 128, 0, D)
            k.act(sq[:, 0:D], xb[:, 0:D], AF.Square, [xk], [sqk])
            k.R.op("vector", lambda e, st=st, sq=sq: e.reduce_sum(out=st[:, 0:1], in_=sq[:, 0:D], axis=AX.X), reads=[sqk], writes=[stk])
            k.ts(st[:, 1:2], st[:, 0:1], 1.0 / D, EPS, ALU.mult, ALU.add, [stk], [stk])
            k.act(st[:, 2:3], st[:, 1:2], AF.Sqrt, [stk], [stk])
            k.R.op("vector", lambda e, st=st: e.reciprocal(out=st[:, 3:4], in_=st[:, 2:3]), reads=[stk], writes=[stk])
            k.stt(sq[:, 0:D], xb[:, 0:D], st[:, 3:4], gs[:, s, :], ALU.mult, ALU.mult, [xk, stk, gsk], [sqk])
            k.tt(sq[:, 0:D], sq[:, 0:D], sh[:, s, :], ALU.add, [sqk, shk], [sqk])
            k.store(self.H, t * 128, t * 128 + 128, 0, D, sq[:, 0:D], sqk)
        self.EP.unpin(gsk); self.EP.unpin(shk)
        self.transpose(self.H, self.HT, ntile * 128, D)

    def residual_epi(self, XS):
        k = self.k
        def epi(ps, pk, m0, mt, n0, nw):
            s = 0 if m0 < LAT else 1
            xb, xk = self.EP.next(); gb, gk = self.EP.next()
            k.load(xb[0:mt, 0:nw], xk, XS, m0, m0 + mt, n0, n0 + nw)
            k.load(gb[0:mt, 0:nw], gk, self.ADAB, s * 128, s * 128 + mt, 2 * D + n0, 2 * D + n0 + nw)
            k.tt(gb[0:mt, 0:nw], ps, gb[0:mt, 0:nw], ALU.mult, [pk, gk], [gk])
            k.tt(xb[0:mt, 0:nw], xb[0:mt, 0:nw], gb[0:mt, 0:nw], ALU.add, [xk, gk], [xk])
            k.store(XS, m0, m0 + mt, n0, n0 + nw, xb[0:mt, 0:nw], xk)
        return epi

    def copy_dram(self, SRC, DST, rows, cols, s0=(0, 0), d0=(0, 0)):
        k = self.k
        for r in range(0, rows, 128):
            rw = min(128, rows - r)
            k.dma(DST.ap[d0[0] + r:d0[0] + r + rw, d0[1]:d0[1] + cols], SRC.ap[s0[0] + r:s0[0] + r + rw, s0[1]:s0[1] + cols],
                  SRC.keys(s0[0] + r, s0[0] + r + rw, s0[1], s0[1] + cols), DST.keys(d0[0] + r, d0[0] + r + rw, d0[1], d0[1] + cols))

    def cast_dram(self, SRC, DST, rows, cols, s0=(0, 0), d0=(0, 0)):
        k = self.k; engs = ("vector", "gpsimd", "scalar"); n = 0
        for r in range(0, rows, 128):
            rw = min(128, rows - r)
            for c in range(0, cols, 4096):
                cw = min(4096, cols - c)
                a, ak = self.EP.next(); b, bk = self.EB.next()
                k.load(a[0:rw, 0:cw], ak, SRC, s0[0] + r, s0[0] + r + rw, s0[1] + c, s0[1] + c + cw)
                e = engs[n % 3]; n += 1
                if e == "scalar": k.act(b[0:rw, 0:cw], a[0:rw, 0:cw], AF.Copy, [ak], [bk])
                else: k.copy(b[0:rw, 0:cw], a[0:rw, 0:cw], [ak], [bk], eng=e)
                k.store(DST, d0[0] + r, d0[0] + r + rw, d0[1] + c, d0[1] + c + cw, b[0:rw, 0:cw], bk)

    def final_norm(self, XS, FG, OUT):
        k = self.k
        gb, gk = self.EP.pin()
        k.dma(gb[:, 0:D], FG.ap[0:1, :].to_broadcast([128, D]), FG.keys(0, 1, 0, D), [gk])
        stp = k.pool("st", [128, 8], 2)
        for t in range(LAT // 128):
            xb, xk = self.EP.next(); sq, sqk = self.EP.next(); st, stk = stp.next()
            k.load(xb[:, 0:D], xk, XS, t * 128, t * 128 + 128, 0, D)
            k.act(sq[:, 0:D], xb[:, 0:D], AF.Square, [xk], [sqk])
            k.R.op("vector", lambda e, st=st, sq=sq: e.reduce_sum(out=st[:, 0:1], in_=sq[:, 0:D], axis=AX.X), reads=[sqk], writes=[stk])
            k.ts(st[:, 1:2], st[:, 0:1], 1.0 / D, EPS, ALU.mult, ALU.add, [stk], [stk])
            k.act(st[:, 2:3], st[:, 1:2], AF.Sqrt, [stk], [stk])
            k.R.op("vector", lambda e, st=st: e.reciprocal(out=st[:, 3:4], in_=st[:, 2:3]), reads=[stk], writes=[stk])
            k.stt(sq[:, 0:D], xb[:, 0:D], st[:, 3:4], gb[:, 0:D], ALU.mult, ALU.mult, [xk, stk, gk], [sqk])
            k.store(OUT, t * 128, t * 128 + 128, 0, D, sq[:, 0:D], sqk)
        self.EP.unpin(gk)

    def gemm(self, L, R, OUT, K, M, N, l0=(0, 0), r0=(0, 0), o0=(0, 0), epi=None, act=None):
        k = self.k
        MG, NB, KP = 512, 512, 8
        assert L.dtype == R.dtype, (L.name, R.name)
        PP = self.EB if L.dtype == BF16 else self.EP
        kch = (K + 127) // 128; kparts = min(K, 128); npan = (kch + KP - 1) // KP
        for mg in range(0, M, MG):
            mw = min(MG, M - mg)
            lcache = None
            for nb in range(0, N, NB):
                nw = min(NB, N - nb)
                pss = [self.PS.next() for _ in range((mw + 127) // 128)]
                for kp in range(npan):
                    kc0 = kp * KP; kcn = min(KP, kch - kc0)
                    ka = kc0 * 128; kb = min(K, (kc0 + kcn) * 128)
                    if True:
                        lb, lk = PP.next()
                        k.load(lb[0:kparts, 0:kcn * mw].rearrange("p (c m) -> p c m", m=mw), lk, L, l0[0] + ka, l0[0] + kb,
                               l0[1] + mg, l0[1] + mg + mw, "(c p) m -> p c m", p=kparts)
                        lcache = (lb, lk)
                    rb, rk = PP.next()
                    k.load(rb[0:kparts, 0:kcn * nw].rearrange("p (c m) -> p c m", m=nw), rk, R, r0[0] + ka, r0[0] + kb,
                           r0[1] + nb, r0[1] + nb + nw, "(c p) m -> p c m", p=kparts)
                    for g, (ps, pk) in enumerate(pss):
                        mt = min(128, mw - g * 128)
                        for kc in range(kcn):
                            k.mm(ps[0:mt, 0:nw], pk, lb[0:kparts, kc * mw + g * 128:kc * mw + g * 128 + mt],
                                 rb[0:kparts, kc * nw:(kc + 1) * nw],
                                 start=(kp == 0 and kc == 0), stop=(kp == npan - 1 and kc == kcn - 1), reads=[lk, rk])
                for g, (ps, pk) in enumerate(pss):
                    mt = min(128, mw - g * 128); m0 = mg + g * 128
                    if epi is not None:
                        epi(ps[0:mt, 0:nw], pk, m0, mt, nb, nw)
                    else:
                        ob, ok = self.EP.next()
                        k.act(ob[0:mt, 0:nw], ps[0:mt, 0:nw], act(m0) if callable(act) else (act or AF.Copy), [pk], [ok])
                        k.store(OUT, o0[0] + m0, o0[0] + m0 + mt, o0[1] + nb, o0[1] + nb + nw, ob[0:mt, 0:nw], ok)

    def transpose(self, SRC, DST, Rn, Cn, s0=(0, 0), d0=(0, 0)):
        k = self.k
        for rb in range(0, Rn, 512):
            rw = min(512, Rn - rb); nrt = rw // 128
            for cb in range(0, Cn, 512):
                cw = min(512, Cn - cb)
                ib, ik = self.EP.next()
                k.load(ib[:, 0:nrt * cw].rearrange("p (t c) -> p t c", c=cw), ik, SRC, s0[0] + rb, s0[0] + rb + rw,
                       s0[1] + cb, s0[1] + cb + cw, "(t p) c -> p t c", p=128)
                for ct in range(cw // 128):
                    ps, pk = self.PS.next()
                    for t in range(nrt):
                        o = ps[:, t * 128:(t + 1) * 128]; i = ib[:, t * cw + ct * 128:t * cw + ct * 128 + 128]
                        k.R.op("tensor", lambda e, o=o, i=i: e.transpose(o, i, self.ident), reads=[ik, self.ck], writes=[pk], pe_accum=(t > 0))
                    ob, ok = (self.EB if DST.dtype == BF16 else self.EP).next()
                    k.copy(ob[:, 0:rw], ps[:, 0:rw], [pk], [ok])
                    k.store(DST, d0[0] + cb + ct * 128, d0[0] + cb + ct * 128 + 128, d0[1] + rb, d0[1] + rb + rw, ob[:, 0:rw], ok)

    def conformer(self, XS, W_IN, DW_W, DW_B, LN_G, LN_B, W_OUT, need_ctx=True):
        k = self.k
        Tn = T if need_ctx else LAT
        segs = [(0, LAT)] + ([(LAT, CTX)] if need_ctx else [])
        cwp = k.pool("cfw", [128, NE, 34], 1); cw, cwk = cwp.next()
        self.rows_to_pp(DW_W, 0, 31, 0, E, cw[:, :, 0:31], cwk)
        for n, V in enumerate((DW_B, LN_G, LN_B)):
            self.rows_to_pp(V, 0, 1, 0, E, cw[:, :, 31 + n:32 + n], cwk)
        fn = lambda m0: AF.Copy if m0 < E else (AF.Sigmoid if m0 < 2 * E else AF.Silu)
        self.cast_dram(W_IN, self.WB, D, 3 * E); self.cast_dram(W_OUT, self.WOB, E, D)
        self.gemm(self.WB, self.HT, self.PR, D, 3 * E, Tn, act=fn)
        st0, st0k = self.EP.pin(); st1, st1k = self.EP.pin()
        k.memset(st0[:, 0:T], 0.0, [st0k]); k.memset(st1[:, 0:T], 0.0, [st1k])
        PAD = 15
        for j in range(NE):
            a, ak = self.EP.next(); sb, sk = self.EP.next(); u0, uk = self.EP.next()
            acc, acck = self.EP.next(); acc2, acc2k = self.EP.next()
            k.load(a[:, 0:Tn], ak, self.PR, j * 128, j * 128 + 128, 0, Tn)
            k.load(sb[:, 0:Tn], sk, self.PR, E + j * 128, E + j * 128 + 128, 0, Tn)
            k.memset(u0[:, 0:Tn + 4 * PAD], 0.0, [uk], eng="gpsimd")
            for si, (s0, sl) in enumerate(segs):
                o = s0 + (2 * si + 1) * PAD
                k.tt(u0[:, o:o + sl], a[:, s0:s0 + sl], sb[:, s0:s0 + sl], ALU.mult, [ak, sk], [uk])
            NV = 31
            for si, (s0, sl) in enumerate(segs):
                o = s0 + 2 * si * PAD
                for tap in range(31):
                    eng, ac, ack = ("vector", acc, acck) if tap < NV else ("gpsimd", acc2, acc2k)
                    src = u0[:, o + tap:o + tap + sl]
                    if tap == 0:
                        k.ts(ac[:, s0:s0 + sl], src, cw[:, j, 0:1], cw[:, j, 31:32], ALU.mult, ALU.add, [uk, cwk], [ack], eng=eng)
                    elif tap == NV:
                        k.ts(ac[:, s0:s0 + sl], src, cw[:, j, tap:tap + 1], None, ALU.mult, None, [uk, cwk], [ack], eng=eng)
                    else:
                        k.stt(ac[:, s0:s0 + sl], src, cw[:, j, tap:tap + 1], ac[:, s0:s0 + sl], ALU.mult, ALU.add, [uk, cwk, ack], [ack], eng=eng)
            k.act(acc2[:, 0:Tn], acc[:, 0:Tn], AF.Square, [acck], [acc2k])
            k.tt(st0[:, 0:Tn], st0[:, 0:Tn], acc[:, 0:Tn], ALU.add, [st0k, acck], [st0k], eng="gpsimd")
            k.tt(st1[:, 0:Tn], st1[:, 0:Tn], acc2[:, 0:Tn], ALU.add, [st1k, acc2k], [st1k], eng="gpsimd")
            k.store(self.U, j * 128, j * 128 + 128, 0, Tn, acc[:, 0:Tn], acck)
        for nb in range(0, Tn, 512):
            nw = min(512, Tn - nb)
            p1, p1k = self.PS.next(); p2, p2k = self.PS.next()
            k.mm(p1[:, 0:nw], p1k, self.ones, st0[:, nb:nb + nw], True, True, [self.ck, st0k])
            k.mm(p2[:, 0:nw], p2k, self.ones, st1[:, nb:nb + nw], True, True, [self.ck, st1k])
            m, mk = self.EP.next()
            k.ts(st0[:, nb:nb + nw], p1[:, 0:nw], 1.0 / E, None, ALU.mult, None, [p1k], [st0k])
            k.tt(m[:, 0:nw], st0[:, nb:nb + nw], st0[:, nb:nb + nw], ALU.mult, [st0k], [mk])
            k.stt(m[:, 0:nw], p2[:, 0:nw], 1.0 / E, m[:, 0:nw], ALU.mult, ALU.subtract, [p2k, mk], [mk])
            k.ts(m[:, 0:nw], m[:, 0:nw], EPS, None, ALU.add, None, [mk], [mk])
            k.act(m[:, 0:nw], m[:, 0:nw], AF.Sqrt, [mk], [mk])
            k.R.op("vector", lambda e, o=st1[:, nb:nb + nw], i=m[:, 0:nw]: e.reciprocal(out=o, in_=i), reads=[mk], writes=[st1k])
        for j in range(NE):
            u, uk = self.EP.next(); g, gk = self.EP.next()
            k.load(u[:, 0:Tn], uk, self.U, j * 128, j * 128 + 128, 0, Tn)
            k.load(g[:, 0:Tn], gk, self.PR, 2 * E + j * 128, 2 * E + j * 128 + 128, 0, Tn)
            k.tt(u[:, 0:Tn], u[:, 0:Tn], st0[:, 0:Tn], ALU.subtract, [uk, st0k], [uk])
            k.tt(u[:, 0:Tn], u[:, 0:Tn], st1[:, 0:Tn], ALU.mult, [uk, st1k], [uk])
            k.act(u[:, 0:Tn], u[:, 0:Tn], AF.Silu, [uk, cwk], [uk], bias=cw[:, j, 33:34], scale=cw[:, j, 32:33])
            sb_, sbk_ = self.EB.next()
            k.tt(sb_[:, 0:Tn], u[:, 0:Tn], g[:, 0:Tn], ALU.mult, [uk, gk], [sbk_], eng="gpsimd")
            k.store(self.S, j * 128, j * 128 + 128, 0, Tn, sb_[:, 0:Tn], sbk_)
        self.EP.unpin(st0k); self.EP.unpin(st1k)
        self.gemm(self.S, self.WOB, None, E, Tn, D, epi=self.residual_epi(XS))


TWO_PI = 2.0 * math.pi


def hyena_consts():
    c = {}
    def fmat(L):
        N = 2 * L
        t = np.arange(L, dtype=np.float64)[:, None]; kk = np.arange(L, dtype=np.float64)[None, :]
        ang = 2.0 * np.pi * ((t * kk) % N) / N
        A = np.cos(ang); B = -np.sin(ang); B[:, 0] = np.cos(np.pi * t[:, 0])
        return np.concatenate([A, B], 1).astype(np.float32)
    c["flat"] = fmat(LAT); c["fctx"] = fmat(CTX)
    negt = np.zeros((128, 18), np.float32)
    negt[:, 0:16] = -(np.linspace(0.0, 1.0, LAT, dtype=np.float32).reshape(16, 128).T)
    negt[:, 16:18] = -(np.linspace(0.0, 1.0, CTX, dtype=np.float32).reshape(2, 128).T)
    c["negt"] = negt
    c["delta"] = np.abs(np.linspace(math.log(1e-2) / 1.5, math.log(1e-2) / 0.3, E, dtype=np.float32))[None, :]
    def feats(L):
        t = np.linspace(0.0, 1.0, L, dtype=np.float32)[:, None]
        w = (2.0 * math.pi / L) * np.arange(L, dtype=np.float32)[:, None]
        f = np.linspace(1e-4, 15, 16, dtype=np.float32)[None, :]
        return np.concatenate([t, np.cos(f * w), -np.sin(f * w)], -1).astype(np.float32).T
    c["feats"] = np.concatenate([feats(LAT), feats(CTX)], 1)
    return {k_: np.ascontiguousarray(v, dtype=np.float32) for k_, v in c.items()}


class Hyena:
    def __init__(self, net, FLAT, FCTX, NEGT, DELTA, FEATS):
        self.net = net; k = net.k; nc = k.nc
        self.F = {LAT: FLAT, CTX: FCTX}; self.NEGT = NEGT; self.DELTA = DELTA; self.FEATS = FEATS
        self.FT = {LAT: DT(nc, "FTLAT", [2 * LAT, LAT], dtype=BF16), CTX: DT(nc, "FTCTX", [2 * CTX, CTX], dtype=BF16)}
        self.FB = {LAT: DT(nc, "FBLAT", [LAT, 2 * LAT], dtype=BF16), CTX: DT(nc, "FBCTX", [CTX, 2 * CTX], dtype=BF16)}
        self.H1T = DT(nc, "H1T", [64, LAT]); self.H2T = DT(nc, "H2T", [64, LAT])
        self.HF = DT(nc, "HF", [LAT, 4 * E]); self.HS = [DT(nc, f"HS{n}", [LAT, E], dtype=BF16) for n in range(2)]
        self.HD = [DT(nc, f"HD{n}", [LAT, E], dtype=BF16) for n in range(2)]
        self.SPEC = [DT(nc, f"SPEC{n}", [2 * LAT, E]) for n in range(2)]
        self.ZT = DT(nc, "ZT", [LAT, E], dtype=BF16); self.ZF = DT(nc, "ZF", [2 * LAT, E]); self.YF = DT(nc, "YF", [2 * LAT, E], dtype=BF16); self.Y = DT(nc, "Y", [E, LAT])
        self.Z1 = DT(nc, "Z1", [E, T])
        self.ft_done = False

    def setup(self):
        net = self.net; k = net.k
        if not self.ft_done:
            for L in (LAT, CTX):
                net.transpose(self.F[L], self.FT[L], L, 2 * L)
                net.cast_dram(self.F[L], self.FB[L], L, 2 * L)
            sp = k.pool("hyneg", [128, 18], 1); self.negt, self.negtk = sp.next()
            k.load(self.negt[:], self.negtk, self.NEGT, 0, 128, 0, 18)
            self.ft_done = True

    def layer(self, XS, W_IN, CONV_WB, HYF, F_W1, F_W2, F_W3, SKIP, W_OUT, need_ctx):
        net = self.net; k = net.k; EP = net.EP; PS = net.PS
        self.setup()
        Tn = T if need_ctx else LAT
        segs = [(0, LAT, 0)] + ([(LAT, CTX, 16)] if need_ctx else [])
        cvp = k.pool("hycv", [128, 96, 4], 1); cv, cvk = cvp.next()
        net.rows_to_pp(CONV_WB, 0, 4, 0, 3 * E, cv, cvk)
        skp = k.pool("hysk", [128, NE, 2], 1); sk, skk = skp.next()
        net.rows_to_pp(SKIP, 0, 2, 0, E, sk, skk)
        fpp_p = k.pool("hyf", [128, 4], 1); fpp, fppk = fpp_p.next()
        tb, tk = EP.next()
        k.load(tb[0:4, 0:64], tk, HYF, 0, 4, 0, 64)
        ps, pk = PS.next()
        k.mm(ps[0:64, 0:4], pk, tb[0:4, 0:64], net.ident[0:4, 0:4], True, True, [tk, net.ck])
        k.copy(fpp[0:64, :], ps[0:64, 0:4], [pk], [fppk])
        dl, dlk = EP.pin()
        k.dma(dl[:, 0:E], self.DELTA.ap[0:1, :].to_broadcast([128, E]), self.DELTA.keys(0, 1, 0, E), [dlk])
        net.cast_dram(W_IN, net.WB, D, 4 * E); net.cast_dram(W_OUT, net.WOB, E, D)
        net.gemm(net.WB, net.HT, net.PR, D, 4 * E, Tn, act=lambda m0: AF.Silu if m0 >= 3 * E else AF.Copy)
        for j in range(96):
            u, uk = EP.next(); o, ok = EP.next()
            for si, (s0, sl, _) in enumerate(segs):
                b = s0 + 3 * si
                k.memset(u[:, b:b + 1], 0.0, [uk], eng="gpsimd"); k.memset(u[:, b + sl + 1:b + sl + 2], 0.0, [uk], eng="gpsimd")
                k.load(u[:, b + 1:b + 1 + sl], uk, net.PR, j * 128, j * 128 + 128, s0, s0 + sl)
            for si, (s0, sl, _) in enumerate(segs):
                b = s0 + 3 * si
                k.ts(o[:, s0:s0 + sl], u[:, b:b + sl], cv[:, j, 0:1], cv[:, j, 3:4], ALU.mult, ALU.add, [uk, cvk], [ok])
                for tap in (1, 2):
                    k.stt(o[:, s0:s0 + sl], u[:, b + tap:b + tap + sl], cv[:, j, tap:tap + 1], o[:, s0:s0 + sl], ALU.mult, ALU.add, [uk, cvk, ok], [ok])
            k.store(net.PR, j * 128, j * 128 + 128, 0, Tn, o[:, 0:Tn], ok)
        for (s0, L, nt0) in segs:
            self.filters(L, s0, nt0, HYF, F_W1, F_W2, F_W3, fpp, fppk, dl, dlk)
            self.longconv(L, 0, net.PR, 0, s0)
            self.combine(L, s0, 0, sk, skk, zsrc=(net.PR, 0), xrow=E, dst=self.Z1)
            self.longconv(L, 1, self.Z1, 0, s0)
            self.combine(L, s0, 1, sk, skk, zsrc=(self.Z1, 0), xrow=2 * E, dst=net.S, gate_row=3 * E)
        EP.unpin(dlk)
        net.gemm(net.S, net.WOB, None, E, Tn, D, epi=net.residual_epi(XS))

    def sin_epi(self, OUT, fpp, fppk, cb, cfr):
        net = self.net; k = net.k; EP = net.EP
        def epi(ps, pk, m0, mt, n0, nw):
            a, ak = EP.next()
            k.ts(a[0:mt, 0:nw], ps, fpp[0:mt, cb:cb + 1], fpp[0:mt, cfr:cfr + 1], ALU.add, ALU.mult, [pk, fppk], [ak])
            m, mk = EP.next()
            for lvl in range(2):
                k.ts(m[0:mt, 0:nw], a[0:mt, 0:nw], math.pi, -TWO_PI, ALU.is_gt, ALU.mult, [ak], [mk])
                k.tt(a[0:mt, 0:nw], a[0:mt, 0:nw], m[0:mt, 0:nw], ALU.add, [ak, mk], [ak])
                k.ts(m[0:mt, 0:nw], a[0:mt, 0:nw], -math.pi, TWO_PI, ALU.is_lt, ALU.mult, [ak], [mk])
                k.tt(a[0:mt, 0:nw], a[0:mt, 0:nw], m[0:mt, 0:nw], ALU.add, [ak, mk], [ak])
            k.act(a[0:mt, 0:nw], a[0:mt, 0:nw], AF.Sin, [ak], [ak])
            k.store(OUT, m0, m0 + mt, n0, n0 + nw, a[0:mt, 0:nw], ak)
        return epi

    def filters(self, L, s0, nt0, HYF, F_W1, F_W2, F_W3, fpp, fppk, dl, dlk):
        net = self.net; k = net.k; EP = net.EP
        N = 2 * L; F = self.FB[L]
        net.gemm(F_W1, self.FEATS, None, 33, 64, L, r0=(0, s0), epi=self.sin_epi(self.H1T, fpp, fppk, 0, 1))
        net.gemm(F_W2, self.H1T, None, 64, 64, L, epi=self.sin_epi(self.H2T, fpp, fppk, 2, 3))
        def wepi(ps, pk, m0, mt, n0, nw):
            e0 = n0 % E
            w, wk = EP.next(); ob, ok = EP.next()
            k.act(w[0:mt, 0:nw], dl[0:mt, e0:e0 + nw], AF.Exp, [dlk, self.negtk], [wk], scale=self.negt[0:mt, nt0 + m0 // 128:nt0 + m0 // 128 + 1])
            k.tt(ob[0:mt, 0:nw], ps, w[0:mt, 0:nw], ALU.mult, [pk, wk], [ok])
            k.store(self.HF, m0, m0 + mt, n0, n0 + nw, ob[0:mt, 0:nw], ok)
        net.gemm(self.H2T, F_W3, None, 64, L, 4 * E, epi=wepi)
        for n in range(2):
            for t in range(L // 128):
                for eb in range(0, E, 2048):
                    f, fk = EP.next(); b, bk = EP.next(); hs, hsk = net.EB.next(); hd, hdk = net.EB.next()
                    k.load(f[:, 0:2048], fk, self.HF, t * 128, t * 128 + 128, (2 * n) * E + eb, (2 * n) * E + eb + 2048)
                    k.load(b[:, 0:2048], bk, self.HF, t * 128, t * 128 + 128, (2 * n + 1) * E + eb, (2 * n + 1) * E + eb + 2048)
                    if t == 0:
                        k.memset(b[0:1, 0:2048], 0.0, [bk])
                    k.tt(hs[:, 0:2048], f[:, 0:2048], b[:, 0:2048], ALU.add, [fk, bk], [hsk])
                    k.tt(hd[:, 0:2048], f[:, 0:2048], b[:, 0:2048], ALU.subtract, [fk, bk], [hdk], eng="gpsimd")
                    k.store(self.HS[n], t * 128, t * 128 + 128, eb, eb + 2048, hs[:, 0:2048], hsk)
                    k.store(self.HD[n], t * 128, t * 128 + 128, eb, eb + 2048, hd[:, 0:2048], hdk)
            def sepi(row_off, fix0, scale, row0_only=False):
                def epi(ps, pk, m0, mt, n0, nw):
                    if row0_only:
                        mt = 1
                        ps = ps[0:1, :]
                    ob, ok = EP.next()
                    k.act(ob[0:mt, 0:nw], ps, AF.Copy, [pk], [ok], scale=scale)
                    if fix0 and m0 == 0:
                        k.ts(ob[0:1, 0:nw], ob[0:1, 0:nw], 0.5, None, ALU.mult, None, [ok], [ok])
                    k.store(self.SPEC[n], row_off + m0, row_off + m0 + mt, n0, n0 + nw, ob[0:mt, 0:nw], ok)
                return epi
            net.gemm(F, self.HS[n], None, L, L, E, l0=(0, 0), epi=sepi(0, True, 2.0 / N))
            net.gemm(F, self.HD[n], None, L, L, E, l0=(0, L), epi=sepi(L, False, 2.0 / N))
            net.gemm(F, self.HS[n], None, L, 128, E, l0=(0, L), epi=sepi(L, False, 1.0 / N, True))

    def longconv(self, L, n, SRC, row0, s0):
        net = self.net; k = net.k; EP = net.EP
        F = self.FB[L]; FT = self.FT[L]; SP = self.SPEC[n]
        net.transpose(SRC, self.ZT, E, L, s0=(row0, s0))
        net.gemm(F, self.ZT, self.ZF, L, 2 * L, E)
        for kt in range(L // 128):
            for eb in range(0, E, 2048):
                a, ak = EP.next(); b, bk = EP.next(); sa, sak = EP.next(); sb, sbk = EP.next()
                ya, yak = EP.next(); yb, ybk = EP.next(); t1, t1k = EP.next(); oa, oak = net.EB.next(); ob2, ob2k = net.EB.next()
                W = 2048
                r = kt * 128
                k.load(a[:, 0:W], ak, self.ZF, r, r + 128, eb, eb + W); k.load(b[:, 0:W], bk, self.ZF, L + r, L + r + 128, eb, eb + W)
                k.load(sa[:, 0:W], sak, SP, r, r + 128, eb, eb + W); k.load(sb[:, 0:W], sbk, SP, L + r, L + r + 128, eb, eb + W)
                k.tt(ya[:, 0:W], a[:, 0:W], sa[:, 0:W], ALU.mult, [ak, sak], [yak])
                k.tt(t1[:, 0:W], b[:, 0:W], sb[:, 0:W], ALU.mult, [bk, sbk], [t1k], eng="gpsimd")
                k.tt(yb[:, 0:W], a[:, 0:W], sb[:, 0:W], ALU.mult, [ak, sbk], [ybk])
                k.tt(t1[:, 0:W], ya[:, 0:W], t1[:, 0:W], ALU.subtract, [yak, t1k], [t1k])
                k.tt(b[:, 0:W], b[:, 0:W], sa[:, 0:W], ALU.mult, [bk, sak], [bk], eng="gpsimd")
                k.tt(yb[:, 0:W], yb[:, 0:W], b[:, 0:W], ALU.add, [ybk, bk], [ybk])
                if kt == 0:
                    k.copy(t1[0:1, 0:W], ya[0:1, 0:W], [yak], [t1k])
                    k.load(b[0:1, 0:W], bk, self.ZF, L, L + 1, eb, eb + W)
                    k.tt(yb[0:1, 0:W], b[0:1, 0:W], sb[0:1, 0:W], ALU.mult, [bk, sbk], [ybk])
                k.copy(oa[:, 0:W], t1[:, 0:W], [t1k], [oak], eng="gpsimd")
                k.act(ob2[:, 0:W], yb[:, 0:W], AF.Copy, [ybk], [ob2k])
                k.store(self.YF, r, r + 128, eb, eb + W, oa[:, 0:W], oak)
                k.store(self.YF, L + r, L + r + 128, eb, eb + W, ob2[:, 0:W], ob2k)
        net.gemm(self.YF, FT, self.Y, 2 * L, E, L)

    def combine(self, L, s0, n, sk, skk, zsrc, xrow, dst, gate_row=None):
        net = self.net; k = net.k; EP = net.EP
        ZS, zrow = zsrc
        for j in range(NE):
            y, yk = EP.next(); z, zk = EP.next(); x, xk = EP.next()
            k.load(y[:, 0:L], yk, self.Y, j * 128, j * 128 + 128, 0, L)
            k.load(z[:, 0:L], zk, ZS, zrow + j * 128, zrow + j * 128 + 128, s0, s0 + L)
            k.load(x[:, 0:L], xk, net.PR, xrow + j * 128, xrow + j * 128 + 128, s0, s0 + L)
            k.stt(y[:, 0:L], z[:, 0:L], sk[:, j, n:n + 1], y[:, 0:L], ALU.mult, ALU.add, [zk, skk, yk], [yk])
            k.tt(y[:, 0:L], y[:, 0:L], x[:, 0:L], ALU.mult, [yk, xk], [yk])
            if gate_row is not None:
                k.load(z[:, 0:L], zk, net.PR, gate_row + j * 128, gate_row + j * 128 + 128, s0, s0 + L)
                ob, obk = net.EB.next()
                k.tt(ob[:, 0:L], y[:, 0:L], z[:, 0:L], ALU.mult, [yk, zk], [obk], eng="gpsimd")
                k.store(dst, j * 128, j * 128 + 128, s0, s0 + L, ob[:, 0:L], obk)
                continue
            k.store(dst, j * 128, j * 128 + 128, s0, s0 + L, y[:, 0:L], yk)


RT_H = 8; RT_DK = 256; RT_DV = 512; CH = 128


def ret_consts():
    c = {}
    half = 64
    inv = (10000.0 ** (-np.arange(half, dtype=np.float32) / half)).astype(np.float32)
    rows = np.repeat(np.arange(LAT // 64), 64).astype(np.float32); cols = np.tile(np.arange(64), LAT // 64).astype(np.float32)
    ar = rows[:, None] * inv; ac = cols[:, None] * inv
    c["rcos"] = np.concatenate([np.cos(ar), np.cos(ac)], 1)
    c["rsin"] = np.concatenate([np.sin(ar), np.sin(ac)], 1)
    p = np.arange(128, dtype=np.float32)
    j = p[:, None]; i = p[None, :]
    c["rmask"] = np.concatenate([np.maximum(i - j, 0), (i >= j).astype(np.float32), np.maximum(j - i, 0), (j >= i).astype(np.float32)], 1)
    c["rcols"] = np.stack([p + 1, CH - 1 - p, CH - p, p], 1)
    return {k_: np.ascontiguousarray(v, dtype=np.float32) for k_, v in c.items()}


class Retention:
    def __init__(self, net, RCOS, RSIN, RMASK, RCOLS):
        self.net = net; nc = net.k.nc
        self.RCOS = RCOS; self.RSIN = RSIN; self.RMASK = RMASK; self.RCOLS = RCOLS
        self.QKV = DT(nc, "QKV", [T, 2 * D + E]); self.QKT = DT(nc, "QKT", [2 * D, T])
        self.OACC = DT(nc, "OACC", [T, E]); self.OT = DT(nc, "OTR", [E, T])

    def layer(self, XS, W_IN, DECAY, GN_GB, W_OUT):
        net = self.net; k = net.k; EP = net.EP; PS = net.PS
        def qkv_epi(ps, pk, m0, mt, n0, nw):
            ob, ok = EP.next()
            k.act(ob[0:mt, 0:nw], ps, AF.Copy, [pk], [ok], scale=(RT_DK ** -0.5 if D <= n0 < 2 * D else 1.0))
            k.store(self.QKV, m0, m0 + mt, n0, n0 + nw, ob[0:mt, 0:nw], ok)
        net.cast_dram(W_IN, net.WB, D, 2 * D + 2 * E); net.cast_dram(W_OUT, net.WOB, E, D)
        net.gemm(net.HT, net.WB, None, D, T, 2 * D + E, epi=qkv_epi)
        net.gemm(net.WB, net.HT, net.PR, D, E, T, l0=(0, 2 * D + E), act=AF.Silu)
        for t in range(LAT // 128):
            x, xk = EP.next(); o, ok = EP.next(); cs, csk = EP.next(); t1, t1k = EP.next(); t2, t2k = EP.next()
            k.load(x[:, 0:2 * D], xk, self.QKV, t * 128, t * 128 + 128, 0, 2 * D)
            k.load(cs[:, 0:128], csk, self.RCOS, t * 128, t * 128 + 128, 0, 128)
            k.load(cs[:, 128:256], csk, self.RSIN, t * 128, t * 128 + 128, 0, 128)
            xv = x[:, 0:2 * D].rearrange("p (h a s f) -> p h a s f", h=16, a=2, s=2)
            ov = o[:, 0:2 * D].rearrange("p (h a s f) -> p h a s f", h=16, a=2, s=2)
            cv = cs[:, 0:128].rearrange("p (a f) -> p a f", a=2); sv = cs[:, 128:256].rearrange("p (a f) -> p a f", a=2)
            t1v = t1[:, 0:128].rearrange("p (a f) -> p a f", a=2); t2v = t2[:, 0:128].rearrange("p (a f) -> p a f", a=2)
            for h in range(16):
                e1, e2 = ("vector", "gpsimd")
                k.tt(t1v, xv[:, h, :, 0, :], cv, ALU.mult, [xk, csk], [t1k], eng=e1)
                k.tt(t2v, xv[:, h, :, 1, :], sv, ALU.mult, [xk, csk], [t2k], eng=e2)
                k.tt(ov[:, h, :, 0, :], t1v, t2v, ALU.subtract, [t1k, t2k], [ok], eng=e1)
                k.tt(t1v, xv[:, h, :, 0, :], sv, ALU.mult, [xk, csk], [t1k], eng=e1)
                k.tt(t2v, xv[:, h, :, 1, :], cv, ALU.mult, [xk, csk], [t2k], eng=e2)
                k.tt(ov[:, h, :, 1, :], t1v, t2v, ALU.add, [t1k, t2k], [ok], eng=e1)
            k.store(self.QKV, t * 128, t * 128 + 128, 0, 2 * D, o[:, 0:2 * D], ok)
        net.transpose(self.QKV, self.QKT, T, 2 * D)
        cp = k.pool("rtc", [128, 1024], 1); cb, cbk = cp.next()
        lg = cb[:, 0:16]; cdec = cb[:, 16:32]; cols = cb[:, 32:36]; dec = cb[:, 64:128]
        rmask = cb[:, 128:640]; mk = cb[:, 640:896]
        k.dma(lg, DECAY.ap[0:1, 0:16].to_broadcast([128, 16]), DECAY.keys(0, 1, 0, 16), [cbk])
        k.load(cols, cbk, self.RCOLS, 0, 128, 0, 4)
        k.load(rmask, cbk, self.RMASK, 0, 128, 0, 512)
        k.act(lg, lg, AF.Exp, [cbk], [cbk], scale=-1.0)
        k.act(lg, lg, AF.Ln, [cbk], [cbk], bias=1.0)
        k.ts(lg, lg, -1.0, None, ALU.mult, None, [cbk], [cbk])
        k.act(cdec, lg, AF.Exp, [cbk], [cbk], scale=float(CH))
        decv = dec.rearrange("p (c s) -> p c s", c=4)
        for c4 in range(4):
            k.act(decv[:, c4, :], lg, AF.Exp, [cbk], [cbk], scale=cols[:, c4:c4 + 1])
        stp = k.pool("rtst", [128, 2, 512], 2)
        chunks_f = [(LAT + c * CH) for c in range(CTX // CH)] + [c * CH for c in range(LAT // CH)]
        chunks_b = [(LAT + c * CH) for c in reversed(range(CTX // CH))] + [c * CH for c in reversed(range(LAT // CH))]
        for h in range(RT_H):
            qT, qTk = EP.pin(); kT, kTk = EP.pin(); cT, cTk = EP.pin()
            for dc in range(2):
                r = h * RT_DK + dc * 128
                k.load(qT[:, dc * LAT:(dc + 1) * LAT], qTk, self.QKT, r, r + 128, 0, LAT)
                k.load(kT[:, dc * LAT:(dc + 1) * LAT], kTk, self.QKT, D + r, D + r + 128, 0, LAT)
                k.load(cT[:, dc * CTX:(dc + 1) * CTX], cTk, self.QKT, r, r + 128, LAT, T)
                k.load(cT[:, (2 + dc) * CTX:(3 + dc) * CTX], cTk, self.QKT, D + r, D + r + 128, LAT, T)
            def qk_slices(t0):
                if t0 < LAT:
                    return ([qT[:, dc * LAT + t0:dc * LAT + t0 + CH] for dc in range(2)],
                            [kT[:, dc * LAT + t0:dc * LAT + t0 + CH] for dc in range(2)], [qTk, kTk])
                c0 = t0 - LAT
                return ([cT[:, dc * CTX + c0:dc * CTX + c0 + CH] for dc in range(2)],
                        [cT[:, (2 + dc) * CTX + c0:(2 + dc) * CTX + c0 + CH] for dc in range(2)], [cTk])
            for di, chunks in enumerate((chunks_f, chunks_b)):
                col = di * 8 + h
                mt_ = mk[:, di * 128:(di + 1) * 128]
                k.act(mt_, rmask[:, di * 256:di * 256 + 128], AF.Exp, [cbk], [cbk], scale=lg[:, col:col + 1])
                k.tt(mt_, mt_, rmask[:, di * 256 + 128:di * 256 + 256], ALU.mult, [cbk], [cbk])
                st, stk = stp.next()
                k.memset(st[:], 0.0, [stk])
                for t0 in chunks:
                    qs, ks, qkk = qk_slices(t0)
                    kv, kvk = EP.next()
                    k.load(kv[:, 0:256], kvk, self.QKV, t0, t0 + CH, D + h * RT_DK, D + (h + 1) * RT_DK)
                    k.load(kv[:, 256:768], kvk, self.QKV, t0, t0 + CH, 2 * D + h * RT_DV, 2 * D + (h + 1) * RT_DV)
                    ps_s, pssk = PS.next()
                    for dc in range(2):
                        k.mm(ps_s[:, 0:CH], pssk, ks[dc], qs[dc], dc == 0, dc == 1, qkk)
                    sc, sck = EP.next()
                    k.tt(sc[:, 0:CH], ps_s[:, 0:CH], mt_, ALU.mult, [pssk, cbk], [sck])
                    p1, p1k = PS.next(); p2, p2k = PS.next()
                    k.mm(p1[:, 0:RT_DV], p1k, sc[:, 0:CH], kv[:, 256:768], True, True, [sck, kvk])
                    for dc in range(2):
                        k.mm(p2[:, 0:RT_DV], p2k, qs[dc], st[:, dc, :], dc == 0, dc == 1, qkk + [stk])
                    o, ok = EP.next()
                    k.act(o[:, 0:RT_DV], p1[:, 0:RT_DV], AF.Copy, [p1k], [ok])
                    k.stt(o[:, 0:RT_DV], p2[:, 0:RT_DV], decv[:, 2 * di, col:col + 1], o[:, 0:RT_DV], ALU.mult, ALU.add, [p2k, cbk, ok], [ok])
                    if di == 1:
                        pv, pvk = EP.next()
                        k.load(pv[:, 0:RT_DV], pvk, self.OACC, t0, t0 + CH, h * RT_DV, (h + 1) * RT_DV)
                        k.tt(o[:, 0:RT_DV], o[:, 0:RT_DV], pv[:, 0:RT_DV], ALU.add, [ok, pvk], [ok], eng="gpsimd")
                    k.store(self.OACC, t0, t0 + CH, h * RT_DV, (h + 1) * RT_DV, o[:, 0:RT_DV], ok)
                    k.ts(kv[:, 768:1024], kv[:, 0:256], decv[:, 2 * di + 1, col:col + 1], None, ALU.mult, None, [kvk, cbk], [kvk])
                    for dc in range(2):
                        p3, p3k = PS.next()
                        k.mm(p3[:, 0:RT_DV], p3k, kv[:, 768 + dc * 128:768 + (dc + 1) * 128], kv[:, 256:768], True, True, [kvk])
                        k.stt(st[:, dc, :], st[:, dc, :], cdec[:, col:col + 1], p3[:, 0:RT_DV], ALU.mult, ALU.add, [stk, cbk, p3k], [stk])
            EP.unpin(qTk); EP.unpin(kTk); EP.unpin(cTk)
        gg, ggk = EP.pin(); gb, gbk = EP.pin()
        k.dma(gg[:, 0:E], GN_GB.ap[0:1, :].to_broadcast([128, E]), GN_GB.keys(0, 1, 0, E), [ggk])
        k.dma(gb[:, 0:E], GN_GB.ap[1:2, :].to_broadcast([128, E]), GN_GB.keys(1, 2, 0, E), [gbk])
        smp = k.pool("rtsm", [128, 64], 2)
        for t in range(T // 128):
            o, ok = EP.next(); sq, sqk = EP.next(); sm, smk = smp.next()
            k.load(o[:, 0:E], ok, self.OACC, t * 128, t * 128 + 128, 0, E)
            k.act(sq[:, 0:E], o[:, 0:E], AF.Square, [ok], [sqk])
            ov = o[:, 0:E].rearrange("p (h v) -> p h v", h=RT_H); sqv = sq[:, 0:E].rearrange("p (h v) -> p h v", h=RT_H)
            k.R.op("vector", lambda e, sm=sm, ov=ov: e.reduce_sum(out=sm[:, 0:8], in_=ov, axis=AX.X), reads=[ok], writes=[smk])
            k.R.op("vector", lambda e, sm=sm, sqv=sqv: e.reduce_sum(out=sm[:, 8:16], in_=sqv, axis=AX.X), reads=[sqk], writes=[smk])
            k.ts(sm[:, 0:16], sm[:, 0:16], 1.0 / RT_DV, None, ALU.mult, None, [smk], [smk])
            k.tt(sm[:, 16:24], sm[:, 0:8], sm[:, 0:8], ALU.mult, [smk], [smk])
            k.tt(sm[:, 24:32], sm[:, 8:16], sm[:, 16:24], ALU.subtract, [smk], [smk])
            k.ts(sm[:, 24:32], sm[:, 24:32], EPS, None, ALU.add, None, [smk], [smk])
            k.act(sm[:, 24:32], sm[:, 24:32], AF.Sqrt, [smk], [smk])
            k.R.op("vector", lambda e, sm=sm: e.reciprocal(out=sm[:, 32:40], in_=sm[:, 24:32]), reads=[smk], writes=[smk])
            for h in range(RT_H):
                k.ts(ov[:, h, :], ov[:, h, :], sm[:, h:h + 1], sm[:, 32 + h:33 + h], ALU.subtract, ALU.mult, [ok, smk], [ok])
            k.tt(o[:, 0:E], o[:, 0:E], gg[:, 0:E], ALU.mult, [ok, ggk], [ok])
            k.tt(o[:, 0:E], o[:, 0:E], gb[:, 0:E], ALU.add, [ok, gbk], [ok], eng="gpsimd")
            k.store(self.OACC, t * 128, t * 128 + 128, 0, E, o[:, 0:E], ok)
        EP.unpin(ggk); EP.unpin(gbk)
        net.transpose(self.OACC, self.OT, T, E)
        for j in range(NE):
            a, ak = EP.next(); g, gk = EP.next()
            k.load(a[:, 0:T], ak, self.OT, j * 128, j * 128 + 128, 0, T)
            k.load(g[:, 0:T], gk, net.PR, j * 128, j * 128 + 128, 0, T)
            ob, obk = net.EB.next()
            k.tt(ob[:, 0:T], a[:, 0:T], g[:, 0:T], ALU.mult, [ak, gk], [obk])
            k.store(net.S, j * 128, j * 128 + 128, 0, T, ob[:, 0:T], obk)
        net.gemm(net.S, net.WOB, None, E, T, D, epi=net.residual_epi(XS))


DEPTH = 4; NCORES = 8


def build_program():
    nc = bass.Bass("TRN2", target_bir_lowering=False)
    st = contextlib.ExitStack()
    k = K(nc, st); net = Net(k)
    ext = lambda n, shp: DT(nc, n, shp, kind="ExternalInput")
    XIN = ext("xin", [T, D]); CC = ext("cc", [2, D]); CONSTS = ext("consts", [128, 128])
    ADA_W = ext("ada_w", [4 * D, 3 * D]); ADA_B = ext("ada_b", [4, 3 * D]); NORM_G = ext("norm_g", [4, D])
    FNG = ext("final_norm_g", [1, D])
    FLAT = ext("flat", [LAT, 2 * LAT]); FCTX = ext("fctx", [CTX, 2 * CTX]); NEGT = ext("negt", [128, 18])
    DELTA = ext("delta", [1, E]); FEATS = ext("feats", [33, T])
    HY = []
    for j in range(2):
        HY.append(dict(W_IN=ext(f"hy_w_in{j}", [D, 4 * E]), CONV_WB=ext(f"hy_conv_wb{j}", [4, 3 * E]), HYF=ext(f"hy_f{j}", [4, 64]),
                       F_W1=ext(f"hy_f_w1{j}", [33, 64]), F_W2=ext(f"hy_f_w2{j}", [64, 64]), F_W3=ext(f"hy_f_w3{j}", [64, 4 * E]),
                       SKIP=ext(f"hy_skip{j}", [2, E]), W_OUT=ext(f"hy_w_out{j}", [E, D])))
    CF = dict(W_IN=ext("cf_w_in", [D, 3 * E]), DW_W=ext("cf_dw_w", [31, E]), DW_B=ext("cf_dw_b", [1, E]),
              LN_G=ext("cf_ln_g", [1, E]), LN_B=ext("cf_ln_b", [1, E]), W_OUT=ext("cf_w_out", [E, D]))
    RCOS = ext("rcos", [LAT, 128]); RSIN = ext("rsin", [LAT, 128]); RMASK = ext("rmask", [128, 512]); RCOLS = ext("rcols", [128, 4])
    RT = dict(W_IN=ext("rt_w_in", [D, 2 * D + 2 * E]), DECAY=ext("rt_decay", [1, 16]), GN_GB=ext("rt_gn_gb", [2, E]),
              W_OUT=ext("rt_w_out", [E, D]))
    XS = DT(nc, "XS", [T, D]); OUT = DT(nc, "out", [LAT, D], kind="ExternalOutput")
    hy = Hyena(net, FLAT, FCTX, NEGT, DELTA, FEATS); rt = Retention(net, RCOS, RSIN, RMASK, RCOLS)
    net.init_consts(CONSTS)
    net.copy_dram(XIN, XS, T, D)
    net.build_mrep(CC)
    for i in range(DEPTH):
        kind, j = i % 3, i // 3
        last = i == DEPTH - 1
        need_ctx = (not last) or kind == 2
        net.adaln(ADA_W, ADA_B, i)
        net.prenorm(XS, NORM_G, i, (T if need_ctx else LAT) // 128)
        if kind == 0:
            hy.layer(XS, need_ctx=need_ctx, **HY[j])
        elif kind == 1:
            net.conformer(XS, CF["W_IN"], CF["DW_W"], CF["DW_B"], CF["LN_G"], CF["LN_B"], CF["W_OUT"], need_ctx=need_ctx)
        else:
            rt.layer(XS, RT["W_IN"], RT["DECAY"], RT["GN_GB"], RT["W_OUT"])
    net.final_norm(XS, FNG, OUT)
    k.R.final_wait("sync", [k.R.last_w[key] for key in OUT.keys(0, LAT, 0, D)])
    k.R.emit(nc, None)
    st.close()
    return nc


def kernel(x, c, ctx, c_ctx, ada_w, ada_b, norm_g, final_norm_g,
           hy_w_in, hy_conv_w, hy_conv_b, hy_f_w1, hy_f_b1, hy_f_fr1, hy_f_w2, hy_f_b2,
           hy_f_fr2, hy_f_w3, hy_skip, hy_w_out,
           cf_w_in, cf_dw_w, cf_dw_b, cf_ln_g, cf_ln_b, cf_w_out,
           rt_w_in, rt_decay_logit, rt_gn_g, rt_gn_b, rt_w_out):
    f = lambda a: np.ascontiguousarray(np.asarray(a), dtype=np.float32)
    x, c, ctx, c_ctx = f(x), f(c), f(ctx), f(c_ctx)
    shared = {"consts": np.eye(128, dtype=np.float32), "ada_w": f(ada_w).reshape(4 * D, 3 * D), "ada_b": f(ada_b),
              "norm_g": f(norm_g), "final_norm_g": f(final_norm_g).reshape(1, D),
              "cf_w_in": f(cf_w_in)[0], "cf_dw_w": f(cf_dw_w)[0], "cf_dw_b": f(cf_dw_b).reshape(1, E),
              "cf_ln_g": f(cf_ln_g).reshape(1, E), "cf_ln_b": f(cf_ln_b).reshape(1, E), "cf_w_out": f(cf_w_out)[0],
              "rt_w_in": f(rt_w_in)[0], "rt_decay": f(rt_decay_logit)[0].reshape(1, 16),
              "rt_gn_gb": np.stack([f(rt_gn_g)[0], f(rt_gn_b)[0]]), "rt_w_out": f(rt_w_out)[0]}
    for j in range(2):
        shared.update({f"hy_w_in{j}": f(hy_w_in)[j], f"hy_conv_wb{j}": np.concatenate([f(hy_conv_w)[j], f(hy_conv_b)[j][None]], 0),
                       f"hy_f{j}": np.stack([f(hy_f_b1)[j], f(hy_f_fr1)[j], f(hy_f_b2)[j], f(hy_f_fr2)[j]]),
                       f"hy_f_w1{j}": f(hy_f_w1)[j], f"hy_f_w2{j}": f(hy_f_w2)[j], f"hy_f_w3{j}": f(hy_f_w3)[j],
                       f"hy_skip{j}": f(hy_skip)[j], f"hy_w_out{j}": f(hy_w_out)[j]})
    shared.update(hyena_consts()); shared.update(ret_consts())
    shared = {k_: np.ascontiguousarray(v, dtype=np.float32) for k_, v in shared.items()}
    in_maps = []
    for b in range(NCORES):
        m = dict(shared)
        m["xin"] = np.ascontiguousarray(np.concatenate([x[b], ctx[b]], 0))
        m["cc"] = np.ascontiguousarray(np.stack([c[b], c_ctx]))
        in_maps.append(m)
    nc = build_program()
    res = run_bass_kernel_spmd(nc, in_maps, core_ids=list(range(NCORES)))
    return np.stack([np.asarray(res.results[b]["out"], dtype=np.float32) for b in range(NCORES)], 0)
```

```python
import contextlib, math
import numpy as np
import concourse.bass as bass
import concourse.mybir as mybir
from concourse.bass_utils import run_bass_kernel_spmd

COMPUTE = ("tensor", "vector", "scalar", "gpsimd")
DMAQ = ("sync", "gpsimd")
NDS = 6


class Op:
    __slots__ = ("eng", "fn", "waits", "idx", "is_dma", "tick", "dsem", "dval", "name")


class Rec:
    def __init__(self):
        self.ops = {e: [] for e in ("tensor", "vector", "scalar", "gpsimd", "sync")}
        self.last_w = {}
        self.readers = {}
        self.ndma = {q: 0 for q in DMAQ}
        self.dma_tok = {q: [] for q in DMAQ}

    def op(self, eng, fn, reads=(), writes=(), dma=False, pe_accum=False, name=None):
        o = Op(); o.eng = eng; o.fn = fn; o.waits = []; o.is_dma = dma; o.tick = False
        o.name = name; o.dsem = None; o.dval = None
        deps = []
        for r in reads:
            t = self.last_w.get(r)
            if t is not None: deps.append(t)
        for w in writes:
            t = self.last_w.get(w)
            if t is not None and not (pe_accum and t[0] == "c" and t[1] == "tensor"):
                deps.append(t)
            deps.extend(self.readers.get(w, ()))
        if dma:
            q = eng; i = self.ndma[q]; self.ndma[q] += 1
            if i >= NDS: deps.append(self.dma_tok[q][i - NDS])
            tok = ("d", q, i % NDS, 16 * (i // NDS + 1))
            self.dma_tok[q].append(tok); o.dsem = (q, i % NDS)
        else:
            tok = ("c", eng, o)
        seen = set()
        for d in deps:
            if d is tok or id(d) in seen: continue
            seen.add(id(d))
            if d[0] == "c": d[2].tick = True
            o.waits.append(d)
        self.ops[eng].append(o)
        for w in writes:
            self.last_w[w] = tok; self.readers[w] = []
        for r in reads:
            if r not in writes:
                lst = self.readers.setdefault(r, [])
                if tok[0] == "c":
                    lst[:] = [t for t in lst if not (t[0] == "c" and t[1] == eng)]
                lst.append(tok)
        return tok

    def final_wait(self, eng, toks):
        o = Op(); o.eng = eng; o.fn = None; o.waits = list(toks); o.is_dma = False
        o.tick = False; o.name = "final"; o.dsem = None; o.dval = None
        for d in toks:
            if d[0] == "c": d[2].tick = True
        self.ops[eng].append(o)

    def emit(self, nc, block_engines):
        import contextlib
        for e in COMPUTE:
            n = 0
            for o in self.ops[e]:
                if o.tick and not o.is_dma:
                    n += 1; o.idx = n
        with contextlib.ExitStack() as st:
            csem = {e: st.enter_context(nc.semaphore("c_" + e)) for e in COMPUTE}
            dsem = {(q, k): st.enter_context(nc.semaphore(f"d_{q}{k}")) for q in DMAQ for k in range(NDS)}
            blk = st.enter_context(nc.Block())
            for e, lst in self.ops.items():
                if not lst: continue
                def body(eng, lst=lst, e=e):
                    known = {}
                    for o in lst:
                        need = {}
                        for d in o.waits:
                            if d[0] == "c": s, v = csem[d[1]], d[2].idx
                            else: s, v = dsem[(d[1], d[2])], d[3]
                            k = id(s)
                            if known.get(k, 0) >= v: continue
                            if k not in need or need[k][1] < v: need[k] = (s, v)
                        for k, (s, v) in need.items():
                            eng.wait_ge(s, v); known[k] = v
                        if o.fn is None: continue
                        ins = o.fn(eng)
                        if o.is_dma: ins.then_inc(dsem[o.dsem], 16)
                        elif o.tick: ins.then_inc(csem[e], 1)
                getattr(blk, e)(body)


F32 = mybir.dt.float32
BF16 = mybir.dt.bfloat16
AF = mybir.ActivationFunctionType
ALU = mybir.AluOpType
AX = mybir.AxisListType


class DT:
    def __init__(self, nc, name, shape, kind="Internal", rb=128, cb=512, dtype=F32):
        self.name = name; self.shape = tuple(shape); self.dtype = dtype
        self.ap = nc.dram_tensor(name, list(shape), dtype, kind=kind).ap()
        self.rb = rb; self.cb = cb

    def keys(self, r0, r1, c0, c1):
        return [(self.name, i, j) for i in range(r0 // self.rb, (r1 - 1) // self.rb + 1)
                for j in range(c0 // self.cb, (c1 - 1) // self.cb + 1)]


class Pool:
    def __init__(self, nc, st, name, shape, n, psum=False, dtype=F32):
        mk = nc.psum_tensor if psum else nc.sbuf_tensor
        self.bufs = [st.enter_context(mk(f"{name}{i}", list(shape), dtype)) for i in range(n)]
        self.keys = [(name, i) for i in range(n)]
        self.i = 0; self.pinned = set()

    def next(self):
        while True:
            k = self.i % len(self.bufs); self.i += 1
            if k not in self.pinned:
                return self.bufs[k], self.keys[k]

    def pin(self):
        b, key = self.next()
        self.pinned.add(self.keys.index(key))
        return b, key

    def unpin(self, key):
        self.pinned.discard(self.keys.index(key))


class K:
    def __init__(self, nc, st):
        self.nc = nc; self.st = st; self.R = Rec(); self.pools = {}
        self.qi = 0

    def pool(self, name, shape, n, psum=False, dtype=F32):
        if name not in self.pools:
            self.pools[name] = Pool(self.nc, self.st, name, shape, n, psum, dtype)
        return self.pools[name]

    def dma(self, out, in_, reads, writes, q=None):
        if q is None:
            q = "sync"
        self.R.op(q, lambda e: e.dma_start(out=out, in_=in_), reads=reads, writes=writes, dma=True)

    def load(self, sb, sbkey, dt, r0, r1, c0, c1, pat=None, q="sync", **kw):
        src = dt.ap[r0:r1, c0:c1]
        if pat: src = src.rearrange(pat, **kw)
        self.dma(sb, src, dt.keys(r0, r1, c0, c1), [sbkey], q=q)

    def store(self, dt, r0, r1, c0, c1, sb, sbkey, pat=None, q="gpsimd", **kw):
        dst = dt.ap[r0:r1, c0:c1]
        if pat: dst = dst.rearrange(pat, **kw)
        self.dma(dst, sb, [sbkey], dt.keys(r0, r1, c0, c1), q=q)

    def mm(self, ps, pskey, lhsT, rhs, start, stop, reads):
        self.R.op("tensor", lambda e: e.matmul(ps, lhsT=lhsT, rhs=rhs, start=start, stop=stop),
                  reads=reads, writes=[pskey], pe_accum=not start)

    def act(self, out, in_, func, reads, writes, eng="scalar", **kw):
        self.R.op(eng, lambda e: e.activation(out=out, in_=in_, func=func, **kw), reads=reads, writes=writes)

    def tt(self, out, a, b, op, reads, writes, eng="vector"):
        self.R.op(eng, lambda e: e.tensor_tensor(out=out, in0=a, in1=b, op=op), reads=reads, writes=writes)

    def ts(self, out, a, s1, s2, op0, op1, reads, writes, eng="vector", **kw):
        if s2 is None:
            self.R.op(eng, lambda e: e.tensor_scalar(out=out, in0=a, scalar1=s1, scalar2=None, op0=op0, **kw), reads=reads, writes=writes)
        else:
            self.R.op(eng, lambda e: e.tensor_scalar(out=out, in0=a, scalar1=s1, scalar2=s2, op0=op0, op1=op1, **kw), reads=reads, writes=writes)

    def stt(self, out, a, s, b, op0, op1, reads, writes, eng="vector"):
        self.R.op(eng, lambda e: e.scalar_tensor_tensor(out=out, in0=a, scalar=s, in1=b, op0=op0, op1=op1), reads=reads, writes=writes)

    def copy(self, out, in_, reads, writes, eng="vector"):
        self.R.op(eng, lambda e: e.tensor_copy(out=out, in_=in_), reads=reads, writes=writes)

    def memset(self, ap, val, writes, eng="vector"):
        self.R.op(eng, lambda e: e.memset(ap, val), reads=[], writes=writes)


D = 2048; E = 4096; LAT = 2048; CTX = 256; T = LAT + CTX; EPS = 1e-6
NE = E // 128


class Net:
    def __init__(self, k):
        self.k = k; nc = k.nc
        self.EP = k.pool("E", [128, 4096], 9)
        self.EB = k.pool("EB", [128, 4096], 5, dtype=BF16)
        self.PS = k.pool("ps", [128, 512], 8, psum=True)
        cp = k.pool("const", [128, 256], 1); cb, ck = cp.next()
        self.cb = cb; self.ck = ck
        self.ident = cb[:, 0:128]; self.ones = cb[:, 128:256]
        self.MREP = DT(nc, "MREP", [D, 256], dtype=BF16); self.ADAB = DT(nc, "ADAB", [256, 3 * D])
        self.H = DT(nc, "H", [T, D]); self.HT = DT(nc, "HT", [D, T], dtype=BF16)
        self.PR = DT(nc, "PR", [4 * E, T]); self.U = DT(nc, "U", [E, T]); self.S = DT(nc, "S", [E, T], dtype=BF16)
        self.WB = DT(nc, "WB", [D, 4 * E], dtype=BF16); self.WOB = DT(nc, "WOB", [E, D], dtype=BF16); self.ADAWB = DT(nc, "ADAWB", [D, 3 * D], dtype=BF16)

    def init_consts(self, CONSTS):
        k = self.k
        k.load(self.cb[:, 0:128], self.ck, CONSTS, 0, 128, 0, 128)
        k.memset(self.cb[:, 128:256], 1.0, [self.ck])

    def rows_to_pp(self, SRC, r0, n, c0, C, dst, dkey):
        k = self.k
        for cb in range(0, C, 2048):
            cw = min(2048, C - cb)
            sb, sk = self.EP.next()
            k.load(sb[0:n, 0:cw], sk, SRC, r0, r0 + n, c0 + cb, c0 + cb + cw)
            per = 512 // n
            for j0 in range(0, cw // 128, per):
                jn = min(per, cw // 128 - j0)
                ps, pk = self.PS.next()
                for j in range(jn):
                    k.mm(ps[:, j * n:(j + 1) * n], pk, sb[0:n, (j0 + j) * 128:(j0 + j + 1) * 128], self.ident[0:n, 0:n],
                         True, True, [sk, self.ck])
                t0 = cb // 128 + j0
                k.copy(dst[:, t0:t0 + jn, :], ps[:, 0:jn * n].rearrange("p (j n) -> p j n", n=n), [pk], [dkey])

    def build_mrep(self, CC):
        k = self.k
        tp = k.pool("tmpv", [128, 16, 2], 1); tb, tk = tp.next()
        self.rows_to_pp(CC, 0, 2, 0, D, tb, tk)
        sp = k.pool("tmps", [128, 16, 2], 1); sb, sk = sp.next()
        k.act(sb[:], tb[:], AF.Silu, [tk], [sk])
        for kc in range(16):
            rb, rk = self.EB.next()
            for s in range(2):
                k.ts(rb[:, s * 128:(s + 1) * 128], self.ones, sb[:, kc, s:s + 1], None, ALU.mult, None, [sk, self.ck], [rk])
            k.store(self.MREP, kc * 128, kc * 128 + 128, 0, 256, rb[:, 0:256], rk)

    def adaln(self, ADA_W, ADA_B, i):
        k = self.k
        def epi(ps, pk, m0, mt, n0, nw):
            bb, bk = self.EP.next()
            k.dma(bb[:, 0:nw], ADA_B.ap[i:i + 1, n0:n0 + nw].to_broadcast([128, nw]), ADA_B.keys(i, i + 1, n0, n0 + nw), [bk])
            ob, ok = self.EP.next()
            k.tt(ob[0:mt, 0:nw], ps, bb[0:mt, 0:nw], ALU.add, [pk, bk], [ok])
            k.store(self.ADAB, m0, m0 + mt, n0, n0 + nw, ob[0:mt, 0:nw], ok)
        self.cast_dram(ADA_W, self.ADAWB, D, 3 * D, s0=(i * D, 0))
        self.gemm(self.MREP, self.ADAWB, None, D, 256, 3 * D, epi=epi)

    def prenorm(self, XS, NORM_G, i, ntile):
        k = self.k
        gs, gsk = self.EP.pin(); sh, shk = self.EP.pin()
        gs = gs[:, :].rearrange("p (s d) -> p s d", s=2); sh = sh[:, :].rearrange("p (s d) -> p s d", s=2)
        gb, gk = self.EP.next()
        k.dma(gb[:, 0:D], NORM_G.ap[i:i + 1, :].to_broadcast([128, D]), NORM_G.keys(i, i + 1, 0, D), [gk])
        for s in range(2):
            tb, tk = self.EP.next()
            k.load(tb[:, 0:D], tk, self.ADAB, s * 128, s * 128 + 128, D, 2 * D)
            k.stt(gs[:, s, :], tb[:, 0:D], 1.0, gb[:, 0:D], ALU.add, ALU.mult, [tk, gk], [gsk])
            k.load(sh[:, s, :], shk, self.ADAB, s * 128, s * 128 + 128, 0, D)
        stp = k.pool("st", [128, 8], 2)
        for t in range(ntile):
            s = 0 if t < LAT // 128 else 1
            xb, xk = self.EP.next(); sq, sqk = self.EP.next(); st, stk = stp.next()
            k.load(xb[:, 0:D], xk, XS, t * 128, t * 128 + 128, 0, D)
            k.act(sq[:, 0:D], xb[:, 0:D], AF.Square, [xk], [sqk])
            k.R.op("vector", lambda e, st=st, sq=sq: e.reduce_sum(out=st[:, 0:1], in_=sq[:, 0:D], axis=AX.X), reads=[sqk], writes=[stk])
            k.ts(st[:, 1:2], st[:, 0:1], 1.0 / D, EPS, ALU.mult, ALU.add, [stk], [stk])
            k.act(st[:, 2:3], st[:, 1:2], AF.Sqrt, [stk], [stk])
            k.R.op("vector", lambda e, st=st: e.reciprocal(out=st[:, 3:4], in_=st[:, 2:3]), reads=[stk], writes=[stk])
            k.stt(sq[:, 0:D], xb[:, 0:D], st[:, 3:4], gs[:, s, :], ALU.mult, ALU.mult, [xk, stk, gsk], [sqk])
            k.tt(sq[:, 0:D], sq[:, 0:D], sh[:, s, :], ALU.add, [sqk, shk], [sqk])
            k.store(self.H, t * 128, t * 128 + 128, 0, D, sq[:, 0:D], sqk)
        self.EP.unpin(gsk); self.EP.unpin(shk)
        self.transpose(self.H, self.HT, ntile * 128, D)

    def residual_epi(self, XS):
        k = self.k
        def epi(ps, pk, m0, mt, n0, nw):
            s = 0 if m0 < LAT else 1
            xb, xk = self.EP.next(); gb, gk = self.EP.next()
            k.load(xb[0:mt, 0:nw], xk, XS, m0, m0 + mt, n0, n0 + nw)
            k.load(gb[0:mt, 0:nw], gk, self.ADAB, s * 128, s * 128 + mt, 2 * D + n0, 2 * D + n0 + nw)
            k.tt(gb[0:mt, 0:nw], ps, gb[0:mt, 0:nw], ALU.mult, [pk, gk], [gk])
            k.tt(xb[0:mt, 0:nw], xb[0:mt, 0:nw], gb[0:mt, 0:nw], ALU.add, [xk, gk], [xk])
            k.store(XS, m0, m0 + mt, n0, n0 + nw, xb[0:mt, 0:nw], xk)
        return epi

    def copy_dram(self, SRC, DST, rows, cols, s0=(0, 0), d0=(0, 0)):
        k = self.k
        for r in range(0, rows, 128):
            rw = min(128, rows - r)
            k.dma(DST.ap[d0[0] + r:d0[0] + r + rw, d0[1]:d0[1] + cols], SRC.ap[s0[0] + r:s0[0] + r + rw, s0[1]:s0[1] + cols],
                  SRC.keys(s0[0] + r, s0[0] + r + rw, s0[1], s0[1] + cols), DST.keys(d0[0] + r, d0[0] + r + rw, d0[1], d0[1] + cols))

    def cast_dram(self, SRC, DST, rows, cols, s0=(0, 0), d0=(0, 0)):
        k = self.k; engs = ("vector", "gpsimd", "scalar"); n = 0
        for r in range(0, rows, 128):
            rw = min(128, rows - r)
            for c in range(0, cols, 4096):
                cw = min(4096, cols - c)
                a, ak = self.EP.next(); b, bk = self.EB.next()
                k.load(a[0:rw, 0:cw], ak, SRC, s0[0] + r, s0[0] + r + rw, s0[1] + c, s0[1] + c + cw)
                e = engs[n % 3]; n += 1
                if e == "scalar": k.act(b[0:rw, 0:cw], a[0:rw, 0:cw], AF.Copy, [ak], [bk])
                else: k.copy(b[0:rw, 0:cw], a[0:rw, 0:cw], [ak], [bk], eng=e)
                k.store(DST, d0[0] + r, d0[0] + r + rw, d0[1] + c, d0[1] + c + cw, b[0:rw, 0:cw], bk)

    def final_norm(self, XS, FG, OUT):
        k = self.k
        gb, gk = self.EP.pin()
        k.dma(gb[:, 0:D], FG.ap[0:1, :].to_broadcast([128, D]), FG.keys(0, 1, 0, D), [gk])
        stp = k.pool("st", [128, 8], 2)
        for t in range(LAT // 128):
            xb, xk = self.EP.next(); sq, sqk = self.EP.next(); st, stk = stp.next()
            k.load(xb[:, 0:D], xk, XS, t * 128, t * 128 + 128, 0, D)
            k.act(sq[:, 0:D], xb[:, 0:D], AF.Square, [xk], [sqk])
            k.R.op("vector", lambda e, st=st, sq=sq: e.reduce_sum(out=st[:, 0:1], in_=sq[:, 0:D], axis=AX.X), reads=[sqk], writes=[stk])
            k.ts(st[:, 1:2], st[:, 0:1], 1.0 / D, EPS, ALU.mult, ALU.add, [stk], [stk])
            k.act(st[:, 2:3], st[:, 1:2], AF.Sqrt, [stk], [stk])
            k.R.op("vector", lambda e, st=st: e.reciprocal(out=st[:, 3:4], in_=st[:, 2:3]), reads=[stk], writes=[stk])
            k.stt(sq[:, 0:D], xb[:, 0:D], st[:, 3:4], gb[:, 0:D], ALU.mult, ALU.mult, [xk, stk, gk], [sqk])
            k.store(OUT, t * 128, t * 128 + 128, 0, D, sq[:, 0:D], sqk)
        self.EP.unpin(gk)

    def gemm(self, L, R, OUT, K, M, N, l0=(0, 0), r0=(0, 0), o0=(0, 0), epi=None, act=None):
        k = self.k
        assert L.dtype == R.dtype, (L.name, R.name)
        PP = self.EB if L.dtype == BF16 else self.EP
        MG, NB, KP = 512, 512, 8
        assert KP * max(MG, NB) <= 4096
        kch = (K + 127) // 128; kparts = min(K, 128); npan = (kch + KP - 1) // KP
        reuse_l = (npan <= 2) and (N > NB)
        for mg in range(0, M, MG):
            mw = min(MG, M - mg)
            held = {}
            for nb in range(0, N, NB):
                nw = min(NB, N - nb)
                pss = [self.PS.next() for _ in range((mw + 127) // 128)]
                for kp in range(npan):
                    kc0 = kp * KP; kcn = min(KP, kch - kc0)
                    ka = kc0 * 128; kb = min(K, (kc0 + kcn) * 128)
                    if kp in held:
                        lb, lk = held[kp]
                    else:
                        lb, lk = PP.pin() if reuse_l else PP.next()
                        k.load(lb[0:kparts, 0:kcn * mw].rearrange("p (c m) -> p c m", m=mw), lk, L, l0[0] + ka, l0[0] + kb,
                               l0[1] + mg, l0[1] + mg + mw, "(c p) m -> p c m", p=kparts)
                        if reuse_l: held[kp] = (lb, lk)
                    rb, rk = PP.next()
                    k.load(rb[0:kparts, 0:kcn * nw].rearrange("p (c m) -> p c m", m=nw), rk, R, r0[0] + ka, r0[0] + kb,
                           r0[1] + nb, r0[1] + nb + nw, "(c p) m -> p c m", p=kparts)
                    for g, (ps, pk) in enumerate(pss):
                        mt = min(128, mw - g * 128)
                        for kc in range(kcn):
                            k.mm(ps[0:mt, 0:nw], pk, lb[0:kparts, kc * mw + g * 128:kc * mw + g * 128 + mt],
                                 rb[0:kparts, kc * nw:(kc + 1) * nw],
                                 start=(kp == 0 and kc == 0), stop=(kp == npan - 1 and kc == kcn - 1), reads=[lk, rk])
                for g, (ps, pk) in enumerate(pss):
                    mt = min(128, mw - g * 128); m0 = mg + g * 128
                    if epi is not None:
                        epi(ps[0:mt, 0:nw], pk, m0, mt, nb, nw)
                    else:
                        ob, ok = self.EP.next()
                        k.act(ob[0:mt, 0:nw], ps[0:mt, 0:nw], act(m0) if callable(act) else (act or AF.Copy), [pk], [ok])
                        k.store(OUT, o0[0] + m0, o0[0] + m0 + mt, o0[1] + nb, o0[1] + nb + nw, ob[0:mt, 0:nw], ok)
            for (_, lk_) in held.values():
                PP.unpin(lk_)

    def transpose(self, SRC, DST, Rn, Cn, s0=(0, 0), d0=(0, 0)):
        k = self.k
        for rb in range(0, Rn, 512):
            rw = min(512, Rn - rb); nrt = rw // 128
            for cb in range(0, Cn, 512):
                cw = min(512, Cn - cb)
                ib, ik = self.EP.next()
                k.load(ib[:, 0:nrt * cw].rearrange("p (t c) -> p t c", c=cw), ik, SRC, s0[0] + rb, s0[0] + rb + rw,
                       s0[1] + cb, s0[1] + cb + cw, "(t p) c -> p t c", p=128)
                for ct in range(cw // 128):
                    ps, pk = self.PS.next()
                    for t in range(nrt):
                        o = ps[:, t * 128:(t + 1) * 128]; i = ib[:, t * cw + ct * 128:t * cw + ct * 128 + 128]
                        k.R.op("tensor", lambda e, o=o, i=i: e.transpose(o, i, self.ident), reads=[ik, self.ck], writes=[pk], pe_accum=(t > 0))
                    ob, ok = (self.EB if DST.dtype == BF16 else self.EP).next()
                    k.copy(ob[:, 0:rw], ps[:, 0:rw], [pk], [ok])
                    k.store(DST, d0[0] + cb + ct * 128, d0[0] + cb + ct * 128 + 128, d0[1] + rb, d0[1] + rb + rw, ob[:, 0:rw], ok)

    def conformer(self, XS, W_IN, DW_W, DW_B, LN_G, LN_B, W_OUT, need_ctx=True):
        k = self.k
        Tn = T if need_ctx else LAT
        segs = [(0, LAT)] + ([(LAT, CTX)] if need_ctx else [])
        cwp = k.pool("cfw", [128, NE, 34], 1); cw, cwk = cwp.next()
        self.rows_to_pp(DW_W, 0, 31, 0, E, cw[:, :, 0:31], cwk)
        for n, V in enumerate((DW_B, LN_G, LN_B)):
            self.rows_to_pp(V, 0, 1, 0, E, cw[:, :, 31 + n:32 + n], cwk)
        fn = lambda m0: AF.Copy if m0 < E else (AF.Sigmoid if m0 < 2 * E else AF.Silu)
        self.cast_dram(W_IN, self.WB, D, 3 * E); self.cast_dram(W_OUT, self.WOB, E, D)
        self.gemm(self.WB, self.HT, self.PR, D, 3 * E, Tn, act=fn)
        st0, st0k = self.EP.pin(); st1, st1k = self.EP.pin()
        k.memset(st0[:, 0:T], 0.0, [st0k]); k.memset(st1[:, 0:T], 0.0, [st1k])
        PAD = 15
        for j in range(NE):
            a, ak = self.EP.next(); sb, sk = self.EP.next(); u0, uk = self.EP.next()
            acc, acck = self.EP.next(); acc2, acc2k = self.EP.next()
            k.load(a[:, 0:Tn], ak, self.PR, j * 128, j * 128 + 128, 0, Tn)
            k.load(sb[:, 0:Tn], sk, self.PR, E + j * 128, E + j * 128 + 128, 0, Tn)
            k.memset(u0[:, 0:Tn + 4 * PAD], 0.0, [uk], eng="gpsimd")
            for si, (s0, sl) in enumerate(segs):
                o = s0 + (2 * si + 1) * PAD
                k.tt(u0[:, o:o + sl], a[:, s0:s0 + sl], sb[:, s0:s0 + sl], ALU.mult, [ak, sk], [uk])
            NV = 31
            for si, (s0, sl) in enumerate(segs):
                o = s0 + 2 * si * PAD
                for tap in range(31):
                    eng, ac, ack = ("vector", acc, acck) if tap < NV else ("gpsimd", acc2, acc2k)
                    src = u0[:, o + tap:o + tap + sl]
                    if tap == 0:
                        k.ts(ac[:, s0:s0 + sl], src, cw[:, j, 0:1], cw[:, j, 31:32], ALU.mult, ALU.add, [uk, cwk], [ack], eng=eng)
                    elif tap == NV:
                        k.ts(ac[:, s0:s0 + sl], src, cw[:, j, tap:tap + 1], None, ALU.mult, None, [uk, cwk], [ack], eng=eng)
                    else:
                        k.stt(ac[:, s0:s0 + sl], src, cw[:, j, tap:tap + 1], ac[:, s0:s0 + sl], ALU.mult, ALU.add, [uk, cwk, ack], [ack], eng=eng)
            k.act(acc2[:, 0:Tn], acc[:, 0:Tn], AF.Square, [acck], [acc2k])
            k.tt(st0[:, 0:Tn], st0[:, 0:Tn], acc[:, 0:Tn], ALU.add, [st0k, acck], [st0k], eng="gpsimd")
            k.tt(st1[:, 0:Tn], st1[:, 0:Tn], acc2[:, 0:Tn], ALU.add, [st1k, acc2k], [st1k], eng="gpsimd")
            k.store(self.U, j * 128, j * 128 + 128, 0, Tn, acc[:, 0:Tn], acck)
        for nb in range(0, Tn, 512):
            nw = min(512, Tn - nb)
            p1, p1k = self.PS.next(); p2, p2k = self.PS.next()
            k.mm(p1[:, 0:nw], p1k, self.ones, st0[:, nb:nb + nw], True, True, [self.ck, st0k])
            k.mm(p2[:, 0:nw], p2k, self.ones, st1[:, nb:nb + nw], True, True, [self.ck, st1k])
            m, mk = self.EP.next()
            k.ts(st0[:, nb:nb + nw], p1[:, 0:nw], 1.0 / E, None, ALU.mult, None, [p1k], [st0k])
            k.tt(m[:, 0:nw], st0[:, nb:nb + nw], st0[:, nb:nb + nw], ALU.mult, [st0k], [mk])
            k.stt(m[:, 0:nw], p2[:, 0:nw], 1.0 / E, m[:, 0:nw], ALU.mult, ALU.subtract, [p2k, mk], [mk])
            k.ts(m[:, 0:nw], m[:, 0:nw], EPS, None, ALU.add, None, [mk], [mk])
            k.act(m[:, 0:nw], m[:, 0:nw], AF.Sqrt, [mk], [mk])
            k.R.op("vector", lambda e, o=st1[:, nb:nb + nw], i=m[:, 0:nw]: e.reciprocal(out=o, in_=i), reads=[mk], writes=[st1k])
        for j in range(NE):
            u, uk = self.EP.next(); g, gk = self.EP.next()
            k.load(u[:, 0:Tn], uk, self.U, j * 128, j * 128 + 128, 0, Tn)
            k.load(g[:, 0:Tn], gk, self.PR, 2 * E + j * 128, 2 * E + j * 128 + 128, 0, Tn)
            k.tt(u[:, 0:Tn], u[:, 0:Tn], st0[:, 0:Tn], ALU.subtract, [uk, st0k], [uk])
            k.tt(u[:, 0:Tn], u[:, 0:Tn], st1[:, 0:Tn], ALU.mult, [uk, st1k], [uk])
            k.act(u[:, 0:Tn], u[:, 0:Tn], AF.Silu, [uk, cwk], [uk], bias=cw[:, j, 33:34], scale=cw[:, j, 32:33])
            sb_, sbk_ = self.EB.next()
            k.tt(sb_[:, 0:Tn], u[:, 0:Tn], g[:, 0:Tn], ALU.mult, [uk, gk], [sbk_], eng="gpsimd")
            k.store(self.S, j * 128, j * 128 + 128, 0, Tn, sb_[:, 0:Tn], sbk_)
        self.EP.unpin(st0k); self.EP.unpin(st1k)
        self.gemm(self.S, self.WOB, None, E, Tn, D, epi=self.residual_epi(XS))


TWO_PI = 2.0 * math.pi


def hyena_consts():
    c = {}
    def fmat(L):
        N = 2 * L
        t = np.arange(L, dtype=np.float64)[:, None]; kk = np.arange(L, dtype=np.float64)[None, :]
        ang = 2.0 * np.pi * ((t * kk) % N) / N
        A = np.cos(ang); B = -np.sin(ang); B[:, 0] = np.cos(np.pi * t[:, 0])
        return np.concatenate([A, B], 1).astype(np.float32)
    c["flat"] = fmat(LAT); c["fctx"] = fmat(CTX)
    negt = np.zeros((128, 18), np.float32)
    negt[:, 0:16] = -(np.linspace(0.0, 1.0, LAT, dtype=np.float32).reshape(16, 128).T)
    negt[:, 16:18] = -(np.linspace(0.0, 1.0, CTX, dtype=np.float32).reshape(2, 128).T)
    c["negt"] = negt
    c["delta"] = np.abs(np.linspace(math.log(1e-2) / 1.5, math.log(1e-2) / 0.3, E, dtype=np.float32))[None, :]
    def feats(L):
        t = np.linspace(0.0, 1.0, L, dtype=np.float32)[:, None]
        w = (2.0 * math.pi / L) * np.arange(L, dtype=np.float32)[:, None]
        f = np.linspace(1e-4, 15, 16, dtype=np.float32)[None, :]
        return np.concatenate([t, np.cos(f * w), -np.sin(f * w)], -1).astype(np.float32).T
    c["feats"] = np.concatenate([feats(LAT), feats(CTX)], 1)
    return {k_: np.ascontiguousarray(v, dtype=np.float32) for k_, v in c.items()}


class Hyena:
    def __init__(self, net, FLAT, FCTX, NEGT, DELTA, FEATS):
        self.net = net; k = net.k; nc = k.nc
        self.F = {LAT: FLAT, CTX: FCTX}; self.NEGT = NEGT; self.DELTA = DELTA; self.FEATS = FEATS
        self.FT = {LAT: DT(nc, "FTLAT", [2 * LAT, LAT], dtype=BF16), CTX: DT(nc, "FTCTX", [2 * CTX, CTX], dtype=BF16)}
        self.FB = {LAT: DT(nc, "FBLAT", [LAT, 2 * LAT], dtype=BF16), CTX: DT(nc, "FBCTX", [CTX, 2 * CTX], dtype=BF16)}
        self.H1T = DT(nc, "H1T", [64, LAT]); self.H2T = DT(nc, "H2T", [64, LAT])
        self.HF = DT(nc, "HF", [LAT, 4 * E]); self.HS = [DT(nc, f"HS{n}", [LAT, E], dtype=BF16) for n in range(2)]
        self.HD = [DT(nc, f"HD{n}", [LAT, E], dtype=BF16) for n in range(2)]
        self.SPEC = [DT(nc, f"SPEC{n}", [2 * LAT, E]) for n in range(2)]
        self.ZT = DT(nc, "ZT", [LAT, E], dtype=BF16); self.ZF = DT(nc, "ZF", [2 * LAT, E]); self.YF = DT(nc, "YF", [2 * LAT, E], dtype=BF16); self.Y = DT(nc, "Y", [E, LAT])
        self.Z1 = DT(nc, "Z1", [E, T])
        self.ft_done = False

    def setup(self):
        net = self.net; k = net.k
        if not self.ft_done:
            for L in (LAT, CTX):
                net.transpose(self.F[L], self.FT[L], L, 2 * L)
                net.cast_dram(self.F[L], self.FB[L], L, 2 * L)
            sp = k.pool("hyneg", [128, 18], 1); self.negt, self.negtk = sp.next()
            k.load(self.negt[:], self.negtk, self.NEGT, 0, 128, 0, 18)
            self.ft_done = True

    def layer(self, XS, W_IN, CONV_WB, HYF, F_W1, F_W2, F_W3, SKIP, W_OUT, need_ctx):
        net = self.net; k = net.k; EP = net.EP; PS = net.PS
        self.setup()
        Tn = T if need_ctx else LAT
        segs = [(0, LAT, 0)] + ([(LAT, CTX, 16)] if need_ctx else [])
        cvp = k.pool("hycv", [128, 96, 4], 1); cv, cvk = cvp.next()
        net.rows_to_pp(CONV_WB, 0, 4, 0, 3 * E, cv, cvk)
        skp = k.pool("hysk", [128, NE, 2], 1); sk, skk = skp.next()
        net.rows_to_pp(SKIP, 0, 2, 0, E, sk, skk)
        fpp_p = k.pool("hyf", [128, 4], 1); fpp, fppk = fpp_p.next()
        tb, tk = EP.next()
        k.load(tb[0:4, 0:64], tk, HYF, 0, 4, 0, 64)
        ps, pk = PS.next()
        k.mm(ps[0:64, 0:4], pk, tb[0:4, 0:64], net.ident[0:4, 0:4], True, True, [tk, net.ck])
        k.copy(fpp[0:64, :], ps[0:64, 0:4], [pk], [fppk])
        dl, dlk = EP.pin()
        k.dma(dl[:, 0:E], self.DELTA.ap[0:1, :].to_broadcast([128, E]), self.DELTA.keys(0, 1, 0, E), [dlk])
        net.cast_dram(W_IN, net.WB, D, 4 * E); net.cast_dram(W_OUT, net.WOB, E, D)
        net.gemm(net.WB, net.HT, net.PR, D, 4 * E, Tn, act=lambda m0: AF.Silu if m0 >= 3 * E else AF.Copy)
        for j in range(96):
            u, uk = EP.next(); o, ok = EP.next()
            for si, (s0, sl, _) in enumerate(segs):
                b = s0 + 3 * si
                k.memset(u[:, b:b + 1], 0.0, [uk], eng="gpsimd"); k.memset(u[:, b + sl + 1:b + sl + 2], 0.0, [uk], eng="gpsimd")
                k.load(u[:, b + 1:b + 1 + sl], uk, net.PR, j * 128, j * 128 + 128, s0, s0 + sl)
            for si, (s0, sl, _) in enumerate(segs):
                b = s0 + 3 * si
                k.ts(o[:, s0:s0 + sl], u[:, b:b + sl], cv[:, j, 0:1], cv[:, j, 3:4], ALU.mult, ALU.add, [uk, cvk], [ok])
                for tap in (1, 2):
                    k.stt(o[:, s0:s0 + sl], u[:, b + tap:b + tap + sl], cv[:, j, tap:tap + 1], o[:, s0:s0 + sl], ALU.mult, ALU.add, [uk, cvk, ok], [ok])
            k.store(net.PR, j * 128, j * 128 + 128, 0, Tn, o[:, 0:Tn], ok)
        for (s0, L, nt0) in segs:
            self.filters(L, s0, nt0, HYF, F_W1, F_W2, F_W3, fpp, fppk, dl, dlk)
            self.longconv(L, 0, net.PR, 0, s0)
            self.combine(L, s0, 0, sk, skk, zsrc=(net.PR, 0), xrow=E, dst=self.Z1)
            self.longconv(L, 1, self.Z1, 0, s0)
            self.combine(L, s0, 1, sk, skk, zsrc=(self.Z1, 0), xrow=2 * E, dst=net.S, gate_row=3 * E)
        EP.unpin(dlk)
        net.gemm(net.S, net.WOB, None, E, Tn, D, epi=net.residual_epi(XS))

    def sin_epi(self, OUT, fpp, fppk, cb, cfr):
        net = self.net; k = net.k; EP = net.EP
        def epi(ps, pk, m0, mt, n0, nw):
            a, ak = EP.next()
            k.ts(a[0:mt, 0:nw], ps, fpp[0:mt, cb:cb + 1], fpp[0:mt, cfr:cfr + 1], ALU.add, ALU.mult, [pk, fppk], [ak])
            m, mk = EP.next()
            for lvl in range(2):
                k.ts(m[0:mt, 0:nw], a[0:mt, 0:nw], math.pi, -TWO_PI, ALU.is_gt, ALU.mult, [ak], [mk])
                k.tt(a[0:mt, 0:nw], a[0:mt, 0:nw], m[0:mt, 0:nw], ALU.add, [ak, mk], [ak])
                k.ts(m[0:mt, 0:nw], a[0:mt, 0:nw], -math.pi, TWO_PI, ALU.is_lt, ALU.mult, [ak], [mk])
                k.tt(a[0:mt, 0:nw], a[0:mt, 0:nw], m[0:mt, 0:nw], ALU.add, [ak, mk], [ak])
            k.act(a[0:mt, 0:nw], a[0:mt, 0:nw], AF.Sin, [ak], [ak])
            k.store(OUT, m0, m0 + mt, n0, n0 + nw, a[0:mt, 0:nw], ak)
        return epi

    def filters(self, L, s0, nt0, HYF, F_W1, F_W2, F_W3, fpp, fppk, dl, dlk):
        net = self.net; k = net.k; EP = net.EP
        N = 2 * L; F = self.FB[L]
        net.gemm(F_W1, self.FEATS, None, 33, 64, L, r0=(0, s0), epi=self.sin_epi(self.H1T, fpp, fppk, 0, 1))
        net.gemm(F_W2, self.H1T, None, 64, 64, L, epi=self.sin_epi(self.H2T, fpp, fppk, 2, 3))
        def wepi(ps, pk, m0, mt, n0, nw):
            e0 = n0 % E
            w, wk = EP.next(); ob, ok = EP.next()
            k.act(w[0:mt, 0:nw], dl[0:mt, e0:e0 + nw], AF.Exp, [dlk, self.negtk], [wk], scale=self.negt[0:mt, nt0 + m0 // 128:nt0 + m0 // 128 + 1])
            k.tt(ob[0:mt, 0:nw], ps, w[0:mt, 0:nw], ALU.mult, [pk, wk], [ok])
            k.store(self.HF, m0, m0 + mt, n0, n0 + nw, ob[0:mt, 0:nw], ok)
        net.gemm(self.H2T, F_W3, None, 64, L, 4 * E, epi=wepi)
        for n in range(2):
            for t in range(L // 128):
                for eb in range(0, E, 2048):
                    f, fk = EP.next(); b, bk = EP.next(); hs, hsk = net.EB.next(); hd, hdk = net.EB.next()
                    k.load(f[:, 0:2048], fk, self.HF, t * 128, t * 128 + 128, (2 * n) * E + eb, (2 * n) * E + eb + 2048)
                    k.load(b[:, 0:2048], bk, self.HF, t * 128, t * 128 + 128, (2 * n + 1) * E + eb, (2 * n + 1) * E + eb + 2048)
                    if t == 0:
                        k.memset(b[0:1, 0:2048], 0.0, [bk])
                    k.tt(hs[:, 0:2048], f[:, 0:2048], b[:, 0:2048], ALU.add, [fk, bk], [hsk])
                    k.tt(hd[:, 0:2048], f[:, 0:2048], b[:, 0:2048], ALU.subtract, [fk, bk], [hdk], eng="gpsimd")
                    k.store(self.HS[n], t * 128, t * 128 + 128, eb, eb + 2048, hs[:, 0:2048], hsk)
                    k.store(self.HD[n], t * 128, t * 128 + 128, eb, eb + 2048, hd[:, 0:2048], hdk)
            def sepi(row_off, fix0, scale, row0_only=False):
                def epi(ps, pk, m0, mt, n0, nw):
                    if row0_only:
                        mt = 1
                        ps = ps[0:1, :]
                    ob, ok = EP.next()
                    k.act(ob[0:mt, 0:nw], ps, AF.Copy, [pk], [ok], scale=scale)
                    if fix0 and m0 == 0:
                        k.ts(ob[0:1, 0:nw], ob[0:1, 0:nw], 0.5, None, ALU.mult, None, [ok], [ok])
                    k.store(self.SPEC[n], row_off + m0, row_off + m0 + mt, n0, n0 + nw, ob[0:mt, 0:nw], ok)
                return epi
            net.gemm(F, self.HS[n], None, L, L, E, l0=(0, 0), epi=sepi(0, True, 2.0 / N))
            net.gemm(F, self.HD[n], None, L, L, E, l0=(0, L), epi=sepi(L, False, 2.0 / N))
            net.gemm(F, self.HS[n], None, L, 128, E, l0=(0, L), epi=sepi(L, False, 1.0 / N, True))

    def longconv(self, L, n, SRC, row0, s0):
        net = self.net; k = net.k; EP = net.EP
        F = self.FB[L]; FT = self.FT[L]; SP = self.SPEC[n]
        net.transpose(SRC, self.ZT, E, L, s0=(row0, s0))
        net.gemm(F, self.ZT, self.ZF, L, 2 * L, E)
        for kt in range(L // 128):
            for eb in range(0, E, 2048):
                a, ak = EP.next(); b, bk = EP.next(); sa, sak = EP.next(); sb, sbk = EP.next()
                ya, yak = EP.next(); yb, ybk = EP.next(); t1, t1k = EP.next(); oa, oak = net.EB.next(); ob2, ob2k = net.EB.next()
                W = 2048
                r = kt * 128
                k.load(a[:, 0:W], ak, self.ZF, r, r + 128, eb, eb + W); k.load(b[:, 0:W], bk, self.ZF, L + r, L + r + 128, eb, eb + W)
                k.load(sa[:, 0:W], sak, SP, r, r + 128, eb, eb + W); k.load(sb[:, 0:W], sbk, SP, L + r, L + r + 128, eb, eb + W)
                k.tt(ya[:, 0:W], a[:, 0:W], sa[:, 0:W], ALU.mult, [ak, sak], [yak])
                k.tt(t1[:, 0:W], b[:, 0:W], sb[:, 0:W], ALU.mult, [bk, sbk], [t1k], eng="gpsimd")
                k.tt(yb[:, 0:W], a[:, 0:W], sb[:, 0:W], ALU.mult, [ak, sbk], [ybk])
                k.tt(t1[:, 0:W], ya[:, 0:W], t1[:, 0:W], ALU.subtract, [yak, t1k], [t1k])
                k.tt(b[:, 0:W], b[:, 0:W], sa[:, 0:W], ALU.mult, [bk, sak], [bk], eng="gpsimd")
                k.tt(yb[:, 0:W], yb[:, 0:W], b[:, 0:W], ALU.add, [ybk, bk], [ybk])
                if kt == 0:
                    k.copy(t1[0:1, 0:W], ya[0:1, 0:W], [yak], [t1k])
                    k.load(b[0:1, 0:W], bk, self.ZF, L, L + 1, eb, eb + W)
                    k.tt(yb[0:1, 0:W], b[0:1, 0:W], sb[0:1, 0:W], ALU.mult, [bk, sbk], [ybk])
                k.copy(oa[:, 0:W], t1[:, 0:W], [t1k], [oak], eng="gpsimd")
                k.act(ob2[:, 0:W], yb[:, 0:W], AF.Copy, [ybk], [ob2k])
                k.store(self.YF, r, r + 128, eb, eb + W, oa[:, 0:W], oak)
                k.store(self.YF, L + r, L + r + 128, eb, eb + W, ob2[:, 0:W], ob2k)
        net.gemm(self.YF, FT, self.Y, 2 * L, E, L)

    def combine(self, L, s0, n, sk, skk, zsrc, xrow, dst, gate_row=None):
        net = self.net; k = net.k; EP = net.EP
        ZS, zrow = zsrc
        for j in range(NE):
            y, yk = EP.next(); z, zk = EP.next(); x, xk = EP.next()
            k.load(y[:, 0:L], yk, self.Y, j * 128, j * 128 + 128, 0, L)
            k.load(z[:, 0:L], zk, ZS, zrow + j * 128, zrow + j * 128 + 128, s0, s0 + L)
            k.load(x[:, 0:L], xk, net.PR, xrow + j * 128, xrow + j * 128 + 128, s0, s0 + L)
            k.stt(y[:, 0:L], z[:, 0:L], sk[:, j, n:n + 1], y[:, 0:L], ALU.mult, ALU.add, [zk, skk, yk], [yk])
            k.tt(y[:, 0:L], y[:, 0:L], x[:, 0:L], ALU.mult, [yk, xk], [yk])
            if gate_row is not None:
                k.load(z[:, 0:L], zk, net.PR, gate_row + j * 128, gate_row + j * 128 + 128, s0, s0 + L)
                ob, obk = net.EB.next()
                k.tt(ob[:, 0:L], y[:, 0:L], z[:, 0:L], ALU.mult, [yk, zk], [obk], eng="gpsimd")
                k.store(dst, j * 128, j * 128 + 128, s0, s0 + L, ob[:, 0:L], obk)
                continue
            k.store(dst, j * 128, j * 128 + 128, s0, s0 + L, y[:, 0:L], yk)


RT_H = 8; RT_DK = 256; RT_DV = 512; CH = 128


def ret_consts():
    c = {}
    half = 64
    inv = (10000.0 ** (-np.arange(half, dtype=np.float32) / half)).astype(np.float32)
    rows = np.repeat(np.arange(LAT // 64), 64).astype(np.float32); cols = np.tile(np.arange(64), LAT // 64).astype(np.float32)
    ar = rows[:, None] * inv; ac = cols[:, None] * inv
    c["rcos"] = np.concatenate([np.cos(ar), np.cos(ac)], 1)
    c["rsin"] = np.concatenate([np.sin(ar), np.sin(ac)], 1)
    p = np.arange(128, dtype=np.float32)
    j = p[:, None]; i = p[None, :]
    c["rmask"] = np.concatenate([np.maximum(i - j, 0), (i >= j).astype(np.float32), np.maximum(j - i, 0), (j >= i).astype(np.float32)], 1)
    c["rcols"] = np.stack([p + 1, CH - 1 - p, CH - p, p], 1)
    return {k_: np.ascontiguousarray(v, dtype=np.float32) for k_, v in c.items()}


class Retention:
    def __init__(self, net, RCOS, RSIN, RMASK, RCOLS):
        self.net = net; nc = net.k.nc
        self.RCOS = RCOS; self.RSIN = RSIN; self.RMASK = RMASK; self.RCOLS = RCOLS
        self.QKV = DT(nc, "QKV", [T, 2 * D + E]); self.QKT = DT(nc, "QKT", [2 * D, T])
        self.OACC = DT(nc, "OACC", [T, E]); self.OT = DT(nc, "OTR", [E, T])

    def layer(self, XS, W_IN, DECAY, GN_GB, W_OUT):
        net = self.net; k = net.k; EP = net.EP; PS = net.PS
        def qkv_epi(ps, pk, m0, mt, n0, nw):
            ob, ok = EP.next()
            k.act(ob[0:mt, 0:nw], ps, AF.Copy, [pk], [ok], scale=(RT_DK ** -0.5 if D <= n0 < 2 * D else 1.0))
            k.store(self.QKV, m0, m0 + mt, n0, n0 + nw, ob[0:mt, 0:nw], ok)
        net.cast_dram(W_IN, net.WB, D, 2 * D + 2 * E); net.cast_dram(W_OUT, net.WOB, E, D)
        net.gemm(net.HT, net.WB, None, D, T, 2 * D + E, epi=qkv_epi)
        net.gemm(net.WB, net.HT, net.PR, D, E, T, l0=(0, 2 * D + E), act=AF.Silu)
        for t in range(LAT // 128):
            x, xk = EP.next(); o, ok = EP.next(); cs, csk = EP.next(); t1, t1k = EP.next(); t2, t2k = EP.next()
            k.load(x[:, 0:2 * D], xk, self.QKV, t * 128, t * 128 + 128, 0, 2 * D)
            k.load(cs[:, 0:128], csk, self.RCOS, t * 128, t * 128 + 128, 0, 128)
            k.load(cs[:, 128:256], csk, self.RSIN, t * 128, t * 128 + 128, 0, 128)
            xv = x[:, 0:2 * D].rearrange("p (h a s f) -> p h a s f", h=16, a=2, s=2)
            ov = o[:, 0:2 * D].rearrange("p (h a s f) -> p h a s f", h=16, a=2, s=2)
            cv = cs[:, 0:128].rearrange("p (a f) -> p a f", a=2); sv = cs[:, 128:256].rearrange("p (a f) -> p a f", a=2)
            t1v = t1[:, 0:128].rearrange("p (a f) -> p a f", a=2); t2v = t2[:, 0:128].rearrange("p (a f) -> p a f", a=2)
            for h in range(16):
                e1, e2 = ("vector", "gpsimd")
                k.tt(t1v, xv[:, h, :, 0, :], cv, ALU.mult, [xk, csk], [t1k], eng=e1)
                k.tt(t2v, xv[:, h, :, 1, :], sv, ALU.mult, [xk, csk], [t2k], eng=e2)
                k.tt(ov[:, h, :, 0, :], t1v, t2v, ALU.subtract, [t1k, t2k], [ok], eng=e1)
                k.tt(t1v, xv[:, h, :, 0, :], sv, ALU.mult, [xk, csk], [t1k], eng=e1)
                k.tt(t2v, xv[:, h, :, 1, :], cv, ALU.mult, [xk, csk], [t2k], eng=e2)
                k.tt(ov[:, h, :, 1, :], t1v, t2v, ALU.add, [t1k, t2k], [ok], eng=e1)
            k.store(self.QKV, t * 128, t * 128 + 128, 0, 2 * D, o[:, 0:2 * D], ok)
        net.transpose(self.QKV, self.QKT, T, 2 * D)
        cp = k.pool("rtc", [128, 1024], 1); cb, cbk = cp.next()
        lg = cb[:, 0:16]; cdec = cb[:, 16:32]; cols = cb[:, 32:36]; dec = cb[:, 64:128]
        rmask = cb[:, 128:640]; mk = cb[:, 640:896]
        k.dma(lg, DECAY.ap[0:1, 0:16].to_broadcast([128, 16]), DECAY.keys(0, 1, 0, 16), [cbk])
        k.load(cols, cbk, self.RCOLS, 0, 128, 0, 4)
        k.load(rmask, cbk, self.RMASK, 0, 128, 0, 512)
        k.act(lg, lg, AF.Exp, [cbk], [cbk], scale=-1.0)
        k.act(lg, lg, AF.Ln, [cbk], [cbk], bias=1.0)
        k.ts(lg, lg, -1.0, None, ALU.mult, None, [cbk], [cbk])
        k.act(cdec, lg, AF.Exp, [cbk], [cbk], scale=float(CH))
        decv = dec.rearrange("p (c s) -> p c s", c=4)
        for c4 in range(4):
            k.act(decv[:, c4, :], lg, AF.Exp, [cbk], [cbk], scale=cols[:, c4:c4 + 1])
        stp = k.pool("rtst", [128, 2, 512], 2)
        chunks_f = [(LAT + c * CH) for c in range(CTX // CH)] + [c * CH for c in range(LAT // CH)]
        chunks_b = [(LAT + c * CH) for c in reversed(range(CTX // CH))] + [c * CH for c in reversed(range(LAT // CH))]
        for h in range(RT_H):
            qT, qTk = EP.pin(); kT, kTk = EP.pin(); cT, cTk = EP.pin()
            for dc in range(2):
                r = h * RT_DK + dc * 128
                k.load(qT[:, dc * LAT:(dc + 1) * LAT], qTk, self.QKT, r, r + 128, 0, LAT)
                k.load(kT[:, dc * LAT:(dc + 1) * LAT], kTk, self.QKT, D + r, D + r + 128, 0, LAT)
                k.load(cT[:, dc * CTX:(dc + 1) * CTX], cTk, self.QKT, r, r + 128, LAT, T)
                k.load(cT[:, (2 + dc) * CTX:(3 + dc) * CTX], cTk, self.QKT, D + r, D + r + 128, LAT, T)
            def qk_slices(t0):
                if t0 < LAT:
                    return ([qT[:, dc * LAT + t0:dc * LAT + t0 + CH] for dc in range(2)],
                            [kT[:, dc * LAT + t0:dc * LAT + t0 + CH] for dc in range(2)], [qTk, kTk])
                c0 = t0 - LAT
                return ([cT[:, dc * CTX + c0:dc * CTX + c0 + CH] for dc in range(2)],
                        [cT[:, (2 + dc) * CTX + c0:(2 + dc) * CTX + c0 + CH] for dc in range(2)], [cTk])
            for di, chunks in enumerate((chunks_f, chunks_b)):
                col = di * 8 + h
                mt_ = mk[:, di * 128:(di + 1) * 128]
                k.act(mt_, rmask[:, di * 256:di * 256 + 128], AF.Exp, [cbk], [cbk], scale=lg[:, col:col + 1])
                k.tt(mt_, mt_, rmask[:, di * 256 + 128:di * 256 + 256], ALU.mult, [cbk], [cbk])
                st, stk = stp.next()
                k.memset(st[:], 0.0, [stk])
                for t0 in chunks:
                    qs, ks, qkk = qk_slices(t0)
                    kv, kvk = EP.next()
                    k.load(kv[:, 0:256], kvk, self.QKV, t0, t0 + CH, D + h * RT_DK, D + (h + 1) * RT_DK)
                    k.load(kv[:, 256:768], kvk, self.QKV, t0, t0 + CH, 2 * D + h * RT_DV, 2 * D + (h + 1) * RT_DV)
                    ps_s, pssk = PS.next()
                    for dc in range(2):
                        k.mm(ps_s[:, 0:CH], pssk, ks[dc], qs[dc], dc == 0, dc == 1, qkk)
                    sc, sck = EP.next()
                    k.tt(sc[:, 0:CH], ps_s[:, 0:CH], mt_, ALU.mult, [pssk, cbk], [sck])
                    p1, p1k = PS.next(); p2, p2k = PS.next()
                    k.mm(p1[:, 0:RT_DV], p1k, sc[:, 0:CH], kv[:, 256:768], True, True, [sck, kvk])
                    for dc in range(2):
                        k.mm(p2[:, 0:RT_DV], p2k, qs[dc], st[:, dc, :], dc == 0, dc == 1, qkk + [stk])
                    o, ok = EP.next()
                    k.act(o[:, 0:RT_DV], p1[:, 0:RT_DV], AF.Copy, [p1k], [ok])
                    k.stt(o[:, 0:RT_DV], p2[:, 0:RT_DV], decv[:, 2 * di, col:col + 1], o[:, 0:RT_DV], ALU.mult, ALU.add, [p2k, cbk, ok], [ok])
                    if di == 1:
                        pv, pvk = EP.next()
                        k.load(pv[:, 0:RT_DV], pvk, self.OACC, t0, t0 + CH, h * RT_DV, (h + 1) * RT_DV)
                        k.tt(o[:, 0:RT_DV], o[:, 0:RT_DV], pv[:, 0:RT_DV], ALU.add, [ok, pvk], [ok], eng="gpsimd")
                    k.store(self.OACC, t0, t0 + CH, h * RT_DV, (h + 1) * RT_DV, o[:, 0:RT_DV], ok)
                    k.ts(kv[:, 768:1024], kv[:, 0:256], decv[:, 2 * di + 1, col:col + 1], None, ALU.mult, None, [kvk, cbk], [kvk])
                    for dc in range(2):
                        p3, p3k = PS.next()
                        k.mm(p3[:, 0:RT_DV], p3k, kv[:, 768 + dc * 128:768 + (dc + 1) * 128], kv[:, 256:768], True, True, [kvk])
                        k.stt(st[:, dc, :], st[:, dc, :], cdec[:, col:col + 1], p3[:, 0:RT_DV], ALU.mult, ALU.add, [stk, cbk, p3k], [stk])
            EP.unpin(qTk); EP.unpin(kTk); EP.unpin(cTk)
        gg, ggk = EP.pin(); gb, gbk = EP.pin()
        k.dma(gg[:, 0:E], GN_GB.ap[0:1, :].to_broadcast([128, E]), GN_GB.keys(0, 1, 0, E), [ggk])
        k.dma(gb[:, 0:E], GN_GB.ap[1:2, :].to_broadcast([128, E]), GN_GB.keys(1, 2, 0, E), [gbk])
        smp = k.pool("rtsm", [128, 64], 2)
        for t in range(T // 128):
            o, ok = EP.next(); sq, sqk = EP.next(); sm, smk = smp.next()
            k.load(o[:, 0:E], ok, self.OACC, t * 128, t * 128 + 128, 0, E)
            k.act(sq[:, 0:E], o[:, 0:E], AF.Square, [ok], [sqk])
            ov = o[:, 0:E].rearrange("p (h v) -> p h v", h=RT_H); sqv = sq[:, 0:E].rearrange("p (h v) -> p h v", h=RT_H)
            k.R.op("vector", lambda e, sm=sm, ov=ov: e.reduce_sum(out=sm[:, 0:8], in_=ov, axis=AX.X), reads=[ok], writes=[smk])
            k.R.op("vector", lambda e, sm=sm, sqv=sqv: e.reduce_sum(out=sm[:, 8:16], in_=sqv, axis=AX.X), reads=[sqk], writes=[smk])
            k.ts(sm[:, 0:16], sm[:, 0:16], 1.0 / RT_DV, None, ALU.mult, None, [smk], [smk])
            k.tt(sm[:, 16:24], sm[:, 0:8], sm[:, 0:8], ALU.mult, [smk], [smk])
            k.tt(sm[:, 24:32], sm[:, 8:16], sm[:, 16:24], ALU.subtract, [smk], [smk])
            k.ts(sm[:, 24:32], sm[:, 24:32], EPS, None, ALU.add, None, [smk], [smk])
            k.act(sm[:, 24:32], sm[:, 24:32], AF.Sqrt, [smk], [smk])
            k.R.op("vector", lambda e, sm=sm: e.reciprocal(out=sm[:, 32:40], in_=sm[:, 24:32]), reads=[smk], writes=[smk])
            for h in range(RT_H):
                k.ts(ov[:, h, :], ov[:, h, :], sm[:, h:h + 1], sm[:, 32 + h:33 + h], ALU.subtract, ALU.mult, [ok, smk], [ok])
            k.tt(o[:, 0:E], o[:, 0:E], gg[:, 0:E], ALU.mult, [ok, ggk], [ok])
            k.tt(o[:, 0:E], o[:, 0:E], gb[:, 0:E], ALU.add, [ok, gbk], [ok], eng="gpsimd")
            k.store(self.OACC, t * 128, t * 128 + 128, 0, E, o[:, 0:E], ok)
        EP.unpin(ggk); EP.unpin(gbk)
        net.transpose(self.OACC, self.OT, T, E)
        for j in range(NE):
            a, ak = EP.next(); g, gk = EP.next()
            k.load(a[:, 0:T], ak, self.OT, j * 128, j * 128 + 128, 0, T)
            k.load(g[:, 0:T], gk, net.PR, j * 128, j * 128 + 128, 0, T)
            ob, obk = net.EB.next()
            k.tt(ob[:, 0:T], a[:, 0:T], g[:, 0:T], ALU.mult, [ak, gk], [obk])
            k.store(net.S, j * 128, j * 128 + 128, 0, T, ob[:, 0:T], obk)
        net.gemm(net.S, net.WOB, None, E, T, D, epi=net.residual_epi(XS))


DEPTH = 4; NCORES = 8


def build_program():
    nc = bass.Bass("TRN2", target_bir_lowering=False)
    st = contextlib.ExitStack()
    k = K(nc, st); net = Net(k)
    ext = lambda n, shp: DT(nc, n, shp, kind="ExternalInput")
    XIN = ext("xin", [T, D]); CC = ext("cc", [2, D]); CONSTS = ext("consts", [128, 128])
    ADA_W = ext("ada_w", [4 * D, 3 * D]); ADA_B = ext("ada_b", [4, 3 * D]); NORM_G = ext("norm_g", [4, D])
    FNG = ext("final_norm_g", [1, D])
    FLAT = ext("flat", [LAT, 2 * LAT]); FCTX = ext("fctx", [CTX, 2 * CTX]); NEGT = ext("negt", [128, 18])
    DELTA = ext("delta", [1, E]); FEATS = ext("feats", [33, T])
    HY = []
    for j in range(2):
        HY.append(dict(W_IN=ext(f"hy_w_in{j}", [D, 4 * E]), CONV_WB=ext(f"hy_conv_wb{j}", [4, 3 * E]), HYF=ext(f"hy_f{j}", [4, 64]),
                       F_W1=ext(f"hy_f_w1{j}", [33, 64]), F_W2=ext(f"hy_f_w2{j}", [64, 64]), F_W3=ext(f"hy_f_w3{j}", [64, 4 * E]),
                       SKIP=ext(f"hy_skip{j}", [2, E]), W_OUT=ext(f"hy_w_out{j}", [E, D])))
    CF = dict(W_IN=ext("cf_w_in", [D, 3 * E]), DW_W=ext("cf_dw_w", [31, E]), DW_B=ext("cf_dw_b", [1, E]),
              LN_G=ext("cf_ln_g", [1, E]), LN_B=ext("cf_ln_b", [1, E]), W_OUT=ext("cf_w_out", [E, D]))
    RCOS = ext("rcos", [LAT, 128]); RSIN = ext("rsin", [LAT, 128]); RMASK = ext("rmask", [128, 512]); RCOLS = ext("rcols", [128, 4])
    RT = dict(W_IN=ext("rt_w_in", [D, 2 * D + 2 * E]), DECAY=ext("rt_decay", [1, 16]), GN_GB=ext("rt_gn_gb", [2, E]),
              W_OUT=ext("rt_w_out", [E, D]))
    XS = DT(nc, "XS", [T, D]); OUT = DT(nc, "out", [LAT, D], kind="ExternalOutput")
    hy = Hyena(net, FLAT, FCTX, NEGT, DELTA, FEATS); rt = Retention(net, RCOS, RSIN, RMASK, RCOLS)
    net.init_consts(CONSTS)
    net.copy_dram(XIN, XS, T, D)
    net.build_mrep(CC)
    for i in range(DEPTH):
        kind, j = i % 3, i // 3
        last = i == DEPTH - 1
        need_ctx = (not last) or kind == 2
        net.adaln(ADA_W, ADA_B, i)
        net.prenorm(XS, NORM_G, i, (T if need_ctx else LAT) // 128)
        if kind == 0:
            hy.layer(XS, need_ctx=need_ctx, **HY[j])
        elif kind == 1:
            net.conformer(XS, CF["W_IN"], CF["DW_W"], CF["DW_B"], CF["LN_G"], CF["LN_B"], CF["W_OUT"], need_ctx=need_ctx)
        else:
            rt.layer(XS, RT["W_IN"], RT["DECAY"], RT["GN_GB"], RT["W_OUT"])
    net.final_norm(XS, FNG, OUT)
    k.R.final_wait("sync", [k.R.last_w[key] for key in OUT.keys(0, LAT, 0, D)])
    k.R.emit(nc, None)
    st.close()
    return nc


def kernel(x, c, ctx, c_ctx, ada_w, ada_b, norm_g, final_norm_g,
           hy_w_in, hy_conv_w, hy_conv_b, hy_f_w1, hy_f_b1, hy_f_fr1, hy_f_w2, hy_f_b2,
           hy_f_fr2, hy_f_w3, hy_skip, hy_w_out,
           cf_w_in, cf_dw_w, cf_dw_b, cf_ln_g, cf_ln_b, cf_w_out,
           rt_w_in, rt_decay_logit, rt_gn_g, rt_gn_b, rt_w_out):
    f = lambda a: np.ascontiguousarray(np.asarray(a), dtype=np.float32)
    x, c, ctx, c_ctx = f(x), f(c), f(ctx), f(c_ctx)
    shared = {"consts": np.eye(128, dtype=np.float32), "ada_w": f(ada_w).reshape(4 * D, 3 * D), "ada_b": f(ada_b),
              "norm_g": f(norm_g), "final_norm_g": f(final_norm_g).reshape(1, D),
              "cf_w_in": f(cf_w_in)[0], "cf_dw_w": f(cf_dw_w)[0], "cf_dw_b": f(cf_dw_b).reshape(1, E),
              "cf_ln_g": f(cf_ln_g).reshape(1, E), "cf_ln_b": f(cf_ln_b).reshape(1, E), "cf_w_out": f(cf_w_out)[0],
              "rt_w_in": f(rt_w_in)[0], "rt_decay": f(rt_decay_logit)[0].reshape(1, 16),
              "rt_gn_gb": np.stack([f(rt_gn_g)[0], f(rt_gn_b)[0]]), "rt_w_out": f(rt_w_out)[0]}
    for j in range(2):
        shared.update({f"hy_w_in{j}": f(hy_w_in)[j], f"hy_conv_wb{j}": np.concatenate([f(hy_conv_w)[j], f(hy_conv_b)[j][None]], 0),
                       f"hy_f{j}": np.stack([f(hy_f_b1)[j], f(hy_f_fr1)[j], f(hy_f_b2)[j], f(hy_f_fr2)[j]]),
                       f"hy_f_w1{j}": f(hy_f_w1)[j], f"hy_f_w2{j}": f(hy_f_w2)[j], f"hy_f_w3{j}": f(hy_f_w3)[j],
                       f"hy_skip{j}": f(hy_skip)[j], f"hy_w_out{j}": f(hy_w_out)[j]})
    shared.update(hyena_consts()); shared.update(ret_consts())
    shared = {k_: np.ascontiguousarray(v, dtype=np.float32) for k_, v in shared.items()}
    in_maps = []
    for b in range(NCORES):
        m = dict(shared)
        m["xin"] = np.ascontiguousarray(np.concatenate([x[b], ctx[b]], 0))
        m["cc"] = np.ascontiguousarray(np.stack([c[b], c_ctx]))
        in_maps.append(m)
    nc = build_program()
    res = run_bass_kernel_spmd(nc, in_maps, core_ids=list(range(NCORES)))
    return np.stack([np.asarray(res.results[b]["out"], dtype=np.float32) for b in range(NCORES)], 0)
```

```python
import contextlib, math
import numpy as np
import concourse.bass as bass
import concourse.mybir as mybir
from concourse.bass_utils import run_bass_kernel_spmd

COMPUTE = ("tensor", "vector", "scalar", "gpsimd")
DMAQ = ("sync", "gpsimd")
NDS = 6


class Op:
    __slots__ = ("eng", "fn", "waits", "idx", "is_dma", "tick", "dsem", "dval", "name")


class Rec:
    def __init__(self):
        self.ops = {e: [] for e in ("tensor", "vector", "scalar", "gpsimd", "sync")}
        self.last_w = {}
        self.readers = {}
        self.ndma = {q: 0 for q in DMAQ}
        self.dma_tok = {q: [] for q in DMAQ}

    def op(self, eng, fn, reads=(), writes=(), dma=False, pe_accum=False, name=None):
        o = Op(); o.eng = eng; o.fn = fn; o.waits = []; o.is_dma = dma; o.tick = False
        o.name = name; o.dsem = None; o.dval = None
        deps = []
        for r in reads:
            t = self.last_w.get(r)
            if t is not None: deps.append(t)
        for w in writes:
            t = self.last_w.get(w)
            if t is not None and not (pe_accum and t[0] == "c" and t[1] == "tensor"):
                deps.append(t)
            deps.extend(self.readers.get(w, ()))
        if dma:
            q = eng; i = self.ndma[q]; self.ndma[q] += 1
            if i >= NDS: deps.append(self.dma_tok[q][i - NDS])
            tok = ("d", q, i % NDS, 16 * (i // NDS + 1))
            self.dma_tok[q].append(tok); o.dsem = (q, i % NDS)
        else:
            tok = ("c", eng, o)
        seen = set()
        for d in deps:
            if d is tok or id(d) in seen: continue
            seen.add(id(d))
            if d[0] == "c": d[2].tick = True
            o.waits.append(d)
        self.ops[eng].append(o)
        for w in writes:
            self.last_w[w] = tok; self.readers[w] = []
        for r in reads:
            if r not in writes:
                lst = self.readers.setdefault(r, [])
                if tok[0] == "c":
                    lst[:] = [t for t in lst if not (t[0] == "c" and t[1] == eng)]
                lst.append(tok)
        return tok

    def final_wait(self, eng, toks):
        o = Op(); o.eng = eng; o.fn = None; o.waits = list(toks); o.is_dma = False
        o.tick = False; o.name = "final"; o.dsem = None; o.dval = None
        for d in toks:
            if d[0] == "c": d[2].tick = True
        self.ops[eng].append(o)

    def emit(self, nc, block_engines):
        import contextlib
        for e in COMPUTE:
            n = 0
            for o in self.ops[e]:
                if o.tick and not o.is_dma:
                    n += 1; o.idx = n
        with contextlib.ExitStack() as st:
            csem = {e: st.enter_context(nc.semaphore("c_" + e)) for e in COMPUTE}
            dsem = {(q, k): st.enter_context(nc.semaphore(f"d_{q}{k}")) for q in DMAQ for k in range(NDS)}
            blk = st.enter_context(nc.Block())
            for e, lst in self.ops.items():
                if not lst: continue
                def body(eng, lst=lst, e=e):
                    known = {}
                    for o in lst:
                        need = {}
                        for d in o.waits:
                            if d[0] == "c": s, v = csem[d[1]], d[2].idx
                            else: s, v = dsem[(d[1], d[2])], d[3]
                            k = id(s)
                            if known.get(k, 0) >= v: continue
                            if k not in need or need[k][1] < v: need[k] = (s, v)
                        for k, (s, v) in need.items():
                            eng.wait_ge(s, v); known[k] = v
                        if o.fn is None: continue
                        ins = o.fn(eng)
                        if o.is_dma: ins.then_inc(dsem[o.dsem], 16)
                        elif o.tick: ins.then_inc(csem[e], 1)
                getattr(blk, e)(body)


F32 = mybir.dt.float32
BF16 = mybir.dt.bfloat16
AF = mybir.ActivationFunctionType
ALU = mybir.AluOpType
AX = mybir.AxisListType


class DT:
    def __init__(self, nc, name, shape, kind="Internal", rb=128, cb=512, dtype=F32):
        self.name = name; self.shape = tuple(shape); self.dtype = dtype
        self.ap = nc.dram_tensor(name, list(shape), dtype, kind=kind).ap()
        self.rb = rb; self.cb = cb

    def keys(self, r0, r1, c0, c1):
        return [(self.name, i, j) for i in range(r0 // self.rb, (r1 - 1) // self.rb + 1)
                for j in range(c0 // self.cb, (c1 - 1) // self.cb + 1)]


class Pool:
    def __init__(self, nc, st, name, shape, n, psum=False, dtype=F32):
        mk = nc.psum_tensor if psum else nc.sbuf_tensor
        self.bufs = [st.enter_context(mk(f"{name}{i}", list(shape), dtype)) for i in range(n)]
        self.keys = [(name, i) for i in range(n)]
        self.i = 0; self.pinned = set()

    def next(self):
        while True:
            k = self.i % len(self.bufs); self.i += 1
            if k not in self.pinned:
                return self.bufs[k], self.keys[k]

    def pin(self):
        b, key = self.next()
        self.pinned.add(self.keys.index(key))
        return b, key

    def unpin(self, key):
        self.pinned.discard(self.keys.index(key))


class K:
    def __init__(self, nc, st):
        self.nc = nc; self.st = st; self.R = Rec(); self.pools = {}
        self.qi = 0

    def pool(self, name, shape, n, psum=False, dtype=F32):
        if name not in self.pools:
            self.pools[name] = Pool(self.nc, self.st, name, shape, n, psum, dtype)
        return self.pools[name]

    def dma(self, out, in_, reads, writes, q=None):
        if q is None:
            q = "sync"
        self.R.op(q, lambda e: e.dma_start(out=out, in_=in_), reads=reads, writes=writes, dma=True)

    def load(self, sb, sbkey, dt, r0, r1, c0, c1, pat=None, q="sync", **kw):
        src = dt.ap[r0:r1, c0:c1]
        if pat: src = src.rearrange(pat, **kw)
        self.dma(sb, src, dt.keys(r0, r1, c0, c1), [sbkey], q=q)

    def store(self, dt, r0, r1, c0, c1, sb, sbkey, pat=None, q="gpsimd", **kw):
        dst = dt.ap[r0:r1, c0:c1]
        if pat: dst = dst.rearrange(pat, **kw)
        self.dma(dst, sb, [sbkey], dt.keys(r0, r1, c0, c1), q=q)

    def mm(self, ps, pskey, lhsT, rhs, start, stop, reads):
        self.R.op("tensor", lambda e: e.matmul(ps, lhsT=lhsT, rhs=rhs, start=start, stop=stop),
                  reads=reads, writes=[pskey], pe_accum=not start)

    def act(self, out, in_, func, reads, writes, eng="scalar", **kw):
        self.R.op(eng, lambda e: e.activation(out=out, in_=in_, func=func, **kw), reads=reads, writes=writes)

    def tt(self, out, a, b, op, reads, writes, eng="vector"):
        self.R.op(eng, lambda e: e.tensor_tensor(out=out, in0=a, in1=b, op=op), reads=reads, writes=writes)

    def ts(self, out, a, s1, s2, op0, op1, reads, writes, eng="vector", **kw):
        if s2 is None:
            self.R.op(eng, lambda e: e.tensor_scalar(out=out, in0=a, scalar1=s1, scalar2=None, op0=op0, **kw), reads=reads, writes=writes)
        else:
            self.R.op(eng, lambda e: e.tensor_scalar(out=out, in0=a, scalar1=s1, scalar2=s2, op0=op0, op1=op1, **kw), reads=reads, writes=writes)

    def stt(self, out, a, s, b, op0, op1, reads, writes, eng="vector"):
        self.R.op(eng, lambda e: e.scalar_tensor_tensor(out=out, in0=a, scalar=s, in1=b, op0=op0, op1=op1), reads=reads, writes=writes)

    def copy(self, out, in_, reads, writes, eng="vector"):
        self.R.op(eng, lambda e: e.tensor_copy(out=out, in_=in_), reads=reads, writes=writes)

    def memzero(self, ap, writes, eng="scalar"):
        self.R.op(eng, lambda e: e.memzero(ap), reads=[], writes=writes)

    def memset(self, ap, val, writes, eng="vector"):
        self.R.op(eng, lambda e: e.memset(ap, val), reads=[], writes=writes)


D = 2048; E = 4096; LAT = 2048; CTX = 256; T = LAT + CTX; EPS = 1e-6
NE = E // 128


class Net:
    def __init__(self, k):
        self.k = k; nc = k.nc
        self.EP = k.pool("E", [128, 4096], 9)
        self.EB = k.pool("EB", [128, 4096], 5, dtype=BF16)
        self.PS = k.pool("ps", [128, 512], 8, psum=True)
        cp = k.pool("const", [128, 256], 1); cb, ck = cp.next()
        self.cb = cb; self.ck = ck
        self.ident = cb[:, 0:128]; self.ones = cb[:, 128:256]
        self.MREP = DT(nc, "MREP", [D, 256], dtype=BF16); self.ADAB = DT(nc, "ADAB", [256, 3 * D])
        self.H = DT(nc, "H", [T, D]); self.HT = DT(nc, "HT", [D, T], dtype=BF16)
        self.PR = DT(nc, "PR", [4 * E, T]); self.U = DT(nc, "U", [E, T]); self.S = DT(nc, "S", [E, T], dtype=BF16)
        self.WB = DT(nc, "WB", [D, 4 * E], dtype=BF16); self.WOB = DT(nc, "WOB", [E, D], dtype=BF16); self.ADAWB = DT(nc, "ADAWB", [D, 3 * D], dtype=BF16)

    def init_consts(self, CONSTS):
        k = self.k
        k.load(self.cb[:, 0:128], self.ck, CONSTS, 0, 128, 0, 128)
        k.memset(self.cb[:, 128:256], 1.0, [self.ck])

    def rows_to_pp(self, SRC, r0, n, c0, C, dst, dkey):
        k = self.k
        for cb in range(0, C, 2048):
            cw = min(2048, C - cb)
            sb, sk = self.EP.next()
            k.load(sb[0:n, 0:cw], sk, SRC, r0, r0 + n, c0 + cb, c0 + cb + cw)
            per = 512 // n
            for j0 in range(0, cw // 128, per):
                jn = min(per, cw // 128 - j0)
                ps, pk = self.PS.next()
                for j in range(jn):
                    k.mm(ps[:, j * n:(j + 1) * n], pk, sb[0:n, (j0 + j) * 128:(j0 + j + 1) * 128], self.ident[0:n, 0:n],
                         True, True, [sk, self.ck])
                t0 = cb // 128 + j0
                k.copy(dst[:, t0:t0 + jn, :], ps[:, 0:jn * n].rearrange("p (j n) -> p j n", n=n), [pk], [dkey])

    def build_mrep(self, CC):
        k = self.k
        tp = k.pool("tmpv", [128, 16, 2], 1); tb, tk = tp.next()
        self.rows_to_pp(CC, 0, 2, 0, D, tb, tk)
        sp = k.pool("tmps", [128, 16, 2], 1); sb, sk = sp.next()
        k.act(sb[:], tb[:], AF.Silu, [tk], [sk])
        for kc in range(16):
            rb, rk = self.EB.next()
            for s in range(2):
                k.ts(rb[:, s * 128:(s + 1) * 128], self.ones, sb[:, kc, s:s + 1], None, ALU.mult, None, [sk, self.ck], [rk])
            k.store(self.MREP, kc * 128, kc * 128 + 128, 0, 256, rb[:, 0:256], rk)

    def adaln(self, ADA_W, ADA_B, i):
        k = self.k
        def epi(ps, pk, m0, mt, n0, nw):
            bb, bk = self.EP.next()
            k.dma(bb[:, 0:nw], ADA_B.ap[i:i + 1, n0:n0 + nw].to_broadcast([128, nw]), ADA_B.keys(i, i + 1, n0, n0 + nw), [bk])
            ob, ok = self.EP.next()
            k.tt(ob[0:mt, 0:nw], ps, bb[0:mt, 0:nw], ALU.add, [pk, bk], [ok])
            k.store(self.ADAB, m0, m0 + mt, n0, n0 + nw, ob[0:mt, 0:nw], ok)
        self.cast_dram(ADA_W, self.ADAWB, D, 3 * D, s0=(i * D, 0))
        self.gemm(self.MREP, self.ADAWB, None, D, 256, 3 * D, epi=epi)

    def prenorm(self, XS, NORM_G, i, ntile):
        k = self.k
        gs, gsk = self.EP.pin(); sh, shk = self.EP.pin()
        gs = gs[:, :].rearrange("p (s d) -> p s d", s=2); sh = sh[:, :].rearrange("p (s d) -> p s d", s=2)
        gb, gk = self.EP.next()
        k.dma(gb[:, 0:D], NORM_G.ap[i:i + 1, :].to_broadcast([128, D]), NORM_G.keys(i, i + 1, 0, D), [gk])
        for s in range(2):
            tb, tk = self.EP.next()
            k.load(tb[:, 0:D], tk, self.ADAB, s * 128, s * 128 + 128, D, 2 * D)
            k.stt(gs[:, s, :], tb[:, 0:D], 1.0, gb[:, 0:D], ALU.add, ALU.mult, [tk, gk], [gsk])
            k.load(sh[:, s, :], shk, self.ADAB, s * 128, s * 128 + 128, 0, D)
        stp = k.pool("st", [128, 8], 2)
        for t in range(ntile):
            s = 0 if t < LAT // 128 else 1
            xb, xk = self.EP.next(); sq, sqk = self.EP.next(); st, stk = stp.next()
            k.load(xb[:, 0:D], xk, XS, t * 128, t * 128 + 128, 0, D)
            k.act(sq[:, 0:D], xb[:, 0:D], AF.Square, [xk], [sqk])
            k.R.op("vector", lambda e, st=st, sq=sq: e.reduce_sum(out=st[:, 0:1], in_=sq[:, 0:D], axis=AX.X), reads=[sqk], writes=[stk])
            k.ts(st[:, 1:2], st[:, 0:1], 1.0 / D, EPS, ALU.mult, ALU.add, [stk], [stk])
            k.act(st[:, 2:3], st[:, 1:2], AF.Sqrt, [stk], [stk])
            k.R.op("vector", lambda e, st=st: e.reciprocal(out=st[:, 3:4], in_=st[:, 2:3]), reads=[stk], writes=[stk])
            k.stt(sq[:, 0:D], xb[:, 0:D], st[:, 3:4], gs[:, s, :], ALU.mult, ALU.mult, [xk, stk, gsk], [sqk])
            k.tt(sq[:, 0:D], sq[:, 0:D], sh[:, s, :], ALU.add, [sqk, shk], [sqk])
            k.store(self.H, t * 128, t * 128 + 128, 0, D, sq[:, 0:D], sqk)
        self.EP.unpin(gsk); self.EP.unpin(shk)
        self.transpose(self.H, self.HT, ntile * 128, D)

    def residual_epi(self, XS):
        k = self.k
        def epi(ps, pk, m0, mt, n0, nw):
            s = 0 if m0 < LAT else 1
            xb, xk = self.EP.next(); gb, gk = self.EP.next()
            k.load(xb[0:mt, 0:nw], xk, XS, m0, m0 + mt, n0, n0 + nw)
            k.load(gb[0:mt, 0:nw], gk, self.ADAB, s * 128, s * 128 + mt, 2 * D + n0, 2 * D + n0 + nw)
            k.tt(gb[0:mt, 0:nw], ps, gb[0:mt, 0:nw], ALU.mult, [pk, gk], [gk])
            k.tt(xb[0:mt, 0:nw], xb[0:mt, 0:nw], gb[0:mt, 0:nw], ALU.add, [xk, gk], [xk])
            k.store(XS, m0, m0 + mt, n0, n0 + nw, xb[0:mt, 0:nw], xk)
        return epi

    def copy_dram(self, SRC, DST, rows, cols, s0=(0, 0), d0=(0, 0)):
        k = self.k
        for r in range(0, rows, 128):
            rw = min(128, rows - r)
            k.dma(DST.ap[d0[0] + r:d0[0] + r + rw, d0[1]:d0[1] + cols], SRC.ap[s0[0] + r:s0[0] + r + rw, s0[1]:s0[1] + cols],
                  SRC.keys(s0[0] + r, s0[0] + r + rw, s0[1], s0[1] + cols), DST.keys(d0[0] + r, d0[0] + r + rw, d0[1], d0[1] + cols))

    def cast_dram(self, SRC, DST, rows, cols, s0=(0, 0), d0=(0, 0)):
        k = self.k; engs = ("vector", "gpsimd", "scalar"); n = 0
        for r in range(0, rows, 128):
            rw = min(128, rows - r)
            for c in range(0, cols, 4096):
                cw = min(4096, cols - c)
                a, ak = self.EP.next(); b, bk = self.EB.next()
                k.load(a[0:rw, 0:cw], ak, SRC, s0[0] + r, s0[0] + r + rw, s0[1] + c, s0[1] + c + cw)
                e = engs[n % 3]; n += 1
                if e == "scalar": k.act(b[0:rw, 0:cw], a[0:rw, 0:cw], AF.Copy, [ak], [bk])
                else: k.copy(b[0:rw, 0:cw], a[0:rw, 0:cw], [ak], [bk], eng=e)
                k.store(DST, d0[0] + r, d0[0] + r + rw, d0[1] + c, d0[1] + c + cw, b[0:rw, 0:cw], bk)

    def final_norm(self, XS, FG, OUT):
        k = self.k
        gb, gk = self.EP.pin()
        k.dma(gb[:, 0:D], FG.ap[0:1, :].to_broadcast([128, D]), FG.keys(0, 1, 0, D), [gk])
        stp = k.pool("st", [128, 8], 2)
        for t in range(LAT // 128):
            xb, xk = self.EP.next(); sq, sqk = self.EP.next(); st, stk = stp.next()
            k.load(xb[:, 0:D], xk, XS, t * 128, t * 128 + 128, 0, D)
            k.act(sq[:, 0:D], xb[:, 0:D], AF.Square, [xk], [sqk])
            k.R.op("vector", lambda e, st=st, sq=sq: e.reduce_sum(out=st[:, 0:1], in_=sq[:, 0:D], axis=AX.X), reads=[sqk], writes=[stk])
            k.ts(st[:, 1:2], st[:, 0:1], 1.0 / D, EPS, ALU.mult, ALU.add, [stk], [stk])
            k.act(st[:, 2:3], st[:, 1:2], AF.Sqrt, [stk], [stk])
            k.R.op("vector", lambda e, st=st: e.reciprocal(out=st[:, 3:4], in_=st[:, 2:3]), reads=[stk], writes=[stk])
            k.stt(sq[:, 0:D], xb[:, 0:D], st[:, 3:4], gb[:, 0:D], ALU.mult, ALU.mult, [xk, stk, gk], [sqk])
            k.store(OUT, t * 128, t * 128 + 128, 0, D, sq[:, 0:D], sqk)
        self.EP.unpin(gk)

    def gemm(self, L, R, OUT, K, M, N, l0=(0, 0), r0=(0, 0), o0=(0, 0), epi=None, act=None):
        k = self.k
        assert L.dtype == R.dtype, (L.name, R.name)
        PP = self.EB if L.dtype == BF16 else self.EP
        MG, NB, KP = 512, 512, 8
        assert KP * max(MG, NB) <= 4096
        kch = (K + 127) // 128; kparts = min(K, 128); npan = (kch + KP - 1) // KP
        reuse_l = (npan <= 2) and (N > NB)
        for mg in range(0, M, MG):
            mw = min(MG, M - mg)
            held = {}
            for nb in range(0, N, NB):
                nw = min(NB, N - nb)
                pss = [self.PS.next() for _ in range((mw + 127) // 128)]
                for kp in range(npan):
                    kc0 = kp * KP; kcn = min(KP, kch - kc0)
                    ka = kc0 * 128; kb = min(K, (kc0 + kcn) * 128)
                    if kp in held:
                        lb, lk = held[kp]
                    else:
                        lb, lk = PP.pin() if reuse_l else PP.next()
                        k.load(lb[0:kparts, 0:kcn * mw].rearrange("p (c m) -> p c m", m=mw), lk, L, l0[0] + ka, l0[0] + kb,
                               l0[1] + mg, l0[1] + mg + mw, "(c p) m -> p c m", p=kparts)
                        if reuse_l: held[kp] = (lb, lk)
                    rb, rk = PP.next()
                    k.load(rb[0:kparts, 0:kcn * nw].rearrange("p (c m) -> p c m", m=nw), rk, R, r0[0] + ka, r0[0] + kb,
                           r0[1] + nb, r0[1] + nb + nw, "(c p) m -> p c m", p=kparts)
                    for g, (ps, pk) in enumerate(pss):
                        mt = min(128, mw - g * 128)
                        for kc in range(kcn):
                            k.mm(ps[0:mt, 0:nw], pk, lb[0:kparts, kc * mw + g * 128:kc * mw + g * 128 + mt],
                                 rb[0:kparts, kc * nw:(kc + 1) * nw],
                                 start=(kp == 0 and kc == 0), stop=(kp == npan - 1 and kc == kcn - 1), reads=[lk, rk])
                for g, (ps, pk) in enumerate(pss):
                    mt = min(128, mw - g * 128); m0 = mg + g * 128
                    if epi is not None:
                        epi(ps[0:mt, 0:nw], pk, m0, mt, nb, nw)
                    else:
                        ob, ok = self.EP.next()
                        k.act(ob[0:mt, 0:nw], ps[0:mt, 0:nw], act(m0) if callable(act) else (act or AF.Copy), [pk], [ok])
                        k.store(OUT, o0[0] + m0, o0[0] + m0 + mt, o0[1] + nb, o0[1] + nb + nw, ob[0:mt, 0:nw], ok)
            for (_, lk_) in held.values():
                PP.unpin(lk_)

    def transpose(self, SRC, DST, Rn, Cn, s0=(0, 0), d0=(0, 0)):
        k = self.k
        for rb in range(0, Rn, 512):
            rw = min(512, Rn - rb); nrt = rw // 128
            for cb in range(0, Cn, 512):
                cw = min(512, Cn - cb)
                ib, ik = self.EP.next()
                k.load(ib[:, 0:nrt * cw].rearrange("p (t c) -> p t c", c=cw), ik, SRC, s0[0] + rb, s0[0] + rb + rw,
                       s0[1] + cb, s0[1] + cb + cw, "(t p) c -> p t c", p=128)
                for ct in range(cw // 128):
                    ps, pk = self.PS.next()
                    for t in range(nrt):
                        o = ps[:, t * 128:(t + 1) * 128]; i = ib[:, t * cw + ct * 128:t * cw + ct * 128 + 128]
                        k.R.op("tensor", lambda e, o=o, i=i: e.transpose(o, i, self.ident), reads=[ik, self.ck], writes=[pk], pe_accum=(t > 0))
                    ob, ok = (self.EB if DST.dtype == BF16 else self.EP).next()
                    k.copy(ob[:, 0:rw], ps[:, 0:rw], [pk], [ok])
                    k.store(DST, d0[0] + cb + ct * 128, d0[0] + cb + ct * 128 + 128, d0[1] + rb, d0[1] + rb + rw, ob[:, 0:rw], ok)

    def conformer(self, XS, W_IN, DW_W, DW_B, LN_G, LN_B, W_OUT, need_ctx=True):
        k = self.k
        Tn = T if need_ctx else LAT
        segs = [(0, LAT)] + ([(LAT, CTX)] if need_ctx else [])
        cwp = k.pool("cfw", [128, NE, 34], 1); cw, cwk = cwp.next()
        self.rows_to_pp(DW_W, 0, 31, 0, E, cw[:, :, 0:31], cwk)
        for n, V in enumerate((DW_B, LN_G, LN_B)):
            self.rows_to_pp(V, 0, 1, 0, E, cw[:, :, 31 + n:32 + n], cwk)
        fn = lambda m0: AF.Copy if m0 < E else (AF.Sigmoid if m0 < 2 * E else AF.Silu)
        self.cast_dram(W_IN, self.WB, D, 3 * E); self.cast_dram(W_OUT, self.WOB, E, D)
        self.gemm(self.WB, self.HT, self.PR, D, 3 * E, Tn, act=fn)
        st0, st0k = self.EP.pin(); st1, st1k = self.EP.pin()
        k.memset(st0[:, 0:T], 0.0, [st0k]); k.memset(st1[:, 0:T], 0.0, [st1k])
        PAD = 15
        for j in range(NE):
            a, ak = self.EP.next(); sb, sk = self.EP.next(); u0, uk = self.EP.next()
            acc, acck = self.EP.next(); acc2, acc2k = self.EP.next()
            k.load(a[:, 0:Tn], ak, self.PR, j * 128, j * 128 + 128, 0, Tn)
            k.load(sb[:, 0:Tn], sk, self.PR, E + j * 128, E + j * 128 + 128, 0, Tn)
            for si, (s0, sl) in enumerate(segs):
                o = s0 + (2 * si + 1) * PAD
                k.memset(u0[:, o - PAD:o], 0.0, [uk]); k.memset(u0[:, o + sl:o + sl + PAD], 0.0, [uk])
                k.tt(u0[:, o:o + sl], a[:, s0:s0 + sl], sb[:, s0:s0 + sl], ALU.mult, [ak, sk], [uk])
            NV = 31
            for si, (s0, sl) in enumerate(segs):
                o = s0 + 2 * si * PAD
                for tap in range(31):
                    eng, ac, ack = ("vector", acc, acck) if tap < NV else ("gpsimd", acc2, acc2k)
                    src = u0[:, o + tap:o + tap + sl]
                    if tap == 0:
                        k.ts(ac[:, s0:s0 + sl], src, cw[:, j, 0:1], cw[:, j, 31:32], ALU.mult, ALU.add, [uk, cwk], [ack], eng=eng)
                    elif tap == NV:
                        k.ts(ac[:, s0:s0 + sl], src, cw[:, j, tap:tap + 1], None, ALU.mult, None, [uk, cwk], [ack], eng=eng)
                    else:
                        k.stt(ac[:, s0:s0 + sl], src, cw[:, j, tap:tap + 1], ac[:, s0:s0 + sl], ALU.mult, ALU.add, [uk, cwk, ack], [ack], eng=eng)
            k.act(acc2[:, 0:Tn], acc[:, 0:Tn], AF.Square, [acck], [acc2k])
            k.tt(st0[:, 0:Tn], st0[:, 0:Tn], acc[:, 0:Tn], ALU.add, [st0k, acck], [st0k], eng="gpsimd")
            k.tt(st1[:, 0:Tn], st1[:, 0:Tn], acc2[:, 0:Tn], ALU.add, [st1k, acc2k], [st1k], eng="gpsimd")
            k.store(self.U, j * 128, j * 128 + 128, 0, Tn, acc[:, 0:Tn], acck)
        for nb in range(0, Tn, 512):
            nw = min(512, Tn - nb)
            p1, p1k = self.PS.next(); p2, p2k = self.PS.next()
            k.mm(p1[:, 0:nw], p1k, self.ones, st0[:, nb:nb + nw], True, True, [self.ck, st0k])
            k.mm(p2[:, 0:nw], p2k, self.ones, st1[:, nb:nb + nw], True, True, [self.ck, st1k])
            m, mk = self.EP.next()
            k.ts(st0[:, nb:nb + nw], p1[:, 0:nw], 1.0 / E, None, ALU.mult, None, [p1k], [st0k])
            k.tt(m[:, 0:nw], st0[:, nb:nb + nw], st0[:, nb:nb + nw], ALU.mult, [st0k], [mk])
            k.stt(m[:, 0:nw], p2[:, 0:nw], 1.0 / E, m[:, 0:nw], ALU.mult, ALU.subtract, [p2k, mk], [mk])
            k.ts(m[:, 0:nw], m[:, 0:nw], EPS, None, ALU.add, None, [mk], [mk])
            k.act(m[:, 0:nw], m[:, 0:nw], AF.Sqrt, [mk], [mk])
            k.R.op("vector", lambda e, o=st1[:, nb:nb + nw], i=m[:, 0:nw]: e.reciprocal(out=o, in_=i), reads=[mk], writes=[st1k])
        for j in range(NE):
            u, uk = self.EP.next(); g, gk = self.EP.next()
            k.load(u[:, 0:Tn], uk, self.U, j * 128, j * 128 + 128, 0, Tn)
            k.load(g[:, 0:Tn], gk, self.PR, 2 * E + j * 128, 2 * E + j * 128 + 128, 0, Tn)
            k.tt(u[:, 0:Tn], u[:, 0:Tn], st0[:, 0:Tn], ALU.subtract, [uk, st0k], [uk])
            k.tt(u[:, 0:Tn], u[:, 0:Tn], st1[:, 0:Tn], ALU.mult, [uk, st1k], [uk])
            k.act(u[:, 0:Tn], u[:, 0:Tn], AF.Silu, [uk, cwk], [uk], bias=cw[:, j, 33:34], scale=cw[:, j, 32:33])
            sb_, sbk_ = self.EB.next()
            k.tt(sb_[:, 0:Tn], u[:, 0:Tn], g[:, 0:Tn], ALU.mult, [uk, gk], [sbk_], eng="gpsimd")
            k.store(self.S, j * 128, j * 128 + 128, 0, Tn, sb_[:, 0:Tn], sbk_)
        self.EP.unpin(st0k); self.EP.unpin(st1k)
        self.gemm(self.S, self.WOB, None, E, Tn, D, epi=self.residual_epi(XS))


TWO_PI = 2.0 * math.pi


def hyena_consts():
    c = {}
    def fmat(L):
        N = 2 * L
        t = np.arange(L, dtype=np.float64)[:, None]; kk = np.arange(L, dtype=np.float64)[None, :]
        ang = 2.0 * np.pi * ((t * kk) % N) / N
        A = np.cos(ang); B = -np.sin(ang); B[:, 0] = np.cos(np.pi * t[:, 0])
        return np.concatenate([A, B], 1).astype(np.float32)
    c["flat"] = fmat(LAT); c["fctx"] = fmat(CTX)
    negt = np.zeros((128, 18), np.float32)
    negt[:, 0:16] = -(np.linspace(0.0, 1.0, LAT, dtype=np.float32).reshape(16, 128).T)
    negt[:, 16:18] = -(np.linspace(0.0, 1.0, CTX, dtype=np.float32).reshape(2, 128).T)
    c["negt"] = negt
    c["delta"] = np.abs(np.linspace(math.log(1e-2) / 1.5, math.log(1e-2) / 0.3, E, dtype=np.float32))[None, :]
    def feats(L):
        t = np.linspace(0.0, 1.0, L, dtype=np.float32)[:, None]
        w = (2.0 * math.pi / L) * np.arange(L, dtype=np.float32)[:, None]
        f = np.linspace(1e-4, 15, 16, dtype=np.float32)[None, :]
        return np.concatenate([t, np.cos(f * w), -np.sin(f * w)], -1).astype(np.float32).T
    c["feats"] = np.concatenate([feats(LAT), feats(CTX)], 1)
    return {k_: np.ascontiguousarray(v, dtype=np.float32) for k_, v in c.items()}


class Hyena:
    def __init__(self, net, FLAT, FCTX, NEGT, DELTA, FEATS):
        self.net = net; k = net.k; nc = k.nc
        self.F = {LAT: FLAT, CTX: FCTX}; self.NEGT = NEGT; self.DELTA = DELTA; self.FEATS = FEATS
        self.FT = {LAT: DT(nc, "FTLAT", [2 * LAT, LAT], dtype=BF16), CTX: DT(nc, "FTCTX", [2 * CTX, CTX], dtype=BF16)}
        self.FB = {LAT: DT(nc, "FBLAT", [LAT, 2 * LAT], dtype=BF16), CTX: DT(nc, "FBCTX", [CTX, 2 * CTX], dtype=BF16)}
        self.H1T = DT(nc, "H1T", [64, LAT]); self.H2T = DT(nc, "H2T", [64, LAT])
        self.HF = DT(nc, "HF", [LAT, 4 * E]); self.HS = [DT(nc, f"HS{n}", [LAT, E], dtype=BF16) for n in range(2)]
        self.HD = [DT(nc, f"HD{n}", [LAT, E], dtype=BF16) for n in range(2)]
        self.SPEC = [DT(nc, f"SPEC{n}", [2 * LAT, E]) for n in range(2)]
        self.ZT = DT(nc, "ZT", [LAT, E], dtype=BF16); self.ZF = DT(nc, "ZF", [2 * LAT, E]); self.YF = DT(nc, "YF", [2 * LAT, E], dtype=BF16); self.Y = DT(nc, "Y", [E, LAT])
        self.Z1 = DT(nc, "Z1", [E, T])
        self.ft_done = False

    def setup(self):
        net = self.net; k = net.k
        if not self.ft_done:
            for L in (LAT, CTX):
                net.transpose(self.F[L], self.FT[L], L, 2 * L)
                net.cast_dram(self.F[L], self.FB[L], L, 2 * L)
            sp = k.pool("hyneg", [128, 18], 1); self.negt, self.negtk = sp.next()
            k.load(self.negt[:], self.negtk, self.NEGT, 0, 128, 0, 18)
            self.ft_done = True

    def layer(self, XS, W_IN, CONV_WB, HYF, F_W1, F_W2, F_W3, SKIP, W_OUT, need_ctx):
        net = self.net; k = net.k; EP = net.EP; PS = net.PS
        self.setup()
        Tn = T if need_ctx else LAT
        segs = [(0, LAT, 0)] + ([(LAT, CTX, 16)] if need_ctx else [])
        cvp = k.pool("hycv", [128, 96, 4], 1); cv, cvk = cvp.next()
        net.rows_to_pp(CONV_WB, 0, 4, 0, 3 * E, cv, cvk)
        skp = k.pool("hysk", [128, NE, 2], 1); sk, skk = skp.next()
        net.rows_to_pp(SKIP, 0, 2, 0, E, sk, skk)
        fpp_p = k.pool("hyf", [128, 4], 1); fpp, fppk = fpp_p.next()
        tb, tk = EP.next()
        k.load(tb[0:4, 0:64], tk, HYF, 0, 4, 0, 64)
        ps, pk = PS.next()
        k.mm(ps[0:64, 0:4], pk, tb[0:4, 0:64], net.ident[0:4, 0:4], True, True, [tk, net.ck])
        k.copy(fpp[0:64, :], ps[0:64, 0:4], [pk], [fppk])
        dl, dlk = EP.pin()
        k.dma(dl[:, 0:E], self.DELTA.ap[0:1, :].to_broadcast([128, E]), self.DELTA.keys(0, 1, 0, E), [dlk])
        net.cast_dram(W_IN, net.WB, D, 4 * E); net.cast_dram(W_OUT, net.WOB, E, D)
        net.gemm(net.WB, net.HT, net.PR, D, 4 * E, Tn, act=lambda m0: AF.Silu if m0 >= 3 * E else AF.Copy)
        for j in range(96):
            u, uk = EP.next(); o, ok = EP.next()
            for si, (s0, sl, _) in enumerate(segs):
                b = s0 + 3 * si
                k.memzero(u[:, b:b + 1], [uk]); k.memzero(u[:, b + sl + 1:b + sl + 2], [uk])
                k.load(u[:, b + 1:b + 1 + sl], uk, net.PR, j * 128, j * 128 + 128, s0, s0 + sl)
            for si, (s0, sl, _) in enumerate(segs):
                b = s0 + 3 * si
                k.ts(o[:, s0:s0 + sl], u[:, b:b + sl], cv[:, j, 0:1], cv[:, j, 3:4], ALU.mult, ALU.add, [uk, cvk], [ok])
                for tap in (1, 2):
                    k.stt(o[:, s0:s0 + sl], u[:, b + tap:b + tap + sl], cv[:, j, tap:tap + 1], o[:, s0:s0 + sl], ALU.mult, ALU.add, [uk, cvk, ok], [ok])
            k.store(net.PR, j * 128, j * 128 + 128, 0, Tn, o[:, 0:Tn], ok)
        for (s0, L, nt0) in segs:
            self.filters(L, s0, nt0, HYF, F_W1, F_W2, F_W3, fpp, fppk, dl, dlk)
            self.longconv(L, 0, net.PR, 0, s0)
            self.combine(L, s0, 0, sk, skk, zsrc=(net.PR, 0), xrow=E, dst=self.Z1)
            self.longconv(L, 1, self.Z1, 0, s0)
            self.combine(L, s0, 1, sk, skk, zsrc=(self.Z1, 0), xrow=2 * E, dst=net.S, gate_row=3 * E)
        EP.unpin(dlk)
        net.gemm(net.S, net.WOB, None, E, Tn, D, epi=net.residual_epi(XS))

    def sin_epi(self, OUT, fpp, fppk, cb, cfr):
        net = self.net; k = net.k; EP = net.EP
        def epi(ps, pk, m0, mt, n0, nw):
            a, ak = EP.next()
            k.ts(a[0:mt, 0:nw], ps, fpp[0:mt, cb:cb + 1], fpp[0:mt, cfr:cfr + 1], ALU.add, ALU.mult, [pk, fppk], [ak])
            m, mk = EP.next()
            for lvl in range(2):
                k.ts(m[0:mt, 0:nw], a[0:mt, 0:nw], math.pi, -TWO_PI, ALU.is_gt, ALU.mult, [ak], [mk])
                k.tt(a[0:mt, 0:nw], a[0:mt, 0:nw], m[0:mt, 0:nw], ALU.add, [ak, mk], [ak])
                k.ts(m[0:mt, 0:nw], a[0:mt, 0:nw], -math.pi, TWO_PI, ALU.is_lt, ALU.mult, [ak], [mk])
                k.tt(a[0:mt, 0:nw], a[0:mt, 0:nw], m[0:mt, 0:nw], ALU.add, [ak, mk], [ak])
            k.act(a[0:mt, 0:nw], a[0:mt, 0:nw], AF.Sin, [ak], [ak])
            k.store(OUT, m0, m0 + mt, n0, n0 + nw, a[0:mt, 0:nw], ak)
        return epi

    def filters(self, L, s0, nt0, HYF, F_W1, F_W2, F_W3, fpp, fppk, dl, dlk):
        net = self.net; k = net.k; EP = net.EP
        N = 2 * L; F = self.FB[L]
        net.gemm(F_W1, self.FEATS, None, 33, 64, L, r0=(0, s0), epi=self.sin_epi(self.H1T, fpp, fppk, 0, 1))
        net.gemm(F_W2, self.H1T, None, 64, 64, L, epi=self.sin_epi(self.H2T, fpp, fppk, 2, 3))
        def wepi(ps, pk, m0, mt, n0, nw):
            e0 = n0 % E
            w, wk = EP.next(); ob, ok = EP.next()
            k.act(w[0:mt, 0:nw], dl[0:mt, e0:e0 + nw], AF.Exp, [dlk, self.negtk], [wk], scale=self.negt[0:mt, nt0 + m0 // 128:nt0 + m0 // 128 + 1])
            k.tt(ob[0:mt, 0:nw], ps, w[0:mt, 0:nw], ALU.mult, [pk, wk], [ok])
            k.store(self.HF, m0, m0 + mt, n0, n0 + nw, ob[0:mt, 0:nw], ok)
        net.gemm(self.H2T, F_W3, None, 64, L, 4 * E, epi=wepi)
        for n in range(2):
            for t in range(L // 128):
                for eb in range(0, E, 2048):
                    f, fk = EP.next(); b, bk = EP.next(); hs, hsk = net.EB.next(); hd, hdk = net.EB.next()
                    k.load(f[:, 0:2048], fk, self.HF, t * 128, t * 128 + 128, (2 * n) * E + eb, (2 * n) * E + eb + 2048)
                    k.load(b[:, 0:2048], bk, self.HF, t * 128, t * 128 + 128, (2 * n + 1) * E + eb, (2 * n + 1) * E + eb + 2048)
                    if t == 0:
                        k.memset(b[0:1, 0:2048], 0.0, [bk])
                    k.tt(hs[:, 0:2048], f[:, 0:2048], b[:, 0:2048], ALU.add, [fk, bk], [hsk])
                    k.tt(hd[:, 0:2048], f[:, 0:2048], b[:, 0:2048], ALU.subtract, [fk, bk], [hdk], eng="gpsimd")
                    k.store(self.HS[n], t * 128, t * 128 + 128, eb, eb + 2048, hs[:, 0:2048], hsk)
                    k.store(self.HD[n], t * 128, t * 128 + 128, eb, eb + 2048, hd[:, 0:2048], hdk)
            def sepi(row_off, fix0, scale, row0_only=False):
                def epi(ps, pk, m0, mt, n0, nw):
                    if row0_only:
                        mt = 1
                        ps = ps[0:1, :]
                    ob, ok = EP.next()
                    k.act(ob[0:mt, 0:nw], ps, AF.Copy, [pk], [ok], scale=scale)
                    if fix0 and m0 == 0:
                        k.ts(ob[0:1, 0:nw], ob[0:1, 0:nw], 0.5, None, ALU.mult, None, [ok], [ok])
                    k.store(self.SPEC[n], row_off + m0, row_off + m0 + mt, n0, n0 + nw, ob[0:mt, 0:nw], ok)
                return epi
            net.gemm(F, self.HS[n], None, L, L, E, l0=(0, 0), epi=sepi(0, True, 2.0 / N))
            net.gemm(F, self.HD[n], None, L, L, E, l0=(0, L), epi=sepi(L, False, 2.0 / N))
            net.gemm(F, self.HS[n], None, L, 128, E, l0=(0, L), epi=sepi(L, False, 1.0 / N, True))

    def longconv(self, L, n, SRC, row0, s0):
        net = self.net; k = net.k; EP = net.EP
        F = self.FB[L]; FT = self.FT[L]; SP = self.SPEC[n]
        net.transpose(SRC, self.ZT, E, L, s0=(row0, s0))
        net.gemm(F, self.ZT, self.ZF, L, 2 * L, E)
        for kt in range(L // 128):
            for eb in range(0, E, 2048):
                a, ak = EP.next(); b, bk = EP.next(); sa, sak = EP.next(); sb, sbk = EP.next()
                ya, yak = EP.next(); yb, ybk = EP.next(); t1, t1k = EP.next(); oa, oak = net.EB.next(); ob2, ob2k = net.EB.next()
                W = 2048
                r = kt * 128
                k.load(a[:, 0:W], ak, self.ZF, r, r + 128, eb, eb + W); k.load(b[:, 0:W], bk, self.ZF, L + r, L + r + 128, eb, eb + W)
                k.load(sa[:, 0:W], sak, SP, r, r + 128, eb, eb + W); k.load(sb[:, 0:W], sbk, SP, L + r, L + r + 128, eb, eb + W)
                k.tt(ya[:, 0:W], a[:, 0:W], sa[:, 0:W], ALU.mult, [ak, sak], [yak])
                k.tt(t1[:, 0:W], b[:, 0:W], sb[:, 0:W], ALU.mult, [bk, sbk], [t1k], eng="gpsimd")
                k.tt(yb[:, 0:W], a[:, 0:W], sb[:, 0:W], ALU.mult, [ak, sbk], [ybk])
                k.tt(t1[:, 0:W], ya[:, 0:W], t1[:, 0:W], ALU.subtract, [yak, t1k], [t1k])
                k.tt(b[:, 0:W], b[:, 0:W], sa[:, 0:W], ALU.mult, [bk, sak], [bk], eng="gpsimd")
                k.tt(yb[:, 0:W], yb[:, 0:W], b[:, 0:W], ALU.add, [ybk, bk], [ybk])
                if kt == 0:
                    k.copy(t1[0:1, 0:W], ya[0:1, 0:W], [yak], [t1k])
                    k.load(b[0:1, 0:W], bk, self.ZF, L, L + 1, eb, eb + W)
                    k.tt(yb[0:1, 0:W], b[0:1, 0:W], sb[0:1, 0:W], ALU.mult, [bk, sbk], [ybk])
                k.copy(oa[:, 0:W], t1[:, 0:W], [t1k], [oak], eng="gpsimd")
                k.act(ob2[:, 0:W], yb[:, 0:W], AF.Copy, [ybk], [ob2k])
                k.store(self.YF, r, r + 128, eb, eb + W, oa[:, 0:W], oak)
                k.store(self.YF, L + r, L + r + 128, eb, eb + W, ob2[:, 0:W], ob2k)
        net.gemm(self.YF, FT, self.Y, 2 * L, E, L)

    def combine(self, L, s0, n, sk, skk, zsrc, xrow, dst, gate_row=None):
        net = self.net; k = net.k; EP = net.EP
        ZS, zrow = zsrc
        for j in range(NE):
            y, yk = EP.next(); z, zk = EP.next(); x, xk = EP.next()
            k.load(y[:, 0:L], yk, self.Y, j * 128, j * 128 + 128, 0, L)
            k.load(z[:, 0:L], zk, ZS, zrow + j * 128, zrow + j * 128 + 128, s0, s0 + L)
            k.load(x[:, 0:L], xk, net.PR, xrow + j * 128, xrow + j * 128 + 128, s0, s0 + L)
            k.stt(y[:, 0:L], z[:, 0:L], sk[:, j, n:n + 1], y[:, 0:L], ALU.mult, ALU.add, [zk, skk, yk], [yk])
            k.tt(y[:, 0:L], y[:, 0:L], x[:, 0:L], ALU.mult, [yk, xk], [yk])
            if gate_row is not None:
                k.load(z[:, 0:L], zk, net.PR, gate_row + j * 128, gate_row + j * 128 + 128, s0, s0 + L)
                ob, obk = net.EB.next()
                k.tt(ob[:, 0:L], y[:, 0:L], z[:, 0:L], ALU.mult, [yk, zk], [obk], eng="gpsimd")
                k.store(dst, j * 128, j * 128 + 128, s0, s0 + L, ob[:, 0:L], obk)
                continue
            k.store(dst, j * 128, j * 128 + 128, s0, s0 + L, y[:, 0:L], yk)


RT_H = 8; RT_DK = 256; RT_DV = 512; CH = 128


def ret_consts():
    c = {}
    half = 64
    inv = (10000.0 ** (-np.arange(half, dtype=np.float32) / half)).astype(np.float32)
    rows = np.repeat(np.arange(LAT // 64), 64).astype(np.float32); cols = np.tile(np.arange(64), LAT // 64).astype(np.float32)
    ar = rows[:, None] * inv; ac = cols[:, None] * inv
    c["rcos"] = np.concatenate([np.cos(ar), np.cos(ac)], 1)
    c["rsin"] = np.concatenate([np.sin(ar), np.sin(ac)], 1)
    p = np.arange(128, dtype=np.float32)
    j = p[:, None]; i = p[None, :]
    c["rmask"] = np.concatenate([np.maximum(i - j, 0), (i >= j).astype(np.float32), np.maximum(j - i, 0), (j >= i).astype(np.float32)], 1)
    c["rcols"] = np.stack([p + 1, CH - 1 - p, CH - p, p], 1)
    return {k_: np.ascontiguousarray(v, dtype=np.float32) for k_, v in c.items()}


class Retention:
    def __init__(self, net, RCOS, RSIN, RMASK, RCOLS):
        self.net = net; nc = net.k.nc
        self.RCOS = RCOS; self.RSIN = RSIN; self.RMASK = RMASK; self.RCOLS = RCOLS
        self.QKV = DT(nc, "QKV", [T, 2 * D + E]); self.QKT = DT(nc, "QKT", [2 * D, T])
        self.OACC = DT(nc, "OACC", [T, E]); self.OT = DT(nc, "OTR", [E, T])

    def layer(self, XS, W_IN, DECAY, GN_GB, W_OUT):
        net = self.net; k = net.k; EP = net.EP; PS = net.PS
        def qkv_epi(ps, pk, m0, mt, n0, nw):
            ob, ok = EP.next()
            k.act(ob[0:mt, 0:nw], ps, AF.Copy, [pk], [ok], scale=(RT_DK ** -0.5 if D <= n0 < 2 * D else 1.0))
            k.store(self.QKV, m0, m0 + mt, n0, n0 + nw, ob[0:mt, 0:nw], ok)
        net.cast_dram(W_IN, net.WB, D, 2 * D + 2 * E); net.cast_dram(W_OUT, net.WOB, E, D)
        net.gemm(net.HT, net.WB, None, D, T, 2 * D + E, epi=qkv_epi)
        net.gemm(net.WB, net.HT, net.PR, D, E, T, l0=(0, 2 * D + E), act=AF.Silu)
        for t in range(LAT // 128):
            x, xk = EP.next(); o, ok = EP.next(); cs, csk = EP.next(); t1, t1k = EP.next(); t2, t2k = EP.next()
            k.load(x[:, 0:2 * D], xk, self.QKV, t * 128, t * 128 + 128, 0, 2 * D)
            k.load(cs[:, 0:128], csk, self.RCOS, t * 128, t * 128 + 128, 0, 128)
            k.load(cs[:, 128:256], csk, self.RSIN, t * 128, t * 128 + 128, 0, 128)
            xv = x[:, 0:2 * D].rearrange("p (h a s f) -> p h a s f", h=16, a=2, s=2)
            ov = o[:, 0:2 * D].rearrange("p (h a s f) -> p h a s f", h=16, a=2, s=2)
            cv = cs[:, 0:128].rearrange("p (a f) -> p a f", a=2); sv = cs[:, 128:256].rearrange("p (a f) -> p a f", a=2)
            t1v = t1[:, 0:128].rearrange("p (a f) -> p a f", a=2); t2v = t2[:, 0:128].rearrange("p (a f) -> p a f", a=2)
            for h in range(16):
                e1, e2 = ("vector", "gpsimd")
                k.tt(t1v, xv[:, h, :, 0, :], cv, ALU.mult, [xk, csk], [t1k], eng=e1)
                k.tt(t2v, xv[:, h, :, 1, :], sv, ALU.mult, [xk, csk], [t2k], eng=e2)
                k.tt(ov[:, h, :, 0, :], t1v, t2v, ALU.subtract, [t1k, t2k], [ok], eng=e1)
                k.tt(t1v, xv[:, h, :, 0, :], sv, ALU.mult, [xk, csk], [t1k], eng=e1)
                k.tt(t2v, xv[:, h, :, 1, :], cv, ALU.mult, [xk, csk], [t2k], eng=e2)
                k.tt(ov[:, h, :, 1, :], t1v, t2v, ALU.add, [t1k, t2k], [ok], eng=e1)
            k.store(self.QKV, t * 128, t * 128 + 128, 0, 2 * D, o[:, 0:2 * D], ok)
        net.transpose(self.QKV, self.QKT, T, 2 * D)
        cp = k.pool("rtc", [128, 1024], 1); cb, cbk = cp.next()
        lg = cb[:, 0:16]; cdec = cb[:, 16:32]; cols = cb[:, 32:36]; dec = cb[:, 64:128]
        rmask = cb[:, 128:640]; mk = cb[:, 640:896]
        k.dma(lg, DECAY.ap[0:1, 0:16].to_broadcast([128, 16]), DECAY.keys(0, 1, 0, 16), [cbk])
        k.load(cols, cbk, self.RCOLS, 0, 128, 0, 4)
        k.load(rmask, cbk, self.RMASK, 0, 128, 0, 512)
        k.act(lg, lg, AF.Exp, [cbk], [cbk], scale=-1.0)
        k.act(lg, lg, AF.Ln, [cbk], [cbk], bias=1.0)
        k.ts(lg, lg, -1.0, None, ALU.mult, None, [cbk], [cbk])
        k.act(cdec, lg, AF.Exp, [cbk], [cbk], scale=float(CH))
        decv = dec.rearrange("p (c s) -> p c s", c=4)
        for c4 in range(4):
            k.act(decv[:, c4, :], lg, AF.Exp, [cbk], [cbk], scale=cols[:, c4:c4 + 1])
        stp = k.pool("rtst", [128, 2, 512], 2)
        chunks_f = [(LAT + c * CH) for c in range(CTX // CH)] + [c * CH for c in range(LAT // CH)]
        chunks_b = [(LAT + c * CH) for c in reversed(range(CTX // CH))] + [c * CH for c in reversed(range(LAT // CH))]
        for h in range(RT_H):
            qT, qTk = EP.pin(); kT, kTk = EP.pin(); cT, cTk = EP.pin()
            for dc in range(2):
                r = h * RT_DK + dc * 128
                k.load(qT[:, dc * LAT:(dc + 1) * LAT], qTk, self.QKT, r, r + 128, 0, LAT)
                k.load(kT[:, dc * LAT:(dc + 1) * LAT], kTk, self.QKT, D + r, D + r + 128, 0, LAT)
                k.load(cT[:, dc * CTX:(dc + 1) * CTX], cTk, self.QKT, r, r + 128, LAT, T)
                k.load(cT[:, (2 + dc) * CTX:(3 + dc) * CTX], cTk, self.QKT, D + r, D + r + 128, LAT, T)
            def qk_slices(t0):
                if t0 < LAT:
                    return ([qT[:, dc * LAT + t0:dc * LAT + t0 + CH] for dc in range(2)],
                            [kT[:, dc * LAT + t0:dc * LAT + t0 + CH] for dc in range(2)], [qTk, kTk])
                c0 = t0 - LAT
                return ([cT[:, dc * CTX + c0:dc * CTX + c0 + CH] for dc in range(2)],
                        [cT[:, (2 + dc) * CTX + c0:(2 + dc) * CTX + c0 + CH] for dc in range(2)], [cTk])
            for di, chunks in enumerate((chunks_f, chunks_b)):
                col = di * 8 + h
                mt_ = mk[:, di * 128:(di + 1) * 128]
                k.act(mt_, rmask[:, di * 256:di * 256 + 128], AF.Exp, [cbk], [cbk], scale=lg[:, col:col + 1])
                k.tt(mt_, mt_, rmask[:, di * 256 + 128:di * 256 + 256], ALU.mult, [cbk], [cbk])
                st, stk = stp.next()
                k.memset(st[:], 0.0, [stk])
                for t0 in chunks:
                    qs, ks, qkk = qk_slices(t0)
                    kv, kvk = EP.next()
                    k.load(kv[:, 0:256], kvk, self.QKV, t0, t0 + CH, D + h * RT_DK, D + (h + 1) * RT_DK)
                    k.load(kv[:, 256:768], kvk, self.QKV, t0, t0 + CH, 2 * D + h * RT_DV, 2 * D + (h + 1) * RT_DV)
                    ps_s, pssk = PS.next()
                    for dc in range(2):
                        k.mm(ps_s[:, 0:CH], pssk, ks[dc], qs[dc], dc == 0, dc == 1, qkk)
                    sc, sck = EP.next()
                    k.tt(sc[:, 0:CH], ps_s[:, 0:CH], mt_, ALU.mult, [pssk, cbk], [sck])
                    p1, p1k = PS.next(); p2, p2k = PS.next()
                    k.mm(p1[:, 0:RT_DV], p1k, sc[:, 0:CH], kv[:, 256:768], True, True, [sck, kvk])
                    for dc in range(2):
                        k.mm(p2[:, 0:RT_DV], p2k, qs[dc], st[:, dc, :], dc == 0, dc == 1, qkk + [stk])
                    o, ok = EP.next()
                    k.act(o[:, 0:RT_DV], p1[:, 0:RT_DV], AF.Copy, [p1k], [ok])
                    k.stt(o[:, 0:RT_DV], p2[:, 0:RT_DV], decv[:, 2 * di, col:col + 1], o[:, 0:RT_DV], ALU.mult, ALU.add, [p2k, cbk, ok], [ok])
                    if di == 1:
                        pv, pvk = EP.next()
                        k.load(pv[:, 0:RT_DV], pvk, self.OACC, t0, t0 + CH, h * RT_DV, (h + 1) * RT_DV)
                        k.tt(o[:, 0:RT_DV], o[:, 0:RT_DV], pv[:, 0:RT_DV], ALU.add, [ok, pvk], [ok], eng="gpsimd")
                    k.store(self.OACC, t0, t0 + CH, h * RT_DV, (h + 1) * RT_DV, o[:, 0:RT_DV], ok)
                    k.ts(kv[:, 768:1024], kv[:, 0:256], decv[:, 2 * di + 1, col:col + 1], None, ALU.mult, None, [kvk, cbk], [kvk])
                    for dc in range(2):
                        p3, p3k = PS.next()
                        k.mm(p3[:, 0:RT_DV], p3k, kv[:, 768 + dc * 128:768 + (dc + 1) * 128], kv[:, 256:768], True, True, [kvk])
                        k.stt(st[:, dc, :], st[:, dc, :], cdec[:, col:col + 1], p3[:, 0:RT_DV], ALU.mult, ALU.add, [stk, cbk, p3k], [stk])
            EP.unpin(qTk); EP.unpin(kTk); EP.unpin(cTk)
        gg, ggk = EP.pin(); gb, gbk = EP.pin()
        k.dma(gg[:, 0:E], GN_GB.ap[0:1, :].to_broadcast([128, E]), GN_GB.keys(0, 1, 0, E), [ggk])
        k.dma(gb[:, 0:E], GN_GB.ap[1:2, :].to_broadcast([128, E]), GN_GB.keys(1, 2, 0, E), [gbk])
        smp = k.pool("rtsm", [128, 64], 2)
        for t in range(T // 128):
            o, ok = EP.next(); sq, sqk = EP.next(); sm, smk = smp.next()
            k.load(o[:, 0:E], ok, self.OACC, t * 128, t * 128 + 128, 0, E)
            k.act(sq[:, 0:E], o[:, 0:E], AF.Square, [ok], [sqk])
            ov = o[:, 0:E].rearrange("p (h v) -> p h v", h=RT_H); sqv = sq[:, 0:E].rearrange("p (h v) -> p h v", h=RT_H)
            k.R.op("vector", lambda e, sm=sm, ov=ov: e.reduce_sum(out=sm[:, 0:8], in_=ov, axis=AX.X), reads=[ok], writes=[smk])
            k.R.op("vector", lambda e, sm=sm, sqv=sqv: e.reduce_sum(out=sm[:, 8:16], in_=sqv, axis=AX.X), reads=[sqk], writes=[smk])
            k.ts(sm[:, 0:16], sm[:, 0:16], 1.0 / RT_DV, None, ALU.mult, None, [smk], [smk])
            k.tt(sm[:, 16:24], sm[:, 0:8], sm[:, 0:8], ALU.mult, [smk], [smk])
            k.tt(sm[:, 24:32], sm[:, 8:16], sm[:, 16:24], ALU.subtract, [smk], [smk])
            k.ts(sm[:, 24:32], sm[:, 24:32], EPS, None, ALU.add, None, [smk], [smk])
            k.act(sm[:, 24:32], sm[:, 24:32], AF.Sqrt, [smk], [smk])
            k.R.op("vector", lambda e, sm=sm: e.reciprocal(out=sm[:, 32:40], in_=sm[:, 24:32]), reads=[smk], writes=[smk])
            for h in range(RT_H):
                k.ts(ov[:, h, :], ov[:, h, :], sm[:, h:h + 1], sm[:, 32 + h:33 + h], ALU.subtract, ALU.mult, [ok, smk], [ok])
            k.tt(o[:, 0:E], o[:, 0:E], gg[:, 0:E], ALU.mult, [ok, ggk], [ok])
            k.tt(o[:, 0:E], o[:, 0:E], gb[:, 0:E], ALU.add, [ok, gbk], [ok], eng="gpsimd")
            k.store(self.OACC, t * 128, t * 128 + 128, 0, E, o[:, 0:E], ok)
        EP.unpin(ggk); EP.unpin(gbk)
        net.transpose(self.OACC, self.OT, T, E)
        for j in range(NE):
            a, ak = EP.next(); g, gk = EP.next()
            k.load(a[:, 0:T], ak, self.OT, j * 128, j * 128 + 128, 0, T)
            k.load(g[:, 0:T], gk, net.PR, j * 128, j * 128 + 128, 0, T)
            ob, obk = net.EB.next()
            k.tt(ob[:, 0:T], a[:, 0:T], g[:, 0:T], ALU.mult, [ak, gk], [obk])
            k.store(net.S, j * 128, j * 128 + 128, 0, T, ob[:, 0:T], obk)
        net.gemm(net.S, net.WOB, None, E, T, D, epi=net.residual_epi(XS))


DEPTH = 4; NCORES = 8


def build_program():
    nc = bass.Bass("TRN2", target_bir_lowering=False)
    st = contextlib.ExitStack()
    k = K(nc, st); net = Net(k)
    ext = lambda n, shp: DT(nc, n, shp, kind="ExternalInput")
    XIN = ext("xin", [T, D]); CC = ext("cc", [2, D]); CONSTS = ext("consts", [128, 128])
    ADA_W = ext("ada_w", [4 * D, 3 * D]); ADA_B = ext("ada_b", [4, 3 * D]); NORM_G = ext("norm_g", [4, D])
    FNG = ext("final_norm_g", [1, D])
    FLAT = ext("flat", [LAT, 2 * LAT]); FCTX = ext("fctx", [CTX, 2 * CTX]); NEGT = ext("negt", [128, 18])
    DELTA = ext("delta", [1, E]); FEATS = ext("feats", [33, T])
    HY = []
    for j in range(2):
        HY.append(dict(W_IN=ext(f"hy_w_in{j}", [D, 4 * E]), CONV_WB=ext(f"hy_conv_wb{j}", [4, 3 * E]), HYF=ext(f"hy_f{j}", [4, 64]),
                       F_W1=ext(f"hy_f_w1{j}", [33, 64]), F_W2=ext(f"hy_f_w2{j}", [64, 64]), F_W3=ext(f"hy_f_w3{j}", [64, 4 * E]),
                       SKIP=ext(f"hy_skip{j}", [2, E]), W_OUT=ext(f"hy_w_out{j}", [E, D])))
    CF = dict(W_IN=ext("cf_w_in", [D, 3 * E]), DW_W=ext("cf_dw_w", [31, E]), DW_B=ext("cf_dw_b", [1, E]),
              LN_G=ext("cf_ln_g", [1, E]), LN_B=ext("cf_ln_b", [1, E]), W_OUT=ext("cf_w_out", [E, D]))
    RCOS = ext("rcos", [LAT, 128]); RSIN = ext("rsin", [LAT, 128]); RMASK = ext("rmask", [128, 512]); RCOLS = ext("rcols", [128, 4])
    RT = dict(W_IN=ext("rt_w_in", [D, 2 * D + 2 * E]), DECAY=ext("rt_decay", [1, 16]), GN_GB=ext("rt_gn_gb", [2, E]),
              W_OUT=ext("rt_w_out", [E, D]))
    XS = DT(nc, "XS", [T, D]); OUT = DT(nc, "out", [LAT, D], kind="ExternalOutput")
    hy = Hyena(net, FLAT, FCTX, NEGT, DELTA, FEATS); rt = Retention(net, RCOS, RSIN, RMASK, RCOLS)
    net.init_consts(CONSTS)
    net.copy_dram(XIN, XS, T, D)
    net.build_mrep(CC)
    for i in range(DEPTH):
        kind, j = i % 3, i // 3
        last = i == DEPTH - 1
        need_ctx = (not last) or kind == 2
        net.adaln(ADA_W, ADA_B, i)
        net.prenorm(XS, NORM_G, i, (T if need_ctx else LAT) // 128)
        if kind == 0:
            hy.layer(XS, need_ctx=need_ctx, **HY[j])
        elif kind == 1:
            net.conformer(XS, CF["W_IN"], CF["DW_W"], CF["DW_B"], CF["LN_G"], CF["LN_B"], CF["W_OUT"], need_ctx=need_ctx)
        else:
            rt.layer(XS, RT["W_IN"], RT["DECAY"], RT["GN_GB"], RT["W_OUT"])
    net.final_norm(XS, FNG, OUT)
    k.R.final_wait("sync", [k.R.last_w[key] for key in OUT.keys(0, LAT, 0, D)])
    k.R.emit(nc, None)
    st.close()
    return nc


def kernel(x, c, ctx, c_ctx, ada_w, ada_b, norm_g, final_norm_g,
           hy_w_in, hy_conv_w, hy_conv_b, hy_f_w1, hy_f_b1, hy_f_fr1, hy_f_w2, hy_f_b2,
           hy_f_fr2, hy_f_w3, hy_skip, hy_w_out,
           cf_w_in, cf_dw_w, cf_dw_b, cf_ln_g, cf_ln_b, cf_w_out,
           rt_w_in, rt_decay_logit, rt_gn_g, rt_gn_b, rt_w_out):
    f = lambda a: np.ascontiguousarray(np.asarray(a), dtype=np.float32)
    x, c, ctx, c_ctx = f(x), f(c), f(ctx), f(c_ctx)
    shared = {"consts": np.eye(128, dtype=np.float32), "ada_w": f(ada_w).reshape(4 * D, 3 * D), "ada_b": f(ada_b),
              "norm_g": f(norm_g), "final_norm_g": f(final_norm_g).reshape(1, D),
              "cf_w_in": f(cf_w_in)[0], "cf_dw_w": f(cf_dw_w)[0], "cf_dw_b": f(cf_dw_b).reshape(1, E),
              "cf_ln_g": f(cf_ln_g).reshape(1, E), "cf_ln_b": f(cf_ln_b).reshape(1, E), "cf_w_out": f(cf_w_out)[0],
              "rt_w_in": f(rt_w_in)[0], "rt_decay": f(rt_decay_logit)[0].reshape(1, 16),
              "rt_gn_gb": np.stack([f(rt_gn_g)[0], f(rt_gn_b)[0]]), "rt_w_out": f(rt_w_out)[0]}
    for j in range(2):
        shared.update({f"hy_w_in{j}": f(hy_w_in)[j], f"hy_conv_wb{j}": np.concatenate([f(hy_conv_w)[j], f(hy_conv_b)[j][None]], 0),
                       f"hy_f{j}": np.stack([f(hy_f_b1)[j], f(hy_f_fr1)[j], f(hy_f_b2)[j], f(hy_f_fr2)[j]]),
                       f"hy_f_w1{j}": f(hy_f_w1)[j], f"hy_f_w2{j}": f(hy_f_w2)[j], f"hy_f_w3{j}": f(hy_f_w3)[j],
                       f"hy_skip{j}": f(hy_skip)[j], f"hy_w_out{j}": f(hy_w_out)[j]})
    shared.update(hyena_consts()); shared.update(ret_consts())
    shared = {k_: np.ascontiguousarray(v, dtype=np.float32) for k_, v in shared.items()}
    in_maps = []
    for b in range(NCORES):
        m = dict(shared)
        m["xin"] = np.ascontiguousarray(np.concatenate([x[b], ctx[b]], 0))
        m["cc"] = np.ascontiguousarray(np.stack([c[b], c_ctx]))
        in_maps.append(m)
    nc = build_program()
    res = run_bass_kernel_spmd(nc, in_maps, core_ids=list(range(NCORES)))
    return np.stack([np.asarray(res.results[b]["out"], dtype=np.float32) for b in range(NCORES)], 0)
```

```python
import contextlib, math
import numpy as np
import concourse.bass as bass
import concourse.mybir as mybir
from concourse.bass_utils import run_bass_kernel_spmd

COMPUTE = ("tensor", "vector", "scalar", "gpsimd")
DMAQ = ("sync", "gpsimd")
NDS = 6


class Op:
    __slots__ = ("eng", "fn", "waits", "idx", "is_dma", "tick", "dsem", "dval", "name")


class Rec:
    def __init__(self):
        self.ops = {e: [] for e in ("tensor", "vector", "scalar", "gpsimd", "sync")}
        self.last_w = {}
        self.readers = {}
        self.ndma = {q: 0 for q in DMAQ}
        self.dma_tok = {q: [] for q in DMAQ}

    def op(self, eng, fn, reads=(), writes=(), dma=False, pe_accum=False, name=None):
        o = Op(); o.eng = eng; o.fn = fn; o.waits = []; o.is_dma = dma; o.tick = False
        o.name = name; o.dsem = None; o.dval = None
        deps = []
        for r in reads:
            t = self.last_w.get(r)
            if t is not None: deps.append(t)
        for w in writes:
            t = self.last_w.get(w)
            if t is not None and not (pe_accum and t[0] == "c" and t[1] == "tensor"):
                deps.append(t)
            deps.extend(self.readers.get(w, ()))
        if dma:
            q = eng; i = self.ndma[q]; self.ndma[q] += 1
            if i >= NDS: deps.append(self.dma_tok[q][i - NDS])
            tok = ("d", q, i % NDS, 16 * (i // NDS + 1))
            self.dma_tok[q].append(tok); o.dsem = (q, i % NDS)
        else:
            tok = ("c", eng, o)
        seen = set()
        for d in deps:
            if d is tok or id(d) in seen: continue
            seen.add(id(d))
            if d[0] == "c": d[2].tick = True
            o.waits.append(d)
        self.ops[eng].append(o)
        for w in writes:
            self.last_w[w] = tok; self.readers[w] = []
        for r in reads:
            if r not in writes:
                lst = self.readers.setdefault(r, [])
                if tok[0] == "c":
                    lst[:] = [t for t in lst if not (t[0] == "c" and t[1] == eng)]
                lst.append(tok)
        return tok

    def final_wait(self, eng, toks):
        o = Op(); o.eng = eng; o.fn = None; o.waits = list(toks); o.is_dma = False
        o.tick = False; o.name = "final"; o.dsem = None; o.dval = None
        for d in toks:
            if d[0] == "c": d[2].tick = True
        self.ops[eng].append(o)

    def emit(self, nc, block_engines):
        import contextlib
        for e in COMPUTE:
            n = 0
            for o in self.ops[e]:
                if o.tick and not o.is_dma:
                    n += 1; o.idx = n
        with contextlib.ExitStack() as st:
            csem = {e: st.enter_context(nc.semaphore("c_" + e)) for e in COMPUTE}
            dsem = {(q, k): st.enter_context(nc.semaphore(f"d_{q}{k}")) for q in DMAQ for k in range(NDS)}
            blk = st.enter_context(nc.Block())
            for e, lst in self.ops.items():
                if not lst: continue
                def body(eng, lst=lst, e=e):
                    known = {}
                    for o in lst:
                        need = {}
                        for d in o.waits:
                            if d[0] == "c": s, v = csem[d[1]], d[2].idx
                            else: s, v = dsem[(d[1], d[2])], d[3]
                            k = id(s)
                            if known.get(k, 0) >= v: continue
                            if k not in need or need[k][1] < v: need[k] = (s, v)
                        for k, (s, v) in need.items():
                            eng.wait_ge(s, v); known[k] = v
                        if o.fn is None: continue
                        ins = o.fn(eng)
                        if o.is_dma: ins.then_inc(dsem[o.dsem], 16)
                        elif o.tick: ins.then_inc(csem[e], 1)
                getattr(blk, e)(body)


F32 = mybir.dt.float32
BF16 = mybir.dt.bfloat16
AF = mybir.ActivationFunctionType
ALU = mybir.AluOpType
AX = mybir.AxisListType


class DT:
    def __init__(self, nc, name, shape, kind="Internal", rb=128, cb=512, dtype=F32):
        self.name = name; self.shape = tuple(shape); self.dtype = dtype
        self.ap = nc.dram_tensor(name, list(shape), dtype, kind=kind).ap()
        self.rb = rb; self.cb = cb

    def keys(self, r0, r1, c0, c1):
        return [(self.name, i, j) for i in range(r0 // self.rb, (r1 - 1) // self.rb + 1)
                for j in range(c0 // self.cb, (c1 - 1) // self.cb + 1)]


class Pool:
    def __init__(self, nc, st, name, shape, n, psum=False, dtype=F32):
        mk = nc.psum_tensor if psum else nc.sbuf_tensor
        self.bufs = [st.enter_context(mk(f"{name}{i}", list(shape), dtype)) for i in range(n)]
        self.keys = [(name, i) for i in range(n)]
        self.i = 0; self.pinned = set()

    def next(self):
        while True:
            k = self.i % len(self.bufs); self.i += 1
            if k not in self.pinned:
                return self.bufs[k], self.keys[k]

    def pin(self):
        b, key = self.next()
        self.pinned.add(self.keys.index(key))
        return b, key

    def unpin(self, key):
        self.pinned.discard(self.keys.index(key))


class K:
    def __init__(self, nc, st):
        self.nc = nc; self.st = st; self.R = Rec(); self.pools = {}
        self.qi = 0

    def pool(self, name, shape, n, psum=False, dtype=F32):
        if name not in self.pools:
            self.pools[name] = Pool(self.nc, self.st, name, shape, n, psum, dtype)
        return self.pools[name]

    def dma(self, out, in_, reads, writes, q=None):
        if q is None:
            q = "sync"
        self.R.op(q, lambda e: e.dma_start(out=out, in_=in_), reads=reads, writes=writes, dma=True)

    def load(self, sb, sbkey, dt, r0, r1, c0, c1, pat=None, q="sync", **kw):
        src = dt.ap[r0:r1, c0:c1]
        if pat: src = src.rearrange(pat, **kw)
        self.dma(sb, src, dt.keys(r0, r1, c0, c1), [sbkey], q=q)

    def store(self, dt, r0, r1, c0, c1, sb, sbkey, pat=None, q="gpsimd", **kw):
        dst = dt.ap[r0:r1, c0:c1]
        if pat: dst = dst.rearrange(pat, **kw)
        self.dma(dst, sb, [sbkey], dt.keys(r0, r1, c0, c1), q=q)

    def mm(self, ps, pskey, lhsT, rhs, start, stop, reads):
        self.R.op("tensor", lambda e: e.matmul(ps, lhsT=lhsT, rhs=rhs, start=start, stop=stop),
                  reads=reads, writes=[pskey], pe_accum=not start)

    def act(self, out, in_, func, reads, writes, eng="scalar", **kw):
        self.R.op(eng, lambda e: e.activation(out=out, in_=in_, func=func, **kw), reads=reads, writes=writes)

    def tt(self, out, a, b, op, reads, writes, eng="vector"):
        self.R.op(eng, lambda e: e.tensor_tensor(out=out, in0=a, in1=b, op=op), reads=reads, writes=writes)

    def ts(self, out, a, s1, s2, op0, op1, reads, writes, eng="vector", **kw):
        if s2 is None:
            self.R.op(eng, lambda e: e.tensor_scalar(out=out, in0=a, scalar1=s1, scalar2=None, op0=op0, **kw), reads=reads, writes=writes)
        else:
            self.R.op(eng, lambda e: e.tensor_scalar(out=out, in0=a, scalar1=s1, scalar2=s2, op0=op0, op1=op1, **kw), reads=reads, writes=writes)

    def stt(self, out, a, s, b, op0, op1, reads, writes, eng="vector"):
        self.R.op(eng, lambda e: e.scalar_tensor_tensor(out=out, in0=a, scalar=s, in1=b, op0=op0, op1=op1), reads=reads, writes=writes)

    def copy(self, out, in_, reads, writes, eng="vector"):
        self.R.op(eng, lambda e: e.tensor_copy(out=out, in_=in_), reads=reads, writes=writes)

    def memzero(self, ap, writes, eng="scalar"):
        self.R.op(eng, lambda e: e.memzero(ap), reads=[], writes=writes)

    def memset(self, ap, val, writes, eng="vector"):
        self.R.op(eng, lambda e: e.memset(ap, val), reads=[], writes=writes)


D = 2048; E = 4096; LAT = 2048; CTX = 256; T = LAT + CTX; EPS = 1e-6
NE = E // 128


class Net:
    def __init__(self, k):
        self.k = k; nc = k.nc
        self.EP = k.pool("E", [128, 4096], 9)
        self.EB = k.pool("EB", [128, 4096], 5, dtype=BF16)
        self.PS = k.pool("ps", [128, 512], 8, psum=True)
        cp = k.pool("const", [128, 256], 1); cb, ck = cp.next()
        self.cb = cb; self.ck = ck
        self.ident = cb[:, 0:128]; self.ones = cb[:, 128:256]
        self.MREP = DT(nc, "MREP", [D, 256], dtype=BF16); self.ADAB = DT(nc, "ADAB", [256, 3 * D])
        self.H = DT(nc, "H", [T, D]); self.HT = DT(nc, "HT", [D, T], dtype=BF16)
        self.PR = DT(nc, "PR", [4 * E, T]); self.U = DT(nc, "U", [E, T]); self.S = DT(nc, "S", [E, T], dtype=BF16)
        self.WB = DT(nc, "WB", [D, 4 * E], dtype=BF16); self.WOB = DT(nc, "WOB", [E, D], dtype=BF16); self.ADAWB = DT(nc, "ADAWB", [D, 3 * D], dtype=BF16)

    def init_consts(self, CONSTS):
        k = self.k
        k.load(self.cb[:, 0:128], self.ck, CONSTS, 0, 128, 0, 128)
        k.memset(self.cb[:, 128:256], 1.0, [self.ck])

    def rows_to_pp(self, SRC, r0, n, c0, C, dst, dkey):
        k = self.k
        for cb in range(0, C, 2048):
            cw = min(2048, C - cb)
            sb, sk = self.EP.next()
            k.load(sb[0:n, 0:cw], sk, SRC, r0, r0 + n, c0 + cb, c0 + cb + cw)
            per = 512 // n
            for j0 in range(0, cw // 128, per):
                jn = min(per, cw // 128 - j0)
                ps, pk = self.PS.next()
                for j in range(jn):
                    k.mm(ps[:, j * n:(j + 1) * n], pk, sb[0:n, (j0 + j) * 128:(j0 + j + 1) * 128], self.ident[0:n, 0:n],
                         True, True, [sk, self.ck])
                t0 = cb // 128 + j0
                k.copy(dst[:, t0:t0 + jn, :], ps[:, 0:jn * n].rearrange("p (j n) -> p j n", n=n), [pk], [dkey])

    def build_mrep(self, CC):
        k = self.k
        tp = k.pool("tmpv", [128, 16, 2], 1); tb, tk = tp.next()
        self.rows_to_pp(CC, 0, 2, 0, D, tb, tk)
        sp = k.pool("tmps", [128, 16, 2], 1); sb, sk = sp.next()
        k.act(sb[:], tb[:], AF.Silu, [tk], [sk])
        for kc in range(16):
            rb, rk = self.EB.next()
            for s in range(2):
                k.ts(rb[:, s * 128:(s + 1) * 128], self.ones, sb[:, kc, s:s + 1], None, ALU.mult, None, [sk, self.ck], [rk])
            k.store(self.MREP, kc * 128, kc * 128 + 128, 0, 256, rb[:, 0:256], rk)

    def adaln(self, ADA_W, ADA_B, i):
        k = self.k
        def epi(ps, pk, m0, mt, n0, nw):
            bb, bk = self.EP.next()
            k.dma(bb[:, 0:nw], ADA_B.ap[i:i + 1, n0:n0 + nw].to_broadcast([128, nw]), ADA_B.keys(i, i + 1, n0, n0 + nw), [bk])
            ob, ok = self.EP.next()
            k.tt(ob[0:mt, 0:nw], ps, bb[0:mt, 0:nw], ALU.add, [pk, bk], [ok])
            k.store(self.ADAB, m0, m0 + mt, n0, n0 + nw, ob[0:mt, 0:nw], ok)
        self.cast_dram(ADA_W, self.ADAWB, D, 3 * D, s0=(i * D, 0))
        self.gemm(self.MREP, self.ADAWB, None, D, 256, 3 * D, epi=epi)

    def prenorm(self, XS, NORM_G, i, ntile):
        k = self.k
        gs, gsk = self.EP.pin(); sh, shk = self.EP.pin()
        gs = gs[:, :].rearrange("p (s d) -> p s d", s=2); sh = sh[:, :].rearrange("p (s d) -> p s d", s=2)
        gb, gk = self.EP.next()
        k.dma(gb[:, 0:D], NORM_G.ap[i:i + 1, :].to_broadcast([128, D]), NORM_G.keys(i, i + 1, 0, D), [gk])
        for s in range(2):
            tb, tk = self.EP.next()
            k.load(tb[:, 0:D], tk, self.ADAB, s * 128, s * 128 + 128, D, 2 * D)
            k.stt(gs[:, s, :], tb[:, 0:D], 1.0, gb[:, 0:D], ALU.add, ALU.mult, [tk, gk], [gsk])
            k.load(sh[:, s, :], shk, self.ADAB, s * 128, s * 128 + 128, 0, D)
        stp = k.pool("st", [128, 8], 2)
        for t in range(ntile):
            s = 0 if t < LAT // 128 else 1
            xb, xk = self.EP.next(); sq, sqk = self.EP.next(); st, stk = stp.next()
            k.load(xb[:, 0:D], xk, XS, t * 128, t * 128 + 128, 0, D)
            k.act(sq[:, 0:D], xb[:, 0:D], AF.Square, [xk], [sqk])
            k.R.op("vector", lambda e, st=st, sq=sq: e.reduce_sum(out=st[:, 0:1], in_=sq[:, 0:D], axis=AX.X), reads=[sqk], writes=[stk])
            k.ts(st[:, 1:2], st[:, 0:1], 1.0 / D, EPS, ALU.mult, ALU.add, [stk], [stk])
            k.act(st[:, 2:3], st[:, 1:2], AF.Sqrt, [stk], [stk])
            k.R.op("vector", lambda e, st=st: e.reciprocal(out=st[:, 3:4], in_=st[:, 2:3]), reads=[stk], writes=[stk])
            k.stt(sq[:, 0:D], xb[:, 0:D], st[:, 3:4], gs[:, s, :], ALU.mult, ALU.mult, [xk, stk, gsk], [sqk])
            k.tt(sq[:, 0:D], sq[:, 0:D], sh[:, s, :], ALU.add, [sqk, shk], [sqk])
            k.store(self.H, t * 128, t * 128 + 128, 0, D, sq[:, 0:D], sqk)
        self.EP.unpin(gsk); self.EP.unpin(shk)
        self.transpose(self.H, self.HT, ntile * 128, D)

    def residual_epi(self, XS):
        k = self.k
        def epi(ps, pk, m0, mt, n0, nw):
            s = 0 if m0 < LAT else 1
            xb, xk = self.EP.next(); gb, gk = self.EP.next()
            k.load(xb[0:mt, 0:nw], xk, XS, m0, m0 + mt, n0, n0 + nw)
            k.load(gb[0:mt, 0:nw], gk, self.ADAB, s * 128, s * 128 + mt, 2 * D + n0, 2 * D + n0 + nw)
            k.tt(gb[0:mt, 0:nw], ps, gb[0:mt, 0:nw], ALU.mult, [pk, gk], [gk])
            k.tt(xb[0:mt, 0:nw], xb[0:mt, 0:nw], gb[0:mt, 0:nw], ALU.add, [xk, gk], [xk])
            k.store(XS, m0, m0 + mt, n0, n0 + nw, xb[0:mt, 0:nw], xk)
        return epi

    def copy_dram(self, SRC, DST, rows, cols, s0=(0, 0), d0=(0, 0)):
        k = self.k
        for r in range(0, rows, 128):
            rw = min(128, rows - r)
            k.dma(DST.ap[d0[0] + r:d0[0] + r + rw, d0[1]:d0[1] + cols], SRC.ap[s0[0] + r:s0[0] + r + rw, s0[1]:s0[1] + cols],
                  SRC.keys(s0[0] + r, s0[0] + r + rw, s0[1], s0[1] + cols), DST.keys(d0[0] + r, d0[0] + r + rw, d0[1], d0[1] + cols))

    def cast_dram(self, SRC, DST, rows, cols, s0=(0, 0), d0=(0, 0)):
        k = self.k; engs = ("vector", "gpsimd", "scalar"); n = 0
        for r in range(0, rows, 128):
            rw = min(128, rows - r)
            for c in range(0, cols, 4096):
                cw = min(4096, cols - c)
                a, ak = self.EP.next(); b, bk = self.EB.next()
                k.load(a[0:rw, 0:cw], ak, SRC, s0[0] + r, s0[0] + r + rw, s0[1] + c, s0[1] + c + cw)
                e = engs[n % 3]; n += 1
                if e == "scalar": k.act(b[0:rw, 0:cw], a[0:rw, 0:cw], AF.Copy, [ak], [bk])
                else: k.copy(b[0:rw, 0:cw], a[0:rw, 0:cw], [ak], [bk], eng=e)
                k.store(DST, d0[0] + r, d0[0] + r + rw, d0[1] + c, d0[1] + c + cw, b[0:rw, 0:cw], bk)

    def final_norm(self, XS, FG, OUT):
        k = self.k
        gb, gk = self.EP.pin()
        k.dma(gb[:, 0:D], FG.ap[0:1, :].to_broadcast([128, D]), FG.keys(0, 1, 0, D), [gk])
        stp = k.pool("st", [128, 8], 2)
        for t in range(LAT // 128):
            xb, xk = self.EP.next(); sq, sqk = self.EP.next(); st, stk = stp.next()
            k.load(xb[:, 0:D], xk, XS, t * 128, t * 128 + 128, 0, D)
            k.act(sq[:, 0:D], xb[:, 0:D], AF.Square, [xk], [sqk])
            k.R.op("vector", lambda e, st=st, sq=sq: e.reduce_sum(out=st[:, 0:1], in_=sq[:, 0:D], axis=AX.X), reads=[sqk], writes=[stk])
            k.ts(st[:, 1:2], st[:, 0:1], 1.0 / D, EPS, ALU.mult, ALU.add, [stk], [stk])
            k.act(st[:, 2:3], st[:, 1:2], AF.Sqrt, [stk], [stk])
            k.R.op("vector", lambda e, st=st: e.reciprocal(out=st[:, 3:4], in_=st[:, 2:3]), reads=[stk], writes=[stk])
            k.stt(sq[:, 0:D], xb[:, 0:D], st[:, 3:4], gb[:, 0:D], ALU.mult, ALU.mult, [xk, stk, gk], [sqk])
            k.store(OUT, t * 128, t * 128 + 128, 0, D, sq[:, 0:D], sqk)
        self.EP.unpin(gk)

    def gemm(self, L, R, OUT, K, M, N, l0=(0, 0), r0=(0, 0), o0=(0, 0), epi=None, act=None):
        k = self.k
        assert L.dtype == R.dtype, (L.name, R.name)
        PP = self.EB if L.dtype == BF16 else self.EP
        MG, NB, KP = 512, 512, 8
        assert KP * max(MG, NB) <= 4096
        kch = (K + 127) // 128; kparts = min(K, 128); npan = (kch + KP - 1) // KP
        reuse_l = (npan <= 2) and (N > NB)
        for mg in range(0, M, MG):
            mw = min(MG, M - mg)
            held = {}
            for nb in range(0, N, NB):
                nw = min(NB, N - nb)
                pss = [self.PS.next() for _ in range((mw + 127) // 128)]
                for kp in range(npan):
                    kc0 = kp * KP; kcn = min(KP, kch - kc0)
                    ka = kc0 * 128; kb = min(K, (kc0 + kcn) * 128)
                    if kp in held:
                        lb, lk = held[kp]
                    else:
                        lb, lk = PP.pin() if reuse_l else PP.next()
                        k.load(lb[0:kparts, 0:kcn * mw].rearrange("p (c m) -> p c m", m=mw), lk, L, l0[0] + ka, l0[0] + kb,
                               l0[1] + mg, l0[1] + mg + mw, "(c p) m -> p c m", p=kparts)
                        if reuse_l: held[kp] = (lb, lk)
                    rb, rk = PP.next()
                    k.load(rb[0:kparts, 0:kcn * nw].rearrange("p (c m) -> p c m", m=nw), rk, R, r0[0] + ka, r0[0] + kb,
                           r0[1] + nb, r0[1] + nb + nw, "(c p) m -> p c m", p=kparts)
                    for g, (ps, pk) in enumerate(pss):
                        mt = min(128, mw - g * 128)
                        for kc in range(kcn):
                            k.mm(ps[0:mt, 0:nw], pk, lb[0:kparts, kc * mw + g * 128:kc * mw + g * 128 + mt],
                                 rb[0:kparts, kc * nw:(kc + 1) * nw],
                                 start=(kp == 0 and kc == 0), stop=(kp == npan - 1 and kc == kcn - 1), reads=[lk, rk])
                for g, (ps, pk) in enumerate(pss):
                    mt = min(128, mw - g * 128); m0 = mg + g * 128
                    if epi is not None:
                        epi(ps[0:mt, 0:nw], pk, m0, mt, nb, nw)
                    else:
                        ob, ok = self.EP.next()
                        k.act(ob[0:mt, 0:nw], ps[0:mt, 0:nw], act(m0) if callable(act) else (act or AF.Copy), [pk], [ok])
                        k.store(OUT, o0[0] + m0, o0[0] + m0 + mt, o0[1] + nb, o0[1] + nb + nw, ob[0:mt, 0:nw], ok)
            for (_, lk_) in held.values():
                PP.unpin(lk_)

    def transpose(self, SRC, DST, Rn, Cn, s0=(0, 0), d0=(0, 0)):
        k = self.k
        for rb in range(0, Rn, 512):
            rw = min(512, Rn - rb); nrt = rw // 128
            for cb in range(0, Cn, 512):
                cw = min(512, Cn - cb)
                ib, ik = self.EP.next()
                k.load(ib[:, 0:nrt * cw].rearrange("p (t c) -> p t c", c=cw), ik, SRC, s0[0] + rb, s0[0] + rb + rw,
                       s0[1] + cb, s0[1] + cb + cw, "(t p) c -> p t c", p=128)
                for ct in range(cw // 128):
                    ps, pk = self.PS.next()
                    for t in range(nrt):
                        o = ps[:, t * 128:(t + 1) * 128]; i = ib[:, t * cw + ct * 128:t * cw + ct * 128 + 128]
                        k.R.op("tensor", lambda e, o=o, i=i: e.transpose(o, i, self.ident), reads=[ik, self.ck], writes=[pk], pe_accum=(t > 0))
                    ob, ok = (self.EB if DST.dtype == BF16 else self.EP).next()
                    k.copy(ob[:, 0:rw], ps[:, 0:rw], [pk], [ok])
                    k.store(DST, d0[0] + cb + ct * 128, d0[0] + cb + ct * 128 + 128, d0[1] + rb, d0[1] + rb + rw, ob[:, 0:rw], ok)

    def conformer(self, XS, W_IN, DW_W, DW_B, LN_G, LN_B, W_OUT, need_ctx=True):
        k = self.k
        Tn = T if need_ctx else LAT
        segs = [(0, LAT)] + ([(LAT, CTX)] if need_ctx else [])
        cwp = k.pool("cfw", [128, NE, 34], 1); cw, cwk = cwp.next()
        self.rows_to_pp(DW_W, 0, 31, 0, E, cw[:, :, 0:31], cwk)
        for n, V in enumerate((DW_B, LN_G, LN_B)):
            self.rows_to_pp(V, 0, 1, 0, E, cw[:, :, 31 + n:32 + n], cwk)
        fn = lambda m0: AF.Copy if m0 < E else (AF.Sigmoid if m0 < 2 * E else AF.Silu)
        self.cast_dram(W_IN, self.WB, D, 3 * E); self.cast_dram(W_OUT, self.WOB, E, D)
        self.gemm(self.WB, self.HT, self.PR, D, 3 * E, Tn, act=fn)
        st0, st0k = self.EP.pin(); st1, st1k = self.EP.pin()
        k.memset(st0[:, 0:T], 0.0, [st0k]); k.memset(st1[:, 0:T], 0.0, [st1k])
        PAD = 15
        for j in range(NE):
            a, ak = self.EP.next(); sb, sk = self.EP.next(); u0, uk = self.EP.next()
            acc, acck = self.EP.next()
            acc2, acc2k = a, ak
            k.load(a[:, 0:Tn], ak, self.PR, j * 128, j * 128 + 128, 0, Tn)
            k.load(sb[:, 0:Tn], sk, self.PR, E + j * 128, E + j * 128 + 128, 0, Tn)
            for si, (s0, sl) in enumerate(segs):
                o = s0 + (2 * si + 1) * PAD
                k.memset(u0[:, o - PAD:o], 0.0, [uk]); k.memset(u0[:, o + sl:o + sl + PAD], 0.0, [uk])
                k.tt(u0[:, o:o + sl], a[:, s0:s0 + sl], sb[:, s0:s0 + sl], ALU.mult, [ak, sk], [uk])
            NV = 31
            for si, (s0, sl) in enumerate(segs):
                o = s0 + 2 * si * PAD
                for tap in range(31):
                    eng, ac, ack = ("vector", acc, acck) if tap < NV else ("gpsimd", acc2, acc2k)
                    src = u0[:, o + tap:o + tap + sl]
                    if tap == 0:
                        k.ts(ac[:, s0:s0 + sl], src, cw[:, j, 0:1], cw[:, j, 31:32], ALU.mult, ALU.add, [uk, cwk], [ack], eng=eng)
                    elif tap == NV:
                        k.ts(ac[:, s0:s0 + sl], src, cw[:, j, tap:tap + 1], None, ALU.mult, None, [uk, cwk], [ack], eng=eng)
                    else:
                        k.stt(ac[:, s0:s0 + sl], src, cw[:, j, tap:tap + 1], ac[:, s0:s0 + sl], ALU.mult, ALU.add, [uk, cwk, ack], [ack], eng=eng)
            k.act(acc2[:, 0:Tn], acc[:, 0:Tn], AF.Square, [acck], [acc2k])
            k.tt(st0[:, 0:Tn], st0[:, 0:Tn], acc[:, 0:Tn], ALU.add, [st0k, acck], [st0k], eng="gpsimd")
            k.tt(st1[:, 0:Tn], st1[:, 0:Tn], acc2[:, 0:Tn], ALU.add, [st1k, acc2k], [st1k], eng="gpsimd")
            k.store(self.U, j * 128, j * 128 + 128, 0, Tn, acc[:, 0:Tn], acck)
        for nb in range(0, Tn, 512):
            nw = min(512, Tn - nb)
            p1, p1k = self.PS.next(); p2, p2k = self.PS.next()
            k.mm(p1[:, 0:nw], p1k, self.ones, st0[:, nb:nb + nw], True, True, [self.ck, st0k])
            k.mm(p2[:, 0:nw], p2k, self.ones, st1[:, nb:nb + nw], True, True, [self.ck, st1k])
            m, mk = self.EP.next()
            k.ts(st0[:, nb:nb + nw], p1[:, 0:nw], 1.0 / E, None, ALU.mult, None, [p1k], [st0k])
            k.tt(m[:, 0:nw], st0[:, nb:nb + nw], st0[:, nb:nb + nw], ALU.mult, [st0k], [mk])
            k.stt(m[:, 0:nw], p2[:, 0:nw], 1.0 / E, m[:, 0:nw], ALU.mult, ALU.subtract, [p2k, mk], [mk])
            k.ts(m[:, 0:nw], m[:, 0:nw], EPS, None, ALU.add, None, [mk], [mk])
            k.act(m[:, 0:nw], m[:, 0:nw], AF.Sqrt, [mk], [mk])
            k.R.op("vector", lambda e, o=st1[:, nb:nb + nw], i=m[:, 0:nw]: e.reciprocal(out=o, in_=i), reads=[mk], writes=[st1k])
        for j in range(NE):
            u, uk = self.EP.next(); g, gk = self.EP.next()
            k.load(u[:, 0:Tn], uk, self.U, j * 128, j * 128 + 128, 0, Tn)
            k.load(g[:, 0:Tn], gk, self.PR, 2 * E + j * 128, 2 * E + j * 128 + 128, 0, Tn)
            k.tt(u[:, 0:Tn], u[:, 0:Tn], st0[:, 0:Tn], ALU.subtract, [uk, st0k], [uk])
            k.tt(u[:, 0:Tn], u[:, 0:Tn], st1[:, 0:Tn], ALU.mult, [uk, st1k], [uk])
            k.act(u[:, 0:Tn], u[:, 0:Tn], AF.Silu, [uk, cwk], [uk], bias=cw[:, j, 33:34], scale=cw[:, j, 32:33])
            sb_, sbk_ = self.EB.next()
            k.tt(sb_[:, 0:Tn], u[:, 0:Tn], g[:, 0:Tn], ALU.mult, [uk, gk], [sbk_], eng="gpsimd")
            k.store(self.S, j * 128, j * 128 + 128, 0, Tn, sb_[:, 0:Tn], sbk_)
        self.EP.unpin(st0k); self.EP.unpin(st1k)
        self.gemm(self.S, self.WOB, None, E, Tn, D, epi=self.residual_epi(XS))


TWO_PI = 2.0 * math.pi


def hyena_consts():
    c = {}
    def fmat(L):
        N = 2 * L
        t = np.arange(L, dtype=np.float64)[:, None]; kk = np.arange(L, dtype=np.float64)[None, :]
        ang = 2.0 * np.pi * ((t * kk) % N) / N
        A = np.cos(ang); B = -np.sin(ang); B[:, 0] = np.cos(np.pi * t[:, 0])
        return np.concatenate([A, B], 1).astype(np.float32)
    c["flat"] = fmat(LAT); c["fctx"] = fmat(CTX)
    negt = np.zeros((128, 18), np.float32)
    negt[:, 0:16] = -(np.linspace(0.0, 1.0, LAT, dtype=np.float32).reshape(16, 128).T)
    negt[:, 16:18] = -(np.linspace(0.0, 1.0, CTX, dtype=np.float32).reshape(2, 128).T)
    c["negt"] = negt
    c["delta"] = np.abs(np.linspace(math.log(1e-2) / 1.5, math.log(1e-2) / 0.3, E, dtype=np.float32))[None, :]
    def feats(L):
        t = np.linspace(0.0, 1.0, L, dtype=np.float32)[:, None]
        w = (2.0 * math.pi / L) * np.arange(L, dtype=np.float32)[:, None]
        f = np.linspace(1e-4, 15, 16, dtype=np.float32)[None, :]
        return np.concatenate([t, np.cos(f * w), -np.sin(f * w)], -1).astype(np.float32).T
    c["feats"] = np.concatenate([feats(LAT), feats(CTX)], 1)
    return {k_: np.ascontiguousarray(v, dtype=np.float32) for k_, v in c.items()}


class Hyena:
    def __init__(self, net, FLAT, FCTX, NEGT, DELTA, FEATS):
        self.net = net; k = net.k; nc = k.nc
        self.F = {LAT: FLAT, CTX: FCTX}; self.NEGT = NEGT; self.DELTA = DELTA; self.FEATS = FEATS
        self.FT = {LAT: DT(nc, "FTLAT", [2 * LAT, LAT], dtype=BF16), CTX: DT(nc, "FTCTX", [2 * CTX, CTX], dtype=BF16)}
        self.FB = {LAT: DT(nc, "FBLAT", [LAT, 2 * LAT], dtype=BF16), CTX: DT(nc, "FBCTX", [CTX, 2 * CTX], dtype=BF16)}
        self.H1T = DT(nc, "H1T", [64, LAT]); self.H2T = DT(nc, "H2T", [64, LAT])
        self.HF = DT(nc, "HF", [LAT, 4 * E]); self.HS = [DT(nc, f"HS{n}", [LAT, E], dtype=BF16) for n in range(2)]
        self.HD = [DT(nc, f"HD{n}", [LAT, E], dtype=BF16) for n in range(2)]
        self.SPEC = [DT(nc, f"SPEC{n}", [2 * LAT, E]) for n in range(2)]
        self.ZT = DT(nc, "ZT", [LAT, E], dtype=BF16); self.ZF = DT(nc, "ZF", [2 * LAT, E]); self.YF = DT(nc, "YF", [2 * LAT, E], dtype=BF16); self.Y = DT(nc, "Y", [E, LAT])
        self.Z1 = DT(nc, "Z1", [E, T])
        self.ft_done = False

    def setup(self):
        net = self.net; k = net.k
        if not self.ft_done:
            for L in (LAT, CTX):
                net.transpose(self.F[L], self.FT[L], L, 2 * L)
                net.cast_dram(self.F[L], self.FB[L], L, 2 * L)
            sp = k.pool("hyneg", [128, 18], 1); self.negt, self.negtk = sp.next()
            k.load(self.negt[:], self.negtk, self.NEGT, 0, 128, 0, 18)
            self.ft_done = True

    def layer(self, XS, W_IN, CONV_WB, HYF, F_W1, F_W2, F_W3, SKIP, W_OUT, need_ctx):
        net = self.net; k = net.k; EP = net.EP; PS = net.PS
        self.setup()
        Tn = T if need_ctx else LAT
        segs = [(0, LAT, 0)] + ([(LAT, CTX, 16)] if need_ctx else [])
        cvp = k.pool("hycv", [128, 96, 4], 1); cv, cvk = cvp.next()
        net.rows_to_pp(CONV_WB, 0, 4, 0, 3 * E, cv, cvk)
        skp = k.pool("hysk", [128, NE, 2], 1); sk, skk = skp.next()
        net.rows_to_pp(SKIP, 0, 2, 0, E, sk, skk)
        fpp_p = k.pool("hyf", [128, 4], 1); fpp, fppk = fpp_p.next()
        tb, tk = EP.next()
        k.load(tb[0:4, 0:64], tk, HYF, 0, 4, 0, 64)
        ps, pk = PS.next()
        k.mm(ps[0:64, 0:4], pk, tb[0:4, 0:64], net.ident[0:4, 0:4], True, True, [tk, net.ck])
        k.copy(fpp[0:64, :], ps[0:64, 0:4], [pk], [fppk])
        dl, dlk = EP.pin()
        k.dma(dl[:, 0:E], self.DELTA.ap[0:1, :].to_broadcast([128, E]), self.DELTA.keys(0, 1, 0, E), [dlk])
        net.cast_dram(W_IN, net.WB, D, 4 * E); net.cast_dram(W_OUT, net.WOB, E, D)
        net.gemm(net.WB, net.HT, net.PR, D, 4 * E, Tn, act=lambda m0: AF.Silu if m0 >= 3 * E else AF.Copy)
        for j in range(96):
            u, uk = EP.next(); o, ok = EP.next()
            for si, (s0, sl, _) in enumerate(segs):
                b = s0 + 3 * si
                k.memzero(u[:, b:b + 1], [uk]); k.memzero(u[:, b + sl + 1:b + sl + 2], [uk])
                k.load(u[:, b + 1:b + 1 + sl], uk, net.PR, j * 128, j * 128 + 128, s0, s0 + sl)
            for si, (s0, sl, _) in enumerate(segs):
                b = s0 + 3 * si
                k.ts(o[:, s0:s0 + sl], u[:, b:b + sl], cv[:, j, 0:1], cv[:, j, 3:4], ALU.mult, ALU.add, [uk, cvk], [ok])
                for tap in (1, 2):
                    k.stt(o[:, s0:s0 + sl], u[:, b + tap:b + tap + sl], cv[:, j, tap:tap + 1], o[:, s0:s0 + sl], ALU.mult, ALU.add, [uk, cvk, ok], [ok])
            k.store(net.PR, j * 128, j * 128 + 128, 0, Tn, o[:, 0:Tn], ok)
        for (s0, L, nt0) in segs:
            self.filters(L, s0, nt0, HYF, F_W1, F_W2, F_W3, fpp, fppk, dl, dlk)
            self.longconv(L, 0, net.PR, 0, s0)
            self.combine(L, s0, 0, sk, skk, zsrc=(net.PR, 0), xrow=E, dst=self.Z1)
            self.longconv(L, 1, self.Z1, 0, s0)
            self.combine(L, s0, 1, sk, skk, zsrc=(self.Z1, 0), xrow=2 * E, dst=net.S, gate_row=3 * E)
        EP.unpin(dlk)
        net.gemm(net.S, net.WOB, None, E, Tn, D, epi=net.residual_epi(XS))

    def sin_epi(self, OUT, fpp, fppk, cb, cfr):
        net = self.net; k = net.k; EP = net.EP
        def epi(ps, pk, m0, mt, n0, nw):
            a, ak = EP.next()
            k.ts(a[0:mt, 0:nw], ps, fpp[0:mt, cb:cb + 1], fpp[0:mt, cfr:cfr + 1], ALU.add, ALU.mult, [pk, fppk], [ak])
            m, mk = EP.next()
            for lvl in range(2):
                k.ts(m[0:mt, 0:nw], a[0:mt, 0:nw], math.pi, -TWO_PI, ALU.is_gt, ALU.mult, [ak], [mk])
                k.tt(a[0:mt, 0:nw], a[0:mt, 0:nw], m[0:mt, 0:nw], ALU.add, [ak, mk], [ak])
                k.ts(m[0:mt, 0:nw], a[0:mt, 0:nw], -math.pi, TWO_PI, ALU.is_lt, ALU.mult, [ak], [mk])
                k.tt(a[0:mt, 0:nw], a[0:mt, 0:nw], m[0:mt, 0:nw], ALU.add, [ak, mk], [ak])
            k.act(a[0:mt, 0:nw], a[0:mt, 0:nw], AF.Sin, [ak], [ak])
            k.store(OUT, m0, m0 + mt, n0, n0 + nw, a[0:mt, 0:nw], ak)
        return epi

    def filters(self, L, s0, nt0, HYF, F_W1, F_W2, F_W3, fpp, fppk, dl, dlk):
        net = self.net; k = net.k; EP = net.EP
        N = 2 * L; F = self.FB[L]
        net.gemm(F_W1, self.FEATS, None, 33, 64, L, r0=(0, s0), epi=self.sin_epi(self.H1T, fpp, fppk, 0, 1))
        net.gemm(F_W2, self.H1T, None, 64, 64, L, epi=self.sin_epi(self.H2T, fpp, fppk, 2, 3))
        def wepi(ps, pk, m0, mt, n0, nw):
            e0 = n0 % E
            w, wk = EP.next()
            k.act(w[0:mt, 0:nw], dl[0:mt, e0:e0 + nw], AF.Exp, [dlk, self.negtk], [wk], scale=self.negt[0:mt, nt0 + m0 // 128:nt0 + m0 // 128 + 1])
            k.tt(w[0:mt, 0:nw], ps, w[0:mt, 0:nw], ALU.mult, [pk, wk], [wk])
            k.store(self.HF, m0, m0 + mt, n0, n0 + nw, w[0:mt, 0:nw], wk)
        net.gemm(self.H2T, F_W3, None, 64, L, 4 * E, epi=wepi)
        for n in range(2):
            for t in range(L // 128):
                for eb in range(0, E, 2048):
                    f, fk = EP.next(); b, bk = EP.next(); hs, hsk = net.EB.next(); hd, hdk = net.EB.next()
                    k.load(f[:, 0:2048], fk, self.HF, t * 128, t * 128 + 128, (2 * n) * E + eb, (2 * n) * E + eb + 2048)
                    k.load(b[:, 0:2048], bk, self.HF, t * 128, t * 128 + 128, (2 * n + 1) * E + eb, (2 * n + 1) * E + eb + 2048)
                    if t == 0:
                        k.memset(b[0:1, 0:2048], 0.0, [bk])
                    k.tt(hs[:, 0:2048], f[:, 0:2048], b[:, 0:2048], ALU.add, [fk, bk], [hsk])
                    k.tt(hd[:, 0:2048], f[:, 0:2048], b[:, 0:2048], ALU.subtract, [fk, bk], [hdk], eng="gpsimd")
                    k.store(self.HS[n], t * 128, t * 128 + 128, eb, eb + 2048, hs[:, 0:2048], hsk)
                    k.store(self.HD[n], t * 128, t * 128 + 128, eb, eb + 2048, hd[:, 0:2048], hdk)
            def sepi(row_off, fix0, scale, row0_only=False):
                def epi(ps, pk, m0, mt, n0, nw):
                    if row0_only:
                        mt = 1
                        ps = ps[0:1, :]
                    ob, ok = EP.next()
                    k.act(ob[0:mt, 0:nw], ps, AF.Copy, [pk], [ok], scale=scale)
                    if fix0 and m0 == 0:
                        k.ts(ob[0:1, 0:nw], ob[0:1, 0:nw], 0.5, None, ALU.mult, None, [ok], [ok])
                    k.store(self.SPEC[n], row_off + m0, row_off + m0 + mt, n0, n0 + nw, ob[0:mt, 0:nw], ok)
                return epi
            net.gemm(F, self.HS[n], None, L, L, E, l0=(0, 0), epi=sepi(0, True, 2.0 / N))
            net.gemm(F, self.HD[n], None, L, L, E, l0=(0, L), epi=sepi(L, False, 2.0 / N))
            net.gemm(F, self.HS[n], None, L, 128, E, l0=(0, L), epi=sepi(L, False, 1.0 / N, True))

    def longconv(self, L, n, SRC, row0, s0):
        net = self.net; k = net.k; EP = net.EP
        F = self.FB[L]; FT = self.FT[L]; SP = self.SPEC[n]
        net.transpose(SRC, self.ZT, E, L, s0=(row0, s0))
        net.gemm(F, self.ZT, self.ZF, L, 2 * L, E)
        for kt in range(L // 128):
            for eb in range(0, E, 2048):
                a, ak = EP.next(); b, bk = EP.next(); sa, sak = EP.next(); sb, sbk = EP.next(); t1, t1k = EP.next()
                oa, oak = net.EB.next(); ob2, ob2k = net.EB.next()
                W = 2048
                r = kt * 128
                k.load(a[:, 0:W], ak, self.ZF, r, r + 128, eb, eb + W); k.load(b[:, 0:W], bk, self.ZF, L + r, L + r + 128, eb, eb + W)
                k.load(sa[:, 0:W], sak, SP, r, r + 128, eb, eb + W); k.load(sb[:, 0:W], sbk, SP, L + r, L + r + 128, eb, eb + W)
                k.tt(t1[:, 0:W], a[:, 0:W], sb[:, 0:W], ALU.mult, [ak, sbk], [t1k])
                k.tt(sb[:, 0:W], b[:, 0:W], sb[:, 0:W], ALU.mult, [bk, sbk], [sbk], eng="gpsimd")
                k.tt(a[:, 0:W], a[:, 0:W], sa[:, 0:W], ALU.mult, [ak, sak], [ak])
                k.tt(sb[:, 0:W], a[:, 0:W], sb[:, 0:W], ALU.subtract, [ak, sbk], [sbk])
                k.tt(b[:, 0:W], b[:, 0:W], sa[:, 0:W], ALU.mult, [bk, sak], [bk], eng="gpsimd")
                k.tt(t1[:, 0:W], t1[:, 0:W], b[:, 0:W], ALU.add, [t1k, bk], [t1k])
                if kt == 0:
                    k.copy(sb[0:1, 0:W], a[0:1, 0:W], [ak], [sbk])
                    k.load(b[0:1, 0:W], bk, self.ZF, L, L + 1, eb, eb + W)
                    k.load(sa[0:1, 0:W], sak, SP, L, L + 1, eb, eb + W)
                    k.tt(t1[0:1, 0:W], b[0:1, 0:W], sa[0:1, 0:W], ALU.mult, [bk, sak], [t1k])
                k.copy(oa[:, 0:W], sb[:, 0:W], [sbk], [oak], eng="gpsimd")
                k.act(ob2[:, 0:W], t1[:, 0:W], AF.Copy, [t1k], [ob2k])
                k.store(self.YF, r, r + 128, eb, eb + W, oa[:, 0:W], oak)
                k.store(self.YF, L + r, L + r + 128, eb, eb + W, ob2[:, 0:W], ob2k)
        net.gemm(self.YF, FT, self.Y, 2 * L, E, L)

    def combine(self, L, s0, n, sk, skk, zsrc, xrow, dst, gate_row=None):
        net = self.net; k = net.k; EP = net.EP
        ZS, zrow = zsrc
        for j in range(NE):
            y, yk = EP.next(); z, zk = EP.next(); x, xk = EP.next()
            k.load(y[:, 0:L], yk, self.Y, j * 128, j * 128 + 128, 0, L)
            k.load(z[:, 0:L], zk, ZS, zrow + j * 128, zrow + j * 128 + 128, s0, s0 + L)
            k.load(x[:, 0:L], xk, net.PR, xrow + j * 128, xrow + j * 128 + 128, s0, s0 + L)
            k.stt(y[:, 0:L], z[:, 0:L], sk[:, j, n:n + 1], y[:, 0:L], ALU.mult, ALU.add, [zk, skk, yk], [yk])
            k.tt(y[:, 0:L], y[:, 0:L], x[:, 0:L], ALU.mult, [yk, xk], [yk])
            if gate_row is not None:
                k.load(z[:, 0:L], zk, net.PR, gate_row + j * 128, gate_row + j * 128 + 128, s0, s0 + L)
                ob, obk = net.EB.next()
                k.tt(ob[:, 0:L], y[:, 0:L], z[:, 0:L], ALU.mult, [yk, zk], [obk], eng="gpsimd")
                k.store(dst, j * 128, j * 128 + 128, s0, s0 + L, ob[:, 0:L], obk)
                continue
            k.store(dst, j * 128, j * 128 + 128, s0, s0 + L, y[:, 0:L], yk)


RT_H = 8; RT_DK = 256; RT_DV = 512; CH = 128


def ret_consts():
    c = {}
    half = 64
    inv = (10000.0 ** (-np.arange(half, dtype=np.float32) / half)).astype(np.float32)
    rows = np.repeat(np.arange(LAT // 64), 64).astype(np.float32); cols = np.tile(np.arange(64), LAT // 64).astype(np.float32)
    ar = rows[:, None] * inv; ac = cols[:, None] * inv
    c["rcos"] = np.concatenate([np.cos(ar), np.cos(ac)], 1)
    c["rsin"] = np.concatenate([np.sin(ar), np.sin(ac)], 1)
    p = np.arange(128, dtype=np.float32)
    j = p[:, None]; i = p[None, :]
    c["rmask"] = np.concatenate([np.maximum(i - j, 0), (i >= j).astype(np.float32), np.maximum(j - i, 0), (j >= i).astype(np.float32)], 1)
    c["rcols"] = np.stack([p + 1, CH - 1 - p, CH - p, p], 1)
    return {k_: np.ascontiguousarray(v, dtype=np.float32) for k_, v in c.items()}


class Retention:
    def __init__(self, net, RCOS, RSIN, RMASK, RCOLS):
        self.net = net; nc = net.k.nc
        self.RCOS = RCOS; self.RSIN = RSIN; self.RMASK = RMASK; self.RCOLS = RCOLS
        self.QKV = DT(nc, "QKV", [T, 2 * D + E]); self.QKT = DT(nc, "QKT", [2 * D, T])
        self.OACC = DT(nc, "OACC", [T, E]); self.OT = DT(nc, "OTR", [E, T])

    def layer(self, XS, W_IN, DECAY, GN_GB, W_OUT):
        net = self.net; k = net.k; EP = net.EP; PS = net.PS
        def qkv_epi(ps, pk, m0, mt, n0, nw):
            ob, ok = EP.next()
            k.act(ob[0:mt, 0:nw], ps, AF.Copy, [pk], [ok], scale=(RT_DK ** -0.5 if D <= n0 < 2 * D else 1.0))
            k.store(self.QKV, m0, m0 + mt, n0, n0 + nw, ob[0:mt, 0:nw], ok)
        net.cast_dram(W_IN, net.WB, D, 2 * D + 2 * E); net.cast_dram(W_OUT, net.WOB, E, D)
        net.gemm(net.HT, net.WB, None, D, T, 2 * D + E, epi=qkv_epi)
        net.gemm(net.WB, net.HT, net.PR, D, E, T, l0=(0, 2 * D + E), act=AF.Silu)
        for t in range(LAT // 128):
            x, xk = EP.next(); o, ok = EP.next(); cs, csk = EP.next(); t1, t1k = EP.next(); t2, t2k = EP.next()
            k.load(x[:, 0:2 * D], xk, self.QKV, t * 128, t * 128 + 128, 0, 2 * D)
            k.load(cs[:, 0:128], csk, self.RCOS, t * 128, t * 128 + 128, 0, 128)
            k.load(cs[:, 128:256], csk, self.RSIN, t * 128, t * 128 + 128, 0, 128)
            xv = x[:, 0:2 * D].rearrange("p (h a s f) -> p h a s f", h=16, a=2, s=2)
            ov = o[:, 0:2 * D].rearrange("p (h a s f) -> p h a s f", h=16, a=2, s=2)
            cv = cs[:, 0:128].rearrange("p (a f) -> p a f", a=2); sv = cs[:, 128:256].rearrange("p (a f) -> p a f", a=2)
            t1v = t1[:, 0:128].rearrange("p (a f) -> p a f", a=2); t2v = t2[:, 0:128].rearrange("p (a f) -> p a f", a=2)
            for h in range(16):
                e1, e2 = ("vector", "gpsimd")
                k.tt(t1v, xv[:, h, :, 0, :], cv, ALU.mult, [xk, csk], [t1k], eng=e1)
                k.tt(t2v, xv[:, h, :, 1, :], sv, ALU.mult, [xk, csk], [t2k], eng=e2)
                k.tt(ov[:, h, :, 0, :], t1v, t2v, ALU.subtract, [t1k, t2k], [ok], eng=e1)
                k.tt(t1v, xv[:, h, :, 0, :], sv, ALU.mult, [xk, csk], [t1k], eng=e1)
                k.tt(t2v, xv[:, h, :, 1, :], cv, ALU.mult, [xk, csk], [t2k], eng=e2)
                k.tt(ov[:, h, :, 1, :], t1v, t2v, ALU.add, [t1k, t2k], [ok], eng=e1)
            k.store(self.QKV, t * 128, t * 128 + 128, 0, 2 * D, o[:, 0:2 * D], ok)
        net.transpose(self.QKV, self.QKT, T, 2 * D)
        cp = k.pool("rtc", [128, 1024], 1); cb, cbk = cp.next()
        lg = cb[:, 0:16]; cdec = cb[:, 16:32]; cols = cb[:, 32:36]; dec = cb[:, 64:128]
        rmask = cb[:, 128:640]; mk = cb[:, 640:896]
        k.dma(lg, DECAY.ap[0:1, 0:16].to_broadcast([128, 16]), DECAY.keys(0, 1, 0, 16), [cbk])
        k.load(cols, cbk, self.RCOLS, 0, 128, 0, 4)
        k.load(rmask, cbk, self.RMASK, 0, 128, 0, 512)
        k.act(lg, lg, AF.Exp, [cbk], [cbk], scale=-1.0)
        k.act(lg, lg, AF.Ln, [cbk], [cbk], bias=1.0)
        k.ts(lg, lg, -1.0, None, ALU.mult, None, [cbk], [cbk])
        k.act(cdec, lg, AF.Exp, [cbk], [cbk], scale=float(CH))
        decv = dec.rearrange("p (c s) -> p c s", c=4)
        for c4 in range(4):
            k.act(decv[:, c4, :], lg, AF.Exp, [cbk], [cbk], scale=cols[:, c4:c4 + 1])
        stp = k.pool("rtst", [128, 2, 512], 2)
        chunks_f = [(LAT + c * CH) for c in range(CTX // CH)] + [c * CH for c in range(LAT // CH)]
        chunks_b = [(LAT + c * CH) for c in reversed(range(CTX // CH))] + [c * CH for c in reversed(range(LAT // CH))]
        for h in range(RT_H):
            qT, qTk = EP.pin(); kT, kTk = EP.pin(); cT, cTk = EP.pin()
            for dc in range(2):
                r = h * RT_DK + dc * 128
                k.load(qT[:, dc * LAT:(dc + 1) * LAT], qTk, self.QKT, r, r + 128, 0, LAT)
                k.load(kT[:, dc * LAT:(dc + 1) * LAT], kTk, self.QKT, D + r, D + r + 128, 0, LAT)
                k.load(cT[:, dc * CTX:(dc + 1) * CTX], cTk, self.QKT, r, r + 128, LAT, T)
                k.load(cT[:, (2 + dc) * CTX:(3 + dc) * CTX], cTk, self.QKT, D + r, D + r + 128, LAT, T)
            def qk_slices(t0):
                if t0 < LAT:
                    return ([qT[:, dc * LAT + t0:dc * LAT + t0 + CH] for dc in range(2)],
                            [kT[:, dc * LAT + t0:dc * LAT + t0 + CH] for dc in range(2)], [qTk, kTk])
                c0 = t0 - LAT
                return ([cT[:, dc * CTX + c0:dc * CTX + c0 + CH] for dc in range(2)],
                        [cT[:, (2 + dc) * CTX + c0:(2 + dc) * CTX + c0 + CH] for dc in range(2)], [cTk])
            for di, chunks in enumerate((chunks_f, chunks_b)):
                col = di * 8 + h
                mt_ = mk[:, di * 128:(di + 1) * 128]
                k.act(mt_, rmask[:, di * 256:di * 256 + 128], AF.Exp, [cbk], [cbk], scale=lg[:, col:col + 1])
                k.tt(mt_, mt_, rmask[:, di * 256 + 128:di * 256 + 256], ALU.mult, [cbk], [cbk])
                st, stk = stp.next()
                k.memset(st[:], 0.0, [stk])
                for t0 in chunks:
                    qs, ks, qkk = qk_slices(t0)
                    kv, kvk = EP.next()
                    k.load(kv[:, 0:256], kvk, self.QKV, t0, t0 + CH, D + h * RT_DK, D + (h + 1) * RT_DK)
                    k.load(kv[:, 256:768], kvk, self.QKV, t0, t0 + CH, 2 * D + h * RT_DV, 2 * D + (h + 1) * RT_DV)
                    ps_s, pssk = PS.next()
                    for dc in range(2):
                        k.mm(ps_s[:, 0:CH], pssk, ks[dc], qs[dc], dc == 0, dc == 1, qkk)
                    sc, sck = EP.next()
                    k.tt(sc[:, 0:CH], ps_s[:, 0:CH], mt_, ALU.mult, [pssk, cbk], [sck])
                    p1, p1k = PS.next(); p2, p2k = PS.next()
                    k.mm(p1[:, 0:RT_DV], p1k, sc[:, 0:CH], kv[:, 256:768], True, True, [sck, kvk])
                    for dc in range(2):
                        k.mm(p2[:, 0:RT_DV], p2k, qs[dc], st[:, dc, :], dc == 0, dc == 1, qkk + [stk])
                    o, ok = EP.next()
                    k.act(o[:, 0:RT_DV], p1[:, 0:RT_DV], AF.Copy, [p1k], [ok])
                    k.stt(o[:, 0:RT_DV], p2[:, 0:RT_DV], decv[:, 2 * di, col:col + 1], o[:, 0:RT_DV], ALU.mult, ALU.add, [p2k, cbk, ok], [ok])
                    if di == 1:
                        pv, pvk = EP.next()
                        k.load(pv[:, 0:RT_DV], pvk, self.OACC, t0, t0 + CH, h * RT_DV, (h + 1) * RT_DV)
                        k.tt(o[:, 0:RT_DV], o[:, 0:RT_DV], pv[:, 0:RT_DV], ALU.add, [ok, pvk], [ok], eng="gpsimd")
                    k.store(self.OACC, t0, t0 + CH, h * RT_DV, (h + 1) * RT_DV, o[:, 0:RT_DV], ok)
                    k.ts(kv[:, 768:1024], kv[:, 0:256], decv[:, 2 * di + 1, col:col + 1], None, ALU.mult, None, [kvk, cbk], [kvk])
                    for dc in range(2):
                        p3, p3k = PS.next()
                        k.mm(p3[:, 0:RT_DV], p3k, kv[:, 768 + dc * 128:768 + (dc + 1) * 128], kv[:, 256:768], True, True, [kvk])
                        k.stt(st[:, dc, :], st[:, dc, :], cdec[:, col:col + 1], p3[:, 0:RT_DV], ALU.mult, ALU.add, [stk, cbk, p3k], [stk])
            EP.unpin(qTk); EP.unpin(kTk); EP.unpin(cTk)
        gg, ggk = EP.pin(); gb, gbk = EP.pin()
        k.dma(gg[:, 0:E], GN_GB.ap[0:1, :].to_broadcast([128, E]), GN_GB.keys(0, 1, 0, E), [ggk])
        k.dma(gb[:, 0:E], GN_GB.ap[1:2, :].to_broadcast([128, E]), GN_GB.keys(1, 2, 0, E), [gbk])
        smp = k.pool("rtsm", [128, 64], 2)
        for t in range(T // 128):
            o, ok = EP.next(); sq, sqk = EP.next(); sm, smk = smp.next()
            k.load(o[:, 0:E], ok, self.OACC, t * 128, t * 128 + 128, 0, E)
            k.act(sq[:, 0:E], o[:, 0:E], AF.Square, [ok], [sqk])
            ov = o[:, 0:E].rearrange("p (h v) -> p h v", h=RT_H); sqv = sq[:, 0:E].rearrange("p (h v) -> p h v", h=RT_H)
            k.R.op("vector", lambda e, sm=sm, ov=ov: e.reduce_sum(out=sm[:, 0:8], in_=ov, axis=AX.X), reads=[ok], writes=[smk])
            k.R.op("vector", lambda e, sm=sm, sqv=sqv: e.reduce_sum(out=sm[:, 8:16], in_=sqv, axis=AX.X), reads=[sqk], writes=[smk])
            k.ts(sm[:, 0:16], sm[:, 0:16], 1.0 / RT_DV, None, ALU.mult, None, [smk], [smk])
            k.tt(sm[:, 16:24], sm[:, 0:8], sm[:, 0:8], ALU.mult, [smk], [smk])
            k.tt(sm[:, 24:32], sm[:, 8:16], sm[:, 16:24], ALU.subtract, [smk], [smk])
            k.ts(sm[:, 24:32], sm[:, 24:32], EPS, None, ALU.add, None, [smk], [smk])
            k.act(sm[:, 24:32], sm[:, 24:32], AF.Sqrt, [smk], [smk])
            k.R.op("vector", lambda e, sm=sm: e.reciprocal(out=sm[:, 32:40], in_=sm[:, 24:32]), reads=[smk], writes=[smk])
            for h in range(RT_H):
                k.ts(ov[:, h, :], ov[:, h, :], sm[:, h:h + 1], sm[:, 32 + h:33 + h], ALU.subtract, ALU.mult, [ok, smk], [ok])
            k.tt(o[:, 0:E], o[:, 0:E], gg[:, 0:E], ALU.mult, [ok, ggk], [ok])
            k.tt(o[:, 0:E], o[:, 0:E], gb[:, 0:E], ALU.add, [ok, gbk], [ok], eng="gpsimd")
            k.store(self.OACC, t * 128, t * 128 + 128, 0, E, o[:, 0:E], ok)
        EP.unpin(ggk); EP.unpin(gbk)
        net.transpose(self.OACC, self.OT, T, E)
        for j in range(NE):
            a, ak = EP.next(); g, gk = EP.next()
            k.load(a[:, 0:T], ak, self.OT, j * 128, j * 128 + 128, 0, T)
            k.load(g[:, 0:T], gk, net.PR, j * 128, j * 128 + 128, 0, T)
            ob, obk = net.EB.next()
            k.tt(ob[:, 0:T], a[:, 0:T], g[:, 0:T], ALU.mult, [ak, gk], [obk])
            k.store(net.S, j * 128, j * 128 + 128, 0, T, ob[:, 0:T], obk)
        net.gemm(net.S, net.WOB, None, E, T, D, epi=net.residual_epi(XS))


DEPTH = 4; NCORES = 8


def build_program():
    nc = bass.Bass("TRN2", target_bir_lowering=False)
    st = contextlib.ExitStack()
    k = K(nc, st); net = Net(k)
    ext = lambda n, shp: DT(nc, n, shp, kind="ExternalInput")
    XIN = ext("xin", [T, D]); CC = ext("cc", [2, D]); CONSTS = ext("consts", [128, 128])
    ADA_W = ext("ada_w", [4 * D, 3 * D]); ADA_B = ext("ada_b", [4, 3 * D]); NORM_G = ext("norm_g", [4, D])
    FNG = ext("final_norm_g", [1, D])
    FLAT = ext("flat", [LAT, 2 * LAT]); FCTX = ext("fctx", [CTX, 2 * CTX]); NEGT = ext("negt", [128, 18])
    DELTA = ext("delta", [1, E]); FEATS = ext("feats", [33, T])
    HY = []
    for j in range(2):
        HY.append(dict(W_IN=ext(f"hy_w_in{j}", [D, 4 * E]), CONV_WB=ext(f"hy_conv_wb{j}", [4, 3 * E]), HYF=ext(f"hy_f{j}", [4, 64]),
                       F_W1=ext(f"hy_f_w1{j}", [33, 64]), F_W2=ext(f"hy_f_w2{j}", [64, 64]), F_W3=ext(f"hy_f_w3{j}", [64, 4 * E]),
                       SKIP=ext(f"hy_skip{j}", [2, E]), W_OUT=ext(f"hy_w_out{j}", [E, D])))
    CF = dict(W_IN=ext("cf_w_in", [D, 3 * E]), DW_W=ext("cf_dw_w", [31, E]), DW_B=ext("cf_dw_b", [1, E]),
              LN_G=ext("cf_ln_g", [1, E]), LN_B=ext("cf_ln_b", [1, E]), W_OUT=ext("cf_w_out", [E, D]))
    RCOS = ext("rcos", [LAT, 128]); RSIN = ext("rsin", [LAT, 128]); RMASK = ext("rmask", [128, 512]); RCOLS = ext("rcols", [128, 4])
    RT = dict(W_IN=ext("rt_w_in", [D, 2 * D + 2 * E]), DECAY=ext("rt_decay", [1, 16]), GN_GB=ext("rt_gn_gb", [2, E]),
              W_OUT=ext("rt_w_out", [E, D]))
    XS = DT(nc, "XS", [T, D]); OUT = DT(nc, "out", [LAT, D], kind="ExternalOutput")
    hy = Hyena(net, FLAT, FCTX, NEGT, DELTA, FEATS); rt = Retention(net, RCOS, RSIN, RMASK, RCOLS)
    net.init_consts(CONSTS)
    net.copy_dram(XIN, XS, T, D)
    net.build_mrep(CC)
    for i in range(DEPTH):
        kind, j = i % 3, i // 3
        last = i == DEPTH - 1
        need_ctx = (not last) or kind == 2
        net.adaln(ADA_W, ADA_B, i)
        net.prenorm(XS, NORM_G, i, (T if need_ctx else LAT) // 128)
        if kind == 0:
            hy.layer(XS, need_ctx=need_ctx, **HY[j])
        elif kind == 1:
            net.conformer(XS, CF["W_IN"], CF["DW_W"], CF["DW_B"], CF["LN_G"], CF["LN_B"], CF["W_OUT"], need_ctx=need_ctx)
        else:
            rt.layer(XS, RT["W_IN"], RT["DECAY"], RT["GN_GB"], RT["W_OUT"])
    net.final_norm(XS, FNG, OUT)
    k.R.final_wait("sync", [k.R.last_w[key] for key in OUT.keys(0, LAT, 0, D)])
    k.R.emit(nc, None)
    st.close()
    return nc


def kernel(x, c, ctx, c_ctx, ada_w, ada_b, norm_g, final_norm_g,
           hy_w_in, hy_conv_w, hy_conv_b, hy_f_w1, hy_f_b1, hy_f_fr1, hy_f_w2, hy_f_b2,
           hy_f_fr2, hy_f_w3, hy_skip, hy_w_out,
           cf_w_in, cf_dw_w, cf_dw_b, cf_ln_g, cf_ln_b, cf_w_out,
           rt_w_in, rt_decay_logit, rt_gn_g, rt_gn_b, rt_w_out):
    f = lambda a: np.ascontiguousarray(np.asarray(a), dtype=np.float32)
    x, c, ctx, c_ctx = f(x), f(c), f(ctx), f(c_ctx)
    shared = {"consts": np.eye(128, dtype=np.float32), "ada_w": f(ada_w).reshape(4 * D, 3 * D), "ada_b": f(ada_b),
              "norm_g": f(norm_g), "final_norm_g": f(final_norm_g).reshape(1, D),
              "cf_w_in": f(cf_w_in)[0], "cf_dw_w": f(cf_dw_w)[0], "cf_dw_b": f(cf_dw_b).reshape(1, E),
              "cf_ln_g": f(cf_ln_g).reshape(1, E), "cf_ln_b": f(cf_ln_b).reshape(1, E), "cf_w_out": f(cf_w_out)[0],
              "rt_w_in": f(rt_w_in)[0], "rt_decay": f(rt_decay_logit)[0].reshape(1, 16),
              "rt_gn_gb": np.stack([f(rt_gn_g)[0], f(rt_gn_b)[0]]), "rt_w_out": f(rt_w_out)[0]}
    for j in range(2):
        shared.update({f"hy_w_in{j}": f(hy_w_in)[j], f"hy_conv_wb{j}": np.concatenate([f(hy_conv_w)[j], f(hy_conv_b)[j][None]], 0),
                       f"hy_f{j}": np.stack([f(hy_f_b1)[j], f(hy_f_fr1)[j], f(hy_f_b2)[j], f(hy_f_fr2)[j]]),
                       f"hy_f_w1{j}": f(hy_f_w1)[j], f"hy_f_w2{j}": f(hy_f_w2)[j], f"hy_f_w3{j}": f(hy_f_w3)[j],
                       f"hy_skip{j}": f(hy_skip)[j], f"hy_w_out{j}": f(hy_w_out)[j]})
    shared.update(hyena_consts()); shared.update(ret_consts())
    shared = {k_: np.ascontiguousarray(v, dtype=np.float32) for k_, v in shared.items()}
    in_maps = []
    for b in range(NCORES):
        m = dict(shared)
        m["xin"] = np.ascontiguousarray(np.concatenate([x[b], ctx[b]], 0))
        m["cc"] = np.ascontiguousarray(np.stack([c[b], c_ctx]))
        in_maps.append(m)
    nc = build_program()
    res = run_bass_kernel_spmd(nc, in_maps, core_ids=list(range(NCORES)))
    return np.stack([np.asarray(res.results[b]["out"], dtype=np.float32) for b in range(NCORES)], 0)
```
